# Optimizing a Trainium2 kernel written in Bass

```python
import math
import jax
import jax.numpy as jnp
from jax import lax
import numpy as np

D_MODEL = 1024
BATCH = 16
SEQ = 4096
DEPTH = 1
DEC_BATCH = 16
DEC_SEQ = 2048
PAST_LEN = 128

N_META = 16
D_FF = 2816
EPS = 1e-6
H_A = 8
DH_A = 64
DV_A = 2 * DH_A
Q_BLOCK = 128
N_BUCKETS = 32
MAX_DISTANCE = 128
H_R = 16
N_R = 64
C_R = H_R * N_R
R_W = 64
R_A = 64
R_G = 128
LNX_EPS = 64e-5
ATT_QK = H_A * 2 * DH_A
ATT_V = H_A * DV_A
ATT_COLS = 2 * ATT_QK + ATT_V
RW_SIZES = (C_R, C_R, C_R, R_W, R_W, R_A, R_A, R_G)
RW_COLS = sum(RW_SIZES)
GATE_COLS = 2 * D_MODEL
N_IN = ATT_COLS + RW_COLS + GATE_COLS

kernel_name = 'hybrid_diffattn_rwkv7_encoder'


def _split(x, sizes):
    cuts, acc = [], 0
    for s in sizes[:-1]:
        acc += s
        cuts.append(acc)
    return jnp.split(x, cuts, axis=-1)


def _rmsnorm(x, g):
    xf = x.astype(jnp.float32)
    y = xf * lax.rsqrt(jnp.mean(xf * xf, axis=-1, keepdims=True) + EPS)
    return (y * g.astype(jnp.float32)).astype(x.dtype)


def _swiglu(x, w_gate, w_up, w_down):
    return (jax.nn.silu(x @ w_gate) * (x @ w_up)) @ w_down


def _rel_bucket(rel):
    nb = N_BUCKETS // 2
    max_exact = nb // 2
    n = jnp.abs(rel)
    nf = jnp.maximum(n, 1).astype(jnp.float32)
    large = max_exact + (jnp.log(nf / max_exact) / math.log(MAX_DISTANCE / max_exact) * (nb - max_exact)).astype(jnp.int32)
    large = jnp.minimum(large, nb - 1)
    return (rel > 0).astype(jnp.int32) * nb + jnp.where(n < max_exact, n, large)


def _diff_attention(q, k, v, lam, rel_bias):
    B, L = q.shape[0], q.shape[1]
    n_blk = -(-L // Q_BLOCK)
    Lp = n_blk * Q_BLOCK
    qb = jnp.pad(q, ((0, 0), (0, Lp - L), (0, 0), (0, 0), (0, 0)))
    qb = jnp.moveaxis(qb.reshape(B, n_blk, Q_BLOCK, H_A, 2, DH_A), 1, 0)
    k1, k2 = k[..., 0, :], k[..., 1, :]
    k_pos = jnp.arange(L, dtype=jnp.int32)
    scale = DH_A ** -0.5
    table = rel_bias.astype(jnp.float32)

    def block(args):
        q_blk, start = args
        q_pos = start + jnp.arange(Q_BLOCK, dtype=jnp.int32)
        bias = jnp.transpose(table[_rel_bucket(k_pos[None, :] - q_pos[:, None])], (2, 0, 1))
        s1 = jnp.einsum('bqhd,bkhd->bhqk', q_blk[..., 0, :], k1) * scale + bias
        s2 = jnp.einsum('bqhd,bkhd->bhqk', q_blk[..., 1, :], k2) * scale + bias
        p = jax.nn.softmax(s1, axis=-1) - lam * jax.nn.softmax(s2, axis=-1)
        return jnp.einsum('bhqk,bkhd->bqhd', p, v)

    starts = jnp.arange(n_blk, dtype=jnp.int32) * Q_BLOCK
    out = lax.map(block, (qb, starts))
    return jnp.moveaxis(out, 0, 1).reshape(B, Lp, H_A, DV_A)[:, :L]


def _diff_attn_branch(p, l, P):
    B, L = p.shape[0], p.shape[1]
    q, k, v = _split(p.astype(jnp.float32), (ATT_QK, ATT_QK, ATT_V))
    q = q.reshape(B, L, H_A, 2, DH_A)
    k = k.reshape(B, L, H_A, 2, DH_A)
    v = v.reshape(B, L, H_A, DV_A)
    lam_init = 0.8 - 0.6 * math.exp(-0.3 * l)
    lam = (jnp.exp(jnp.sum(P['attn_lambda_q1'][l].astype(jnp.float32) * P['attn_lambda_k1'][l]))
           - jnp.exp(jnp.sum(P['attn_lambda_q2'][l].astype(jnp.float32) * P['attn_lambda_k2'][l])) + lam_init)
    o = _diff_attention(q, k, v, lam, P['rel_bias'])
    o = _rmsnorm(o, P['attn_subln'][l]) * (1.0 - lam_init)
    return o.reshape(B, L, ATT_V) @ P['w_attn_branch'][l]


def _centred_shift_mix(p, mu_prev, mu_next):
    prev = jnp.pad(p, ((0, 0), (1, 0), (0, 0)))[:, :-1]
    nxt = jnp.pad(p, ((0, 0), (0, 1), (0, 0)))[:, 1:]
    return p + mu_prev * (prev - p) + mu_next * (nxt - p)


def _wkv7_scan(r, w, k, v, a, b, reverse):
    def step(S, inp):
        r_t, w_t, k_t, v_t, a_t, b_t = inp
        sa = jnp.einsum('bhvk,bhk->bhv', S, a_t)
        S = S * w_t[:, :, None, :] + sa[..., None] * b_t[:, :, None, :] + v_t[..., None] * k_t[:, :, None, :]
        return S, jnp.einsum('bhvk,bhk->bhv', S, r_t)

    xs = tuple(jnp.moveaxis(t, 1, 0) for t in (r, w, k, v, a, b))
    S0 = jnp.zeros((r.shape[0], H_R, N_R, N_R), jnp.float32)
    _, ys = lax.scan(step, S0, xs, reverse=reverse)
    return jnp.moveaxis(ys, 0, 1)


def _rwkv7_direction(rh, vh, k, kk, wd, ad, w0, w2, a0, a2, k_a, r_k, reverse):
    B, L = k.shape[0], k.shape[1]
    logw = -jax.nn.softplus(-(w0 + jnp.tanh(wd) @ w2)) - 0.5
    decay = jnp.exp(-jnp.exp(logw)).reshape(B, L, H_R, N_R)
    a = jax.nn.sigmoid(a0 + ad @ a2)
    kd = (k * (1.0 + (a - 1.0) * k_a)).reshape(B, L, H_R, N_R)
    ah = a.reshape(B, L, H_R, N_R)
    y = _wkv7_scan(rh, decay, kd, vh, -kk, kk * ah, reverse)
    bonus = jnp.sum(rh * kd * r_k, axis=-1, keepdims=True) * vh
    return y, bonus


def _rwkv7_branch(p, l, P):
    B, L = p.shape[0], p.shape[1]
    p = _centred_shift_mix(p.astype(jnp.float32), P['rw_mu_prev'][l], P['rw_mu_next'][l])
    r, k, v, wd_f, wd_b, ad_f, ad_b, gd = _split(p, RW_SIZES)
    g = jax.nn.sigmoid(gd) @ P['rw_g2'][l]
    kk = (k * P['rw_k_k'][l]).reshape(B, L, H_R, N_R)
    kk = kk / jnp.maximum(jnp.sqrt(jnp.sum(kk * kk, axis=-1, keepdims=True)), 1e-12)
    rh = r.reshape(B, L, H_R, N_R)
    vh = v.reshape(B, L, H_R, N_R)
    y_f, bonus_f = _rwkv7_direction(rh, vh, k, kk, wd_f, ad_f, P['rw_w0'][l, 0], P['rw_w2'][l, 0],
                                    P['rw_a0'][l, 0], P['rw_a2'][l, 0], P['rw_k_a'][l], P['rw_r_k'][l], False)
    y_b, bonus_b = _rwkv7_direction(rh, vh, k, kk, wd_b, ad_b, P['rw_w0'][l, 1], P['rw_w2'][l, 1],
                                    P['rw_a0'][l, 1], P['rw_a2'][l, 1], P['rw_k_a'][l], P['rw_r_k'][l], True)
    y = y_f + y_b
    mu = jnp.mean(y, axis=-1, keepdims=True)
    var = jnp.mean(jnp.square(y - mu), axis=-1, keepdims=True)
    y = ((y - mu) * lax.rsqrt(var + LNX_EPS)).reshape(B, L, C_R) * P['rw_lnx_w'][l] + P['rw_lnx_b'][l]
    y = y + (bonus_f + bonus_b).reshape(B, L, C_R)
    return (y * g) @ P['w_rw_branch'][l]


def _layer(x, l, P):
    dt = x.dtype
    h = x + (0.5 * _swiglu(_rmsnorm(x, P['ffn1_norm'][l]), P['ffn1_w_gate'][l], P['ffn1_w_up'][l], P['ffn1_w_down'][l])).astype(dt)
    u = _rmsnorm(h, P['mix_norm'][l])
    p_att, p_rw, p_gate = _split(u @ P['w_in'][l], (ATT_COLS, RW_COLS, GATE_COLS))
    y_att = _diff_attn_branch(p_att, l, P)
    y_rw = _rwkv7_branch(p_rw, l, P)
    g_att, g_rw = jnp.split(jax.nn.sigmoid(p_gate.astype(jnp.float32)), 2, axis=-1)
    merged = g_att * y_att + g_rw * y_rw
    h = h + (merged @ P['w_out'][l]).astype(dt)
    h = h + (0.5 * _swiglu(_rmsnorm(h, P['ffn2_norm'][l]), P['ffn2_w_gate'][l], P['ffn2_w_up'][l], P['ffn2_w_down'][l])).astype(dt)
    return h


def _encode(x, P):
    B = x.shape[0]
    meta = jnp.broadcast_to(P['meta_tokens'].astype(x.dtype)[None], (B, N_META, D_MODEL))
    h = jnp.concatenate([meta, x], axis=1)
    for l in range(DEPTH):
        h = _layer(h, l, P)
    return _rmsnorm(h, P['final_norm'])[:, N_META:]


def setup_inputs(seed: int = 0) -> dict:
    key = jax.random.key(seed)
    ks = iter(jax.random.split(key, 48))
    f32 = jnp.float32

    def nrm(shape, scale):
        return jax.random.normal(next(ks), shape, f32) * scale

    def gain(shape):
        return 1.0 + nrm(shape, 0.02)

    def unif(shape, lo, hi):
        return jax.random.uniform(next(ks), shape, f32, lo, hi)

    return {
        'x_prompt': nrm((BATCH, SEQ, D_MODEL), 1.0),
        'x_sample': nrm((DEC_BATCH, DEC_SEQ, D_MODEL), 1.0),
        'meta_tokens': nrm((N_META, D_MODEL), 1.0),
        'rel_bias': nrm((N_BUCKETS, H_A), 0.5),
        'ffn1_norm': gain((DEPTH, D_MODEL)),
        'ffn1_w_gate': nrm((DEPTH, D_MODEL, D_FF), D_MODEL ** -0.5),
        'ffn1_w_up': nrm((DEPTH, D_MODEL, D_FF), D_MODEL ** -0.5),
        'ffn1_w_down': nrm((DEPTH, D_FF, D_MODEL), D_FF ** -0.5),
        'mix_norm': gain((DEPTH, D_MODEL)),
        'w_in': nrm((DEPTH, D_MODEL, N_IN), D_MODEL ** -0.5),
        'attn_lambda_q1': nrm((DEPTH, DH_A), 0.1),
        'attn_lambda_k1': nrm((DEPTH, DH_A), 0.1),
        'attn_lambda_q2': nrm((DEPTH, DH_A), 0.1),
        'attn_lambda_k2': nrm((DEPTH, DH_A), 0.1),
        'attn_subln': gain((DEPTH, DV_A)),
        'w_attn_branch': nrm((DEPTH, ATT_V, D_MODEL), ATT_V ** -0.5),
        'rw_mu_prev': unif((DEPTH, RW_COLS), 0.0, 0.5),
        'rw_mu_next': unif((DEPTH, RW_COLS), 0.0, 0.5),
        'rw_w0': unif((DEPTH, 2, C_R), -6.0, -1.0),
        'rw_w2': nrm((DEPTH, 2, R_W, C_R), 0.5 * R_W ** -0.5),
        'rw_a0': nrm((DEPTH, 2, C_R), 0.5),
        'rw_a2': nrm((DEPTH, 2, R_A, C_R), 0.5 * R_A ** -0.5),
        'rw_g2': nrm((DEPTH, R_G, C_R), R_G ** -0.5),
        'rw_k_k': 0.85 + nrm((DEPTH, C_R), 0.02),
        'rw_k_a': gain((DEPTH, C_R)),
        'rw_r_k': nrm((DEPTH, H_R, N_R), 0.1),
        'rw_lnx_w': gain((DEPTH, C_R)),
        'rw_lnx_b': nrm((DEPTH, C_R), 0.02),
        'w_rw_branch': nrm((DEPTH, C_R, D_MODEL), C_R ** -0.5),
        'w_out': nrm((DEPTH, D_MODEL, D_MODEL), D_MODEL ** -0.5),
        'ffn2_norm': gain((DEPTH, D_MODEL)),
        'ffn2_w_gate': nrm((DEPTH, D_MODEL, D_FF), D_MODEL ** -0.5),
        'ffn2_w_up': nrm((DEPTH, D_MODEL, D_FF), D_MODEL ** -0.5),
        'ffn2_w_down': nrm((DEPTH, D_FF, D_MODEL), D_FF ** -0.5),
        'final_norm': gain((D_MODEL,)),
    }


def reference(x_prompt, x_sample, meta_tokens, rel_bias, ffn1_norm, ffn1_w_gate, ffn1_w_up, ffn1_w_down,
              mix_norm, w_in, attn_lambda_q1, attn_lambda_k1, attn_lambda_q2, attn_lambda_k2, attn_subln,
              w_attn_branch, rw_mu_prev, rw_mu_next, rw_w0, rw_w2, rw_a0, rw_a2, rw_g2, rw_k_k, rw_k_a,
              rw_r_k, rw_lnx_w, rw_lnx_b, w_rw_branch, w_out, ffn2_norm, ffn2_w_gate, ffn2_w_up,
              ffn2_w_down, final_norm):
    P = dict(meta_tokens=meta_tokens, rel_bias=rel_bias, ffn1_norm=ffn1_norm, ffn1_w_gate=ffn1_w_gate,
             ffn1_w_up=ffn1_w_up, ffn1_w_down=ffn1_w_down, mix_norm=mix_norm, w_in=w_in,
             attn_lambda_q1=attn_lambda_q1, attn_lambda_k1=attn_lambda_k1, attn_lambda_q2=attn_lambda_q2,
             attn_lambda_k2=attn_lambda_k2, attn_subln=attn_subln, w_attn_branch=w_attn_branch,
             rw_mu_prev=rw_mu_prev, rw_mu_next=rw_mu_next, rw_w0=rw_w0, rw_w2=rw_w2, rw_a0=rw_a0,
             rw_a2=rw_a2, rw_g2=rw_g2, rw_k_k=rw_k_k, rw_k_a=rw_k_a, rw_r_k=rw_r_k, rw_lnx_w=rw_lnx_w,
             rw_lnx_b=rw_lnx_b, w_rw_branch=w_rw_branch, w_out=w_out, ffn2_norm=ffn2_norm,
             ffn2_w_gate=ffn2_w_gate, ffn2_w_up=ffn2_w_up, ffn2_w_down=ffn2_w_down, final_norm=final_norm)
    y_prompt = _encode(x_prompt, P)
    y_sample = _encode(x_sample, P)
    return (y_prompt, y_sample)
```

```python
import math
from contextlib import ExitStack
import numpy as np
import concourse.bass as bass
import concourse.mybir as mybir
from concourse.bass_utils import run_bass_kernel_spmd

F32 = mybir.dt.float32
BF16 = mybir.dt.bfloat16
AF = mybir.ActivationFunctionType
ALU = mybir.AluOpType
AX = mybir.AxisListType

D = 1024
DFF = 2816
NF = DFF // 128
NMETA = 16
EPS = 1e-6
LNX_EPS = 64e-5
NIN = 8576
EP = 30000
KDMA = 8


class Buf:
    __slots__ = ("w", "weng", "r", "excl")

    def __init__(self, excl=False):
        self.w = None
        self.weng = None
        self.r = {}
        self.excl = excl


class Sched:
    def __init__(self, nc, st):
        self.nc = nc
        self.engs = {"pe": nc.tensor, "act": nc.scalar, "dve": nc.vector, "pool": nc.gpsimd, "sp": nc.sync}
        self.cnt = {e: 0 for e in self.engs}
        self.seen = {e: {} for e in self.engs}
        self.sems = {}
        self.st = st
        self.dqi = {"sp": 0, "pool": 0}
        self.last = {}

    def sem(self, key):
        if key not in self.sems:
            name = "s_" + "_".join(str(x) for x in key)
            self.sems[key] = self.st.enter_context(self.nc.semaphore(name))
        return self.sems[key]

    def _deps(self, e, reads, writes):
        deps = {}

        def add(tok):
            if tok is None:
                return
            k, v = tok
            if deps.get(k, 0) < v:
                deps[k] = v

        for b in reads:
            add(b.w)
            if b.excl:
                for src, tok in b.r.items():
                    if src != e:
                        add(tok)
        for b in writes:
            if not (e == "pe" and b.weng == "pe"):
                add(b.w)
            for tok in b.r.values():
                add(tok)
        return deps

    def _wait(self, e, deps):
        seen = self.seen[e]
        eng = self.engs[e]
        for k, v in deps.items():
            if seen.get(k, 0) < v:
                eng.wait_ge(self.sem(k), v)
                seen[k] = v

    def _mark(self, src, tok, e, reads, writes):
        self.last[src] = tok
        for b in reads:
            b.r[src] = tok
        for b in writes:
            b.w = tok
            b.weng = e
            b.r = {}

    def emit(self, e, fn, reads=(), writes=()):
        self._wait(e, self._deps(e, reads, writes))
        ins = fn(self.engs[e])
        c = self.cnt[e]
        self.cnt[e] += 1
        k = (e, c // EP)
        v = c % EP + 1
        ins.then_inc(self.sem(k), 1)
        self._mark(e, (k, v), e, reads, writes)

    def dma(self, q, out, in_, reads=(), writes=(), **kw):
        deps = self._deps("dma", reads, writes)
        i = self.dqi[q]
        self.dqi[q] += 1
        slot = i % KDMA
        k = ("d", q, slot)
        v = 16 * (i // KDMA + 1)
        if i >= KDMA:
            if deps.get(k, 0) < v - 16:
                deps[k] = v - 16
        self._wait(q, deps)
        ins = self.engs[q].dma_start(out=out, in_=in_, **kw)
        ins.then_inc(self.sem(k), 16)
        self._mark(k, (k, v), "dma", reads, writes)

    def barrier(self):
        deps = {}
        for tok in self.last.values():
            k, v = tok
            if deps.get(k, 0) < v:
                deps[k] = v
        for e in self.engs:
            self._wait(e, deps)


def blocks_of(t0, T):
    out = []
    o = 0
    while o < T:
        n = min(128, T - o)
        out.append((t0 + o, o, n))
        o += n
    return out


def tiles_of(L, TT=512):
    out = []
    t = 0
    while t < L:
        T = min(TT, L - t)
        out.append((t, T))
        t += T
    return out


def build_nc(S0, S1, debug=False, phases="ABCD"):
    nc = bass.Bass("TRN2", target_bir_lowering=False)
    seqS = [S0, S0, S1, S1]
    seqL = [s + NMETA for s in seqS]
    NS = len(seqS)

    def din(name, shape):
        return nc.dram_tensor(name, list(shape), F32, kind="ExternalInput").ap()

    def dscr(name, shape, dt=F32):
        kind = "ExternalOutput" if (debug and name.startswith("dbg_")) else "Internal"
        return nc.dram_tensor(name, list(shape), dt, kind=kind).ap()

    x_in = [din("x_prompt", (2, S0, D)), din("x_sample", (2, S1, D))]
    y_out = [nc.dram_tensor("y_prompt", [2, S0, D], F32, kind="ExternalOutput").ap(),
             nc.dram_tensor("y_sample", [2, S1, D], F32, kind="ExternalOutput").ap()]

    def xsrc(s):
        return x_in[s // 2][s % 2]

    def ydst(s):
        return y_out[s // 2][s % 2]

    meta = din("meta_tokens", (NMETA, D))
    rel_bias = din("rel_bias", (32, 8))
    W = {}
    for nm, shp in [("ffn1_norm", (1, D)), ("ffn1_w_gate", (1, D, DFF)), ("ffn1_w_up", (1, D, DFF)),
                    ("ffn1_w_down", (1, DFF, D)), ("mix_norm", (1, D)), ("w_in", (1, D, NIN)),
                    ("attn_lambda_q1", (1, 64)), ("attn_lambda_k1", (1, 64)), ("attn_lambda_q2", (1, 64)),
                    ("attn_lambda_k2", (1, 64)), ("attn_subln", (1, 128)), ("w_attn_branch", (1, D, D)),
                    ("rw_mu_prev", (1, 3456)), ("rw_mu_next", (1, 3456)), ("rw_w0", (1, 2, D)),
                    ("rw_w2", (1, 2, 64, D)), ("rw_a0", (1, 2, D)), ("rw_a2", (1, 2, 64, D)),
                    ("rw_g2", (1, 128, D)), ("rw_k_k", (1, D)), ("rw_k_a", (1, D)), ("rw_r_k", (1, 16, 64)),
                    ("rw_lnx_w", (1, D)), ("rw_lnx_b", (1, D)), ("w_rw_branch", (1, D, D)), ("w_out", (1, D, D)),
                    ("ffn2_norm", (1, D)), ("ffn2_w_gate", (1, D, DFF)), ("ffn2_w_up", (1, D, DFF)),
                    ("ffn2_w_down", (1, DFF, D)), ("final_norm", (D,))]:
        W[nm] = din(nm, shp)
    c_ident = din("c_ident", (128, 128))
    c_bk = din("c_bk", (128, 896))
    c_masks = din("c_masks", (128, 4, 128))
    c_istack = din("c_istack", (128, 64))
    c_bones = din("c_bones", (128, 128))

    WGU = [dscr(f"wgu{i}", (NF, 128, 2, 8, 128), BF16) for i in range(2)]
    WD = [dscr(f"wd{i}", (2, 128, NF, 512), BF16) for i in range(2)]
    WINS = dscr("wins", (67, 128, 8, 128), BF16)
    WVS = dscr("wvs", (2, 128, 8, 512), BF16)
    WABS = dscr("wabs", (8, 128, 8, 128), BF16)
    WRBS = dscr("wrbs", (8, 128, 8, 128), BF16)
    WOS = dscr("wos", (2, 128, 8, 512), BF16)
    Hs = [dscr(f"dbg_h{s}", (seqL[s], D)) for s in range(NS)]
    QTs = [dscr(f"dbg_qt{s}", (8, 128, seqL[s]), BF16) for s in range(NS)]
    KTs = [dscr(f"dbg_kt{s}", (8, 128, seqL[s]), BF16) for s in range(NS)]
    Vs = [dscr(f"dbg_v{s}", (seqL[s], D), BF16) for s in range(NS)]
    RWs = [dscr(f"dbg_rw{s}", (27, 128, seqL[s])) for s in range(NS)]
    GTs = [dscr(f"dbg_gt{s}", (16, 128, seqL[s])) for s in range(NS)]
    OTs = [dscr(f"dbg_ot{s}", (D, seqL[s]), BF16) for s in range(NS)]
    YSs = [[dscr(f"dbg_y{d}{s}", (seqL[s], D)) for s in range(NS)] for d in range(2)]
    BNs = [[dscr(f"dbg_bn{d}{s}", (seqL[s], D)) for s in range(NS)] for d in range(2)]

    with ExitStack() as st:
        S = Sched(nc, st)

        uid = [0]

        def sb(name, shape, dt=F32, stack=st):
            uid[0] += 1
            return stack.enter_context(nc.sbuf_tensor(f"{name}_{uid[0]}", list(shape), dt))

        psT = [st.enter_context(nc.psum_tensor(f"psT{i}", [128, 1024], BF16)) for i in range(2)]
        psG = [st.enter_context(nc.psum_tensor(f"psG{i}", [128, 512], F32)) for i in range(4)]
        psD = [st.enter_context(nc.psum_tensor(f"psD{i}", [128, 512], F32)) for i in range(2)]
        bT = [Buf(True) for _ in range(2)]
        bG = [Buf(True) for _ in range(4)]
        bD = [Buf(True) for _ in range(2)]
        rr = {"T": 0, "G": 0, "D": 0}

        def nxt(kind, n):
            i = rr[kind]
            rr[kind] = (i + 1) % n
            return i

        ident = sb("ident", (128, 128), BF16)
        b_ident = Buf()
        S.dma("pool", ident[:], c_ident[:, :], writes=[b_ident])
        epsT = sb("epsT", (128, 2))
        b_eps = Buf()
        S.emit("pool", lambda e: e.memset(epsT[:, 0:1], EPS), writes=[b_eps])
        S.emit("pool", lambda e: e.memset(epsT[:, 1:2], LNX_EPS), writes=[b_eps])
        gains = sb("gains", (128, 4, 8))
        b_gains = Buf()
        for i, nm in enumerate(["ffn1_norm", "mix_norm", "ffn2_norm"]):
            S.dma("sp", gains[:, i, :], W[nm][0].rearrange("(c p) -> p c", p=128), writes=[b_gains],
                  allow_slow_non_contiguous=True)
        fin_g = sb("fin_g", (128, D))
        b_fin = Buf()
        S.dma("sp", fin_g[:], W["final_norm"].partition_broadcast(128), writes=[b_fin])

        tb = sb("tb", (128, 256))
        b_tb = Buf()
        S.dma("sp", tb[:], rel_bias.rearrange("b h -> (b h)").partition_broadcast(128), writes=[b_tb])
        lamv = sb("lamv", (128, 4, 64))
        b_lamv = Buf()
        for i, nm in enumerate(["attn_lambda_q1", "attn_lambda_k1", "attn_lambda_q2", "attn_lambda_k2"]):
            S.dma("sp", lamv[:, i, :], W[nm][0].partition_broadcast(128), writes=[b_lamv])
        lams = sb("lams", (128, 8))
        b_lams = Buf()
        for i in range(2):
            S.emit("dve", lambda e: e.tensor_tensor(out=lamv[:, 2 * i, :], in0=lamv[:, 2 * i, :], in1=lamv[:, 2 * i + 1, :],
                                                    op=ALU.mult), reads=[b_lamv], writes=[b_lamv])
            S.emit("dve", lambda e: e.reduce_sum(out=lams[:, i:i + 1], in_=lamv[:, 2 * i, :], axis=AX.X),
                   reads=[b_lamv], writes=[b_lams])
        S.emit("act", lambda e: e.activation(out=lams[:, 2:4], in_=lams[:, 0:2], func=AF.Exp), reads=[b_lams], writes=[b_lams])
        S.emit("dve", lambda e: e.tensor_tensor(out=lams[:, 4:5], in0=lams[:, 3:4], in1=lams[:, 2:3], op=ALU.subtract),
               reads=[b_lams], writes=[b_lams])
        LAM_INIT = 0.8 - 0.6 * math.exp(-0.3 * 0)
        S.emit("dve", lambda e: e.tensor_scalar(out=lams[:, 5:6], in0=lams[:, 4:5], scalar1=-LAM_INIT, scalar2=None, op0=ALU.add),
               reads=[b_lams], writes=[b_lams])
        gsub = sb("gsub", (128, 1))
        b_gsub = Buf()
        S.dma("sp", gsub[:], W["attn_subln"][0].rearrange("(p o) -> p o", o=1), writes=[b_gsub])
        S.emit("pool", lambda e: e.tensor_scalar(out=gsub[:], in0=gsub[:], scalar1=1.0 - LAM_INIT, scalar2=None, op0=ALU.mult),
               reads=[b_gsub], writes=[b_gsub])

        masks = sb("masks", (128, 4, 128))
        b_masks = Buf()
        S.dma("sp", masks[:], c_masks[:, :, :], writes=[b_masks])
        m12 = [sb(f"m12_{d}", (128, 256)) for d in range(2)]
        m345 = [sb(f"m345_{d}", (128, 384)) for d in range(2)]
        b_mm = Buf()
        for d in range(2):
            lo, up, loi, upi = (0, 1, 2, 3) if d == 0 else (1, 0, 3, 2)
            for (dst, off, mi) in [(m12[d], 0, lo), (m12[d], 128, up), (m345[d], 0, up), (m345[d], 128, upi), (m345[d], 256, upi)]:
                S.emit("pool", lambda e: e.tensor_copy(out=dst[:, off:off + 128], in_=masks[:, mi, :]), reads=[b_masks], writes=[b_mm])
        istack = sb("istack", (128, 64), BF16)
        bones = sb("bones", (128, 128), BF16)
        onesb = sb("onesb", (128, 1), BF16)
        b_rc = Buf()
        S.dma("pool", istack[:], c_istack[:, :], writes=[b_rc])
        S.dma("pool", bones[:], c_bones[:, :], writes=[b_rc])
        S.emit("pool", lambda e: e.memset(onesb[:], 1.0), writes=[b_rc])
        mus = sb("mus", (128, 3, 27))
        b_mus = Buf()
        for i, nm in enumerate(["rw_mu_prev", "rw_mu_next"]):
            S.dma("sp", mus[:, i, :], W[nm][0].rearrange("(c p) -> p c", p=128), writes=[b_mus], allow_slow_non_contiguous=True)
        S.emit("dve", lambda e: e.tensor_tensor(out=mus[:, 2, :], in0=mus[:, 0, :], in1=mus[:, 1, :], op=ALU.add), reads=[b_mus], writes=[b_mus])
        S.emit("dve", lambda e: e.tensor_scalar(out=mus[:, 2, :], in0=mus[:, 2, :], scalar1=-1.0, scalar2=1.0, op0=ALU.mult, op1=ALU.add),
               reads=[b_mus], writes=[b_mus])
        rwp = sb("rwp", (128, 7, 8))
        b_rwp = Buf()
        for i, src in enumerate([W["rw_w0"][0, 0], W["rw_w0"][0, 1], W["rw_a0"][0, 0], W["rw_a0"][0, 1], W["rw_k_k"][0], W["rw_k_a"][0],
                                 W["rw_r_k"][0].rearrange("h n -> (h n)")]):
            S.dma("sp", rwp[:, i, :], src.rearrange("(c p) -> p c", p=128), writes=[b_rwp], allow_slow_non_contiguous=True)
        w2sb = sb("w2sb", (128, D), BF16)
        a2sb = sb("a2sb", (128, D), BF16)
        g2sb = sb("g2sb", (128, D), BF16)
        b_w2 = Buf()
        S.dma("pool", w2sb[:], W["rw_w2"][0].rearrange("d r c -> (d r) c"), writes=[b_w2])
        S.dma("pool", a2sb[:], W["rw_a2"][0].rearrange("d r c -> (d r) c"), writes=[b_w2])
        S.dma("pool", g2sb[:], W["rw_g2"][0], writes=[b_w2])

        def prep_weights():
            for i, pre in enumerate(["ffn1", "ffn2"]):
                for which, nm in enumerate(["w_gate", "w_up"]):
                    src = W[f"{pre}_{nm}"][0].rearrange("(k p) (f j) -> f p k j", p=128, j=128)
                    for f in range(NF):
                        S.dma("pool", WGU[i][f, :, which, :, :], src[f])
                src = W[f"{pre}_w_down"][0].rearrange("(f p) (h j) -> h p f j", p=128, j=512)
                for h in range(2):
                    for f0 in range(0, NF, 11):
                        S.dma("pool", WD[i][h, :, f0:f0 + 11, :], src[h][:, f0:f0 + 11, :])
            src = W["w_in"][0].rearrange("(k p) (c j) -> c p k j", p=128, j=128)
            for c in range(67):
                if 16 <= c < 24:
                    continue
                S.dma("pool", WINS[c], src[c])
            src = W["w_in"][0][:, 2048:3072].rearrange("(k p) (h j) -> h p k j", p=128, j=512)
            for h in range(2):
                S.dma("pool", WVS[h], src[h])
            for (dst, nm) in [(WABS, "w_attn_branch"), (WRBS, "w_rw_branch")]:
                src = W[nm][0].rearrange("(k p) (c j) -> c p k j", p=128, j=128)
                for c in range(8):
                    S.dma("pool", dst[c], src[c])
            src = W["w_out"][0].rearrange("(k p) (h j) -> h p k j", p=128, j=512)
            for h in range(2):
                S.dma("pool", WOS[h], src[h])

        prep_weights()
        S.barrier()

        def load_x_block(s, t0, n, xt, bx):
            if t0 == 0:
                S.dma("sp", xt[0:NMETA, :], meta[:, :], writes=[bx])
                S.dma("sp", xt[NMETA:n, :], xsrc(s)[0:n - NMETA, :], writes=[bx])
            else:
                S.dma("sp", xt[0:n, :], xsrc(s)[t0 - NMETA:t0 - NMETA + n, :], writes=[bx])

        def rmsnorm_T(P, blks, xts, bxs, gi, outT, boutT):
            for (tg, o, n), xt, bx in zip(blks, xts, bxs):
                S.emit("pool", lambda e: e.memset(P["ss"][:, 0:1], 0.0), writes=[P["b_ss"]])
                S.emit("act", lambda e: e.activation(out=P["junk"][:n, :], in_=xt[:n, :], func=AF.Square,
                                                     accum_out=P["ss"][:n, 0:1]),
                       reads=[bx], writes=[P["b_ss"], P["b_junk"]])
                S.emit("act", lambda e: e.activation(out=P["ss"][:n, 1:2], in_=P["ss"][:n, 0:1], func=AF.Sqrt,
                                                     bias=epsT[:n, 0:1], scale=1.0 / D),
                       reads=[b_eps], writes=[P["b_ss"]])
                S.emit("dve", lambda e: e.reciprocal(out=P["ss"][:n, 2:3], in_=P["ss"][:n, 1:2]), writes=[P["b_ss"]])
                bi = o // 128
                S.emit("act", lambda e: e.activation(out=P["xn"][bi][:n, :], in_=xt[:n, :], func=AF.Copy,
                                                     scale=P["ss"][:n, 2:3]),
                       reads=[bx, P["b_ss"]], writes=[P["b_xn"][bi]])
            T = sum(b[2] for b in blks)
            for cp in range(4):
                bank = nxt("T", 2)
                for cc in range(2):
                    c = cp * 2 + cc
                    for (tg, o, n) in blks:
                        bi = o // 128
                        S.emit("pe", lambda e: e.transpose(psT[bank][:, cc * 512 + o:cc * 512 + o + n],
                                                           P["xn"][bi][:n, c * 128:(c + 1) * 128], ident[:n, :n]),
                               reads=[P["b_xn"][bi], b_ident], writes=[bT[bank]])
                for cc in range(2):
                    c = cp * 2 + cc
                    S.emit("dve", lambda e: e.tensor_scalar(out=outT[c][:, :T], in0=psT[bank][:, cc * 512:cc * 512 + T],
                                                            scalar1=gains[:, gi, c:c + 1], scalar2=None, op0=ALU.mult),
                           reads=[bT[bank], b_gains], writes=[boutT[c]])

        def ffn(P, fi, blks, xts, bxs, gi):
            T = sum(b[2] for b in blks)
            rmsnorm_T(P, blks, xts, bxs, gi, P["xT"], P["b_xT"])
            for f in range(NF):
                ws = nxt("wgu", len(P["wgu"]))
                S.dma("sp", P["wgu"][ws][:], WGU[fi][f], writes=[P["b_wgu"][ws]])
                banks = [nxt("G", 4), nxt("G", 4)]
                for which in range(2):
                    for k in range(8):
                        S.emit("pe", lambda e: e.matmul(psG[banks[which]][:, :T], P["wgu"][ws][:, which, k, :],
                                                        P["xT"][k][:, :T], start=(k == 0), stop=(k == 7)),
                               reads=[P["b_wgu"][ws], P["b_xT"][k]], writes=[bG[banks[which]]])
                sgi = nxt("sg", 2)
                S.emit("act", lambda e: e.activation(out=P["sg"][sgi][:, :T], in_=psG[banks[0]][:, :T], func=AF.Silu),
                       reads=[bG[banks[0]]], writes=[P["b_sg"][sgi]])
                S.emit("dve", lambda e: e.tensor_tensor(out=P["act"][f][:, :T], in0=P["sg"][sgi][:, :T],
                                                        in1=psG[banks[1]][:, :T], op=ALU.mult),
                       reads=[P["b_sg"][sgi], bG[banks[1]]], writes=[P["b_act"][f]])
            for h in range(2):
                S.dma("sp", P["wd"][:], WD[fi][h], writes=[P["b_wd"]])
                for (tg, o, n), xt, bx in zip(blks, xts, bxs):
                    bank = nxt("D", 2)
                    for f in range(NF):
                        S.emit("pe", lambda e: e.matmul(psD[bank][:n, :], P["act"][f][:, o:o + n], P["wd"][:, f, :],
                                                        start=(f == 0), stop=(f == NF - 1)),
                               reads=[P["b_act"][f], P["b_wd"]], writes=[bD[bank]])
                    S.emit("dve", lambda e: e.scalar_tensor_tensor(out=xt[:n, h * 512:(h + 1) * 512], in0=psD[bank][:n, :],
                                                                   scalar=0.5, in1=xt[:n, h * 512:(h + 1) * 512],
                                                                   op0=ALU.mult, op1=ALU.add),
                           reads=[bD[bank], bx], writes=[bx])

        def phase_A(s):
            L = seqL[s]
            with ExitStack() as pst:
                P = {}
                P["ss"] = sb("a_ss", (128, 4), stack=pst)
                P["b_ss"] = Buf()
                P["junk"] = sb("a_junk", (128, D), BF16, stack=pst)
                P["b_junk"] = Buf()
                P["xn"] = [sb(f"a_xn{i}", (128, D), BF16, stack=pst) for i in range(4)]
                P["b_xn"] = [Buf() for _ in range(4)]
                P["xT"] = [sb(f"a_xT{i}", (128, 512), BF16, stack=pst) for i in range(8)]
                P["b_xT"] = [Buf() for _ in range(8)]
                P["uT"] = [sb(f"a_uT{i}", (128, 512), BF16, stack=pst) for i in range(8)]
                P["b_uT"] = [Buf() for _ in range(8)]
                P["wgu"] = [sb(f"a_wgu{i}", (128, 2, 8, 128), BF16, stack=pst) for i in range(4)]
                P["b_wgu"] = [Buf() for _ in range(4)]
                P["sg"] = [sb(f"a_sg{i}", (128, 512), stack=pst) for i in range(2)]
                P["b_sg"] = [Buf() for _ in range(2)]
                P["act"] = [sb(f"a_act{i}", (128, 512), BF16, stack=pst) for i in range(NF)]
                P["b_act"] = [Buf() for _ in range(NF)]
                P["wd"] = sb("a_wd", (128, NF, 512), BF16, stack=pst)
                P["b_wd"] = Buf()
                rr["wgu"] = 0
                rr["sg"] = 0
                xsets = [[sb(f"a_x{j}_{i}", (128, D), stack=pst) for i in range(4)] for j in range(2)]
                bxsets = [[Buf() for _ in range(4)] for j in range(2)]
                win = [sb(f"a_win{i}", (128, 8, 128), BF16, stack=pst) for i in range(6)]
                b_win = [Buf() for _ in range(6)]
                wv = sb("a_wv", (128, 8, 512), BF16, stack=pst)
                b_wv = Buf()
                stf = [sb(f"a_stf{i}", (128, 512), stack=pst) for i in range(4)]
                b_stf = [Buf() for _ in range(4)]
                stb = [sb(f"a_stb{i}", (128, 512), BF16, stack=pst) for i in range(4)]
                b_stb = [Buf() for _ in range(4)]
                rr["win"] = 0
                rr["stf"] = 0
                rr["stb"] = 0
                tlist = tiles_of(L)

                def load_tile(ti):
                    t0_, T_ = tlist[ti]
                    bl = blocks_of(t0_, T_)
                    for (tg, o, n), xt, bx in zip(bl, xsets[ti % 2], bxsets[ti % 2]):
                        load_x_block(s, tg, n, xt, bx)

                load_tile(0)
                for ti, (t0, T) in enumerate(tlist):
                    blks = blocks_of(t0, T)
                    xts = xsets[ti % 2][:len(blks)]
                    bxs = bxsets[ti % 2][:len(blks)]
                    if ti + 1 < len(tlist):
                        load_tile(ti + 1)
                    ffn(P, 0, blks, xts, bxs, 0)
                    for (tg, o, n), xt, bx in zip(blks, xts, bxs):
                        S.dma("pool", Hs[s][tg:tg + n, :], xt[:n, :], reads=[bx])
                    rmsnorm_T(P, blks, xts, bxs, 1, P["uT"], P["b_uT"])
                    for c in list(range(0, 16)) + list(range(24, 67)):
                        ws = nxt("win", 6)
                        S.dma("sp", win[ws][:], WINS[c], writes=[b_win[ws]])
                        bank = nxt("G", 4)
                        for k in range(8):
                            S.emit("pe", lambda e: e.matmul(psG[bank][:, :T], win[ws][:, k, :], P["uT"][k][:, :T],
                                                            start=(k == 0), stop=(k == 7)),
                                   reads=[b_win[ws], P["b_uT"][k]], writes=[bG[bank]])
                        if c < 16:
                            si = nxt("stb", 4)
                            S.emit("act", lambda e: e.copy(out=stb[si][:, :T], in_=psG[bank][:, :T]),
                                   reads=[bG[bank]], writes=[b_stb[si]])
                            dst = (QTs if c < 8 else KTs)[s][c % 8, :, t0:t0 + T]
                            S.dma("pool", dst, stb[si][:, :T], reads=[b_stb[si]])
                        else:
                            si = nxt("stf", 4)
                            if c < 51:
                                S.emit("act", lambda e: e.copy(out=stf[si][:, :T], in_=psG[bank][:, :T]),
                                       reads=[bG[bank]], writes=[b_stf[si]])
                                dst = RWs[s][c - 24, :, t0:t0 + T]
                            else:
                                S.emit("act", lambda e: e.activation(out=stf[si][:, :T], in_=psG[bank][:, :T],
                                                                     func=AF.Sigmoid),
                                       reads=[bG[bank]], writes=[b_stf[si]])
                                dst = GTs[s][c - 51, :, t0:t0 + T]
                            S.dma("pool", dst, stf[si][:, :T], reads=[b_stf[si]])
                    for h in range(2):
                        S.dma("sp", wv[:], WVS[h], writes=[b_wv])
                        for (tg, o, n) in blks:
                            bank = nxt("D", 2)
                            for k in range(8):
                                S.emit("pe", lambda e: e.matmul(psD[bank][:n, :], P["uT"][k][:, o:o + n], wv[:, k, :],
                                                                start=(k == 0), stop=(k == 7)),
                                       reads=[P["b_uT"][k], b_wv], writes=[bD[bank]])
                            si = nxt("stb", 4)
                            S.emit("act", lambda e: e.copy(out=stb[si][:n, :], in_=psD[bank][:n, :]),
                                   reads=[bD[bank]], writes=[b_stb[si]])
                            S.dma("pool", Vs[s][tg:tg + n, h * 512:(h + 1) * 512], stb[si][:n, :], reads=[b_stb[si]])
                S.barrier()

        for s in range(NS):
            if "A" in phases:
                phase_A(s)


        def phase_B(s):
            L = seqL[s]
            TQ = 384
            qtiles = tiles_of(L, TQ)
            kblocks = tiles_of(L, 128)
            nkb = len(kblocks)
            nfull = L // 128
            ntail = L - nfull * 128
            accb = [psD[0][:], psD[1][:], psT[1][:].bitcast(F32)]
            bacc = [bD[0], bD[1], bT[1]]
            with ExitStack() as pst:
                kt = [sb("b_kt", (128, L), BF16, pst) for _ in range(2)]
                qt = [sb("b_qt", (128, L), BF16, pst) for _ in range(2)]
                va = [sb("b_va", (128, nkb, 129), BF16, pst) for _ in range(2)]
                b_kqv = [Buf(), Buf()]
                PT = [sb("b_PT", (128, TQ), BF16, pst) for _ in range(4)]
                b_PT = [Buf() for _ in range(4)]
                tmpS = [sb("b_tmpS", (128, TQ), F32, pst) for _ in range(2)]
                b_tmpS = [Buf() for _ in range(2)]
                rd = sb("b_rd", (128, 8), F32, pst)
                b_rd = Buf()
                o1 = sb("b_o1", (128, 128), F32, pst)
                oo = sb("b_oo", (128, 128), F32, pst)
                junk = sb("b_junk", (128, 128), F32, pst)
                on = sb("b_on", (128, 128), BF16, pst)
                b_o = Buf()
                ost = [sb("b_ost", (128, TQ), BF16, pst) for _ in range(2)]
                b_ost = [Buf(), Buf()]
                rr["PT"] = 0
                rr["tmpS"] = 0
                rr["ost"] = 0
                for i in range(2):
                    S.emit("pool", lambda e: e.memset(va[i][:, :, 128:129], 1.0), writes=[b_kqv[i]])
                for h in range(8):
                    i = h % 2
                    S.dma("sp", kt[i][:], KTs[s][h], writes=[b_kqv[i]])
                    S.dma("sp", qt[i][:], QTs[s][h], writes=[b_kqv[i]])
                    if nfull:
                        S.dma("sp", va[i][:, 0:nfull, 0:128],
                              Vs[s][0:nfull * 128, h * 128:(h + 1) * 128].rearrange("(kb p) c -> p kb c", p=128),
                              writes=[b_kqv[i]])
                    if ntail:
                        S.dma("sp", va[i][0:ntail, nfull, 0:128], Vs[s][nfull * 128:L, h * 128:(h + 1) * 128],
                              writes=[b_kqv[i]])
                    for (qt0, TQn) in qtiles:
                        subs = blocks_of(qt0, TQn)

                        def qk(kbi):
                            k0, nk = kblocks[kbi]
                            banks = [nxt("G", 4), nxt("G", 4)]
                            for c in range(2):
                                S.emit("pe", lambda e: e.matmul(psG[banks[c]][:nk, :TQn], kt[i][c * 64:(c + 1) * 64, k0:k0 + nk],
                                                                qt[i][c * 64:(c + 1) * 64, qt0:qt0 + TQn], start=True, stop=True),
                                       reads=[b_kqv[i]], writes=[bG[banks[c]]])
                            return banks

                        nb = qk(0)
                        for kbi, (k0, nk) in enumerate(kblocks):
                            banks = nb
                            if kbi + 1 < nkb:
                                nb = qk(kbi + 1)
                            d = k0 - qt0
                            relmin = d - (TQn - 1)
                            relmax = d + nk - 1
                            near = not (relmin >= 91 or relmax <= -91)
                            for c in range(2):
                                pi = nxt("PT", 4)
                                if near:
                                    ti = nxt("tmpS", 2)
                                    m0 = 384 - d
                                    S.emit("dve", lambda e: e.scalar_tensor_tensor(
                                        out=tmpS[ti][:nk, :TQn], in0=psG[banks[c]][:nk, :TQn], scalar=0.125,
                                        in1=Gs[h][:nk, m0:m0 + TQn], op0=ALU.mult, op1=ALU.add),
                                        reads=[bG[banks[c]], b_Gs[h]], writes=[b_tmpS[ti]])
                                    S.emit("act", lambda e: e.activation(out=PT[pi][:nk, :TQn], in_=tmpS[ti][:nk, :TQn], func=AF.Exp),
                                           reads=[b_tmpS[ti]], writes=[b_PT[pi]])
                                else:
                                    col = (31 if relmin >= 91 else 15) * 8 + h
                                    S.emit("act", lambda e: e.activation(out=PT[pi][:nk, :TQn], in_=psG[banks[c]][:nk, :TQn], func=AF.Exp,
                                                                         scale=0.125, bias=tb[:nk, col:col + 1]),
                                           reads=[bG[banks[c]], b_tb], writes=[b_PT[pi]])
                                for si, (qg, o, nq) in enumerate(subs):
                                    first = (kbi == 0 and c == 0)
                                    last = (kbi == nkb - 1 and c == 1)
                                    S.emit("pe", lambda e: e.matmul(accb[si][:nq, c * 129:(c + 1) * 129], PT[pi][:nk, o:o + nq],
                                                                    va[i][:nk, kbi, :], start=first, stop=last),
                                           reads=[b_PT[pi], b_kqv[i]], writes=[bacc[si]])
                        oi = nxt("ost", 2)
                        for si, (qg, o, nq) in enumerate(subs):
                            A = accb[si]
                            S.emit("dve", lambda e: e.reciprocal(out=rd[:nq, 0:1], in_=A[:nq, 128:129]), reads=[bacc[si]], writes=[b_rd])
                            S.emit("dve", lambda e: e.reciprocal(out=rd[:nq, 1:2], in_=A[:nq, 257:258]), reads=[bacc[si]], writes=[b_rd])
                            S.emit("dve", lambda e: e.tensor_scalar(out=rd[:nq, 2:3], in0=rd[:nq, 1:2], scalar1=lams[:nq, 5:6],
                                                                    scalar2=None, op0=ALU.mult), reads=[b_rd, b_lams], writes=[b_rd])
                            S.emit("act", lambda e: e.activation(out=o1[:nq, :], in_=A[:nq, 0:128], func=AF.Copy, scale=rd[:nq, 0:1]),
                                   reads=[bacc[si], b_rd], writes=[b_o])
                            S.emit("dve", lambda e: e.scalar_tensor_tensor(out=oo[:nq, :], in0=A[:nq, 129:257], scalar=rd[:nq, 2:3],
                                                                           in1=o1[:nq, :], op0=ALU.mult, op1=ALU.add),
                                   reads=[bacc[si], b_rd, b_o], writes=[b_o])
                            S.emit("pool", lambda e: e.memset(rd[:, 3:4], 0.0), writes=[b_rd])
                            S.emit("act", lambda e: e.activation(out=junk[:nq, :], in_=oo[:nq, :], func=AF.Square, accum_out=rd[:nq, 3:4]),
                                   reads=[b_o], writes=[b_rd, b_o])
                            S.emit("act", lambda e: e.activation(out=rd[:nq, 4:5], in_=rd[:nq, 3:4], func=AF.Sqrt, bias=epsT[:nq, 0:1],
                                                                 scale=1.0 / 128), reads=[b_rd, b_eps], writes=[b_rd])
                            S.emit("dve", lambda e: e.reciprocal(out=rd[:nq, 5:6], in_=rd[:nq, 4:5]), reads=[b_rd], writes=[b_rd])
                            S.emit("act", lambda e: e.activation(out=on[:nq, :], in_=oo[:nq, :], func=AF.Copy, scale=rd[:nq, 5:6]),
                                   reads=[b_rd, b_o], writes=[b_o])
                            S.emit("pe", lambda e: e.transpose(psT[0][:, o:o + nq], on[:nq, :], ident[:nq, :nq]),
                                   reads=[b_o, b_ident], writes=[bT[0]])
                            S.emit("dve", lambda e: e.tensor_scalar(out=ost[oi][:, o:o + nq], in0=psT[0][:, o:o + nq], scalar1=gsub[:, 0:1],
                                                                    scalar2=None, op0=ALU.mult),
                                   reads=[bT[0], b_gsub], writes=[b_ost[oi]])
                        S.dma("pool", OTs[s][h * 128:(h + 1) * 128, qt0:qt0 + TQn], ost[oi][:, :TQn], reads=[b_ost[oi]])
                S.barrier()

        with ExitStack() as gst:
            bk = sb("bk", (128, 896), F32, gst)
            b_bk = Buf()
            S.dma("sp", bk[:], c_bk[:, :], writes=[b_bk])
            Gs = [sb(f"G{h}", (128, 896), F32, gst) for h in range(8)]
            b_Gs = [Buf() for _ in range(8)]
            gm = [sb("gm", (128, 896), F32, gst) for _ in range(2)]
            b_gm = [Buf(), Buf()]
            for h in range(8):
                S.emit("pool", lambda e: e.memset(Gs[h][:], 0.0), writes=[b_Gs[h]])
            for b in range(32):
                mi = b % 2
                S.emit("pool", lambda e: e.tensor_scalar(out=gm[mi][:], in0=bk[:], scalar1=float(b), scalar2=None, op0=ALU.is_equal),
                       reads=[b_bk], writes=[b_gm[mi]])
                for h in range(8):
                    S.emit("dve", lambda e: e.scalar_tensor_tensor(out=Gs[h][:], in0=gm[mi][:], scalar=tb[:, b * 8 + h:b * 8 + h + 1], in1=Gs[h][:],
                                                                   op0=ALU.mult, op1=ALU.add), reads=[b_gm[mi], b_tb], writes=[b_Gs[h]])

            for s in range(NS):
                if "B" in phases:
                    phase_B(s)
            S.barrier()

        def phase_C(s):
            L = seqL[s]
            SEG = 128
            Lp = ((L + 63) // 64) * 64
            segs = tiles_of(Lp, SEG)
            nseg = len(segs)
            allb = [(psG[i][:], bG[i]) for i in range(4)] + [(psD[i][:], bD[i]) for i in range(2)] + [(psT[1][:].bitcast(F32), bT[1])]
            rr["ps"] = 0

            def bank():
                i = nxt("ps", len(allb))
                return allb[i]

            EXC = 2.0 ** 0
            with ExitStack() as pst:
                NG = 8
                def mk(name, shape, dt=F32, n=1):
                    return [sb(name, shape, dt, pst) for _ in range(n)]
                raw = [[mk("c_raw", (128, SEG + 2), F32, 3) for u in range(NG)] for _ in range(2)]
                rawd = [mk("c_rawd", (128, SEG + 2), F32, 2) for _ in range(2)]
                b_raw = [Buf(), Buf()]
                NOPS = 6
                exp_ = [[[sb("c_exp", (128, SEG // 64, 128), BF16, pst) for o in range(NOPS)] for u in range(NG)] for _ in range(2)]
                b_exp = [[Buf() for u in range(NG)] for _ in range(2)]
                pend = [[sb("c_pend", (128, SEG // 64), F32, pst) for u in range(NG)] for _ in range(2)]
                for gb in range(2):
                    for u in range(NG):
                        for o in range(NOPS):
                            S.emit("pool", lambda e: e.memset(exp_[gb][u][o][:], 0.0), writes=[b_exp[gb][u]])
                TN = ["rs", "ks", "vs", "kk", "t1", "t2", "w", "al", "kd", "bb", "P", "d0", "d1", "Wi", "rW", "Wp", "bt", "ktl"]
                Tsets = [{nm: sb("c_" + nm, (128, SEG), F32, pst) for nm in TN} for _ in range(2)]
                Bsets = [{nm: Buf() for nm in TN} for _ in range(2)]
                sqbs = [sb("c_sqb", (128, SEG), BF16, pst) for _ in range(2)]
                b_sqb = [Buf(), Buf()]
                dsh = [[sb("c_dsh", (128, SEG), F32, pst) for _ in range(2)] for _ in range(2)]
                twd = [sb("c_twd", (128, SEG), BF16, pst) for _ in range(2)]
                tad = [sb("c_tad", (128, SEG), BF16, pst) for _ in range(2)]
                b_d = [Buf(), Buf()]
                Sf = [[sb("c_S", (128, 64), F32, pst) for c in range(8)] for d in range(2)]
                Sb = [[sb("c_Sb", (128, 64), BF16, pst) for c in range(8)] for d in range(2)]
                b_S = [[Buf() for c in range(8)] for d in range(2)]
                for d in range(2):
                    for c in range(8):
                        S.emit("pool", lambda e: e.memset(Sf[d][c][:], 0.0), writes=[b_S[d][c]])
                        S.emit("pool", lambda e: e.memset(Sb[d][c][:], 0.0), writes=[b_S[d][c]])
                PP = [[sb("c_PP", (128, 256), BF16, pst) for _ in range(2)] for u in range(NG)]
                b_PP = [[Buf(), Buf()] for u in range(NG)]
                L3 = [sb("c_L3", (128, 384), BF16, pst) for u in range(NG)]
                b_L3 = [Buf() for u in range(NG)]
                TT = [[sb("c_TT", (128, 128), BF16, pst) for _ in range(2)] for u in range(NG)]
                b_TT = [[Buf(), Buf()] for u in range(NG)]
                BK_ = [sb("c_BK", (128, 256), BF16, pst) for u in range(NG)]
                b_BK = [Buf() for u in range(NG)]
                Vst = [sb("c_Vst", (128, 64), BF16, pst) for u in range(NG)]
                coef = [sb("c_coef", (128, 1), F32, pst) for u in range(NG)]
                b_V = [Buf() for u in range(NG)]
                Xb = [sb("c_Xb", (128, 64), BF16, pst) for u in range(NG)]
                Ub = [sb("c_Ub", (128, 64), BF16, pst) for u in range(NG)]
                b_X = [Buf() for u in range(NG)]
                b_U = [Buf() for u in range(NG)]
                Yst = [sb("c_Yst", (128, NG, 64), F32, pst) for _ in range(2)]
                Bst = [sb("c_Bst", (128, NG, 64), F32, pst) for _ in range(2)]
                b_Yst = [Buf(), Buf()]
                b_Bst = [Buf(), Buf()]
                rr["yst"] = 0

                def shift(dst, src, j, n, rb, wb, eng="dve"):
                    S.emit(eng, lambda e: e.tensor_scalar(out=dst[:, :n], in0=src[:, 1:n + 1], scalar1=mus[:, 2, j:j + 1], scalar2=None, op0=ALU.mult),
                           reads=[b_mus] + rb, writes=wb)
                    S.emit(eng, lambda e: e.scalar_tensor_tensor(out=dst[:, :n], in0=src[:, 0:n], scalar=mus[:, 0, j:j + 1], in1=dst[:, :n],
                                                                 op0=ALU.mult, op1=ALU.add), reads=[b_mus] + rb, writes=wb)
                    S.emit(eng, lambda e: e.scalar_tensor_tensor(out=dst[:, :n], in0=src[:, 2:n + 2], scalar=mus[:, 1, j:j + 1], in1=dst[:, :n],
                                                                 op0=ALU.mult, op1=ALU.add), reads=[b_mus] + rb, writes=wb)

                def load_group(gb, d, half, seg0, n):
                    lo = max(seg0 - 1, 0)
                    hi = min(seg0 + n + 1, L)
                    edge = (seg0 - 1 < 0) or (seg0 + n + 1 > L)
                    tiles = [(raw[gb][u][i], i * 8 + half * NG + u) for u in range(NG) for i in range(3)]
                    tiles += [(rawd[gb][0], 24), (rawd[gb][1], 25)]
                    for (t, j) in tiles:
                        if edge:
                            S.emit("pool", lambda e: e.memset(t[:], 0.0), writes=[b_raw[gb]])
                        S.dma("sp", t[:, lo - (seg0 - 1):hi - (seg0 - 1)], RWs[s][j, :, lo:hi], writes=[b_raw[gb]])

                def prep_group(gb, d, half, seg0, n):
                    nch = n // 64
                    npad0 = max(0, min(n, L - seg0))
                    shift(dsh[gb][0], rawd[gb][0], 24, n, [b_raw[gb]], [b_d[gb]])
                    shift(dsh[gb][1], rawd[gb][1], 25, n, [b_raw[gb]], [b_d[gb]])
                    S.emit("act", lambda e: e.activation(out=twd[gb][:, :n], in_=dsh[gb][0][:, :n], func=AF.Tanh), reads=[b_d[gb]], writes=[b_d[gb]])
                    S.emit("act", lambda e: e.copy(out=tad[gb][:, :n], in_=dsh[gb][1][:, :n]), reads=[b_d[gb]], writes=[b_d[gb]])
                    for u in range(NG):
                        c = half * NG + u
                        E = exp_[gb][u]
                        T_ = Tsets[u % 2]
                        bt = Bsets[u % 2]
                        sqb = sqbs[u % 2]
                        bsq = b_sqb[u % 2]
                        bex = b_exp[gb][u]

                        def op(eng, fn, R=(), Wr=(), xr=(), xw=()):
                            S.emit(eng, fn, reads=[bt[x] for x in R] + list(xr), writes=[bt[x] for x in Wr] + list(xw))

                        shift(T_["rs"], raw[gb][u][0], c, n, [b_raw[gb]], [bt["rs"]])
                        shift(T_["ks"], raw[gb][u][1], 8 + c, n, [b_raw[gb]], [bt["ks"]])
                        shift(T_["vs"], raw[gb][u][2], 16 + c, n, [b_raw[gb]], [bt["vs"]])
                        if npad0 < n:
                            for nm in ("rs", "ks", "vs"):
                                op("pool", lambda e: e.memset(T_[nm][:, npad0:n], 0.0), Wr=[nm])
                        yield
                        op("pool", lambda e: e.tensor_scalar(out=T_["kk"][:, :n], in0=T_["ks"][:, :n], scalar1=rwp[:, 4, c:c + 1], scalar2=None,
                                                             op0=ALU.mult), R=["ks"], Wr=["kk"], xr=[b_rwp])
                        op("dve", lambda e: e.tensor_tensor(out=sqb[:, :n], in0=T_["kk"][:, :n], in1=T_["kk"][:, :n], op=ALU.mult), R=["kk"], xw=[bsq])
                        (pa, bpa) = bank()
                        S.emit("pe", lambda e: e.matmul(pa[:, :n], bones[:], sqb[:, :n], start=True, stop=True), reads=[bsq, b_rc], writes=[bpa])
                        op("act", lambda e: e.activation(out=T_["t1"][:, :n], in_=pa[:, :n], func=AF.Sqrt), Wr=["t1"], xr=[bpa])
                        op("dve", lambda e: e.tensor_scalar(out=T_["t1"][:, :n], in0=T_["t1"][:, :n], scalar1=1e-12, scalar2=None, op0=ALU.max),
                           R=["t1"], Wr=["t1"])
                        op("dve", lambda e: e.reciprocal(out=T_["t1"][:, :n], in_=T_["t1"][:, :n]), R=["t1"], Wr=["t1"])
                        op("dve", lambda e: e.tensor_tensor(out=T_["kk"][:, :n], in0=T_["kk"][:, :n], in1=T_["t1"][:, :n], op=ALU.mult),
                           R=["kk", "t1"], Wr=["kk"])
                        yield
                        rows = slice(d * 64, (d + 1) * 64)
                        (pw, bpw) = bank()
                        S.emit("pe", lambda e: e.matmul(pw[:, :n], w2sb[rows, c * 128:(c + 1) * 128], twd[gb][rows, :n], start=True, stop=True),
                               reads=[b_w2, b_d[gb]], writes=[bpw])
                        op("act", lambda e: e.activation(out=T_["w"][:, :n], in_=pw[:, :n], func=AF.Sigmoid, bias=rwp[:, 0 + d, c:c + 1]),
                           Wr=["w"], xr=[bpw, b_rwp])
                        op("act", lambda e: e.activation(out=T_["w"][:, :n], in_=T_["w"][:, :n], func=AF.Exp, scale=-math.exp(-0.5)),
                           R=["w"], Wr=["w"])
                        if npad0 < n:
                            op("pool", lambda e: e.memset(T_["w"][:, npad0:n], 1.0), Wr=["w"])
                        (pal, bpal) = bank()
                        S.emit("pe", lambda e: e.matmul(pal[:, :n], a2sb[rows, c * 128:(c + 1) * 128], tad[gb][rows, :n], start=True, stop=True),
                               reads=[b_w2, b_d[gb]], writes=[bpal])
                        op("act", lambda e: e.activation(out=T_["al"][:, :n], in_=pal[:, :n], func=AF.Sigmoid, bias=rwp[:, 2 + d, c:c + 1]),
                           Wr=["al"], xr=[bpal, b_rwp])
                        yield
                        op("dve", lambda e: e.tensor_scalar(out=T_["t2"][:, :n], in0=T_["al"][:, :n], scalar1=-1.0, scalar2=rwp[:, 5, c:c + 1],
                                                            op0=ALU.add, op1=ALU.mult), R=["al"], Wr=["t2"], xr=[b_rwp])
                        op("dve", lambda e: e.scalar_tensor_tensor(out=T_["kd"][:, :n], in0=T_["t2"][:, :n], scalar=1.0, in1=T_["ks"][:, :n],
                                                                   op0=ALU.add, op1=ALU.mult), R=["t2", "ks"], Wr=["kd"])
                        op("dve", lambda e: e.tensor_tensor(out=T_["bb"][:, :n], in0=T_["kk"][:, :n], in1=T_["al"][:, :n], op=ALU.mult),
                           R=["kk", "al"], Wr=["bb"])
                        yield
                        def v3(nm):
                            return T_[nm][:, :n].rearrange("p (c t) -> p c t", t=64)
                        op("pool", lambda e: e.tensor_copy(out=T_["d0"][:, :n], in_=T_["w"][:, :n]), R=["w"], Wr=["d0"])
                        op("pool", lambda e: e.memset(v3("d0")[:, :, 0:1], 0.0), Wr=["d0"])
                        op("pool", lambda e: e.memset(T_["d1"][:, :n], 0.0), Wr=["d1"])
                        op("pool", lambda e: e.tensor_copy(out=v3("d1")[:, :, 0:1], in_=v3("w")[:, :, 0:1]), R=["w"], Wr=["d1"])
                        op("dve", lambda e: e.tensor_tensor_scan(out=T_["P"][:, :n], data0=T_["d0"][:, :n], data1=T_["d1"][:, :n], initial=0.0,
                                                                 op0=ALU.mult, op1=ALU.add), R=["d0", "d1"], Wr=["P"])
                        op("dve", lambda e: e.tensor_copy(out=pend[gb][u][:, :nch], in_=v3("P")[:, :, 63]), R=["P"], xw=[bex])
                        pe3 = pend[gb][u][:, :nch].unsqueeze(2).to_broadcast([128, nch, 64])
                        op("dve", lambda e: e.reciprocal(out=T_["rW"][:, :n], in_=T_["P"][:, :n]), R=["P"], Wr=["rW"])
                        if d == 0:
                            op("dve", lambda e: e.reciprocal(out=T_["t1"][:, :n], in_=T_["w"][:, :n]), R=["w"], Wr=["t1"])
                            op("dve", lambda e: e.tensor_tensor(out=T_["Wp"][:, :n], in0=T_["P"][:, :n], in1=T_["t1"][:, :n], op=ALU.mult),
                               R=["P", "t1"], Wr=["Wp"])
                            Wi, rW = "P", "rW"
                        else:
                            op("dve", lambda e: e.tensor_tensor(out=v3("Wp"), in0=v3("rW"), in1=pe3, op=ALU.mult), R=["rW"], Wr=["Wp"], xr=[bex])
                            op("dve", lambda e: e.tensor_tensor(out=T_["Wi"][:, :n], in0=T_["Wp"][:, :n], in1=T_["w"][:, :n], op=ALU.mult),
                               R=["Wp", "w"], Wr=["Wi"])
                            op("dve", lambda e: e.reciprocal(out=T_["t1"][:, :n], in_=T_["Wi"][:, :n]), R=["Wi"], Wr=["t1"])
                            Wi, rW = "Wi", "t1"

                        def expw(o, nm):
                            for hh in range(2):
                                oap = E[o][hh * 64:(hh + 1) * 64, :nch, hh * 64:(hh + 1) * 64]
                                iap = T_[nm][hh * 64:(hh + 1) * 64, :n].rearrange("p (c t) -> p c t", t=64)
                                if hh == 0:
                                    op("pool", lambda e: e.tensor_copy(out=oap, in_=iap), R=[nm], xw=[bex])
                                else:
                                    op("act", lambda e: e.copy(out=oap, in_=iap), R=[nm], xw=[bex])
                        yield
                        op("dve", lambda e: e.scalar_tensor_tensor(out=T_["t2"][:, :n], in0=T_["kk"][:, :n], scalar=-1.0, in1=T_["Wp"][:, :n],
                                                                   op0=ALU.mult, op1=ALU.mult), R=["kk", "Wp"], Wr=["t2"])
                        expw(0, "t2")
                        op("dve", lambda e: e.tensor_tensor(out=T_["d0"][:, :n], in0=T_["rs"][:, :n], in1=T_[Wi][:, :n], op=ALU.mult),
                           R=["rs", Wi], Wr=["d0"])
                        expw(1, "d0")
                        yield
                        op("dve", lambda e: e.tensor_tensor(out=T_["bt"][:, :n], in0=T_["bb"][:, :n], in1=T_[rW][:, :n], op=ALU.mult),
                           R=["bb", rW], Wr=["bt"])
                        expw(2, "bt")
                        op("dve", lambda e: e.tensor_tensor(out=T_["ktl"][:, :n], in0=T_["kd"][:, :n], in1=T_[rW][:, :n], op=ALU.mult),
                           R=["kd", rW], Wr=["ktl"])
                        expw(3, "ktl")
                        expw(4, "vs")
                        yield
                        op("dve", lambda e: e.scalar_tensor_tensor(out=T_["d1"][:, :n], in0=T_["rs"][:, :n], scalar=rwp[:, 6, c:c + 1], in1=T_["kd"][:, :n],
                                                                   op0=ALU.mult, op1=ALU.mult), R=["rs", "kd"], Wr=["d1"], xr=[b_rwp])
                        expw(5, "d1")

                def chunk_group(gb, d, half, seg0, n, ch):
                    tok0 = seg0 + ch * 64
                    nt = max(0, min(64, L - tok0))
                    units = range(NG)
                    yi = nxt("yst", 2)
                    for u in units:
                        E = exp_[gb][u]
                        A, Rr, B, K, V, RK = [E[o][:, ch, :] for o in range(NOPS)]
                        be = b_exp[gb][u]
                        (p1, bp1) = bank()
                        S.emit("pe", lambda e: e.matmul(p1[:, 0:128], A, B, start=True, stop=True), reads=[be], writes=[bp1])
                        S.emit("pe", lambda e: e.matmul(p1[:, 128:256], B, A, start=True, stop=True), reads=[be], writes=[bp1])
                        S.emit("dve", lambda e: e.tensor_tensor(out=PP[u][0][:], in0=p1[:, 0:256], in1=m12[d][:], op=ALU.mult),
                               reads=[bp1, b_mm], writes=[b_PP[u][0]])
                        S.emit("pool", lambda e: e.tensor_tensor(out=TT[u][0][:], in0=PP[u][0][:, 128:256], in1=ident[:], op=ALU.add),
                               reads=[b_PP[u][0], b_ident], writes=[b_TT[u][0]])
                        (p2, bp2) = bank()
                        S.emit("pe", lambda e: e.matmul(p2[:, 0:128], K, A, start=True, stop=True), reads=[be], writes=[bp2])
                        S.emit("pe", lambda e: e.matmul(p2[:, 128:256], B, Rr, start=True, stop=True), reads=[be], writes=[bp2])
                        S.emit("pe", lambda e: e.matmul(p2[:, 256:384], K, Rr, start=True, stop=True), reads=[be], writes=[bp2])
                        S.emit("dve", lambda e: e.tensor_tensor(out=L3[u][:], in0=p2[:, 0:384], in1=m345[d][:], op=ALU.mult),
                               reads=[bp2, b_mm], writes=[b_L3[u]])
                        (p3, bp3) = bank()
                        p3b = p3.bitcast(BF16)
                        S.emit("pe", lambda e: e.transpose(p3b[:, 0:128], B, ident[:]), reads=[be, b_ident], writes=[bp3])
                        S.emit("pe", lambda e: e.transpose(p3b[:, 128:256], K, ident[:]), reads=[be, b_ident], writes=[bp3])
                        S.emit("pe", lambda e: e.matmul(p3[:, 128:192], V, istack[:], start=True, stop=True), reads=[be, b_rc], writes=[bp3])
                        S.emit("pe", lambda e: e.matmul(p3[:, 192:193], RK, onesb[:], start=True, stop=True), reads=[be, b_rc], writes=[bp3])
                        S.emit("act", lambda e: e.copy(out=BK_[u][:], in_=p3b[:, 0:256]), reads=[bp3], writes=[b_BK[u]])
                        S.emit("act", lambda e: e.copy(out=Vst[u][:], in_=p3[:, 128:192]), reads=[bp3], writes=[b_V[u]])
                        S.emit("act", lambda e: e.copy(out=coef[u][:], in_=p3[:, 192:193]), reads=[bp3], writes=[b_V[u]])
                        S.emit("pool", lambda e: e.tensor_scalar(out=Bst[yi][:, u, :], in0=Vst[u][:], scalar1=coef[u][:, 0:1], scalar2=None, op0=ALU.mult),
                               reads=[b_V[u]], writes=[b_Bst[yi]])
                    yield
                    for j in range(6):
                        if j > 0:
                            yield
                        cur, new = j % 2, (j + 1) % 2
                        for u in units:
                            (pq, bpq) = bank()
                            Pj = PP[u][cur][:, 0:128]
                            PjT = PP[u][cur][:, 128:256]
                            if j < 5:
                                S.emit("pe", lambda e: e.matmul(pq[:, 0:128], PjT, Pj, start=True, stop=True), reads=[b_PP[u][cur]], writes=[bpq])
                                S.emit("pe", lambda e: e.matmul(pq[:, 128:256], Pj, PjT, start=True, stop=True), reads=[b_PP[u][cur]], writes=[bpq])
                            if j >= 1:
                                tc_, tn_ = (j - 1) % 2, j % 2
                                S.emit("pe", lambda e: e.matmul(pq[:, 256:384], ident[:], TT[u][tc_][:], start=True, stop=False),
                                       reads=[b_ident, b_TT[u][tc_]], writes=[bpq])
                                S.emit("pe", lambda e: e.matmul(pq[:, 256:384], Pj, TT[u][tc_][:], start=False, stop=True),
                                       reads=[b_PP[u][cur], b_TT[u][tc_]], writes=[bpq])
                                if u % 4 != 3:
                                    S.emit("act", lambda e: e.copy(out=TT[u][tn_][:], in_=pq[:, 256:384]), reads=[bpq], writes=[b_TT[u][tn_]])
                                else:
                                    S.emit("dve", lambda e: e.tensor_copy(out=TT[u][tn_][:], in_=pq[:, 256:384]), reads=[bpq], writes=[b_TT[u][tn_]])
                            if j < 5:
                                if u % 4 != 3:
                                    S.emit("act", lambda e: e.copy(out=PP[u][new][:], in_=pq[:, 0:256]), reads=[bpq], writes=[b_PP[u][new]])
                                else:
                                    S.emit("dve", lambda e: e.tensor_copy(out=PP[u][new][:], in_=pq[:, 0:256]), reads=[bpq], writes=[b_PP[u][new]])
                    tfin = 5 % 2
                    yield
                    for u in units:
                        c = half * NG + u
                        E = exp_[gb][u]
                        A = E[0][:, ch, :]
                        (px, bpx) = bank()
                        S.emit("pe", lambda e: e.matmul(px[:, 0:64], A, Sb[d][c][:], start=True, stop=False), reads=[b_exp[gb][u], b_S[d][c]], writes=[bpx])
                        S.emit("pe", lambda e: e.matmul(px[:, 0:64], L3[u][:, 0:128], Vst[u][:], start=False, stop=True), reads=[b_L3[u], b_V[u]], writes=[bpx])
                        S.emit("act", lambda e: e.copy(out=Xb[u][:], in_=px[:, 0:64]), reads=[bpx], writes=[b_X[u]])
                    yield
                    for u in units:
                        (pu, bpu) = bank()
                        S.emit("pe", lambda e: e.matmul(pu[:, 0:64], TT[u][tfin][:], Xb[u][:], start=True, stop=True), reads=[b_TT[u][tfin], b_X[u]], writes=[bpu])
                        S.emit("dve", lambda e: e.tensor_copy(out=Ub[u][:], in_=pu[:, 0:64]), reads=[bpu], writes=[b_U[u]])
                    yield
                    for u in units:
                        c = half * NG + u
                        E = exp_[gb][u]
                        Rr = E[1][:, ch, :]
                        (py, bpy) = bank()
                        S.emit("pe", lambda e: e.matmul(py[:, 0:64], Rr, Sb[d][c][:], start=True, stop=False), reads=[b_exp[gb][u], b_S[d][c]], writes=[bpy])
                        S.emit("pe", lambda e: e.matmul(py[:, 0:64], L3[u][:, 128:256], Ub[u][:], start=False, stop=False), reads=[b_L3[u], b_U[u]], writes=[bpy])
                        S.emit("pe", lambda e: e.matmul(py[:, 0:64], L3[u][:, 256:384], Vst[u][:], start=False, stop=True), reads=[b_L3[u], b_V[u]], writes=[bpy])
                        S.emit("act", lambda e: e.copy(out=Yst[yi][:, u, :], in_=py[:, 0:64]), reads=[bpy], writes=[b_Yst[yi]])
                        (pS, bpS) = bank()
                        S.emit("pe", lambda e: e.matmul(pS[:, 0:64], BK_[u][:, 0:128], Ub[u][:], start=True, stop=False), reads=[b_BK[u], b_U[u]], writes=[bpS])
                        S.emit("pe", lambda e: e.matmul(pS[:, 0:64], BK_[u][:, 128:256], Vst[u][:], start=False, stop=True), reads=[b_BK[u], b_V[u]], writes=[bpS])
                        S.emit("pool", lambda e: e.tensor_scalar(out=Sf[d][c][:], in0=Sf[d][c][:], scalar1=pend[gb][u][:, ch:ch + 1], scalar2=None, op0=ALU.mult),
                               reads=[b_exp[gb][u]], writes=[b_S[d][c]])
                        S.emit("dve", lambda e: e.scalar_tensor_tensor(out=Sf[d][c][:], in0=pS[:, 0:64], scalar=pend[gb][u][:, ch:ch + 1], in1=Sf[d][c][:],
                                                                       op0=ALU.mult, op1=ALU.add), reads=[bpS, b_exp[gb][u]], writes=[b_S[d][c]])
                        S.emit("act", lambda e: e.copy(out=Sb[d][c][:], in_=Sf[d][c][:]), reads=[b_S[d][c]], writes=[b_S[d][c]])
                    yield
                    if nt > 0:
                        for (stg, bst, dst) in [(Yst[yi], b_Yst[yi], YSs[d][s]), (Bst[yi], b_Bst[yi], BNs[d][s])]:
                            dv = dst[tok0:tok0 + nt, :].rearrange("t (c h v) -> t c h v", h=2, v=64)
                            for hh in range(2):
                                S.dma("sp", dv[:, half * NG:(half + 1) * NG, hh, :], stg[hh * 64:hh * 64 + nt, :, :], reads=[bst])

                work = []
                for i in range(nseg):
                    for d in range(2):
                        si = i if d == 0 else nseg - 1 - i
                        for half in range(8 // NG):
                            work.append((d, half, segs[si][0], segs[si][1]))
                def run_all(gen):
                    for _ in gen:
                        pass

                def chunks_gen(wi):
                    d, half, seg0, n = work[wi]
                    chs = list(range(n // 64))
                    if d == 1:
                        chs = chs[::-1]
                    for ch in chs:
                        yield from chunk_group(wi % 2, d, half, seg0, n, ch)

                NW = len(work)
                load_group(0, *work[0])
                run_all(prep_group(0, *work[0]))
                if NW > 1:
                    load_group(1, *work[1])
                for wi in range(NW):
                    pg = prep_group((wi + 1) % 2, *work[wi + 1]) if wi + 1 < NW else iter(())
                    nsteps = 14 * (work[wi][3] // 64)
                    psteps = 8 * NG if wi + 1 < NW else 0
                    per = -(-psteps // max(nsteps, 1)) if psteps else 0
                    first = True
                    for _ in chunks_gen(wi):
                        for _k in range(per):
                            next(pg, None)
                    run_all(pg)
                    if wi + 2 < NW:
                        load_group(wi % 2, *work[wi + 2])
                S.barrier()

        for s in range(NS):
            if "C" in phases:
                phase_C(s)

        lnx = sb("lnx", (128, 2, D))
        b_lnx = Buf()
        S.dma("sp", lnx[:, 0, :], W["rw_lnx_w"][0].partition_broadcast(128), writes=[b_lnx])
        S.dma("sp", lnx[:, 1, :], W["rw_lnx_b"][0].partition_broadcast(128), writes=[b_lnx])

        def phase_D(s):
            L = seqL[s]
            TD = 256
            with ExitStack() as pst:
                P = {}
                P["ss"] = sb("d_ss", (128, 4), F32, pst)
                P["b_ss"] = Buf()
                P["junk"] = sb("d_junk", (128, D), BF16, pst)
                P["b_junk"] = Buf()
                P["xn"] = [sb("d_xn", (128, D), BF16, pst) for i in range(2)]
                P["b_xn"] = [Buf() for _ in range(2)]
                P["xT"] = [sb("d_xT", (128, TD), BF16, pst) for i in range(8)]
                P["b_xT"] = [Buf() for _ in range(8)]
                P["wgu"] = [sb("d_wgu", (128, 2, 8, 128), BF16, pst) for i in range(3)]
                P["b_wgu"] = [Buf() for _ in range(3)]
                P["sg"] = [sb("d_sg", (128, TD), F32, pst) for i in range(2)]
                P["b_sg"] = [Buf() for _ in range(2)]
                P["act"] = [sb("d_act", (128, TD), BF16, pst) for i in range(NF)]
                P["b_act"] = [Buf() for _ in range(NF)]
                P["wd"] = sb("d_wd", (128, NF, 512), BF16, pst)
                P["b_wd"] = Buf()
                rr["wgu"] = 0
                rr["sg"] = 0
                hts = [sb("d_h", (128, D), F32, pst) for i in range(2)]
                b_h = [Buf(), Buf()]
                yin = [sb("d_yin", (128, D), F32, pst) for i in range(4)]
                b_yin = Buf()
                ysum = sb("d_ysum", (128, D), F32, pst)
                ytmp = sb("d_ytmp", (128, D), F32, pst)
                b_y = Buf()
                st16 = sb("d_st16", (128, 4, 16), F32, pst)
                b_st = Buf()
                zb = sb("d_zb", (128, D), BF16, pst)
                b_zb = Buf()
                zT = [sb("d_zT", (128, TD), BF16, pst) for i in range(8)]
                b_zT = [Buf() for _ in range(8)]
                oT = [sb("d_oT", (128, TD), BF16, pst) for i in range(8)]
                b_oT = [Buf() for _ in range(8)]
                mT = [sb("d_mT", (128, TD), BF16, pst) for i in range(8)]
                b_mT = [Buf() for _ in range(8)]
                gts = [sb("d_gt", (128, TD), F32, pst) for i in range(4)]
                b_gts = [Buf() for _ in range(4)]
                mtmp = sb("d_mtmp", (128, TD), F32, pst)
                b_mtmp = Buf()
                wbr = [sb("d_wbr", (128, 8, 128), BF16, pst) for i in range(4)]
                b_wbr = [Buf() for _ in range(4)]
                wo = sb("d_wo", (128, 8, 512), BF16, pst)
                b_wo = Buf()
                gdraw = sb("d_gdraw", (128, TD + 2), F32, pst)
                gdsh = sb("d_gdsh", (128, TD), F32, pst)
                sgd = sb("d_sgd", (128, TD), BF16, pst)
                b_gd = Buf()
                ost = sb("d_ost", (128, D), F32, pst)
                b_ost = Buf()
                rr["gts"] = 0
                rr["wbr"] = 0
                for (t0, T) in tiles_of(L, TD):
                    blks = blocks_of(t0, T)
                    lo = max(t0 - 1, 0)
                    hi = min(t0 + T + 1, L)
                    if (t0 - 1 < 0) or (t0 + T + 1 > L):
                        S.emit("pool", lambda e: e.memset(gdraw[:], 0.0), writes=[b_gd])
                    S.dma("sp", gdraw[:, lo - (t0 - 1):hi - (t0 - 1)], RWs[s][26, :, lo:hi], writes=[b_gd])
                    S.emit("pool", lambda e: e.tensor_scalar(out=gdsh[:, :T], in0=gdraw[:, 1:T + 1], scalar1=mus[:, 2, 26:27], scalar2=None, op0=ALU.mult),
                           reads=[b_mus, b_gd], writes=[b_gd])
                    S.emit("dve", lambda e: e.scalar_tensor_tensor(out=gdsh[:, :T], in0=gdraw[:, 0:T], scalar=mus[:, 0, 26:27], in1=gdsh[:, :T],
                                                                    op0=ALU.mult, op1=ALU.add), reads=[b_mus, b_gd], writes=[b_gd])
                    S.emit("dve", lambda e: e.scalar_tensor_tensor(out=gdsh[:, :T], in0=gdraw[:, 2:T + 2], scalar=mus[:, 1, 26:27], in1=gdsh[:, :T],
                                                                    op0=ALU.mult, op1=ALU.add), reads=[b_mus, b_gd], writes=[b_gd])
                    S.emit("act", lambda e: e.activation(out=sgd[:, :T], in_=gdsh[:, :T], func=AF.Sigmoid), reads=[b_gd], writes=[b_gd])
                    for k in range(8):
                        S.dma("sp", oT[k][:, :T], OTs[s][k * 128:(k + 1) * 128, t0:t0 + T], writes=[b_oT[k]])
                    for bi, (tg, o, n) in enumerate(blks):
                        S.dma("sp", hts[bi][:n, :], Hs[s][tg:tg + n, :], writes=[b_h[bi]])
                        for i, src in enumerate([YSs[0][s], YSs[1][s], BNs[0][s], BNs[1][s]]):
                            S.dma("sp", yin[i][:n, :], src[tg:tg + n, :], writes=[b_yin])
                        gbank = [nxt("D", 2), nxt("D", 2)]
                        for hf in range(2):
                            S.emit("pe", lambda e: e.matmul(psD[gbank[hf]][:n, :], sgd[:, o:o + n], g2sb[:, hf * 512:(hf + 1) * 512], start=True, stop=True),
                                   reads=[b_gd, b_w2], writes=[bD[gbank[hf]]])
                        def v3(t):
                            return t[:n, :].rearrange("p (h v) -> p h v", v=64)
                        def bc(col):
                            return st16[:n, col, :].unsqueeze(2).to_broadcast([n, 16, 64])
                        S.emit("dve", lambda e: e.tensor_tensor(out=ysum[:n, :], in0=yin[0][:n, :], in1=yin[1][:n, :], op=ALU.add), reads=[b_yin], writes=[b_y])
                        S.emit("dve", lambda e: e.reduce_sum(out=st16[:n, 0, :], in_=v3(ysum), axis=AX.X), reads=[b_y], writes=[b_st])
                        S.emit("dve", lambda e: e.tensor_scalar(out=st16[:n, 0, :], in0=st16[:n, 0, :], scalar1=1.0 / 64, scalar2=None, op0=ALU.mult),
                               reads=[b_st], writes=[b_st])
                        S.emit("dve", lambda e: e.tensor_tensor(out=v3(ysum), in0=v3(ysum), in1=bc(0), op=ALU.subtract), reads=[b_y, b_st], writes=[b_y])
                        S.emit("pool", lambda e: e.tensor_tensor(out=ytmp[:n, :], in0=ysum[:n, :], in1=ysum[:n, :], op=ALU.mult), reads=[b_y], writes=[b_y])
                        S.emit("dve", lambda e: e.reduce_sum(out=st16[:n, 1, :], in_=v3(ytmp), axis=AX.X), reads=[b_y], writes=[b_st])
                        S.emit("act", lambda e: e.activation(out=st16[:n, 2, :], in_=st16[:n, 1, :], func=AF.Sqrt, bias=epsT[:n, 1:2], scale=1.0 / 64),
                               reads=[b_st, b_eps], writes=[b_st])
                        S.emit("dve", lambda e: e.reciprocal(out=st16[:n, 3, :], in_=st16[:n, 2, :]), reads=[b_st], writes=[b_st])
                        S.emit("dve", lambda e: e.tensor_tensor(out=v3(ysum), in0=v3(ysum), in1=bc(3), op=ALU.mult), reads=[b_y, b_st], writes=[b_y])
                        S.emit("pool", lambda e: e.tensor_tensor(out=ysum[:n, :], in0=ysum[:n, :], in1=lnx[:n, 0, :], op=ALU.mult), reads=[b_y, b_lnx], writes=[b_y])
                        S.emit("pool", lambda e: e.tensor_tensor(out=ysum[:n, :], in0=ysum[:n, :], in1=lnx[:n, 1, :], op=ALU.add), reads=[b_y, b_lnx], writes=[b_y])
                        S.emit("pool", lambda e: e.tensor_tensor(out=ytmp[:n, :], in0=yin[2][:n, :], in1=yin[3][:n, :], op=ALU.add), reads=[b_yin, b_y], writes=[b_y])
                        S.emit("dve", lambda e: e.tensor_tensor(out=ysum[:n, :], in0=ysum[:n, :], in1=ytmp[:n, :], op=ALU.add), reads=[b_y], writes=[b_y])
                        for hf in range(2):
                            S.emit("dve", lambda e: e.tensor_tensor(out=zb[:n, hf * 512:(hf + 1) * 512], in0=ysum[:n, hf * 512:(hf + 1) * 512],
                                                                    in1=psD[gbank[hf]][:n, :], op=ALU.mult),
                                   reads=[b_y, bD[gbank[hf]]], writes=[b_zb])
                        for cp in range(4):
                            tb_ = nxt("T", 2)
                            for cc in range(2):
                                c = cp * 2 + cc
                                S.emit("pe", lambda e: e.transpose(psT[tb_][:, cc * 512:cc * 512 + n], zb[:n, c * 128:(c + 1) * 128], ident[:n, :n]),
                                       reads=[b_zb, b_ident], writes=[bT[tb_]])
                            for cc in range(2):
                                c = cp * 2 + cc
                                S.emit("act", lambda e: e.copy(out=zT[c][:, o:o + n], in_=psT[tb_][:, cc * 512:cc * 512 + n]),
                                       reads=[bT[tb_]], writes=[b_zT[c]])
                    for j in range(8):
                        wa = nxt("wbr", 4)
                        S.dma("sp", wbr[wa][:], WABS[j], writes=[b_wbr[wa]])
                        wr_ = nxt("wbr", 4)
                        S.dma("sp", wbr[wr_][:], WRBS[j], writes=[b_wbr[wr_]])
                        ga = nxt("gts", 4)
                        S.dma("sp", gts[ga][:, :T], GTs[s][j, :, t0:t0 + T], writes=[b_gts[ga]])
                        gb_ = nxt("gts", 4)
                        S.dma("sp", gts[gb_][:, :T], GTs[s][8 + j, :, t0:t0 + T], writes=[b_gts[gb_]])
                        b1, b2 = nxt("G", 4), nxt("G", 4)
                        for k in range(8):
                            S.emit("pe", lambda e: e.matmul(psG[b1][:, :T], wbr[wa][:, k, :], oT[k][:, :T], start=(k == 0), stop=(k == 7)),
                                   reads=[b_wbr[wa], b_oT[k]], writes=[bG[b1]])
                        for k in range(8):
                            S.emit("pe", lambda e: e.matmul(psG[b2][:, :T], wbr[wr_][:, k, :], zT[k][:, :T], start=(k == 0), stop=(k == 7)),
                                   reads=[b_wbr[wr_], b_zT[k]], writes=[bG[b2]])
                        S.emit("dve", lambda e: e.tensor_tensor(out=mtmp[:, :T], in0=gts[ga][:, :T], in1=psG[b1][:, :T], op=ALU.mult),
                               reads=[b_gts[ga], bG[b1]], writes=[b_mtmp])
                        S.emit("dve", lambda e: e.tensor_tensor(out=gts[gb_][:, :T], in0=gts[gb_][:, :T], in1=psG[b2][:, :T], op=ALU.mult),
                               reads=[b_gts[gb_], bG[b2]], writes=[b_gts[gb_]])
                        S.emit("pool", lambda e: e.tensor_tensor(out=mT[j][:, :T], in0=mtmp[:, :T], in1=gts[gb_][:, :T], op=ALU.add),
                               reads=[b_mtmp, b_gts[gb_]], writes=[b_mT[j]])
                    for hf in range(2):
                        S.dma("sp", wo[:], WOS[hf], writes=[b_wo])
                        for bi, (tg, o, n) in enumerate(blks):
                            bank_ = nxt("D", 2)
                            for k in range(8):
                                S.emit("pe", lambda e: e.matmul(psD[bank_][:n, :], mT[k][:, o:o + n], wo[:, k, :], start=(k == 0), stop=(k == 7)),
                                       reads=[b_mT[k], b_wo], writes=[bD[bank_]])
                            S.emit("dve", lambda e: e.tensor_tensor(out=hts[bi][:n, hf * 512:(hf + 1) * 512], in0=hts[bi][:n, hf * 512:(hf + 1) * 512],
                                                                    in1=psD[bank_][:n, :], op=ALU.add), reads=[bD[bank_], b_h[bi]], writes=[b_h[bi]])
                    xts = hts[:len(blks)]
                    bxs = b_h[:len(blks)]
                    ffn(P, 1, blks, xts, bxs, 2)
                    for bi, (tg, o, n) in enumerate(blks):
                        xt, bx = hts[bi], b_h[bi]
                        S.emit("pool", lambda e: e.memset(P["ss"][:, 0:1], 0.0), writes=[P["b_ss"]])
                        S.emit("act", lambda e: e.activation(out=P["junk"][:n, :], in_=xt[:n, :], func=AF.Square, accum_out=P["ss"][:n, 0:1]),
                               reads=[bx], writes=[P["b_ss"], P["b_junk"]])
                        S.emit("act", lambda e: e.activation(out=P["ss"][:n, 1:2], in_=P["ss"][:n, 0:1], func=AF.Sqrt, bias=epsT[:n, 0:1], scale=1.0 / D),
                               reads=[b_eps], writes=[P["b_ss"]])
                        S.emit("dve", lambda e: e.reciprocal(out=P["ss"][:n, 2:3], in_=P["ss"][:n, 1:2]), writes=[P["b_ss"]])
                        S.emit("act", lambda e: e.activation(out=ost[:n, :], in_=xt[:n, :], func=AF.Copy, scale=P["ss"][:n, 2:3]),
                               reads=[bx, P["b_ss"]], writes=[b_ost])
                        S.emit("dve", lambda e: e.tensor_tensor(out=ost[:n, :], in0=ost[:n, :], in1=fin_g[:n, :], op=ALU.mult), reads=[b_ost, b_fin], writes=[b_ost])
                        lo_t = max(tg, NMETA)
                        if tg + n > lo_t:
                            S.dma("pool", ydst(s)[lo_t - NMETA:tg + n - NMETA, :], ost[lo_t - tg:n, :], reads=[b_ost])
                S.barrier()

        for s in range(NS):
            if "D" in phases:
                phase_D(s)
    return nc


_NC_CACHE = {}


def _rel_bucket_np(rel):
    try:
        import jax
        import jax.numpy as jnp
        with jax.default_device(jax.devices("cpu")[0]):
            r = jnp.asarray(rel, dtype=jnp.int32)
            nb = 16
            max_exact = 8
            n = jnp.abs(r)
            nf = jnp.maximum(n, 1).astype(jnp.float32)
            large = max_exact + (jnp.log(nf / max_exact) / math.log(128 / max_exact) * (nb - max_exact)).astype(jnp.int32)
            large = jnp.minimum(large, nb - 1)
            out = (r > 0).astype(jnp.int32) * nb + jnp.where(n < max_exact, n, large)
            return np.asarray(out)
    except Exception:
        rel = np.asarray(rel, dtype=np.int32)
        n = np.abs(rel)
        nf = np.maximum(n, 1).astype(np.float32)
        large = 8 + (np.log(nf / np.float32(8)) / np.float32(math.log(16.0)) * np.float32(8)).astype(np.int32)
        large = np.minimum(large, 15)
        return (rel > 0).astype(np.int32) * 16 + np.where(n < 8, n, large)


def _consts():
    p = np.arange(128)[:, None]
    m = np.arange(896)[None, :]
    bkt = _rel_bucket_np(p - m + 384).astype(np.float32)
    t = np.arange(64)
    lo = (t[None, :] < t[:, None]).astype(np.float32)
    up = lo.T.copy()
    eye = np.eye(64, dtype=np.float32)

    def bd(mm):
        z = np.zeros((128, 128), np.float32)
        z[:64, :64] = mm
        z[64:, 64:] = mm
        return z

    masks = np.stack([bd(lo), bd(up), bd(lo + eye), bd(up + eye)], axis=1)
    istack = np.concatenate([eye, eye], axis=0)
    bones = bd(np.ones((64, 64), np.float32))
    return {"c_ident": np.eye(128, dtype=np.float32), "c_bk": bkt, "c_masks": masks, "c_istack": istack, "c_bones": bones}


def kernel(**inputs):
    xp = np.ascontiguousarray(inputs["x_prompt"], dtype=np.float32)
    xs = np.ascontiguousarray(inputs["x_sample"], dtype=np.float32)
    S0, S1 = xp.shape[1], xs.shape[1]
    ncores = 8
    key = (S0, S1)
    if key not in _NC_CACHE:
        _NC_CACHE[key] = build_nc(S0, S1)
    nc = _NC_CACHE[key]
    shared = {k: np.ascontiguousarray(v, dtype=np.float32) for k, v in inputs.items()
              if k not in ("x_prompt", "x_sample")}
    shared.update(_consts())
    in_maps = []
    for c in range(ncores):
        m = dict(shared)
        m["x_prompt"] = xp[2 * c:2 * c + 2]
        m["x_sample"] = xs[2 * c:2 * c + 2]
        in_maps.append(m)
    res = run_bass_kernel_spmd(nc, in_maps, core_ids=list(range(ncores)))
    yp = np.concatenate([r["y_prompt"] for r in res.results], axis=0)
    ys = np.concatenate([r["y_sample"] for r in res.results], axis=0)
    return (yp.astype(np.float32), ys.astype(np.float32))
```

```python
import math
from contextlib import ExitStack
import numpy as np
import concourse.bass as bass
import concourse.mybir as mybir
from concourse.bass_utils import run_bass_kernel_spmd

F32 = mybir.dt.float32
BF16 = mybir.dt.bfloat16
AF = mybir.ActivationFunctionType
ALU = mybir.AluOpType
AX = mybir.AxisListType

D = 1024
DFF = 2816
NF = DFF // 128
NMETA = 16
EPS = 1e-6
LNX_EPS = 64e-5
NIN = 8576
EP = 30000
KDMA = 8


class Buf:
    __slots__ = ("w", "weng", "r", "excl")

    def __init__(self, excl=False):
        self.w = None
        self.weng = None
        self.r = {}
        self.excl = excl


class Sched:
    def __init__(self, nc, st):
        self.nc = nc
        self.engs = {"pe": nc.tensor, "act": nc.scalar, "dve": nc.vector, "pool": nc.gpsimd, "sp": nc.sync}
        self.cnt = {e: 0 for e in self.engs}
        self.seen = {e: {} for e in self.engs}
        self.sems = {}
        self.st = st
        self.dqi = {"sp": 0, "pool": 0}
        self.last = {}

    def sem(self, key):
        if key not in self.sems:
            name = "s_" + "_".join(str(x) for x in key)
            self.sems[key] = self.st.enter_context(self.nc.semaphore(name))
        return self.sems[key]

    def _deps(self, e, reads, writes):
        deps = {}

        def add(tok):
            if tok is None:
                return
            k, v = tok
            if deps.get(k, 0) < v:
                deps[k] = v

        for b in reads:
            add(b.w)
            if b.excl:
                for src, tok in b.r.items():
                    if src != e:
                        add(tok)
        for b in writes:
            if not (e == "pe" and b.weng == "pe"):
                add(b.w)
            for tok in b.r.values():
                add(tok)
        return deps

    def _wait(self, e, deps):
        seen = self.seen[e]
        eng = self.engs[e]
        for k, v in deps.items():
            if seen.get(k, 0) < v:
                eng.wait_ge(self.sem(k), v)
                seen[k] = v

    def _mark(self, src, tok, e, reads, writes):
        self.last[src] = tok
        for b in reads:
            b.r[src] = tok
        for b in writes:
            b.w = tok
            b.weng = e
            b.r = {}

    def emit(self, e, fn, reads=(), writes=()):
        self._wait(e, self._deps(e, reads, writes))
        ins = fn(self.engs[e])
        c = self.cnt[e]
        self.cnt[e] += 1
        k = (e, c // EP)
        v = c % EP + 1
        ins.then_inc(self.sem(k), 1)
        self._mark(e, (k, v), e, reads, writes)

    def dma(self, q, out, in_, reads=(), writes=(), **kw):
        deps = self._deps("dma", reads, writes)
        i = self.dqi[q]
        self.dqi[q] += 1
        slot = i % KDMA
        k = ("d", q, slot)
        v = 16 * (i // KDMA + 1)
        if i >= KDMA:
            if deps.get(k, 0) < v - 16:
                deps[k] = v - 16
        self._wait(q, deps)
        ins = self.engs[q].dma_start(out=out, in_=in_, **kw)
        ins.then_inc(self.sem(k), 16)
        self._mark(k, (k, v), "dma", reads, writes)

    def barrier(self):
        deps = {}
        for tok in self.last.values():
            k, v = tok
            if deps.get(k, 0) < v:
                deps[k] = v
        for e in self.engs:
            self._wait(e, deps)


def blocks_of(t0, T):
    out = []
    o = 0
    while o < T:
        n = min(128, T - o)
        out.append((t0 + o, o, n))
        o += n
    return out


def tiles_of(L, TT=512):
    out = []
    t = 0
    while t < L:
        T = min(TT, L - t)
        out.append((t, T))
        t += T
    return out


def build_nc(S0, S1, debug=False, phases="ABCD"):
    nc = bass.Bass("TRN2", target_bir_lowering=False)
    seqS = [S0, S0, S1, S1]
    seqL = [s + NMETA for s in seqS]
    NS = len(seqS)

    def din(name, shape):
        return nc.dram_tensor(name, list(shape), F32, kind="ExternalInput").ap()

    def dscr(name, shape, dt=F32):
        kind = "ExternalOutput" if (debug and name.startswith("dbg_")) else "Internal"
        return nc.dram_tensor(name, list(shape), dt, kind=kind).ap()

    x_in = [din("x_prompt", (2, S0, D)), din("x_sample", (2, S1, D))]
    y_out = [nc.dram_tensor("y_prompt", [2, S0, D], F32, kind="ExternalOutput").ap(),
             nc.dram_tensor("y_sample", [2, S1, D], F32, kind="ExternalOutput").ap()]

    def xsrc(s):
        return x_in[s // 2][s % 2]

    def ydst(s):
        return y_out[s // 2][s % 2]

    meta = din("meta_tokens", (NMETA, D))
    rel_bias = din("rel_bias", (32, 8))
    W = {}
    for nm, shp in [("ffn1_norm", (1, D)), ("ffn1_w_gate", (1, D, DFF)), ("ffn1_w_up", (1, D, DFF)),
                    ("ffn1_w_down", (1, DFF, D)), ("mix_norm", (1, D)), ("w_in", (1, D, NIN)),
                    ("attn_lambda_q1", (1, 64)), ("attn_lambda_k1", (1, 64)), ("attn_lambda_q2", (1, 64)),
                    ("attn_lambda_k2", (1, 64)), ("attn_subln", (1, 128)), ("w_attn_branch", (1, D, D)),
                    ("rw_mu_prev", (1, 3456)), ("rw_mu_next", (1, 3456)), ("rw_w0", (1, 2, D)),
                    ("rw_w2", (1, 2, 64, D)), ("rw_a0", (1, 2, D)), ("rw_a2", (1, 2, 64, D)),
                    ("rw_g2", (1, 128, D)), ("rw_k_k", (1, D)), ("rw_k_a", (1, D)), ("rw_r_k", (1, 16, 64)),
                    ("rw_lnx_w", (1, D)), ("rw_lnx_b", (1, D)), ("w_rw_branch", (1, D, D)), ("w_out", (1, D, D)),
                    ("ffn2_norm", (1, D)), ("ffn2_w_gate", (1, D, DFF)), ("ffn2_w_up", (1, D, DFF)),
                    ("ffn2_w_down", (1, DFF, D)), ("final_norm", (D,))]:
        W[nm] = din(nm, shp)
    c_ident = din("c_ident", (128, 128))
    c_bk = din("c_bk", (128, 896))
    c_masks = din("c_masks", (128, 4, 128))
    c_istack = din("c_istack", (128, 64))
    c_bones = din("c_bones", (128, 128))

    WGU = [dscr(f"wgu{i}", (NF, 128, 2, 8, 128), BF16) for i in range(2)]
    WD = [dscr(f"wd{i}", (2, 128, NF, 512), BF16) for i in range(2)]
    WINS = dscr("wins", (67, 128, 8, 128), BF16)
    WVS = dscr("wvs", (2, 128, 8, 512), BF16)
    WABS = dscr("wabs", (8, 128, 8, 128), BF16)
    WRBS = dscr("wrbs", (8, 128, 8, 128), BF16)
    WOS = dscr("wos", (2, 128, 8, 512), BF16)
    Hs = [dscr(f"dbg_h{s}", (seqL[s], D)) for s in range(NS)]
    QTs = [dscr(f"dbg_qt{s}", (8, 128, seqL[s]), BF16) for s in range(NS)]
    KTs = [dscr(f"dbg_kt{s}", (8, 128, seqL[s]), BF16) for s in range(NS)]
    Vs = [dscr(f"dbg_v{s}", (seqL[s], D), BF16) for s in range(NS)]
    RWs = [dscr(f"dbg_rw{s}", (27, 128, seqL[s])) for s in range(NS)]
    GTs = [dscr(f"dbg_gt{s}", (16, 128, seqL[s])) for s in range(NS)]
    OTs = [dscr(f"dbg_ot{s}", (D, seqL[s]), BF16) for s in range(NS)]
    YSs = [[dscr(f"dbg_y{d}{s}", (seqL[s], D)) for s in range(NS)] for d in range(2)]
    BNs = [[dscr(f"dbg_bn{d}{s}", (seqL[s], D)) for s in range(NS)] for d in range(2)]

    with ExitStack() as st:
        S = Sched(nc, st)

        uid = [0]

        def sb(name, shape, dt=F32, stack=st):
            uid[0] += 1
            return stack.enter_context(nc.sbuf_tensor(f"{name}_{uid[0]}", list(shape), dt))

        psT = [st.enter_context(nc.psum_tensor(f"psT{i}", [128, 1024], BF16)) for i in range(2)]
        psG = [st.enter_context(nc.psum_tensor(f"psG{i}", [128, 512], F32)) for i in range(4)]
        psD = [st.enter_context(nc.psum_tensor(f"psD{i}", [128, 512], F32)) for i in range(2)]
        bT = [Buf(True) for _ in range(2)]
        bG = [Buf(True) for _ in range(4)]
        bD = [Buf(True) for _ in range(2)]
        rr = {"T": 0, "G": 0, "D": 0}

        def nxt(kind, n):
            i = rr[kind]
            rr[kind] = (i + 1) % n
            return i

        ident = sb("ident", (128, 128), BF16)
        b_ident = Buf()
        S.dma("pool", ident[:], c_ident[:, :], writes=[b_ident])
        epsT = sb("epsT", (128, 2))
        b_eps = Buf()
        S.emit("pool", lambda e: e.memset(epsT[:, 0:1], EPS), writes=[b_eps])
        S.emit("pool", lambda e: e.memset(epsT[:, 1:2], LNX_EPS), writes=[b_eps])
        gains = sb("gains", (128, 4, 8))
        b_gains = Buf()
        for i, nm in enumerate(["ffn1_norm", "mix_norm", "ffn2_norm"]):
            S.dma("sp", gains[:, i, :], W[nm][0].rearrange("(c p) -> p c", p=128), writes=[b_gains],
                  allow_slow_non_contiguous=True)
        fin_g = sb("fin_g", (128, D))
        b_fin = Buf()
        S.dma("sp", fin_g[:], W["final_norm"].partition_broadcast(128), writes=[b_fin])

        tb = sb("tb", (128, 256))
        b_tb = Buf()
        S.dma("sp", tb[:], rel_bias.rearrange("b h -> (b h)").partition_broadcast(128), writes=[b_tb])
        lamv = sb("lamv", (128, 4, 64))
        b_lamv = Buf()
        for i, nm in enumerate(["attn_lambda_q1", "attn_lambda_k1", "attn_lambda_q2", "attn_lambda_k2"]):
            S.dma("sp", lamv[:, i, :], W[nm][0].partition_broadcast(128), writes=[b_lamv])
        lams = sb("lams", (128, 8))
        b_lams = Buf()
        for i in range(2):
            S.emit("dve", lambda e: e.tensor_tensor(out=lamv[:, 2 * i, :], in0=lamv[:, 2 * i, :], in1=lamv[:, 2 * i + 1, :],
                                                    op=ALU.mult), reads=[b_lamv], writes=[b_lamv])
            S.emit("dve", lambda e: e.reduce_sum(out=lams[:, i:i + 1], in_=lamv[:, 2 * i, :], axis=AX.X),
                   reads=[b_lamv], writes=[b_lams])
        S.emit("act", lambda e: e.activation(out=lams[:, 2:4], in_=lams[:, 0:2], func=AF.Exp), reads=[b_lams], writes=[b_lams])
        S.emit("dve", lambda e: e.tensor_tensor(out=lams[:, 4:5], in0=lams[:, 3:4], in1=lams[:, 2:3], op=ALU.subtract),
               reads=[b_lams], writes=[b_lams])
        LAM_INIT = 0.8 - 0.6 * math.exp(-0.3 * 0)
        S.emit("dve", lambda e: e.tensor_scalar(out=lams[:, 5:6], in0=lams[:, 4:5], scalar1=-LAM_INIT, scalar2=None, op0=ALU.add),
               reads=[b_lams], writes=[b_lams])
        gsub = sb("gsub", (128, 1))
        b_gsub = Buf()
        S.dma("sp", gsub[:], W["attn_subln"][0].rearrange("(p o) -> p o", o=1), writes=[b_gsub])
        S.emit("pool", lambda e: e.tensor_scalar(out=gsub[:], in0=gsub[:], scalar1=1.0 - LAM_INIT, scalar2=None, op0=ALU.mult),
               reads=[b_gsub], writes=[b_gsub])

        masks = sb("masks", (128, 4, 128))
        b_masks = Buf()
        S.dma("sp", masks[:], c_masks[:, :, :], writes=[b_masks])
        m12 = [sb(f"m12_{d}", (128, 256)) for d in range(2)]
        m345 = [sb(f"m345_{d}", (128, 384)) for d in range(2)]
        b_mm = Buf()
        for d in range(2):
            lo, up, loi, upi = (0, 1, 2, 3) if d == 0 else (1, 0, 3, 2)
            for (dst, off, mi) in [(m12[d], 0, lo), (m12[d], 128, up), (m345[d], 0, up), (m345[d], 128, upi), (m345[d], 256, upi)]:
                S.emit("pool", lambda e: e.tensor_copy(out=dst[:, off:off + 128], in_=masks[:, mi, :]), reads=[b_masks], writes=[b_mm])
        istack = sb("istack", (128, 64), BF16)
        bones = sb("bones", (128, 128), BF16)
        onesb = sb("onesb", (128, 1), BF16)
        b_rc = Buf()
        S.dma("pool", istack[:], c_istack[:, :], writes=[b_rc])
        S.dma("pool", bones[:], c_bones[:, :], writes=[b_rc])
        S.emit("pool", lambda e: e.memset(onesb[:], 1.0), writes=[b_rc])
        mus = sb("mus", (128, 3, 27))
        b_mus = Buf()
        for i, nm in enumerate(["rw_mu_prev", "rw_mu_next"]):
            S.dma("sp", mus[:, i, :], W[nm][0].rearrange("(c p) -> p c", p=128), writes=[b_mus], allow_slow_non_contiguous=True)
        S.emit("dve", lambda e: e.tensor_tensor(out=mus[:, 2, :], in0=mus[:, 0, :], in1=mus[:, 1, :], op=ALU.add), reads=[b_mus], writes=[b_mus])
        S.emit("dve", lambda e: e.tensor_scalar(out=mus[:, 2, :], in0=mus[:, 2, :], scalar1=-1.0, scalar2=1.0, op0=ALU.mult, op1=ALU.add),
               reads=[b_mus], writes=[b_mus])
        rwp = sb("rwp", (128, 7, 8))
        b_rwp = Buf()
        for i, src in enumerate([W["rw_w0"][0, 0], W["rw_w0"][0, 1], W["rw_a0"][0, 0], W["rw_a0"][0, 1], W["rw_k_k"][0], W["rw_k_a"][0],
                                 W["rw_r_k"][0].rearrange("h n -> (h n)")]):
            S.dma("sp", rwp[:, i, :], src.rearrange("(c p) -> p c", p=128), writes=[b_rwp], allow_slow_non_contiguous=True)
        w2sb = sb("w2sb", (128, D), BF16)
        a2sb = sb("a2sb", (128, D), BF16)
        g2sb = sb("g2sb", (128, D), BF16)
        b_w2 = Buf()
        S.dma("pool", w2sb[:], W["rw_w2"][0].rearrange("d r c -> (d r) c"), writes=[b_w2])
        S.dma("pool", a2sb[:], W["rw_a2"][0].rearrange("d r c -> (d r) c"), writes=[b_w2])
        S.dma("pool", g2sb[:], W["rw_g2"][0], writes=[b_w2])

        def prep_weights():
            for i, pre in enumerate(["ffn1", "ffn2"]):
                for which, nm in enumerate(["w_gate", "w_up"]):
                    src = W[f"{pre}_{nm}"][0].rearrange("(k p) (f j) -> f p k j", p=128, j=128)
                    for f in range(NF):
                        S.dma("pool", WGU[i][f, :, which, :, :], src[f])
                src = W[f"{pre}_w_down"][0].rearrange("(f p) (h j) -> h p f j", p=128, j=512)
                for h in range(2):
                    for f0 in range(0, NF, 11):
                        S.dma("pool", WD[i][h, :, f0:f0 + 11, :], src[h][:, f0:f0 + 11, :])
            src = W["w_in"][0].rearrange("(k p) (c j) -> c p k j", p=128, j=128)
            for c in range(67):
                if 16 <= c < 24:
                    continue
                S.dma("pool", WINS[c], src[c])
            src = W["w_in"][0][:, 2048:3072].rearrange("(k p) (h j) -> h p k j", p=128, j=512)
            for h in range(2):
                S.dma("pool", WVS[h], src[h])
            for (dst, nm) in [(WABS, "w_attn_branch"), (WRBS, "w_rw_branch")]:
                src = W[nm][0].rearrange("(k p) (c j) -> c p k j", p=128, j=128)
                for c in range(8):
                    S.dma("pool", dst[c], src[c])
            src = W["w_out"][0].rearrange("(k p) (h j) -> h p k j", p=128, j=512)
            for h in range(2):
                S.dma("pool", WOS[h], src[h])

        prep_weights()
        S.barrier()

        def load_x_block(s, t0, n, xt, bx):
            if t0 == 0:
                S.dma("sp", xt[0:NMETA, :], meta[:, :], writes=[bx])
                S.dma("sp", xt[NMETA:n, :], xsrc(s)[0:n - NMETA, :], writes=[bx])
            else:
                S.dma("sp", xt[0:n, :], xsrc(s)[t0 - NMETA:t0 - NMETA + n, :], writes=[bx])

        def rmsnorm_T(P, blks, xts, bxs, gi, outT, boutT):
            for (tg, o, n), xt, bx in zip(blks, xts, bxs):
                S.emit("pool", lambda e: e.memset(P["ss"][:, 0:1], 0.0), writes=[P["b_ss"]])
                S.emit("act", lambda e: e.activation(out=P["junk"][:n, :], in_=xt[:n, :], func=AF.Square,
                                                     accum_out=P["ss"][:n, 0:1]),
                       reads=[bx], writes=[P["b_ss"], P["b_junk"]])
                S.emit("act", lambda e: e.activation(out=P["ss"][:n, 1:2], in_=P["ss"][:n, 0:1], func=AF.Sqrt,
                                                     bias=epsT[:n, 0:1], scale=1.0 / D),
                       reads=[b_eps], writes=[P["b_ss"]])
                S.emit("dve", lambda e: e.reciprocal(out=P["ss"][:n, 2:3], in_=P["ss"][:n, 1:2]), writes=[P["b_ss"]])
                bi = o // 128
                S.emit("act", lambda e: e.activation(out=P["xn"][bi][:n, :], in_=xt[:n, :], func=AF.Copy,
                                                     scale=P["ss"][:n, 2:3]),
                       reads=[bx, P["b_ss"]], writes=[P["b_xn"][bi]])
            T = sum(b[2] for b in blks)
            for cp in range(4):
                bank = nxt("T", 2)
                for cc in range(2):
                    c = cp * 2 + cc
                    for (tg, o, n) in blks:
                        bi = o // 128
                        S.emit("pe", lambda e: e.transpose(psT[bank][:, cc * 512 + o:cc * 512 + o + n],
                                                           P["xn"][bi][:n, c * 128:(c + 1) * 128], ident[:n, :n]),
                               reads=[P["b_xn"][bi], b_ident], writes=[bT[bank]])
                for cc in range(2):
                    c = cp * 2 + cc
                    S.emit("dve", lambda e: e.tensor_scalar(out=outT[c][:, :T], in0=psT[bank][:, cc * 512:cc * 512 + T],
                                                            scalar1=gains[:, gi, c:c + 1], scalar2=None, op0=ALU.mult),
                           reads=[bT[bank], b_gains], writes=[boutT[c]])

        def ffn(P, fi, blks, xts, bxs, gi):
            T = sum(b[2] for b in blks)
            rmsnorm_T(P, blks, xts, bxs, gi, P["xT"], P["b_xT"])
            for f in range(NF):
                ws = nxt("wgu", len(P["wgu"]))
                S.dma("sp", P["wgu"][ws][:], WGU[fi][f], writes=[P["b_wgu"][ws]])
                banks = [nxt("G", 4), nxt("G", 4)]
                for which in range(2):
                    for k in range(8):
                        S.emit("pe", lambda e: e.matmul(psG[banks[which]][:, :T], P["wgu"][ws][:, which, k, :],
                                                        P["xT"][k][:, :T], start=(k == 0), stop=(k == 7)),
                               reads=[P["b_wgu"][ws], P["b_xT"][k]], writes=[bG[banks[which]]])
                sgi = nxt("sg", 2)
                S.emit("act", lambda e: e.activation(out=P["sg"][sgi][:, :T], in_=psG[banks[0]][:, :T], func=AF.Silu),
                       reads=[bG[banks[0]]], writes=[P["b_sg"][sgi]])
                S.emit("dve", lambda e: e.tensor_tensor(out=P["act"][f][:, :T], in0=P["sg"][sgi][:, :T],
                                                        in1=psG[banks[1]][:, :T], op=ALU.mult),
                       reads=[P["b_sg"][sgi], bG[banks[1]]], writes=[P["b_act"][f]])
            for h in range(2):
                S.dma("sp", P["wd"][:], WD[fi][h], writes=[P["b_wd"]])
                for (tg, o, n), xt, bx in zip(blks, xts, bxs):
                    bank = nxt("D", 2)
                    for f in range(NF):
                        S.emit("pe", lambda e: e.matmul(psD[bank][:n, :], P["act"][f][:, o:o + n], P["wd"][:, f, :],
                                                        start=(f == 0), stop=(f == NF - 1)),
                               reads=[P["b_act"][f], P["b_wd"]], writes=[bD[bank]])
                    S.emit("dve", lambda e: e.scalar_tensor_tensor(out=xt[:n, h * 512:(h + 1) * 512], in0=psD[bank][:n, :],
                                                                   scalar=0.5, in1=xt[:n, h * 512:(h + 1) * 512],
                                                                   op0=ALU.mult, op1=ALU.add),
                           reads=[bD[bank], bx], writes=[bx])

        def phase_A(s):
            L = seqL[s]
            with ExitStack() as pst:
                P = {}
                P["ss"] = sb("a_ss", (128, 4), stack=pst)
                P["b_ss"] = Buf()
                P["junk"] = sb("a_junk", (128, D), BF16, stack=pst)
                P["b_junk"] = Buf()
                P["xn"] = [sb(f"a_xn{i}", (128, D), BF16, stack=pst) for i in range(4)]
                P["b_xn"] = [Buf() for _ in range(4)]
                P["xT"] = [sb(f"a_xT{i}", (128, 512), BF16, stack=pst) for i in range(8)]
                P["b_xT"] = [Buf() for _ in range(8)]
                P["uT"] = [sb(f"a_uT{i}", (128, 512), BF16, stack=pst) for i in range(8)]
                P["b_uT"] = [Buf() for _ in range(8)]
                P["wgu"] = [sb(f"a_wgu{i}", (128, 2, 8, 128), BF16, stack=pst) for i in range(4)]
                P["b_wgu"] = [Buf() for _ in range(4)]
                P["sg"] = [sb(f"a_sg{i}", (128, 512), stack=pst) for i in range(2)]
                P["b_sg"] = [Buf() for _ in range(2)]
                P["act"] = [sb(f"a_act{i}", (128, 512), BF16, stack=pst) for i in range(NF)]
                P["b_act"] = [Buf() for _ in range(NF)]
                P["wd"] = sb("a_wd", (128, NF, 512), BF16, stack=pst)
                P["b_wd"] = Buf()
                rr["wgu"] = 0
                rr["sg"] = 0
                xsets = [[sb(f"a_x{j}_{i}", (128, D), stack=pst) for i in range(4)] for j in range(2)]
                bxsets = [[Buf() for _ in range(4)] for j in range(2)]
                win = [sb(f"a_win{i}", (128, 8, 128), BF16, stack=pst) for i in range(6)]
                b_win = [Buf() for _ in range(6)]
                wv = sb("a_wv", (128, 8, 512), BF16, stack=pst)
                b_wv = Buf()
                stf = [sb(f"a_stf{i}", (128, 512), stack=pst) for i in range(4)]
                b_stf = [Buf() for _ in range(4)]
                stb = [sb(f"a_stb{i}", (128, 512), BF16, stack=pst) for i in range(4)]
                b_stb = [Buf() for _ in range(4)]
                rr["win"] = 0
                rr["stf"] = 0
                rr["stb"] = 0
                tlist = tiles_of(L)

                def load_tile(ti):
                    t0_, T_ = tlist[ti]
                    bl = blocks_of(t0_, T_)
                    for (tg, o, n), xt, bx in zip(bl, xsets[ti % 2], bxsets[ti % 2]):
                        load_x_block(s, tg, n, xt, bx)

                load_tile(0)
                for ti, (t0, T) in enumerate(tlist):
                    blks = blocks_of(t0, T)
                    xts = xsets[ti % 2][:len(blks)]
                    bxs = bxsets[ti % 2][:len(blks)]
                    if ti + 1 < len(tlist):
                        load_tile(ti + 1)
                    ffn(P, 0, blks, xts, bxs, 0)
                    for (tg, o, n), xt, bx in zip(blks, xts, bxs):
                        S.dma("pool", Hs[s][tg:tg + n, :], xt[:n, :], reads=[bx])
                    rmsnorm_T(P, blks, xts, bxs, 1, P["uT"], P["b_uT"])
                    for c in list(range(0, 16)) + list(range(24, 67)):
                        ws = nxt("win", 6)
                        S.dma("sp", win[ws][:], WINS[c], writes=[b_win[ws]])
                        bank = nxt("G", 4)
                        for k in range(8):
                            S.emit("pe", lambda e: e.matmul(psG[bank][:, :T], win[ws][:, k, :], P["uT"][k][:, :T],
                                                            start=(k == 0), stop=(k == 7)),
                                   reads=[b_win[ws], P["b_uT"][k]], writes=[bG[bank]])
                        if c < 16:
                            si = nxt("stb", 4)
                            S.emit("act", lambda e: e.copy(out=stb[si][:, :T], in_=psG[bank][:, :T]),
                                   reads=[bG[bank]], writes=[b_stb[si]])
                            dst = (QTs if c < 8 else KTs)[s][c % 8, :, t0:t0 + T]
                            S.dma("pool", dst, stb[si][:, :T], reads=[b_stb[si]])
                        else:
                            si = nxt("stf", 4)
                            if c < 51:
                                S.emit("act", lambda e: e.copy(out=stf[si][:, :T], in_=psG[bank][:, :T]),
                                       reads=[bG[bank]], writes=[b_stf[si]])
                                dst = RWs[s][c - 24, :, t0:t0 + T]
                            else:
                                S.emit("act", lambda e: e.activation(out=stf[si][:, :T], in_=psG[bank][:, :T],
                                                                     func=AF.Sigmoid),
                                       reads=[bG[bank]], writes=[b_stf[si]])
                                dst = GTs[s][c - 51, :, t0:t0 + T]
                            S.dma("pool", dst, stf[si][:, :T], reads=[b_stf[si]])
                    for h in range(2):
                        S.dma("sp", wv[:], WVS[h], writes=[b_wv])
                        for (tg, o, n) in blks:
                            bank = nxt("D", 2)
                            for k in range(8):
                                S.emit("pe", lambda e: e.matmul(psD[bank][:n, :], P["uT"][k][:, o:o + n], wv[:, k, :],
                                                                start=(k == 0), stop=(k == 7)),
                                       reads=[P["b_uT"][k], b_wv], writes=[bD[bank]])
                            si = nxt("stb", 4)
                            S.emit("act", lambda e: e.copy(out=stb[si][:n, :], in_=psD[bank][:n, :]),
                                   reads=[bD[bank]], writes=[b_stb[si]])
                            S.dma("pool", Vs[s][tg:tg + n, h * 512:(h + 1) * 512], stb[si][:n, :], reads=[b_stb[si]])
                S.barrier()

        for s in range(NS):
            if "A" in phases:
                phase_A(s)


        def phase_B(s):
            L = seqL[s]
            TQ = 384
            qtiles = tiles_of(L, TQ)
            kblocks = tiles_of(L, 128)
            nkb = len(kblocks)
            nfull = L // 128
            ntail = L - nfull * 128
            accb = [psD[0][:], psD[1][:], psT[1][:].bitcast(F32)]
            bacc = [bD[0], bD[1], bT[1]]
            with ExitStack() as pst:
                kt = [sb("b_kt", (128, L), BF16, pst) for _ in range(2)]
                qt = [sb("b_qt", (128, L), BF16, pst) for _ in range(2)]
                va = [sb("b_va", (128, nkb, 129), BF16, pst) for _ in range(2)]
                b_kqv = [Buf(), Buf()]
                PT = [sb("b_PT", (128, TQ), BF16, pst) for _ in range(4)]
                b_PT = [Buf() for _ in range(4)]
                tmpS = [sb("b_tmpS", (128, TQ), F32, pst) for _ in range(2)]
                b_tmpS = [Buf() for _ in range(2)]
                rd = sb("b_rd", (128, 8), F32, pst)
                b_rd = Buf()
                o1 = sb("b_o1", (128, 128), F32, pst)
                oo = sb("b_oo", (128, 128), F32, pst)
                junk = sb("b_junk", (128, 128), F32, pst)
                on = sb("b_on", (128, 128), BF16, pst)
                b_o = Buf()
                ost = [sb("b_ost", (128, TQ), BF16, pst) for _ in range(2)]
                b_ost = [Buf(), Buf()]
                rr["PT"] = 0
                rr["tmpS"] = 0
                rr["ost"] = 0
                for i in range(2):
                    S.emit("pool", lambda e: e.memset(va[i][:, :, 128:129], 1.0), writes=[b_kqv[i]])
                for h in range(8):
                    i = h % 2
                    S.dma("sp", kt[i][:], KTs[s][h], writes=[b_kqv[i]])
                    S.dma("sp", qt[i][:], QTs[s][h], writes=[b_kqv[i]])
                    if nfull:
                        S.dma("sp", va[i][:, 0:nfull, 0:128],
                              Vs[s][0:nfull * 128, h * 128:(h + 1) * 128].rearrange("(kb p) c -> p kb c", p=128),
                              writes=[b_kqv[i]])
                    if ntail:
                        S.dma("sp", va[i][0:ntail, nfull, 0:128], Vs[s][nfull * 128:L, h * 128:(h + 1) * 128],
                              writes=[b_kqv[i]])
                    for (qt0, TQn) in qtiles:
                        subs = blocks_of(qt0, TQn)

                        def qk(kbi):
                            k0, nk = kblocks[kbi]
                            banks = [nxt("G", 4), nxt("G", 4)]
                            for c in range(2):
                                S.emit("pe", lambda e: e.matmul(psG[banks[c]][:nk, :TQn], kt[i][c * 64:(c + 1) * 64, k0:k0 + nk],
                                                                qt[i][c * 64:(c + 1) * 64, qt0:qt0 + TQn], start=True, stop=True),
                                       reads=[b_kqv[i]], writes=[bG[banks[c]]])
                            return banks

                        nb = qk(0)
                        for kbi, (k0, nk) in enumerate(kblocks):
                            banks = nb
                            if kbi + 1 < nkb:
                                nb = qk(kbi + 1)
                            d = k0 - qt0
                            relmin = d - (TQn - 1)
                            relmax = d + nk - 1
                            near = not (relmin >= 91 or relmax <= -91)
                            for c in range(2):
                                pi = nxt("PT", 4)
                                if near:
                                    ti = nxt("tmpS", 2)
                                    m0 = 384 - d
                                    S.emit("dve", lambda e: e.scalar_tensor_tensor(
                                        out=tmpS[ti][:nk, :TQn], in0=psG[banks[c]][:nk, :TQn], scalar=0.125,
                                        in1=Gs[h][:nk, m0:m0 + TQn], op0=ALU.mult, op1=ALU.add),
                                        reads=[bG[banks[c]], b_Gs[h]], writes=[b_tmpS[ti]])
                                    S.emit("act", lambda e: e.activation(out=PT[pi][:nk, :TQn], in_=tmpS[ti][:nk, :TQn], func=AF.Exp),
                                           reads=[b_tmpS[ti]], writes=[b_PT[pi]])
                                else:
                                    col = (31 if relmin >= 91 else 15) * 8 + h
                                    S.emit("act", lambda e: e.activation(out=PT[pi][:nk, :TQn], in_=psG[banks[c]][:nk, :TQn], func=AF.Exp,
                                                                         scale=0.125, bias=tb[:nk, col:col + 1]),
                                           reads=[bG[banks[c]], b_tb], writes=[b_PT[pi]])
                                for si, (qg, o, nq) in enumerate(subs):
                                    first = (kbi == 0 and c == 0)
                                    last = (kbi == nkb - 1 and c == 1)
                                    S.emit("pe", lambda e: e.matmul(accb[si][:nq, c * 129:(c + 1) * 129], PT[pi][:nk, o:o + nq],
                                                                    va[i][:nk, kbi, :], start=first, stop=last),
                                           reads=[b_PT[pi], b_kqv[i]], writes=[bacc[si]])
                        oi = nxt("ost", 2)
                        for si, (qg, o, nq) in enumerate(subs):
                            A = accb[si]
                            S.emit("dve", lambda e: e.reciprocal(out=rd[:nq, 0:1], in_=A[:nq, 128:129]), reads=[bacc[si]], writes=[b_rd])
                            S.emit("dve", lambda e: e.reciprocal(out=rd[:nq, 1:2], in_=A[:nq, 257:258]), reads=[bacc[si]], writes=[b_rd])
                            S.emit("dve", lambda e: e.tensor_scalar(out=rd[:nq, 2:3], in0=rd[:nq, 1:2], scalar1=lams[:nq, 5:6],
                                                                    scalar2=None, op0=ALU.mult), reads=[b_rd, b_lams], writes=[b_rd])
                            S.emit("act", lambda e: e.activation(out=o1[:nq, :], in_=A[:nq, 0:128], func=AF.Copy, scale=rd[:nq, 0:1]),
                                   reads=[bacc[si], b_rd], writes=[b_o])
                            S.emit("dve", lambda e: e.scalar_tensor_tensor(out=oo[:nq, :], in0=A[:nq, 129:257], scalar=rd[:nq, 2:3],
                                                                           in1=o1[:nq, :], op0=ALU.mult, op1=ALU.add),
                                   reads=[bacc[si], b_rd, b_o], writes=[b_o])
                            S.emit("pool", lambda e: e.memset(rd[:, 3:4], 0.0), writes=[b_rd])
                            S.emit("act", lambda e: e.activation(out=junk[:nq, :], in_=oo[:nq, :], func=AF.Square, accum_out=rd[:nq, 3:4]),
                                   reads=[b_o], writes=[b_rd, b_o])
                            S.emit("act", lambda e: e.activation(out=rd[:nq, 4:5], in_=rd[:nq, 3:4], func=AF.Sqrt, bias=epsT[:nq, 0:1],
                                                                 scale=1.0 / 128), reads=[b_rd, b_eps], writes=[b_rd])
                            S.emit("dve", lambda e: e.reciprocal(out=rd[:nq, 5:6], in_=rd[:nq, 4:5]), reads=[b_rd], writes=[b_rd])
                            S.emit("act", lambda e: e.activation(out=on[:nq, :], in_=oo[:nq, :], func=AF.Copy, scale=rd[:nq, 5:6]),
                                   reads=[b_rd, b_o], writes=[b_o])
                            S.emit("pe", lambda e: e.transpose(psT[0][:, o:o + nq], on[:nq, :], ident[:nq, :nq]),
                                   reads=[b_o, b_ident], writes=[bT[0]])
                            S.emit("dve", lambda e: e.tensor_scalar(out=ost[oi][:, o:o + nq], in0=psT[0][:, o:o + nq], scalar1=gsub[:, 0:1],
                                                                    scalar2=None, op0=ALU.mult),
                                   reads=[bT[0], b_gsub], writes=[b_ost[oi]])
                        S.dma("pool", OTs[s][h * 128:(h + 1) * 128, qt0:qt0 + TQn], ost[oi][:, :TQn], reads=[b_ost[oi]])
                S.barrier()

        with ExitStack() as gst:
            bk = sb("bk", (128, 896), F32, gst)
            b_bk = Buf()
            S.dma("sp", bk[:], c_bk[:, :], writes=[b_bk])
            Gs = [sb(f"G{h}", (128, 896), F32, gst) for h in range(8)]
            b_Gs = [Buf() for _ in range(8)]
            gm = [sb("gm", (128, 896), F32, gst) for _ in range(2)]
            b_gm = [Buf(), Buf()]
            for h in range(8):
                S.emit("pool", lambda e: e.memset(Gs[h][:], 0.0), writes=[b_Gs[h]])
            for b in range(32):
                mi = b % 2
                S.emit("pool", lambda e: e.tensor_scalar(out=gm[mi][:], in0=bk[:], scalar1=float(b), scalar2=None, op0=ALU.is_equal),
                       reads=[b_bk], writes=[b_gm[mi]])
                for h in range(8):
                    S.emit("dve", lambda e: e.scalar_tensor_tensor(out=Gs[h][:], in0=gm[mi][:], scalar=tb[:, b * 8 + h:b * 8 + h + 1], in1=Gs[h][:],
                                                                   op0=ALU.mult, op1=ALU.add), reads=[b_gm[mi], b_tb], writes=[b_Gs[h]])

            for s in range(NS):
                if "B" in phases:
                    phase_B(s)
            S.barrier()

        def phase_C(s):
            L = seqL[s]
            SEG = 128
            Lp = ((L + 63) // 64) * 64
            segs = tiles_of(Lp, SEG)
            nseg = len(segs)
            allb = [(psG[i][:], bG[i]) for i in range(4)] + [(psD[i][:], bD[i]) for i in range(2)] + [(psT[1][:].bitcast(F32), bT[1])]
            rr["ps"] = 0

            def bank():
                i = nxt("ps", len(allb))
                return allb[i]

            EXC = 2.0 ** 0
            with ExitStack() as pst:
                NG = 8
                def mk(name, shape, dt=F32, n=1):
                    return [sb(name, shape, dt, pst) for _ in range(n)]
                nchm = SEG // 64
                RAW = [sb("c_RAW", (128, NG, SEG + 2), F32, pst) for _ in range(3)]
                b_RAW = [Buf() for _ in range(3)]
                rawd = [mk("c_rawd", (128, SEG + 2), F32, 2) for _ in range(2)]
                b_raw = [Buf(), Buf()]
                NOPS = 6
                EXP = [[sb("c_EXP", (128, NG, nchm, 128), BF16, pst) for o in range(NOPS)] for _ in range(2)]
                PEND = [sb("c_PEND", (128, NG, nchm), F32, pst) for _ in range(2)]
                b_expg = [Buf(), Buf()]
                exp_ = [[[EXP[gb][o][:, u] for o in range(NOPS)] for u in range(NG)] for gb in range(2)]
                b_exp = [[b_expg[gb] for u in range(NG)] for gb in range(2)]
                pend = [[PEND[gb][:, u] for u in range(NG)] for gb in range(2)]
                for gb in range(2):
                    for o in range(NOPS):
                        S.emit("pool", lambda e: e.memset(EXP[gb][o][:], 0.0), writes=[b_expg[gb]])
                TN = ["rs", "ks", "vs", "kk", "t1", "t2", "w", "al", "kd", "bb", "P", "rW", "Wp", "Wi"]
                TB = {nm: sb("c_" + nm, (128, NG, SEG), F32, pst) for nm in TN}
                BB = {nm: Buf() for nm in TN}
                SQ = sb("c_SQ", (128, NG, SEG), BF16, pst)
                b_SQ = Buf()
                zer = sb("c_zer", (128, 64), F32, pst)
                b_zer = Buf()
                S.emit("pool", lambda e: e.memset(zer[:], 0.0), writes=[b_zer])
                dsh = [[sb("c_dsh", (128, SEG), F32, pst) for _ in range(2)] for _ in range(2)]
                twd = [sb("c_twd", (128, SEG), BF16, pst) for _ in range(2)]
                tad = [sb("c_tad", (128, SEG), BF16, pst) for _ in range(2)]
                b_d = [Buf(), Buf()]
                Sf = [[sb("c_S", (128, 64), F32, pst) for c in range(8)] for d in range(2)]
                Sb = [[sb("c_Sb", (128, 64), BF16, pst) for c in range(8)] for d in range(2)]
                b_S = [[Buf() for c in range(8)] for d in range(2)]
                for d in range(2):
                    for c in range(8):
                        S.emit("pool", lambda e: e.memset(Sf[d][c][:], 0.0), writes=[b_S[d][c]])
                        S.emit("pool", lambda e: e.memset(Sb[d][c][:], 0.0), writes=[b_S[d][c]])
                PP = [[sb("c_PP", (128, 256), BF16, pst) for _ in range(2)] for u in range(NG)]
                b_PP = [[Buf(), Buf()] for u in range(NG)]
                L3 = [sb("c_L3", (128, 384), BF16, pst) for u in range(NG)]
                b_L3 = [Buf() for u in range(NG)]
                TT = [[sb("c_TT", (128, 128), BF16, pst) for _ in range(2)] for u in range(NG)]
                b_TT = [[Buf(), Buf()] for u in range(NG)]
                BK_ = [sb("c_BK", (128, 256), BF16, pst) for u in range(NG)]
                b_BK = [Buf() for u in range(NG)]
                Vst = [sb("c_Vst", (128, 64), BF16, pst) for u in range(NG)]
                coef = [sb("c_coef", (128, 1), F32, pst) for u in range(NG)]
                b_V = [Buf() for u in range(NG)]
                Xb = [sb("c_Xb", (128, 64), BF16, pst) for u in range(NG)]
                Ub = [sb("c_Ub", (128, 64), BF16, pst) for u in range(NG)]
                b_X = [Buf() for u in range(NG)]
                b_U = [Buf() for u in range(NG)]
                Yst = [sb("c_Yst", (128, NG, 64), F32, pst) for _ in range(2)]
                Bst = [sb("c_Bst", (128, NG, 64), F32, pst) for _ in range(2)]
                b_Yst = [Buf(), Buf()]
                b_Bst = [Buf(), Buf()]
                rr["yst"] = 0

                def shift(dst, src, j, n, rb, wb, eng="dve"):
                    S.emit(eng, lambda e: e.tensor_scalar(out=dst[:, :n], in0=src[:, 1:n + 1], scalar1=mus[:, 2, j:j + 1], scalar2=None, op0=ALU.mult),
                           reads=[b_mus] + rb, writes=wb)
                    S.emit(eng, lambda e: e.scalar_tensor_tensor(out=dst[:, :n], in0=src[:, 0:n], scalar=mus[:, 0, j:j + 1], in1=dst[:, :n],
                                                                 op0=ALU.mult, op1=ALU.add), reads=[b_mus] + rb, writes=wb)
                    S.emit(eng, lambda e: e.scalar_tensor_tensor(out=dst[:, :n], in0=src[:, 2:n + 2], scalar=mus[:, 1, j:j + 1], in1=dst[:, :n],
                                                                 op0=ALU.mult, op1=ALU.add), reads=[b_mus] + rb, writes=wb)

                def load_group(gb, d, half, seg0, n):
                    lo = max(seg0 - 1, 0)
                    hi = min(seg0 + n + 1, L)
                    edge = (seg0 - 1 < 0) or (seg0 + n + 1 > L)
                    for i in range(3):
                        if edge:
                            S.emit("pool", lambda e: e.memset(RAW[i][:], 0.0), writes=[b_RAW[i]])
                        S.dma("sp", RAW[i][:, :, lo - (seg0 - 1):hi - (seg0 - 1)],
                              RWs[s][i * 8:(i + 1) * 8, :, lo:hi].rearrange("u p t -> p u t"), writes=[b_RAW[i]])
                    for (t, j) in [(rawd[gb][0], 24), (rawd[gb][1], 25)]:
                        if edge:
                            S.emit("pool", lambda e: e.memset(t[:], 0.0), writes=[b_raw[gb]])
                        S.dma("sp", t[:, lo - (seg0 - 1):hi - (seg0 - 1)], RWs[s][j, :, lo:hi], writes=[b_raw[gb]])

                def prep_group(gb, d, half, seg0, n):
                    nch = n // 64
                    npad0 = max(0, min(n, L - seg0))
                    bex = b_expg[gb]

                    def op(eng, fn, R=(), Wr=(), xr=(), xw=()):
                        S.emit(eng, fn, reads=[BB[x] for x in R] + list(xr), writes=[BB[x] for x in Wr] + list(xw))

                    def T(nm):
                        return TB[nm][:, :, :n]

                    def bc(ap2):
                        return ap2.unsqueeze(2).to_broadcast([128, NG, n])

                    shift(dsh[gb][0], rawd[gb][0], 24, n, [b_raw[gb]], [b_d[gb]])
                    shift(dsh[gb][1], rawd[gb][1], 25, n, [b_raw[gb]], [b_d[gb]])
                    S.emit("act", lambda e: e.activation(out=twd[gb][:, :n], in_=dsh[gb][0][:, :n], func=AF.Tanh), reads=[b_d[gb]], writes=[b_d[gb]])
                    S.emit("act", lambda e: e.copy(out=tad[gb][:, :n], in_=dsh[gb][1][:, :n]), reads=[b_d[gb]], writes=[b_d[gb]])
                    yield
                    for i, nm in enumerate(["rs", "ks", "vs"]):
                        j0 = i * 8
                        tmpn = "t1" if i % 2 == 0 else "t2"
                        op("dve", lambda e: e.tensor_tensor(out=T(nm), in0=RAW[i][:, :, 1:n + 1], in1=bc(mus[:, 2, j0:j0 + NG]), op=ALU.mult),
                           Wr=[nm], xr=[b_RAW[i], b_mus])
                        op("pool", lambda e: e.tensor_tensor(out=T(tmpn), in0=RAW[i][:, :, 0:n], in1=bc(mus[:, 0, j0:j0 + NG]), op=ALU.mult),
                           Wr=[tmpn], xr=[b_RAW[i], b_mus])
                        op("dve", lambda e: e.tensor_tensor(out=T(nm), in0=T(nm), in1=T(tmpn), op=ALU.add), R=[nm, tmpn], Wr=[nm])
                        op("pool", lambda e: e.tensor_tensor(out=T(tmpn), in0=RAW[i][:, :, 2:n + 2], in1=bc(mus[:, 1, j0:j0 + NG]), op=ALU.mult),
                           R=[nm], Wr=[tmpn], xr=[b_RAW[i], b_mus])
                        op("dve", lambda e: e.tensor_tensor(out=T(nm), in0=T(nm), in1=T(tmpn), op=ALU.add), R=[nm, tmpn], Wr=[nm])
                        if npad0 < n:
                            op("pool", lambda e: e.memset(TB[nm][:, :, npad0:n], 0.0), Wr=[nm])
                        yield
                    op("pool", lambda e: e.tensor_tensor(out=T("kk"), in0=T("ks"), in1=bc(rwp[:, 4, 0:NG]), op=ALU.mult), R=["ks"], Wr=["kk"], xr=[b_rwp])
                    op("dve", lambda e: e.tensor_tensor(out=SQ[:, :, :n], in0=T("kk"), in1=T("kk"), op=ALU.mult), R=["kk"], xw=[b_SQ])
                    for j in range(NG // 4):
                        (pa, bpa) = bank()
                        for uu in range(4):
                            u = j * 4 + uu
                            S.emit("pe", lambda e: e.matmul(pa[:, uu * 128:uu * 128 + n], bones[:], SQ[:, u, :n], start=True, stop=True),
                                   reads=[b_SQ, b_rc], writes=[bpa])
                        op("act", lambda e: e.activation(out=TB["t1"][:, j * 4:(j + 1) * 4, :n], in_=pa[:, :].rearrange("p (u t) -> p u t", t=128)[:, :, :n],
                                                         func=AF.Sqrt), Wr=["t1"], xr=[bpa])
                    op("dve", lambda e: e.tensor_scalar(out=T("t1"), in0=T("t1"), scalar1=1e-12, scalar2=None, op0=ALU.max), R=["t1"], Wr=["t1"])
                    op("dve", lambda e: e.reciprocal(out=T("t1"), in_=T("t1")), R=["t1"], Wr=["t1"])
                    op("dve", lambda e: e.tensor_tensor(out=T("kk"), in0=T("kk"), in1=T("t1"), op=ALU.mult), R=["kk", "t1"], Wr=["kk"])
                    yield
                    rows = slice(d * 64, (d + 1) * 64)
                    for (wsb, src, dstn, bi) in [(w2sb, twd, "w", 0), (a2sb, tad, "al", 2)]:
                        for j in range(NG // 4):
                            (pw, bpw) = bank()
                            for uu in range(4):
                                u = j * 4 + uu
                                S.emit("pe", lambda e: e.matmul(pw[:, uu * 128:uu * 128 + n], wsb[rows, u * 128:(u + 1) * 128], src[gb][rows, :n],
                                                                start=True, stop=True), reads=[b_w2, b_d[gb]], writes=[bpw])
                            for uu in range(4):
                                u = j * 4 + uu
                                op("act", lambda e: e.activation(out=TB[dstn][:, u, :n], in_=pw[:, uu * 128:uu * 128 + n], func=AF.Sigmoid,
                                                                 bias=rwp[:, bi + d, u:u + 1]), Wr=[dstn], xr=[bpw, b_rwp])
                        yield
                    op("act", lambda e: e.activation(out=T("w"), in_=T("w"), func=AF.Exp, scale=-math.exp(-0.5)), R=["w"], Wr=["w"])
                    if npad0 < n:
                        op("pool", lambda e: e.memset(TB["w"][:, :, npad0:n], 1.0), Wr=["w"])
                    op("dve", lambda e: e.tensor_scalar(out=T("t1"), in0=T("al"), scalar1=-1.0, scalar2=None, op0=ALU.add), R=["al"], Wr=["t1"])
                    op("pool", lambda e: e.tensor_tensor(out=T("t1"), in0=T("t1"), in1=bc(rwp[:, 5, 0:NG]), op=ALU.mult), R=["t1"], Wr=["t1"], xr=[b_rwp])
                    op("dve", lambda e: e.scalar_tensor_tensor(out=T("kd"), in0=T("t1"), scalar=1.0, in1=T("ks"), op0=ALU.add, op1=ALU.mult),
                       R=["t1", "ks"], Wr=["kd"])
                    op("pool", lambda e: e.tensor_tensor(out=T("bb"), in0=T("kk"), in1=T("al"), op=ALU.mult), R=["kk", "al"], Wr=["bb"])
                    yield
                    for u in range(NG):
                        for c_ in range(nch):
                            op("dve", lambda e: e.tensor_tensor_scan(out=TB["P"][:, u, c_ * 64:(c_ + 1) * 64], data0=TB["w"][:, u, c_ * 64:(c_ + 1) * 64],
                                                                     data1=zer[:, :], initial=1.0, op0=ALU.mult, op1=ALU.add),
                               R=["w"], Wr=["P"], xr=[b_zer])
                    yield
                    P4 = T("P").rearrange("p u (c t) -> p u c t", t=64)
                    op("dve", lambda e: e.tensor_copy(out=PEND[gb][:, :, :nch], in_=P4[:, :, :, 63]), R=["P"], xw=[bex])
                    pe4 = PEND[gb][:, :, :nch].unsqueeze(3).to_broadcast([128, NG, nch, 64])

                    def v4(nm):
                        return T(nm).rearrange("p u (c t) -> p u c t", t=64)
                    op("dve", lambda e: e.reciprocal(out=T("rW"), in_=T("P")), R=["P"], Wr=["rW"])
                    if d == 0:
                        op("dve", lambda e: e.reciprocal(out=T("t1"), in_=T("w")), R=["w"], Wr=["t1"])
                        op("pool", lambda e: e.tensor_tensor(out=T("Wp"), in0=T("P"), in1=T("t1"), op=ALU.mult), R=["P", "t1"], Wr=["Wp"])
                        Wi, rW = "P", "rW"
                    else:
                        op("dve", lambda e: e.tensor_tensor(out=v4("Wp"), in0=v4("rW"), in1=pe4, op=ALU.mult), R=["rW"], Wr=["Wp"], xr=[bex])
                        op("pool", lambda e: e.tensor_tensor(out=T("Wi"), in0=T("Wp"), in1=T("w"), op=ALU.mult), R=["Wp", "w"], Wr=["Wi"])
                        op("dve", lambda e: e.reciprocal(out=T("t1"), in_=T("Wi")), R=["Wi"], Wr=["t1"])
                        Wi, rW = "Wi", "t1"
                    yield

                    def halves(o, nm):
                        for hh in range(2):
                            ps_ = slice(hh * 64, (hh + 1) * 64)
                            yield (EXP[gb][o][ps_, :, 0:nch, hh * 64:(hh + 1) * 64],
                                   lambda name: TB[name][ps_, :, :n].rearrange("p u (c t) -> p u c t", t=64))
                    for (oap, iv) in halves(0, None):
                        op("dve", lambda e: e.scalar_tensor_tensor(out=oap, in0=iv("kk"), scalar=-1.0, in1=iv("Wp"), op0=ALU.mult, op1=ALU.mult),
                           R=["kk", "Wp"], xw=[bex])
                    for (oap, iv) in halves(1, None):
                        op("pool", lambda e: e.tensor_tensor(out=oap, in0=iv("rs"), in1=iv(Wi), op=ALU.mult), R=["rs", Wi], xw=[bex])
                    yield
                    for (oap, iv) in halves(2, None):
                        op("dve", lambda e: e.tensor_tensor(out=oap, in0=iv("bb"), in1=iv(rW), op=ALU.mult), R=["bb", rW], xw=[bex])
                    for (oap, iv) in halves(3, None):
                        op("pool", lambda e: e.tensor_tensor(out=oap, in0=iv("kd"), in1=iv(rW), op=ALU.mult), R=["kd", rW], xw=[bex])
                    yield
                    for (oap, iv) in halves(4, None):
                        op("act", lambda e: e.copy(out=oap, in_=iv("vs")), R=["vs"], xw=[bex])
                    op("pool", lambda e: e.tensor_tensor(out=T("t2"), in0=T("rs"), in1=bc(rwp[:, 6, 0:NG]), op=ALU.mult), R=["rs"], Wr=["t2"], xr=[b_rwp])
                    for (oap, iv) in halves(5, None):
                        op("dve", lambda e: e.tensor_tensor(out=oap, in0=iv("t2"), in1=iv("kd"), op=ALU.mult), R=["t2", "kd"], xw=[bex])
                    yield

                def chunk_group(gb, d, half, seg0, n, ch):
                    tok0 = seg0 + ch * 64
                    nt = max(0, min(64, L - tok0))
                    units = range(NG)
                    yi = nxt("yst", 2)
                    for u in units:
                        E = exp_[gb][u]
                        A, Rr, B, K, V, RK = [E[o][:, ch, :] for o in range(NOPS)]
                        be = b_exp[gb][u]
                        (p1, bp1) = bank()
                        S.emit("pe", lambda e: e.matmul(p1[:, 0:128], A, B, start=True, stop=True), reads=[be], writes=[bp1])
                        S.emit("pe", lambda e: e.matmul(p1[:, 128:256], B, A, start=True, stop=True), reads=[be], writes=[bp1])
                        S.emit("dve", lambda e: e.tensor_tensor(out=PP[u][0][:], in0=p1[:, 0:256], in1=m12[d][:], op=ALU.mult),
                               reads=[bp1, b_mm], writes=[b_PP[u][0]])
                        S.emit("pool", lambda e: e.tensor_tensor(out=TT[u][0][:], in0=PP[u][0][:, 128:256], in1=ident[:], op=ALU.add),
                               reads=[b_PP[u][0], b_ident], writes=[b_TT[u][0]])
                        (p2, bp2) = bank()
                        S.emit("pe", lambda e: e.matmul(p2[:, 0:128], K, A, start=True, stop=True), reads=[be], writes=[bp2])
                        S.emit("pe", lambda e: e.matmul(p2[:, 128:256], B, Rr, start=True, stop=True), reads=[be], writes=[bp2])
                        S.emit("pe", lambda e: e.matmul(p2[:, 256:384], K, Rr, start=True, stop=True), reads=[be], writes=[bp2])
                        S.emit("dve", lambda e: e.tensor_tensor(out=L3[u][:], in0=p2[:, 0:384], in1=m345[d][:], op=ALU.mult),
                               reads=[bp2, b_mm], writes=[b_L3[u]])
                        (p3, bp3) = bank()
                        p3b = p3.bitcast(BF16)
                        S.emit("pe", lambda e: e.transpose(p3b[:, 0:128], B, ident[:]), reads=[be, b_ident], writes=[bp3])
                        S.emit("pe", lambda e: e.transpose(p3b[:, 128:256], K, ident[:]), reads=[be, b_ident], writes=[bp3])
                        S.emit("pe", lambda e: e.matmul(p3[:, 128:192], V, istack[:], start=True, stop=True), reads=[be, b_rc], writes=[bp3])
                        S.emit("pe", lambda e: e.matmul(p3[:, 192:193], RK, onesb[:], start=True, stop=True), reads=[be, b_rc], writes=[bp3])
                        S.emit("act", lambda e: e.copy(out=BK_[u][:], in_=p3b[:, 0:256]), reads=[bp3], writes=[b_BK[u]])
                        S.emit("act", lambda e: e.copy(out=Vst[u][:], in_=p3[:, 128:192]), reads=[bp3], writes=[b_V[u]])
                        S.emit("act", lambda e: e.copy(out=coef[u][:], in_=p3[:, 192:193]), reads=[bp3], writes=[b_V[u]])
                        S.emit("pool", lambda e: e.tensor_scalar(out=Bst[yi][:, u, :], in0=Vst[u][:], scalar1=coef[u][:, 0:1], scalar2=None, op0=ALU.mult),
                               reads=[b_V[u]], writes=[b_Bst[yi]])
                    yield
                    for j in range(6):
                        if j > 0:
                            yield
                        cur, new = j % 2, (j + 1) % 2
                        for u in units:
                            (pq, bpq) = bank()
                            Pj = PP[u][cur][:, 0:128]
                            PjT = PP[u][cur][:, 128:256]
                            if j < 5:
                                S.emit("pe", lambda e: e.matmul(pq[:, 0:128], PjT, Pj, start=True, stop=True), reads=[b_PP[u][cur]], writes=[bpq])
                                S.emit("pe", lambda e: e.matmul(pq[:, 128:256], Pj, PjT, start=True, stop=True), reads=[b_PP[u][cur]], writes=[bpq])
                            if j >= 1:
                                tc_, tn_ = (j - 1) % 2, j % 2
                                S.emit("pe", lambda e: e.matmul(pq[:, 256:384], ident[:], TT[u][tc_][:], start=True, stop=False),
                                       reads=[b_ident, b_TT[u][tc_]], writes=[bpq])
                                S.emit("pe", lambda e: e.matmul(pq[:, 256:384], Pj, TT[u][tc_][:], start=False, stop=True),
                                       reads=[b_PP[u][cur], b_TT[u][tc_]], writes=[bpq])
                                if u % 4 != 3:
                                    S.emit("act", lambda e: e.copy(out=TT[u][tn_][:], in_=pq[:, 256:384]), reads=[bpq], writes=[b_TT[u][tn_]])
                                else:
                                    S.emit("dve", lambda e: e.tensor_copy(out=TT[u][tn_][:], in_=pq[:, 256:384]), reads=[bpq], writes=[b_TT[u][tn_]])
                            if j < 5:
                                if u % 4 != 3:
                                    S.emit("act", lambda e: e.copy(out=PP[u][new][:], in_=pq[:, 0:256]), reads=[bpq], writes=[b_PP[u][new]])
                                else:
                                    S.emit("dve", lambda e: e.tensor_copy(out=PP[u][new][:], in_=pq[:, 0:256]), reads=[bpq], writes=[b_PP[u][new]])
                    tfin = 5 % 2
                    yield
                    for u in units:
                        c = half * NG + u
                        E = exp_[gb][u]
                        A = E[0][:, ch, :]
                        (px, bpx) = bank()
                        S.emit("pe", lambda e: e.matmul(px[:, 0:64], A, Sb[d][c][:], start=True, stop=False), reads=[b_exp[gb][u], b_S[d][c]], writes=[bpx])
                        S.emit("pe", lambda e: e.matmul(px[:, 0:64], L3[u][:, 0:128], Vst[u][:], start=False, stop=True), reads=[b_L3[u], b_V[u]], writes=[bpx])
                        S.emit("act", lambda e: e.copy(out=Xb[u][:], in_=px[:, 0:64]), reads=[bpx], writes=[b_X[u]])
                    yield
                    for u in units:
                        (pu, bpu) = bank()
                        S.emit("pe", lambda e: e.matmul(pu[:, 0:64], TT[u][tfin][:], Xb[u][:], start=True, stop=True), reads=[b_TT[u][tfin], b_X[u]], writes=[bpu])
                        S.emit("dve", lambda e: e.tensor_copy(out=Ub[u][:], in_=pu[:, 0:64]), reads=[bpu], writes=[b_U[u]])
                    yield
                    for u in units:
                        c = half * NG + u
                        E = exp_[gb][u]
                        Rr = E[1][:, ch, :]
                        (py, bpy) = bank()
                        S.emit("pe", lambda e: e.matmul(py[:, 0:64], Rr, Sb[d][c][:], start=True, stop=False), reads=[b_exp[gb][u], b_S[d][c]], writes=[bpy])
                        S.emit("pe", lambda e: e.matmul(py[:, 0:64], L3[u][:, 128:256], Ub[u][:], start=False, stop=False), reads=[b_L3[u], b_U[u]], writes=[bpy])
                        S.emit("pe", lambda e: e.matmul(py[:, 0:64], L3[u][:, 256:384], Vst[u][:], start=False, stop=True), reads=[b_L3[u], b_V[u]], writes=[bpy])
                        S.emit("act", lambda e: e.copy(out=Yst[yi][:, u, :], in_=py[:, 0:64]), reads=[bpy], writes=[b_Yst[yi]])
                        (pS, bpS) = bank()
                        S.emit("pe", lambda e: e.matmul(pS[:, 0:64], BK_[u][:, 0:128], Ub[u][:], start=True, stop=False), reads=[b_BK[u], b_U[u]], writes=[bpS])
                        S.emit("pe", lambda e: e.matmul(pS[:, 0:64], BK_[u][:, 128:256], Vst[u][:], start=False, stop=True), reads=[b_BK[u], b_V[u]], writes=[bpS])
                        S.emit("pool", lambda e: e.tensor_scalar(out=Sf[d][c][:], in0=Sf[d][c][:], scalar1=pend[gb][u][:, ch:ch + 1], scalar2=None, op0=ALU.mult),
                               reads=[b_exp[gb][u]], writes=[b_S[d][c]])
                        S.emit("dve", lambda e: e.scalar_tensor_tensor(out=Sf[d][c][:], in0=pS[:, 0:64], scalar=pend[gb][u][:, ch:ch + 1], in1=Sf[d][c][:],
                                                                       op0=ALU.mult, op1=ALU.add), reads=[bpS, b_exp[gb][u]], writes=[b_S[d][c]])
                        S.emit("act", lambda e: e.copy(out=Sb[d][c][:], in_=Sf[d][c][:]), reads=[b_S[d][c]], writes=[b_S[d][c]])
                    yield
                    if nt > 0:
                        for (stg, bst, dst) in [(Yst[yi], b_Yst[yi], YSs[d][s]), (Bst[yi], b_Bst[yi], BNs[d][s])]:
                            dv = dst[tok0:tok0 + nt, :].rearrange("t (c h v) -> t c h v", h=2, v=64)
                            for hh in range(2):
                                S.dma("sp", dv[:, half * NG:(half + 1) * NG, hh, :], stg[hh * 64:hh * 64 + nt, :, :], reads=[bst])

                work = []
                for i in range(nseg):
                    for d in range(2):
                        si = i if d == 0 else nseg - 1 - i
                        for half in range(8 // NG):
                            work.append((d, half, segs[si][0], segs[si][1]))
                def run_all(gen):
                    for _ in gen:
                        pass

                def chunks_gen(wi):
                    d, half, seg0, n = work[wi]
                    chs = list(range(n // 64))
                    if d == 1:
                        chs = chs[::-1]
                    for ch in chs:
                        yield from chunk_group(wi % 2, d, half, seg0, n, ch)

                NW = len(work)
                load_group(0, *work[0])
                run_all(prep_group(0, *work[0]))
                if NW > 1:
                    load_group(1, *work[1])
                for wi in range(NW):
                    pg = prep_group((wi + 1) % 2, *work[wi + 1]) if wi + 1 < NW else iter(())
                    nsteps = 14 * (work[wi][3] // 64)
                    psteps = 14 if wi + 1 < NW else 0
                    per = -(-psteps // max(nsteps, 1)) if psteps else 0
                    first = True
                    for _ in chunks_gen(wi):
                        for _k in range(per):
                            next(pg, None)
                    run_all(pg)
                    if wi + 2 < NW:
                        load_group(wi % 2, *work[wi + 2])
                S.barrier()

        for s in range(NS):
            if "C" in phases:
                phase_C(s)

        lnx = sb("lnx", (128, 2, D))
        b_lnx = Buf()
        S.dma("sp", lnx[:, 0, :], W["rw_lnx_w"][0].partition_broadcast(128), writes=[b_lnx])
        S.dma("sp", lnx[:, 1, :], W["rw_lnx_b"][0].partition_broadcast(128), writes=[b_lnx])

        def phase_D(s):
            L = seqL[s]
            TD = 256
            with ExitStack() as pst:
                P = {}
                P["ss"] = sb("d_ss", (128, 4), F32, pst)
                P["b_ss"] = Buf()
                P["junk"] = sb("d_junk", (128, D), BF16, pst)
                P["b_junk"] = Buf()
                P["xn"] = [sb("d_xn", (128, D), BF16, pst) for i in range(2)]
                P["b_xn"] = [Buf() for _ in range(2)]
                P["xT"] = [sb("d_xT", (128, TD), BF16, pst) for i in range(8)]
                P["b_xT"] = [Buf() for _ in range(8)]
                P["wgu"] = [sb("d_wgu", (128, 2, 8, 128), BF16, pst) for i in range(3)]
                P["b_wgu"] = [Buf() for _ in range(3)]
                P["sg"] = [sb("d_sg", (128, TD), F32, pst) for i in range(2)]
                P["b_sg"] = [Buf() for _ in range(2)]
                P["act"] = [sb("d_act", (128, TD), BF16, pst) for i in range(NF)]
                P["b_act"] = [Buf() for _ in range(NF)]
                P["wd"] = sb("d_wd", (128, NF, 512), BF16, pst)
                P["b_wd"] = Buf()
                rr["wgu"] = 0
                rr["sg"] = 0
                hts = [sb("d_h", (128, D), F32, pst) for i in range(2)]
                b_h = [Buf(), Buf()]
                yin = [sb("d_yin", (128, D), F32, pst) for i in range(4)]
                b_yin = Buf()
                ysum = sb("d_ysum", (128, D), F32, pst)
                ytmp = sb("d_ytmp", (128, D), F32, pst)
                b_y = Buf()
                st16 = sb("d_st16", (128, 4, 16), F32, pst)
                b_st = Buf()
                zb = sb("d_zb", (128, D), BF16, pst)
                b_zb = Buf()
                zT = [sb("d_zT", (128, TD), BF16, pst) for i in range(8)]
                b_zT = [Buf() for _ in range(8)]
                oT = [sb("d_oT", (128, TD), BF16, pst) for i in range(8)]
                b_oT = [Buf() for _ in range(8)]
                mT = [sb("d_mT", (128, TD), BF16, pst) for i in range(8)]
                b_mT = [Buf() for _ in range(8)]
                gts = [sb("d_gt", (128, TD), F32, pst) for i in range(4)]
                b_gts = [Buf() for _ in range(4)]
                mtmp = sb("d_mtmp", (128, TD), F32, pst)
                b_mtmp = Buf()
                wbr = [sb("d_wbr", (128, 8, 128), BF16, pst) for i in range(4)]
                b_wbr = [Buf() for _ in range(4)]
                wo = sb("d_wo", (128, 8, 512), BF16, pst)
                b_wo = Buf()
                gdraw = sb("d_gdraw", (128, TD + 2), F32, pst)
                gdsh = sb("d_gdsh", (128, TD), F32, pst)
                sgd = sb("d_sgd", (128, TD), BF16, pst)
                b_gd = Buf()
                ost = sb("d_ost", (128, D), F32, pst)
                b_ost = Buf()
                rr["gts"] = 0
                rr["wbr"] = 0
                for (t0, T) in tiles_of(L, TD):
                    blks = blocks_of(t0, T)
                    lo = max(t0 - 1, 0)
                    hi = min(t0 + T + 1, L)
                    if (t0 - 1 < 0) or (t0 + T + 1 > L):
                        S.emit("pool", lambda e: e.memset(gdraw[:], 0.0), writes=[b_gd])
                    S.dma("sp", gdraw[:, lo - (t0 - 1):hi - (t0 - 1)], RWs[s][26, :, lo:hi], writes=[b_gd])
                    S.emit("pool", lambda e: e.tensor_scalar(out=gdsh[:, :T], in0=gdraw[:, 1:T + 1], scalar1=mus[:, 2, 26:27], scalar2=None, op0=ALU.mult),
                           reads=[b_mus, b_gd], writes=[b_gd])
                    S.emit("dve", lambda e: e.scalar_tensor_tensor(out=gdsh[:, :T], in0=gdraw[:, 0:T], scalar=mus[:, 0, 26:27], in1=gdsh[:, :T],
                                                                    op0=ALU.mult, op1=ALU.add), reads=[b_mus, b_gd], writes=[b_gd])
                    S.emit("dve", lambda e: e.scalar_tensor_tensor(out=gdsh[:, :T], in0=gdraw[:, 2:T + 2], scalar=mus[:, 1, 26:27], in1=gdsh[:, :T],
                                                                    op0=ALU.mult, op1=ALU.add), reads=[b_mus, b_gd], writes=[b_gd])
                    S.emit("act", lambda e: e.activation(out=sgd[:, :T], in_=gdsh[:, :T], func=AF.Sigmoid), reads=[b_gd], writes=[b_gd])
                    for k in range(8):
                        S.dma("sp", oT[k][:, :T], OTs[s][k * 128:(k + 1) * 128, t0:t0 + T], writes=[b_oT[k]])
                    for bi, (tg, o, n) in enumerate(blks):
                        S.dma("sp", hts[bi][:n, :], Hs[s][tg:tg + n, :], writes=[b_h[bi]])
                        for i, src in enumerate([YSs[0][s], YSs[1][s], BNs[0][s], BNs[1][s]]):
                            S.dma("sp", yin[i][:n, :], src[tg:tg + n, :], writes=[b_yin])
                        gbank = [nxt("D", 2), nxt("D", 2)]
                        for hf in range(2):
                            S.emit("pe", lambda e: e.matmul(psD[gbank[hf]][:n, :], sgd[:, o:o + n], g2sb[:, hf * 512:(hf + 1) * 512], start=True, stop=True),
                                   reads=[b_gd, b_w2], writes=[bD[gbank[hf]]])
                        def v3(t):
                            return t[:n, :].rearrange("p (h v) -> p h v", v=64)
                        def bc(col):
                            return st16[:n, col, :].unsqueeze(2).to_broadcast([n, 16, 64])
                        S.emit("dve", lambda e: e.tensor_tensor(out=ysum[:n, :], in0=yin[0][:n, :], in1=yin[1][:n, :], op=ALU.add), reads=[b_yin], writes=[b_y])
                        S.emit("dve", lambda e: e.reduce_sum(out=st16[:n, 0, :], in_=v3(ysum), axis=AX.X), reads=[b_y], writes=[b_st])
                        S.emit("dve", lambda e: e.tensor_scalar(out=st16[:n, 0, :], in0=st16[:n, 0, :], scalar1=1.0 / 64, scalar2=None, op0=ALU.mult),
                               reads=[b_st], writes=[b_st])
                        S.emit("dve", lambda e: e.tensor_tensor(out=v3(ysum), in0=v3(ysum), in1=bc(0), op=ALU.subtract), reads=[b_y, b_st], writes=[b_y])
                        S.emit("pool", lambda e: e.tensor_tensor(out=ytmp[:n, :], in0=ysum[:n, :], in1=ysum[:n, :], op=ALU.mult), reads=[b_y], writes=[b_y])
                        S.emit("dve", lambda e: e.reduce_sum(out=st16[:n, 1, :], in_=v3(ytmp), axis=AX.X), reads=[b_y], writes=[b_st])
                        S.emit("act", lambda e: e.activation(out=st16[:n, 2, :], in_=st16[:n, 1, :], func=AF.Sqrt, bias=epsT[:n, 1:2], scale=1.0 / 64),
                               reads=[b_st, b_eps], writes=[b_st])
                        S.emit("dve", lambda e: e.reciprocal(out=st16[:n, 3, :], in_=st16[:n, 2, :]), reads=[b_st], writes=[b_st])
                        S.emit("dve", lambda e: e.tensor_tensor(out=v3(ysum), in0=v3(ysum), in1=bc(3), op=ALU.mult), reads=[b_y, b_st], writes=[b_y])
                        S.emit("pool", lambda e: e.tensor_tensor(out=ysum[:n, :], in0=ysum[:n, :], in1=lnx[:n, 0, :], op=ALU.mult), reads=[b_y, b_lnx], writes=[b_y])
                        S.emit("pool", lambda e: e.tensor_tensor(out=ysum[:n, :], in0=ysum[:n, :], in1=lnx[:n, 1, :], op=ALU.add), reads=[b_y, b_lnx], writes=[b_y])
                        S.emit("pool", lambda e: e.tensor_tensor(out=ytmp[:n, :], in0=yin[2][:n, :], in1=yin[3][:n, :], op=ALU.add), reads=[b_yin, b_y], writes=[b_y])
                        S.emit("dve", lambda e: e.tensor_tensor(out=ysum[:n, :], in0=ysum[:n, :], in1=ytmp[:n, :], op=ALU.add), reads=[b_y], writes=[b_y])
                        for hf in range(2):
                            S.emit("dve", lambda e: e.tensor_tensor(out=zb[:n, hf * 512:(hf + 1) * 512], in0=ysum[:n, hf * 512:(hf + 1) * 512],
                                                                    in1=psD[gbank[hf]][:n, :], op=ALU.mult),
                                   reads=[b_y, bD[gbank[hf]]], writes=[b_zb])
                        for cp in range(4):
                            tb_ = nxt("T", 2)
                            for cc in range(2):
                                c = cp * 2 + cc
                                S.emit("pe", lambda e: e.transpose(psT[tb_][:, cc * 512:cc * 512 + n], zb[:n, c * 128:(c + 1) * 128], ident[:n, :n]),
                                       reads=[b_zb, b_ident], writes=[bT[tb_]])
                            for cc in range(2):
                                c = cp * 2 + cc
                                S.emit("act", lambda e: e.copy(out=zT[c][:, o:o + n], in_=psT[tb_][:, cc * 512:cc * 512 + n]),
                                       reads=[bT[tb_]], writes=[b_zT[c]])
                    for j in range(8):
                        wa = nxt("wbr", 4)
                        S.dma("sp", wbr[wa][:], WABS[j], writes=[b_wbr[wa]])
                        wr_ = nxt("wbr", 4)
                        S.dma("sp", wbr[wr_][:], WRBS[j], writes=[b_wbr[wr_]])
                        ga = nxt("gts", 4)
                        S.dma("sp", gts[ga][:, :T], GTs[s][j, :, t0:t0 + T], writes=[b_gts[ga]])
                        gb_ = nxt("gts", 4)
                        S.dma("sp", gts[gb_][:, :T], GTs[s][8 + j, :, t0:t0 + T], writes=[b_gts[gb_]])
                        b1, b2 = nxt("G", 4), nxt("G", 4)
                        for k in range(8):
                            S.emit("pe", lambda e: e.matmul(psG[b1][:, :T], wbr[wa][:, k, :], oT[k][:, :T], start=(k == 0), stop=(k == 7)),
                                   reads=[b_wbr[wa], b_oT[k]], writes=[bG[b1]])
                        for k in range(8):
                            S.emit("pe", lambda e: e.matmul(psG[b2][:, :T], wbr[wr_][:, k, :], zT[k][:, :T], start=(k == 0), stop=(k == 7)),
                                   reads=[b_wbr[wr_], b_zT[k]], writes=[bG[b2]])
                        S.emit("dve", lambda e: e.tensor_tensor(out=mtmp[:, :T], in0=gts[ga][:, :T], in1=psG[b1][:, :T], op=ALU.mult),
                               reads=[b_gts[ga], bG[b1]], writes=[b_mtmp])
                        S.emit("dve", lambda e: e.tensor_tensor(out=gts[gb_][:, :T], in0=gts[gb_][:, :T], in1=psG[b2][:, :T], op=ALU.mult),
                               reads=[b_gts[gb_], bG[b2]], writes=[b_gts[gb_]])
                        S.emit("pool", lambda e: e.tensor_tensor(out=mT[j][:, :T], in0=mtmp[:, :T], in1=gts[gb_][:, :T], op=ALU.add),
                               reads=[b_mtmp, b_gts[gb_]], writes=[b_mT[j]])
                    for hf in range(2):
                        S.dma("sp", wo[:], WOS[hf], writes=[b_wo])
                        for bi, (tg, o, n) in enumerate(blks):
                            bank_ = nxt("D", 2)
                            for k in range(8):
                                S.emit("pe", lambda e: e.matmul(psD[bank_][:n, :], mT[k][:, o:o + n], wo[:, k, :], start=(k == 0), stop=(k == 7)),
                                       reads=[b_mT[k], b_wo], writes=[bD[bank_]])
                            S.emit("dve", lambda e: e.tensor_tensor(out=hts[bi][:n, hf * 512:(hf + 1) * 512], in0=hts[bi][:n, hf * 512:(hf + 1) * 512],
                                                                    in1=psD[bank_][:n, :], op=ALU.add), reads=[bD[bank_], b_h[bi]], writes=[b_h[bi]])
                    xts = hts[:len(blks)]
                    bxs = b_h[:len(blks)]
                    ffn(P, 1, blks, xts, bxs, 2)
                    for bi, (tg, o, n) in enumerate(blks):
                        xt, bx = hts[bi], b_h[bi]
                        S.emit("pool", lambda e: e.memset(P["ss"][:, 0:1], 0.0), writes=[P["b_ss"]])
                        S.emit("act", lambda e: e.activation(out=P["junk"][:n, :], in_=xt[:n, :], func=AF.Square, accum_out=P["ss"][:n, 0:1]),
                               reads=[bx], writes=[P["b_ss"], P["b_junk"]])
                        S.emit("act", lambda e: e.activation(out=P["ss"][:n, 1:2], in_=P["ss"][:n, 0:1], func=AF.Sqrt, bias=epsT[:n, 0:1], scale=1.0 / D),
                               reads=[b_eps], writes=[P["b_ss"]])
                        S.emit("dve", lambda e: e.reciprocal(out=P["ss"][:n, 2:3], in_=P["ss"][:n, 1:2]), writes=[P["b_ss"]])
                        S.emit("act", lambda e: e.activation(out=ost[:n, :], in_=xt[:n, :], func=AF.Copy, scale=P["ss"][:n, 2:3]),
                               reads=[bx, P["b_ss"]], writes=[b_ost])
                        S.emit("dve", lambda e: e.tensor_tensor(out=ost[:n, :], in0=ost[:n, :], in1=fin_g[:n, :], op=ALU.mult), reads=[b_ost, b_fin], writes=[b_ost])
                        lo_t = max(tg, NMETA)
                        if tg + n > lo_t:
                            S.dma("pool", ydst(s)[lo_t - NMETA:tg + n - NMETA, :], ost[lo_t - tg:n, :], reads=[b_ost])
                S.barrier()

        for s in range(NS):
            if "D" in phases:
                phase_D(s)
    return nc


_NC_CACHE = {}


def _rel_bucket_np(rel):
    try:
        import jax
        import jax.numpy as jnp
        with jax.default_device(jax.devices("cpu")[0]):
            r = jnp.asarray(rel, dtype=jnp.int32)
            nb = 16
            max_exact = 8
            n = jnp.abs(r)
            nf = jnp.maximum(n, 1).astype(jnp.float32)
            large = max_exact + (jnp.log(nf / max_exact) / math.log(128 / max_exact) * (nb - max_exact)).astype(jnp.int32)
            large = jnp.minimum(large, nb - 1)
            out = (r > 0).astype(jnp.int32) * nb + jnp.where(n < max_exact, n, large)
            return np.asarray(out)
    except Exception:
        rel = np.asarray(rel, dtype=np.int32)
        n = np.abs(rel)
        nf = np.maximum(n, 1).astype(np.float32)
        large = 8 + (np.log(nf / np.float32(8)) / np.float32(math.log(16.0)) * np.float32(8)).astype(np.int32)
        large = np.minimum(large, 15)
        return (rel > 0).astype(np.int32) * 16 + np.where(n < 8, n, large)


def _consts():
    p = np.arange(128)[:, None]
    m = np.arange(896)[None, :]
    bkt = _rel_bucket_np(p - m + 384).astype(np.float32)
    t = np.arange(64)
    lo = (t[None, :] < t[:, None]).astype(np.float32)
    up = lo.T.copy()
    eye = np.eye(64, dtype=np.float32)

    def bd(mm):
        z = np.zeros((128, 128), np.float32)
        z[:64, :64] = mm
        z[64:, 64:] = mm
        return z

    masks = np.stack([bd(lo), bd(up), bd(lo + eye), bd(up + eye)], axis=1)
    istack = np.concatenate([eye, eye], axis=0)
    bones = bd(np.ones((64, 64), np.float32))
    return {"c_ident": np.eye(128, dtype=np.float32), "c_bk": bkt, "c_masks": masks, "c_istack": istack, "c_bones": bones}


def kernel(**inputs):
    xp = np.ascontiguousarray(inputs["x_prompt"], dtype=np.float32)
    xs = np.ascontiguousarray(inputs["x_sample"], dtype=np.float32)
    S0, S1 = xp.shape[1], xs.shape[1]
    ncores = 8
    key = (S0, S1)
    if key not in _NC_CACHE:
        _NC_CACHE[key] = build_nc(S0, S1)
    nc = _NC_CACHE[key]
    shared = {k: np.ascontiguousarray(v, dtype=np.float32) for k, v in inputs.items()
              if k not in ("x_prompt", "x_sample")}
    shared.update(_consts())
    in_maps = []
    for c in range(ncores):
        m = dict(shared)
        m["x_prompt"] = xp[2 * c:2 * c + 2]
        m["x_sample"] = xs[2 * c:2 * c + 2]
        in_maps.append(m)
    res = run_bass_kernel_spmd(nc, in_maps, core_ids=list(range(ncores)))
    yp = np.concatenate([r["y_prompt"] for r in res.results], axis=0)
    ys = np.concatenate([r["y_sample"] for r in res.results], axis=0)
    return (yp.astype(np.float32), ys.astype(np.float32))
```

```python
import math
from contextlib import ExitStack
import numpy as np
import concourse.bass as bass
import concourse.mybir as mybir
from concourse.bass_utils import run_bass_kernel_spmd

F32 = mybir.dt.float32
BF16 = mybir.dt.bfloat16
AF = mybir.ActivationFunctionType
ALU = mybir.AluOpType
AX = mybir.AxisListType

D = 1024
DFF = 2816
NF = DFF // 128
NMETA = 16
EPS = 1e-6
LNX_EPS = 64e-5
NIN = 8576
EP = 30000
KDMA = 8


class Buf:
    __slots__ = ("w", "weng", "r", "excl")

    def __init__(self, excl=False):
        self.w = None
        self.weng = None
        self.r = {}
        self.excl = excl


class Sched:
    def __init__(self, nc, st):
        self.nc = nc
        self.engs = {"pe": nc.tensor, "act": nc.scalar, "dve": nc.vector, "pool": nc.gpsimd, "sp": nc.sync}
        self.cnt = {e: 0 for e in self.engs}
        self.seen = {e: {} for e in self.engs}
        self.sems = {}
        self.st = st
        self.dqi = {"sp": 0, "pool": 0}
        self.last = {}

    def sem(self, key):
        if key not in self.sems:
            name = "s_" + "_".join(str(x) for x in key)
            self.sems[key] = self.st.enter_context(self.nc.semaphore(name))
        return self.sems[key]

    def _deps(self, e, reads, writes):
        deps = {}

        def add(tok):
            if tok is None:
                return
            k, v = tok
            if deps.get(k, 0) < v:
                deps[k] = v

        for b in reads:
            add(b.w)
            if b.excl:
                for src, tok in b.r.items():
                    if src != e:
                        add(tok)
        for b in writes:
            if not (e == "pe" and b.weng == "pe"):
                add(b.w)
            for tok in b.r.values():
                add(tok)
        return deps

    def _wait(self, e, deps):
        seen = self.seen[e]
        eng = self.engs[e]
        for k, v in deps.items():
            if seen.get(k, 0) < v:
                eng.wait_ge(self.sem(k), v)
                seen[k] = v

    def _mark(self, src, tok, e, reads, writes):
        self.last[src] = tok
        for b in reads:
            b.r[src] = tok
        for b in writes:
            b.w = tok
            b.weng = e
            b.r = {}

    def emit(self, e, fn, reads=(), writes=()):
        self._wait(e, self._deps(e, reads, writes))
        ins = fn(self.engs[e])
        c = self.cnt[e]
        self.cnt[e] += 1
        k = (e, c // EP)
        v = c % EP + 1
        ins.then_inc(self.sem(k), 1)
        self._mark(e, (k, v), e, reads, writes)

    def dma(self, q, out, in_, reads=(), writes=(), **kw):
        deps = self._deps("dma", reads, writes)
        i = self.dqi[q]
        self.dqi[q] += 1
        slot = i % KDMA
        k = ("d", q, slot)
        v = 16 * (i // KDMA + 1)
        if i >= KDMA:
            if deps.get(k, 0) < v - 16:
                deps[k] = v - 16
        self._wait(q, deps)
        ins = self.engs[q].dma_start(out=out, in_=in_, **kw)
        ins.then_inc(self.sem(k), 16)
        self._mark(k, (k, v), "dma", reads, writes)

    def barrier(self):
        deps = {}
        for tok in self.last.values():
            k, v = tok
            if deps.get(k, 0) < v:
                deps[k] = v
        for e in self.engs:
            self._wait(e, deps)


def blocks_of(t0, T):
    out = []
    o = 0
    while o < T:
        n = min(128, T - o)
        out.append((t0 + o, o, n))
        o += n
    return out


def tiles_of(L, TT=512):
    out = []
    t = 0
    while t < L:
        T = min(TT, L - t)
        out.append((t, T))
        t += T
    return out


def build_nc(S0, S1, debug=False, phases="ABCD"):
    nc = bass.Bass("TRN2", target_bir_lowering=False)
    seqS = [S0, S0, S1, S1]
    seqL = [s + NMETA for s in seqS]
    NS = len(seqS)

    def din(name, shape):
        return nc.dram_tensor(name, list(shape), F32, kind="ExternalInput").ap()

    def dscr(name, shape, dt=F32):
        kind = "ExternalOutput" if (debug and name.startswith("dbg_")) else "Internal"
        return nc.dram_tensor(name, list(shape), dt, kind=kind).ap()

    x_in = [din("x_prompt", (2, S0, D)), din("x_sample", (2, S1, D))]
    y_out = [nc.dram_tensor("y_prompt", [2, S0, D], F32, kind="ExternalOutput").ap(),
             nc.dram_tensor("y_sample", [2, S1, D], F32, kind="ExternalOutput").ap()]

    def xsrc(s):
        return x_in[s // 2][s % 2]

    def ydst(s):
        return y_out[s // 2][s % 2]

    meta = din("meta_tokens", (NMETA, D))
    rel_bias = din("rel_bias", (32, 8))
    W = {}
    for nm, shp in [("ffn1_norm", (1, D)), ("ffn1_w_gate", (1, D, DFF)), ("ffn1_w_up", (1, D, DFF)),
                    ("ffn1_w_down", (1, DFF, D)), ("mix_norm", (1, D)), ("w_in", (1, D, NIN)),
                    ("attn_lambda_q1", (1, 64)), ("attn_lambda_k1", (1, 64)), ("attn_lambda_q2", (1, 64)),
                    ("attn_lambda_k2", (1, 64)), ("attn_subln", (1, 128)), ("w_attn_branch", (1, D, D)),
                    ("rw_mu_prev", (1, 3456)), ("rw_mu_next", (1, 3456)), ("rw_w0", (1, 2, D)),
                    ("rw_w2", (1, 2, 64, D)), ("rw_a0", (1, 2, D)), ("rw_a2", (1, 2, 64, D)),
                    ("rw_g2", (1, 128, D)), ("rw_k_k", (1, D)), ("rw_k_a", (1, D)), ("rw_r_k", (1, 16, 64)),
                    ("rw_lnx_w", (1, D)), ("rw_lnx_b", (1, D)), ("w_rw_branch", (1, D, D)), ("w_out", (1, D, D)),
                    ("ffn2_norm", (1, D)), ("ffn2_w_gate", (1, D, DFF)), ("ffn2_w_up", (1, D, DFF)),
                    ("ffn2_w_down", (1, DFF, D)), ("final_norm", (D,))]:
        W[nm] = din(nm, shp)
    c_ident = din("c_ident", (128, 128))
    c_bk = din("c_bk", (128, 896))
    c_masks = din("c_masks", (128, 4, 128))
    c_istack = din("c_istack", (128, 64))
    c_bones = din("c_bones", (128, 128))

    WGU = [dscr(f"wgu{i}", (NF, 128, 2, 8, 128), BF16) for i in range(2)]
    WD = [dscr(f"wd{i}", (2, 128, NF, 512), BF16) for i in range(2)]
    WINS = dscr("wins", (67, 128, 8, 128), BF16)
    WVS = dscr("wvs", (2, 128, 8, 512), BF16)
    WABS = dscr("wabs", (8, 128, 8, 128), BF16)
    WRBS = dscr("wrbs", (8, 128, 8, 128), BF16)
    WOS = dscr("wos", (2, 128, 8, 512), BF16)
    Hs = [dscr(f"dbg_h{s}", (seqL[s], D)) for s in range(NS)]
    QTs = [dscr(f"dbg_qt{s}", (8, 128, seqL[s]), BF16) for s in range(NS)]
    KTs = [dscr(f"dbg_kt{s}", (8, 128, seqL[s]), BF16) for s in range(NS)]
    Vs = [dscr(f"dbg_v{s}", (seqL[s], D), BF16) for s in range(NS)]
    RWs = [dscr(f"dbg_rw{s}", (27, 128, seqL[s])) for s in range(NS)]
    GTs = [dscr(f"dbg_gt{s}", (16, 128, seqL[s])) for s in range(NS)]
    OTs = [dscr(f"dbg_ot{s}", (D, seqL[s]), BF16) for s in range(NS)]
    YSs = [[dscr(f"dbg_y{d}{s}", (seqL[s], D)) for s in range(NS)] for d in range(2)]
    BNs = [[dscr(f"dbg_bn{d}{s}", (seqL[s], D)) for s in range(NS)] for d in range(2)]

    with ExitStack() as st:
        S = Sched(nc, st)

        uid = [0]

        def sb(name, shape, dt=F32, stack=st):
            uid[0] += 1
            return stack.enter_context(nc.sbuf_tensor(f"{name}_{uid[0]}", list(shape), dt))

        psT = [st.enter_context(nc.psum_tensor(f"psT{i}", [128, 1024], BF16)) for i in range(2)]
        psG = [st.enter_context(nc.psum_tensor(f"psG{i}", [128, 512], F32)) for i in range(4)]
        psD = [st.enter_context(nc.psum_tensor(f"psD{i}", [128, 512], F32)) for i in range(2)]
        bT = [Buf(True) for _ in range(2)]
        bG = [Buf(True) for _ in range(4)]
        bD = [Buf(True) for _ in range(2)]
        rr = {"T": 0, "G": 0, "D": 0}

        def nxt(kind, n):
            i = rr[kind]
            rr[kind] = (i + 1) % n
            return i

        ident = sb("ident", (128, 128), BF16)
        b_ident = Buf()
        S.dma("pool", ident[:], c_ident[:, :], writes=[b_ident])
        epsT = sb("epsT", (128, 2))
        b_eps = Buf()
        S.emit("pool", lambda e: e.memset(epsT[:, 0:1], EPS), writes=[b_eps])
        S.emit("pool", lambda e: e.memset(epsT[:, 1:2], LNX_EPS), writes=[b_eps])
        gains = sb("gains", (128, 4, 8))
        b_gains = Buf()
        for i, nm in enumerate(["ffn1_norm", "mix_norm", "ffn2_norm"]):
            S.dma("sp", gains[:, i, :], W[nm][0].rearrange("(c p) -> p c", p=128), writes=[b_gains],
                  allow_slow_non_contiguous=True)
        fin_g = sb("fin_g", (128, D))
        b_fin = Buf()
        S.dma("sp", fin_g[:], W["final_norm"].partition_broadcast(128), writes=[b_fin])

        tb = sb("tb", (128, 256))
        b_tb = Buf()
        S.dma("sp", tb[:], rel_bias.rearrange("b h -> (b h)").partition_broadcast(128), writes=[b_tb])
        lamv = sb("lamv", (128, 4, 64))
        b_lamv = Buf()
        for i, nm in enumerate(["attn_lambda_q1", "attn_lambda_k1", "attn_lambda_q2", "attn_lambda_k2"]):
            S.dma("sp", lamv[:, i, :], W[nm][0].partition_broadcast(128), writes=[b_lamv])
        lams = sb("lams", (128, 8))
        b_lams = Buf()
        for i in range(2):
            S.emit("dve", lambda e: e.tensor_tensor(out=lamv[:, 2 * i, :], in0=lamv[:, 2 * i, :], in1=lamv[:, 2 * i + 1, :],
                                                    op=ALU.mult), reads=[b_lamv], writes=[b_lamv])
            S.emit("dve", lambda e: e.reduce_sum(out=lams[:, i:i + 1], in_=lamv[:, 2 * i, :], axis=AX.X),
                   reads=[b_lamv], writes=[b_lams])
        S.emit("act", lambda e: e.activation(out=lams[:, 2:4], in_=lams[:, 0:2], func=AF.Exp), reads=[b_lams], writes=[b_lams])
        S.emit("dve", lambda e: e.tensor_tensor(out=lams[:, 4:5], in0=lams[:, 3:4], in1=lams[:, 2:3], op=ALU.subtract),
               reads=[b_lams], writes=[b_lams])
        LAM_INIT = 0.8 - 0.6 * math.exp(-0.3 * 0)
        S.emit("dve", lambda e: e.tensor_scalar(out=lams[:, 5:6], in0=lams[:, 4:5], scalar1=-LAM_INIT, scalar2=None, op0=ALU.add),
               reads=[b_lams], writes=[b_lams])
        gsub = sb("gsub", (128, 1))
        b_gsub = Buf()
        S.dma("sp", gsub[:], W["attn_subln"][0].rearrange("(p o) -> p o", o=1), writes=[b_gsub])
        S.emit("pool", lambda e: e.tensor_scalar(out=gsub[:], in0=gsub[:], scalar1=1.0 - LAM_INIT, scalar2=None, op0=ALU.mult),
               reads=[b_gsub], writes=[b_gsub])

        masks = sb("masks", (128, 4, 128))
        b_masks = Buf()
        S.dma("sp", masks[:], c_masks[:, :, :], writes=[b_masks])
        m12 = [sb(f"m12_{d}", (128, 256)) for d in range(2)]
        m345 = [sb(f"m345_{d}", (128, 384)) for d in range(2)]
        b_mm = Buf()
        for d in range(2):
            lo, up, loi, upi = (0, 1, 2, 3) if d == 0 else (1, 0, 3, 2)
            for (dst, off, mi) in [(m12[d], 0, lo), (m12[d], 128, up), (m345[d], 0, up), (m345[d], 128, upi), (m345[d], 256, upi)]:
                S.emit("pool", lambda e: e.tensor_copy(out=dst[:, off:off + 128], in_=masks[:, mi, :]), reads=[b_masks], writes=[b_mm])
        istack = sb("istack", (128, 64), BF16)
        bones = sb("bones", (128, 128), BF16)
        onesb = sb("onesb", (128, 1), BF16)
        b_rc = Buf()
        S.dma("pool", istack[:], c_istack[:, :], writes=[b_rc])
        S.dma("pool", bones[:], c_bones[:, :], writes=[b_rc])
        S.emit("pool", lambda e: e.memset(onesb[:], 1.0), writes=[b_rc])
        mus = sb("mus", (128, 3, 27))
        b_mus = Buf()
        for i, nm in enumerate(["rw_mu_prev", "rw_mu_next"]):
            S.dma("sp", mus[:, i, :], W[nm][0].rearrange("(c p) -> p c", p=128), writes=[b_mus], allow_slow_non_contiguous=True)
        S.emit("dve", lambda e: e.tensor_tensor(out=mus[:, 2, :], in0=mus[:, 0, :], in1=mus[:, 1, :], op=ALU.add), reads=[b_mus], writes=[b_mus])
        S.emit("dve", lambda e: e.tensor_scalar(out=mus[:, 2, :], in0=mus[:, 2, :], scalar1=-1.0, scalar2=1.0, op0=ALU.mult, op1=ALU.add),
               reads=[b_mus], writes=[b_mus])
        rwp = sb("rwp", (128, 7, 8))
        b_rwp = Buf()
        for i, src in enumerate([W["rw_w0"][0, 0], W["rw_w0"][0, 1], W["rw_a0"][0, 0], W["rw_a0"][0, 1], W["rw_k_k"][0], W["rw_k_a"][0],
                                 W["rw_r_k"][0].rearrange("h n -> (h n)")]):
            S.dma("sp", rwp[:, i, :], src.rearrange("(c p) -> p c", p=128), writes=[b_rwp], allow_slow_non_contiguous=True)
        w2sb = sb("w2sb", (128, D), BF16)
        a2sb = sb("a2sb", (128, D), BF16)
        g2sb = sb("g2sb", (128, D), BF16)
        b_w2 = Buf()
        S.dma("pool", w2sb[:], W["rw_w2"][0].rearrange("d r c -> (d r) c"), writes=[b_w2])
        S.dma("pool", a2sb[:], W["rw_a2"][0].rearrange("d r c -> (d r) c"), writes=[b_w2])
        S.dma("pool", g2sb[:], W["rw_g2"][0], writes=[b_w2])

        def prep_weights():
            for i, pre in enumerate(["ffn1", "ffn2"]):
                for which, nm in enumerate(["w_gate", "w_up"]):
                    src = W[f"{pre}_{nm}"][0].rearrange("(k p) (f j) -> f p k j", p=128, j=128)
                    for f in range(NF):
                        S.dma("pool", WGU[i][f, :, which, :, :], src[f])
                src = W[f"{pre}_w_down"][0].rearrange("(f p) (h j) -> h p f j", p=128, j=512)
                for h in range(2):
                    for f0 in range(0, NF, 11):
                        S.dma("pool", WD[i][h, :, f0:f0 + 11, :], src[h][:, f0:f0 + 11, :])
            src = W["w_in"][0].rearrange("(k p) (c j) -> c p k j", p=128, j=128)
            for c in range(67):
                if 16 <= c < 24:
                    continue
                S.dma("pool", WINS[c], src[c])
            src = W["w_in"][0][:, 2048:3072].rearrange("(k p) (h j) -> h p k j", p=128, j=512)
            for h in range(2):
                S.dma("pool", WVS[h], src[h])
            for (dst, nm) in [(WABS, "w_attn_branch"), (WRBS, "w_rw_branch")]:
                src = W[nm][0].rearrange("(k p) (c j) -> c p k j", p=128, j=128)
                for c in range(8):
                    S.dma("pool", dst[c], src[c])
            src = W["w_out"][0].rearrange("(k p) (h j) -> h p k j", p=128, j=512)
            for h in range(2):
                S.dma("pool", WOS[h], src[h])

        prep_weights()
        S.barrier()

        def load_x_block(s, t0, n, xt, bx):
            if t0 == 0:
                S.dma("sp", xt[0:NMETA, :], meta[:, :], writes=[bx])
                S.dma("sp", xt[NMETA:n, :], xsrc(s)[0:n - NMETA, :], writes=[bx])
            else:
                S.dma("sp", xt[0:n, :], xsrc(s)[t0 - NMETA:t0 - NMETA + n, :], writes=[bx])

        def rmsnorm_T(P, blks, xts, bxs, gi, outT, boutT):
            for (tg, o, n), xt, bx in zip(blks, xts, bxs):
                S.emit("pool", lambda e: e.memset(P["ss"][:, 0:1], 0.0), writes=[P["b_ss"]])
                S.emit("act", lambda e: e.activation(out=P["junk"][:n, :], in_=xt[:n, :], func=AF.Square,
                                                     accum_out=P["ss"][:n, 0:1]),
                       reads=[bx], writes=[P["b_ss"], P["b_junk"]])
                S.emit("act", lambda e: e.activation(out=P["ss"][:n, 1:2], in_=P["ss"][:n, 0:1], func=AF.Sqrt,
                                                     bias=epsT[:n, 0:1], scale=1.0 / D),
                       reads=[b_eps], writes=[P["b_ss"]])
                S.emit("dve", lambda e: e.reciprocal(out=P["ss"][:n, 2:3], in_=P["ss"][:n, 1:2]), writes=[P["b_ss"]])
                bi = o // 128
                S.emit("act", lambda e: e.activation(out=P["xn"][bi][:n, :], in_=xt[:n, :], func=AF.Copy,
                                                     scale=P["ss"][:n, 2:3]),
                       reads=[bx, P["b_ss"]], writes=[P["b_xn"][bi]])
            T = sum(b[2] for b in blks)
            for cp in range(4):
                bank = nxt("T", 2)
                for cc in range(2):
                    c = cp * 2 + cc
                    for (tg, o, n) in blks:
                        bi = o // 128
                        S.emit("pe", lambda e: e.transpose(psT[bank][:, cc * 512 + o:cc * 512 + o + n],
                                                           P["xn"][bi][:n, c * 128:(c + 1) * 128], ident[:n, :n]),
                               reads=[P["b_xn"][bi], b_ident], writes=[bT[bank]])
                for cc in range(2):
                    c = cp * 2 + cc
                    S.emit("dve", lambda e: e.tensor_scalar(out=outT[c][:, :T], in0=psT[bank][:, cc * 512:cc * 512 + T],
                                                            scalar1=gains[:, gi, c:c + 1], scalar2=None, op0=ALU.mult),
                           reads=[bT[bank], b_gains], writes=[boutT[c]])

        def ffn(P, fi, blks, xts, bxs, gi):
            T = sum(b[2] for b in blks)
            rmsnorm_T(P, blks, xts, bxs, gi, P["xT"], P["b_xT"])
            for f in range(NF):
                ws = nxt("wgu", len(P["wgu"]))
                S.dma("sp", P["wgu"][ws][:], WGU[fi][f], writes=[P["b_wgu"][ws]])
                banks = [nxt("G", 4), nxt("G", 4)]
                for which in range(2):
                    for k in range(8):
                        S.emit("pe", lambda e: e.matmul(psG[banks[which]][:, :T], P["wgu"][ws][:, which, k, :],
                                                        P["xT"][k][:, :T], start=(k == 0), stop=(k == 7)),
                               reads=[P["b_wgu"][ws], P["b_xT"][k]], writes=[bG[banks[which]]])
                sgi = nxt("sg", 2)
                S.emit("act", lambda e: e.activation(out=P["sg"][sgi][:, :T], in_=psG[banks[0]][:, :T], func=AF.Silu),
                       reads=[bG[banks[0]]], writes=[P["b_sg"][sgi]])
                S.emit("dve", lambda e: e.tensor_tensor(out=P["act"][f][:, :T], in0=P["sg"][sgi][:, :T],
                                                        in1=psG[banks[1]][:, :T], op=ALU.mult),
                       reads=[P["b_sg"][sgi], bG[banks[1]]], writes=[P["b_act"][f]])
            for h in range(2):
                S.dma("sp", P["wd"][:], WD[fi][h], writes=[P["b_wd"]])
                for (tg, o, n), xt, bx in zip(blks, xts, bxs):
                    bank = nxt("D", 2)
                    for f in range(NF):
                        S.emit("pe", lambda e: e.matmul(psD[bank][:n, :], P["act"][f][:, o:o + n], P["wd"][:, f, :],
                                                        start=(f == 0), stop=(f == NF - 1)),
                               reads=[P["b_act"][f], P["b_wd"]], writes=[bD[bank]])
                    S.emit("dve", lambda e: e.scalar_tensor_tensor(out=xt[:n, h * 512:(h + 1) * 512], in0=psD[bank][:n, :],
                                                                   scalar=0.5, in1=xt[:n, h * 512:(h + 1) * 512],
                                                                   op0=ALU.mult, op1=ALU.add),
                           reads=[bD[bank], bx], writes=[bx])

        def phase_A(s):
            L = seqL[s]
            with ExitStack() as pst:
                P = {}
                P["ss"] = sb("a_ss", (128, 4), stack=pst)
                P["b_ss"] = Buf()
                P["junk"] = sb("a_junk", (128, D), BF16, stack=pst)
                P["b_junk"] = Buf()
                P["xn"] = [sb(f"a_xn{i}", (128, D), BF16, stack=pst) for i in range(4)]
                P["b_xn"] = [Buf() for _ in range(4)]
                P["xT"] = [sb(f"a_xT{i}", (128, 512), BF16, stack=pst) for i in range(8)]
                P["b_xT"] = [Buf() for _ in range(8)]
                P["uT"] = [sb(f"a_uT{i}", (128, 512), BF16, stack=pst) for i in range(8)]
                P["b_uT"] = [Buf() for _ in range(8)]
                P["wgu"] = [sb(f"a_wgu{i}", (128, 2, 8, 128), BF16, stack=pst) for i in range(4)]
                P["b_wgu"] = [Buf() for _ in range(4)]
                P["sg"] = [sb(f"a_sg{i}", (128, 512), stack=pst) for i in range(2)]
                P["b_sg"] = [Buf() for _ in range(2)]
                P["act"] = [sb(f"a_act{i}", (128, 512), BF16, stack=pst) for i in range(NF)]
                P["b_act"] = [Buf() for _ in range(NF)]
                P["wd"] = sb("a_wd", (128, NF, 512), BF16, stack=pst)
                P["b_wd"] = Buf()
                rr["wgu"] = 0
                rr["sg"] = 0
                xsets = [[sb(f"a_x{j}_{i}", (128, D), stack=pst) for i in range(4)] for j in range(2)]
                bxsets = [[Buf() for _ in range(4)] for j in range(2)]
                win = [sb(f"a_win{i}", (128, 8, 128), BF16, stack=pst) for i in range(6)]
                b_win = [Buf() for _ in range(6)]
                wv = sb("a_wv", (128, 8, 512), BF16, stack=pst)
                b_wv = Buf()
                stf = [sb(f"a_stf{i}", (128, 512), stack=pst) for i in range(4)]
                b_stf = [Buf() for _ in range(4)]
                stb = [sb(f"a_stb{i}", (128, 512), BF16, stack=pst) for i in range(4)]
                b_stb = [Buf() for _ in range(4)]
                rr["win"] = 0
                rr["stf"] = 0
                rr["stb"] = 0
                tlist = tiles_of(L)

                def load_tile(ti):
                    t0_, T_ = tlist[ti]
                    bl = blocks_of(t0_, T_)
                    for (tg, o, n), xt, bx in zip(bl, xsets[ti % 2], bxsets[ti % 2]):
                        load_x_block(s, tg, n, xt, bx)

                load_tile(0)
                for ti, (t0, T) in enumerate(tlist):
                    blks = blocks_of(t0, T)
                    xts = xsets[ti % 2][:len(blks)]
                    bxs = bxsets[ti % 2][:len(blks)]
                    if ti + 1 < len(tlist):
                        load_tile(ti + 1)
                    ffn(P, 0, blks, xts, bxs, 0)
                    for (tg, o, n), xt, bx in zip(blks, xts, bxs):
                        S.dma("pool", Hs[s][tg:tg + n, :], xt[:n, :], reads=[bx])
                    rmsnorm_T(P, blks, xts, bxs, 1, P["uT"], P["b_uT"])
                    for c in list(range(0, 16)) + list(range(24, 67)):
                        ws = nxt("win", 6)
                        S.dma("sp", win[ws][:], WINS[c], writes=[b_win[ws]])
                        bank = nxt("G", 4)
                        for k in range(8):
                            S.emit("pe", lambda e: e.matmul(psG[bank][:, :T], win[ws][:, k, :], P["uT"][k][:, :T],
                                                            start=(k == 0), stop=(k == 7)),
                                   reads=[b_win[ws], P["b_uT"][k]], writes=[bG[bank]])
                        if c < 16:
                            si = nxt("stb", 4)
                            S.emit("act", lambda e: e.copy(out=stb[si][:, :T], in_=psG[bank][:, :T]),
                                   reads=[bG[bank]], writes=[b_stb[si]])
                            dst = (QTs if c < 8 else KTs)[s][c % 8, :, t0:t0 + T]
                            S.dma("pool", dst, stb[si][:, :T], reads=[b_stb[si]])
                        else:
                            si = nxt("stf", 4)
                            if c < 51:
                                S.emit("act", lambda e: e.copy(out=stf[si][:, :T], in_=psG[bank][:, :T]),
                                       reads=[bG[bank]], writes=[b_stf[si]])
                                dst = RWs[s][c - 24, :, t0:t0 + T]
                            else:
                                S.emit("act", lambda e: e.activation(out=stf[si][:, :T], in_=psG[bank][:, :T],
                                                                     func=AF.Sigmoid),
                                       reads=[bG[bank]], writes=[b_stf[si]])
                                dst = GTs[s][c - 51, :, t0:t0 + T]
                            S.dma("pool", dst, stf[si][:, :T], reads=[b_stf[si]])
                    for h in range(2):
                        S.dma("sp", wv[:], WVS[h], writes=[b_wv])
                        for (tg, o, n) in blks:
                            bank = nxt("D", 2)
                            for k in range(8):
                                S.emit("pe", lambda e: e.matmul(psD[bank][:n, :], P["uT"][k][:, o:o + n], wv[:, k, :],
                                                                start=(k == 0), stop=(k == 7)),
                                       reads=[P["b_uT"][k], b_wv], writes=[bD[bank]])
                            si = nxt("stb", 4)
                            S.emit("act", lambda e: e.copy(out=stb[si][:n, :], in_=psD[bank][:n, :]),
                                   reads=[bD[bank]], writes=[b_stb[si]])
                            S.dma("pool", Vs[s][tg:tg + n, h * 512:(h + 1) * 512], stb[si][:n, :], reads=[b_stb[si]])
                S.barrier()

        for s in range(NS):
            if "A" in phases:
                phase_A(s)


        def phase_B(s):
            L = seqL[s]
            TQ = 384
            qtiles = tiles_of(L, TQ)
            kblocks = tiles_of(L, 128)
            nkb = len(kblocks)
            nfull = L // 128
            ntail = L - nfull * 128
            accb = [psD[0][:], psD[1][:], psT[1][:].bitcast(F32)]
            bacc = [bD[0], bD[1], bT[1]]
            with ExitStack() as pst:
                kt = [sb("b_kt", (128, L), BF16, pst) for _ in range(2)]
                qt = [sb("b_qt", (128, L), BF16, pst) for _ in range(2)]
                va = [sb("b_va", (128, nkb, 129), BF16, pst) for _ in range(2)]
                b_kqv = [Buf(), Buf()]
                PT = [sb("b_PT", (128, TQ), BF16, pst) for _ in range(4)]
                b_PT = [Buf() for _ in range(4)]
                tmpS = [sb("b_tmpS", (128, TQ), F32, pst) for _ in range(2)]
                b_tmpS = [Buf() for _ in range(2)]
                rd = sb("b_rd", (128, 8), F32, pst)
                b_rd = Buf()
                o1 = sb("b_o1", (128, 128), F32, pst)
                oo = sb("b_oo", (128, 128), F32, pst)
                junk = sb("b_junk", (128, 128), F32, pst)
                on = sb("b_on", (128, 128), BF16, pst)
                b_o = Buf()
                ost = [sb("b_ost", (128, TQ), BF16, pst) for _ in range(2)]
                b_ost = [Buf(), Buf()]
                rr["PT"] = 0
                rr["tmpS"] = 0
                rr["ost"] = 0
                for i in range(2):
                    S.emit("pool", lambda e: e.memset(va[i][:, :, 128:129], 1.0), writes=[b_kqv[i]])
                for h in range(8):
                    i = h % 2
                    S.dma("sp", kt[i][:], KTs[s][h], writes=[b_kqv[i]])
                    S.dma("sp", qt[i][:], QTs[s][h], writes=[b_kqv[i]])
                    if nfull:
                        S.dma("sp", va[i][:, 0:nfull, 0:128],
                              Vs[s][0:nfull * 128, h * 128:(h + 1) * 128].rearrange("(kb p) c -> p kb c", p=128),
                              writes=[b_kqv[i]])
                    if ntail:
                        S.dma("sp", va[i][0:ntail, nfull, 0:128], Vs[s][nfull * 128:L, h * 128:(h + 1) * 128],
                              writes=[b_kqv[i]])
                    for (qt0, TQn) in qtiles:
                        subs = blocks_of(qt0, TQn)

                        def qk(kbi):
                            k0, nk = kblocks[kbi]
                            banks = [nxt("G", 4), nxt("G", 4)]
                            for c in range(2):
                                S.emit("pe", lambda e: e.matmul(psG[banks[c]][:nk, :TQn], kt[i][c * 64:(c + 1) * 64, k0:k0 + nk],
                                                                qt[i][c * 64:(c + 1) * 64, qt0:qt0 + TQn], start=True, stop=True),
                                       reads=[b_kqv[i]], writes=[bG[banks[c]]])
                            return banks

                        nb = qk(0)
                        for kbi, (k0, nk) in enumerate(kblocks):
                            banks = nb
                            if kbi + 1 < nkb:
                                nb = qk(kbi + 1)
                            d = k0 - qt0
                            relmin = d - (TQn - 1)
                            relmax = d + nk - 1
                            near = not (relmin >= 91 or relmax <= -91)
                            for c in range(2):
                                pi = nxt("PT", 4)
                                if near:
                                    ti = nxt("tmpS", 2)
                                    m0 = 384 - d
                                    S.emit("dve", lambda e: e.scalar_tensor_tensor(
                                        out=tmpS[ti][:nk, :TQn], in0=psG[banks[c]][:nk, :TQn], scalar=0.125,
                                        in1=Gs[h][:nk, m0:m0 + TQn], op0=ALU.mult, op1=ALU.add),
                                        reads=[bG[banks[c]], b_Gs[h]], writes=[b_tmpS[ti]])
                                    S.emit("act", lambda e: e.activation(out=PT[pi][:nk, :TQn], in_=tmpS[ti][:nk, :TQn], func=AF.Exp),
                                           reads=[b_tmpS[ti]], writes=[b_PT[pi]])
                                else:
                                    col = (31 if relmin >= 91 else 15) * 8 + h
                                    S.emit("act", lambda e: e.activation(out=PT[pi][:nk, :TQn], in_=psG[banks[c]][:nk, :TQn], func=AF.Exp,
                                                                         scale=0.125, bias=tb[:nk, col:col + 1]),
                                           reads=[bG[banks[c]], b_tb], writes=[b_PT[pi]])
                                for si, (qg, o, nq) in enumerate(subs):
                                    first = (kbi == 0 and c == 0)
                                    last = (kbi == nkb - 1 and c == 1)
                                    S.emit("pe", lambda e: e.matmul(accb[si][:nq, c * 129:(c + 1) * 129], PT[pi][:nk, o:o + nq],
                                                                    va[i][:nk, kbi, :], start=first, stop=last),
                                           reads=[b_PT[pi], b_kqv[i]], writes=[bacc[si]])
                        oi = nxt("ost", 2)
                        for si, (qg, o, nq) in enumerate(subs):
                            A = accb[si]
                            S.emit("dve", lambda e: e.reciprocal(out=rd[:nq, 0:1], in_=A[:nq, 128:129]), reads=[bacc[si]], writes=[b_rd])
                            S.emit("dve", lambda e: e.reciprocal(out=rd[:nq, 1:2], in_=A[:nq, 257:258]), reads=[bacc[si]], writes=[b_rd])
                            S.emit("dve", lambda e: e.tensor_scalar(out=rd[:nq, 2:3], in0=rd[:nq, 1:2], scalar1=lams[:nq, 5:6],
                                                                    scalar2=None, op0=ALU.mult), reads=[b_rd, b_lams], writes=[b_rd])
                            S.emit("act", lambda e: e.activation(out=o1[:nq, :], in_=A[:nq, 0:128], func=AF.Copy, scale=rd[:nq, 0:1]),
                                   reads=[bacc[si], b_rd], writes=[b_o])
                            S.emit("dve", lambda e: e.scalar_tensor_tensor(out=oo[:nq, :], in0=A[:nq, 129:257], scalar=rd[:nq, 2:3],
                                                                           in1=o1[:nq, :], op0=ALU.mult, op1=ALU.add),
                                   reads=[bacc[si], b_rd, b_o], writes=[b_o])
                            S.emit("pool", lambda e: e.memset(rd[:, 3:4], 0.0), writes=[b_rd])
                            S.emit("act", lambda e: e.activation(out=junk[:nq, :], in_=oo[:nq, :], func=AF.Square, accum_out=rd[:nq, 3:4]),
                                   reads=[b_o], writes=[b_rd, b_o])
                            S.emit("act", lambda e: e.activation(out=rd[:nq, 4:5], in_=rd[:nq, 3:4], func=AF.Sqrt, bias=epsT[:nq, 0:1],
                                                                 scale=1.0 / 128), reads=[b_rd, b_eps], writes=[b_rd])
                            S.emit("dve", lambda e: e.reciprocal(out=rd[:nq, 5:6], in_=rd[:nq, 4:5]), reads=[b_rd], writes=[b_rd])
                            S.emit("act", lambda e: e.activation(out=on[:nq, :], in_=oo[:nq, :], func=AF.Copy, scale=rd[:nq, 5:6]),
                                   reads=[b_rd, b_o], writes=[b_o])
                            S.emit("pe", lambda e: e.transpose(psT[0][:, o:o + nq], on[:nq, :], ident[:nq, :nq]),
                                   reads=[b_o, b_ident], writes=[bT[0]])
                            S.emit("dve", lambda e: e.tensor_scalar(out=ost[oi][:, o:o + nq], in0=psT[0][:, o:o + nq], scalar1=gsub[:, 0:1],
                                                                    scalar2=None, op0=ALU.mult),
                                   reads=[bT[0], b_gsub], writes=[b_ost[oi]])
                        S.dma("pool", OTs[s][h * 128:(h + 1) * 128, qt0:qt0 + TQn], ost[oi][:, :TQn], reads=[b_ost[oi]])
                S.barrier()

        with ExitStack() as gst:
            bk = sb("bk", (128, 896), F32, gst)
            b_bk = Buf()
            S.dma("sp", bk[:], c_bk[:, :], writes=[b_bk])
            Gs = [sb(f"G{h}", (128, 896), F32, gst) for h in range(8)]
            b_Gs = [Buf() for _ in range(8)]
            gm = [sb("gm", (128, 896), F32, gst) for _ in range(2)]
            b_gm = [Buf(), Buf()]
            for h in range(8):
                S.emit("pool", lambda e: e.memset(Gs[h][:], 0.0), writes=[b_Gs[h]])
            for b in range(32):
                mi = b % 2
                S.emit("pool", lambda e: e.tensor_scalar(out=gm[mi][:], in0=bk[:], scalar1=float(b), scalar2=None, op0=ALU.is_equal),
                       reads=[b_bk], writes=[b_gm[mi]])
                for h in range(8):
                    S.emit("dve", lambda e: e.scalar_tensor_tensor(out=Gs[h][:], in0=gm[mi][:], scalar=tb[:, b * 8 + h:b * 8 + h + 1], in1=Gs[h][:],
                                                                   op0=ALU.mult, op1=ALU.add), reads=[b_gm[mi], b_tb], writes=[b_Gs[h]])

            for s in range(NS):
                if "B" in phases:
                    phase_B(s)
            S.barrier()

        def phase_C(s):
            L = seqL[s]
            SEG = 128
            Lp = ((L + 63) // 64) * 64
            segs = tiles_of(Lp, SEG)
            nseg = len(segs)
            allb = [(psG[i][:], bG[i]) for i in range(4)] + [(psD[i][:], bD[i]) for i in range(2)] + [(psT[1][:].bitcast(F32), bT[1])]
            rr["ps"] = 0

            def bank():
                i = nxt("ps", len(allb))
                return allb[i]

            EXC = 2.0 ** 0
            with ExitStack() as pst:
                NG = 8
                def mk(name, shape, dt=F32, n=1):
                    return [sb(name, shape, dt, pst) for _ in range(n)]
                nchm = SEG // 64
                RAW = [sb("c_RAW", (128, NG, SEG + 2), F32, pst) for _ in range(3)]
                b_RAW = [Buf() for _ in range(3)]
                rawd = [mk("c_rawd", (128, SEG + 2), F32, 2) for _ in range(2)]
                b_raw = [Buf(), Buf()]
                NOPS = 6
                EXP = [[sb("c_EXP", (128, NG, nchm, 128), BF16, pst) for o in range(NOPS)] for _ in range(2)]
                PEND = [sb("c_PEND", (128, NG, nchm), F32, pst) for _ in range(2)]
                RPEND = sb("c_RPEND", (128, NG, nchm), F32, pst)
                b_rpend = Buf()
                b_expg = [Buf(), Buf()]
                exp_ = [[[EXP[gb][o][:, u] for o in range(NOPS)] for u in range(NG)] for gb in range(2)]
                b_exp = [[b_expg[gb] for u in range(NG)] for gb in range(2)]
                pend = [[PEND[gb][:, u] for u in range(NG)] for gb in range(2)]
                for gb in range(2):
                    for o in range(NOPS):
                        S.emit("pool", lambda e: e.memset(EXP[gb][o][:], 0.0), writes=[b_expg[gb]])
                TN = ["rs", "ks", "vs", "kk", "t1", "t2", "w", "al", "kd", "bb", "P", "rW", "Wp", "Wi", "rw", "sg"]
                TB = {nm: sb("c_" + nm, (128, NG, SEG), F32, pst) for nm in TN}
                BB = {nm: Buf() for nm in TN}
                SQ = sb("c_SQ", (128, NG, SEG), BF16, pst)
                b_SQ = Buf()
                zer = sb("c_zer", (128, 64), F32, pst)
                b_zer = Buf()
                S.emit("pool", lambda e: e.memset(zer[:], 0.0), writes=[b_zer])
                dsh = [[sb("c_dsh", (128, SEG), F32, pst) for _ in range(2)] for _ in range(2)]
                twd = [sb("c_twd", (128, SEG), BF16, pst) for _ in range(2)]
                tad = [sb("c_tad", (128, SEG), BF16, pst) for _ in range(2)]
                b_d = [Buf(), Buf()]
                Sf = [[sb("c_S", (128, 64), F32, pst) for c in range(8)] for d in range(2)]
                Sb = [[sb("c_Sb", (128, 64), BF16, pst) for c in range(8)] for d in range(2)]
                b_S = [[Buf() for c in range(8)] for d in range(2)]
                for d in range(2):
                    for c in range(8):
                        S.emit("pool", lambda e: e.memset(Sf[d][c][:], 0.0), writes=[b_S[d][c]])
                        S.emit("pool", lambda e: e.memset(Sb[d][c][:], 0.0), writes=[b_S[d][c]])
                PP = [[sb("c_PP", (128, 256), BF16, pst) for _ in range(2)] for u in range(NG)]
                b_PP = [[Buf(), Buf()] for u in range(NG)]
                L3 = [sb("c_L3", (128, 384), BF16, pst) for u in range(NG)]
                b_L3 = [Buf() for u in range(NG)]
                TT = [[sb("c_TT", (128, 128), BF16, pst) for _ in range(2)] for u in range(NG)]
                b_TT = [[Buf(), Buf()] for u in range(NG)]
                BK_ = [sb("c_BK", (128, 256), BF16, pst) for u in range(NG)]
                b_BK = [Buf() for u in range(NG)]
                Vst = [sb("c_Vst", (128, 64), BF16, pst) for u in range(NG)]
                coef = [sb("c_coef", (128, 1), F32, pst) for u in range(NG)]
                b_V = [Buf() for u in range(NG)]
                Xb = [sb("c_Xb", (128, 64), BF16, pst) for u in range(NG)]
                Ub = [sb("c_Ub", (128, 64), BF16, pst) for u in range(NG)]
                b_X = [Buf() for u in range(NG)]
                b_U = [Buf() for u in range(NG)]
                Yst = [sb("c_Yst", (128, NG, 64), F32, pst) for _ in range(2)]
                Bst = [sb("c_Bst", (128, NG, 64), F32, pst) for _ in range(2)]
                b_Yst = [Buf(), Buf()]
                b_Bst = [Buf(), Buf()]
                rr["yst"] = 0

                def shift(dst, src, j, n, rb, wb, eng="dve"):
                    S.emit(eng, lambda e: e.tensor_scalar(out=dst[:, :n], in0=src[:, 1:n + 1], scalar1=mus[:, 2, j:j + 1], scalar2=None, op0=ALU.mult),
                           reads=[b_mus] + rb, writes=wb)
                    S.emit(eng, lambda e: e.scalar_tensor_tensor(out=dst[:, :n], in0=src[:, 0:n], scalar=mus[:, 0, j:j + 1], in1=dst[:, :n],
                                                                 op0=ALU.mult, op1=ALU.add), reads=[b_mus] + rb, writes=wb)
                    S.emit(eng, lambda e: e.scalar_tensor_tensor(out=dst[:, :n], in0=src[:, 2:n + 2], scalar=mus[:, 1, j:j + 1], in1=dst[:, :n],
                                                                 op0=ALU.mult, op1=ALU.add), reads=[b_mus] + rb, writes=wb)

                def load_group(gb, d, half, seg0, n):
                    lo = max(seg0 - 1, 0)
                    hi = min(seg0 + n + 1, L)
                    edge = (seg0 - 1 < 0) or (seg0 + n + 1 > L)
                    for i in range(3):
                        if edge:
                            S.emit("pool", lambda e: e.memset(RAW[i][:], 0.0), writes=[b_RAW[i]])
                        S.dma("sp", RAW[i][:, :, lo - (seg0 - 1):hi - (seg0 - 1)],
                              RWs[s][i * 8:(i + 1) * 8, :, lo:hi].rearrange("u p t -> p u t"), writes=[b_RAW[i]])
                    for (t, j) in [(rawd[gb][0], 24), (rawd[gb][1], 25)]:
                        if edge:
                            S.emit("pool", lambda e: e.memset(t[:], 0.0), writes=[b_raw[gb]])
                        S.dma("sp", t[:, lo - (seg0 - 1):hi - (seg0 - 1)], RWs[s][j, :, lo:hi], writes=[b_raw[gb]])

                def prep_group(gb, d, half, seg0, n):
                    nch = n // 64
                    npad0 = max(0, min(n, L - seg0))
                    bex = b_expg[gb]

                    def op(eng, fn, R=(), Wr=(), xr=(), xw=()):
                        S.emit(eng, fn, reads=[BB[x] for x in R] + list(xr), writes=[BB[x] for x in Wr] + list(xw))

                    def T(nm):
                        return TB[nm][:, :, :n]

                    def bc(ap2):
                        return ap2.unsqueeze(2).to_broadcast([128, NG, n])

                    shift(dsh[gb][0], rawd[gb][0], 24, n, [b_raw[gb]], [b_d[gb]])
                    shift(dsh[gb][1], rawd[gb][1], 25, n, [b_raw[gb]], [b_d[gb]])
                    S.emit("act", lambda e: e.activation(out=twd[gb][:, :n], in_=dsh[gb][0][:, :n], func=AF.Tanh), reads=[b_d[gb]], writes=[b_d[gb]])
                    S.emit("act", lambda e: e.copy(out=tad[gb][:, :n], in_=dsh[gb][1][:, :n]), reads=[b_d[gb]], writes=[b_d[gb]])
                    yield
                    for i, nm in enumerate(["rs", "ks", "vs"]):
                        j0 = i * 8
                        tmpn = "t1" if i % 2 == 0 else "t2"
                        op("dve", lambda e: e.tensor_tensor(out=T(nm), in0=RAW[i][:, :, 1:n + 1], in1=bc(mus[:, 2, j0:j0 + NG]), op=ALU.mult),
                           Wr=[nm], xr=[b_RAW[i], b_mus])
                        op("pool", lambda e: e.tensor_tensor(out=T(tmpn), in0=RAW[i][:, :, 0:n], in1=bc(mus[:, 0, j0:j0 + NG]), op=ALU.mult),
                           Wr=[tmpn], xr=[b_RAW[i], b_mus])
                        op("dve", lambda e: e.tensor_tensor(out=T(nm), in0=T(nm), in1=T(tmpn), op=ALU.add), R=[nm, tmpn], Wr=[nm])
                        op("pool", lambda e: e.tensor_tensor(out=T(tmpn), in0=RAW[i][:, :, 2:n + 2], in1=bc(mus[:, 1, j0:j0 + NG]), op=ALU.mult),
                           R=[nm], Wr=[tmpn], xr=[b_RAW[i], b_mus])
                        op("dve", lambda e: e.tensor_tensor(out=T(nm), in0=T(nm), in1=T(tmpn), op=ALU.add), R=[nm, tmpn], Wr=[nm])
                        if npad0 < n:
                            op("pool", lambda e: e.memset(TB[nm][:, :, npad0:n], 0.0), Wr=[nm])
                        yield
                    op("pool", lambda e: e.tensor_tensor(out=T("kk"), in0=T("ks"), in1=bc(rwp[:, 4, 0:NG]), op=ALU.mult), R=["ks"], Wr=["kk"], xr=[b_rwp])
                    op("dve", lambda e: e.tensor_tensor(out=SQ[:, :, :n], in0=T("kk"), in1=T("kk"), op=ALU.mult), R=["kk"], xw=[b_SQ])
                    for j in range(NG // 4):
                        (pa, bpa) = bank()
                        for uu in range(4):
                            u = j * 4 + uu
                            S.emit("pe", lambda e: e.matmul(pa[:, uu * 128:uu * 128 + n], bones[:], SQ[:, u, :n], start=True, stop=True),
                                   reads=[b_SQ, b_rc], writes=[bpa])
                        op("dve", lambda e: e.tensor_scalar(out=TB["t1"][:, j * 4:(j + 1) * 4, :n], in0=pa[:, :].rearrange("p (u t) -> p u t", t=128)[:, :, :n],
                                                            scalar1=1e-24, scalar2=None, op0=ALU.max), Wr=["t1"], xr=[bpa])
                    op("act", lambda e: e.activation(out=T("t1"), in_=T("t1"), func=AF.Ln), R=["t1"], Wr=["t1"])
                    op("act", lambda e: e.activation(out=T("t1"), in_=T("t1"), func=AF.Exp, scale=-0.5), R=["t1"], Wr=["t1"])
                    op("dve", lambda e: e.tensor_tensor(out=T("kk"), in0=T("kk"), in1=T("t1"), op=ALU.mult), R=["kk", "t1"], Wr=["kk"])
                    yield
                    rows = slice(d * 64, (d + 1) * 64)
                    for (wsb, src, dstn, bi) in [(w2sb, twd, "sg", 0), (a2sb, tad, "al", 2)]:
                        for j in range(NG // 4):
                            (pw, bpw) = bank()
                            for uu in range(4):
                                u = j * 4 + uu
                                S.emit("pe", lambda e: e.matmul(pw[:, uu * 128:uu * 128 + n], wsb[rows, u * 128:(u + 1) * 128], src[gb][rows, :n],
                                                                start=True, stop=True), reads=[b_w2, b_d[gb]], writes=[bpw])
                            for uu in range(4):
                                u = j * 4 + uu
                                op("act", lambda e: e.activation(out=TB[dstn][:, u, :n], in_=pw[:, uu * 128:uu * 128 + n], func=AF.Sigmoid,
                                                                 bias=rwp[:, bi + d, u:u + 1]), Wr=[dstn], xr=[bpw, b_rwp])
                        yield
                    op("act", lambda e: e.activation(out=T("w"), in_=T("sg"), func=AF.Exp, scale=-math.exp(-0.5)), R=["sg"], Wr=["w"])
                    op("act", lambda e: e.activation(out=T("rw"), in_=T("sg"), func=AF.Exp, scale=math.exp(-0.5)), R=["sg"], Wr=["rw"])
                    if npad0 < n:
                        op("pool", lambda e: e.memset(TB["w"][:, :, npad0:n], 1.0), Wr=["w"])
                        op("pool", lambda e: e.memset(TB["rw"][:, :, npad0:n], 1.0), Wr=["rw"])
                    op("dve", lambda e: e.tensor_scalar(out=T("t1"), in0=T("al"), scalar1=-1.0, scalar2=None, op0=ALU.add), R=["al"], Wr=["t1"])
                    op("pool", lambda e: e.tensor_tensor(out=T("t1"), in0=T("t1"), in1=bc(rwp[:, 5, 0:NG]), op=ALU.mult), R=["t1"], Wr=["t1"], xr=[b_rwp])
                    op("dve", lambda e: e.scalar_tensor_tensor(out=T("kd"), in0=T("t1"), scalar=1.0, in1=T("ks"), op0=ALU.add, op1=ALU.mult),
                       R=["t1", "ks"], Wr=["kd"])
                    op("pool", lambda e: e.tensor_tensor(out=T("bb"), in0=T("kk"), in1=T("al"), op=ALU.mult), R=["kk", "al"], Wr=["bb"])
                    yield
                    for u in range(NG):
                        for c_ in range(nch):
                            op("dve", lambda e: e.tensor_tensor_scan(out=TB["P"][:, u, c_ * 64:(c_ + 1) * 64], data0=TB["w"][:, u, c_ * 64:(c_ + 1) * 64],
                                                                     data1=zer[:, :], initial=1.0, op0=ALU.mult, op1=ALU.add),
                               R=["w"], Wr=["P"], xr=[b_zer])
                            op("dve", lambda e: e.tensor_tensor_scan(out=TB["rW"][:, u, c_ * 64:(c_ + 1) * 64], data0=TB["rw"][:, u, c_ * 64:(c_ + 1) * 64],
                                                                     data1=zer[:, :], initial=1.0, op0=ALU.mult, op1=ALU.add),
                               R=["rw"], Wr=["rW"], xr=[b_zer])
                    yield
                    P4 = T("P").rearrange("p u (c t) -> p u c t", t=64)
                    op("dve", lambda e: e.tensor_copy(out=PEND[gb][:, :, :nch], in_=P4[:, :, :, 63]), R=["P"], xw=[bex])
                    pe4 = PEND[gb][:, :, :nch].unsqueeze(3).to_broadcast([128, NG, nch, 64])

                    def v4(nm):
                        return T(nm).rearrange("p u (c t) -> p u c t", t=64)
                    if d == 0:
                        op("pool", lambda e: e.tensor_tensor(out=T("Wp"), in0=T("P"), in1=T("rw"), op=ALU.mult), R=["P", "rw"], Wr=["Wp"])
                        Wi, rW = "P", "rW"
                    else:
                        rP4 = T("rW").rearrange("p u (c t) -> p u c t", t=64)
                        op("dve", lambda e: e.tensor_copy(out=RPEND[:, :, :nch], in_=rP4[:, :, :, 63]), R=["rW"], xw=[b_rpend])
                        rpe4 = RPEND[:, :, :nch].unsqueeze(3).to_broadcast([128, NG, nch, 64])
                        op("pool", lambda e: e.tensor_tensor(out=T("t1"), in0=T("P"), in1=T("rw"), op=ALU.mult), R=["P", "rw"], Wr=["t1"])
                        op("dve", lambda e: e.tensor_tensor(out=v4("Wp"), in0=v4("rW"), in1=pe4, op=ALU.mult), R=["rW"], Wr=["Wp"], xr=[bex])
                        op("pool", lambda e: e.tensor_tensor(out=T("Wi"), in0=T("Wp"), in1=T("w"), op=ALU.mult), R=["Wp", "w"], Wr=["Wi"])
                        op("dve", lambda e: e.tensor_tensor(out=v4("t1"), in0=v4("t1"), in1=rpe4, op=ALU.mult), R=["t1"], Wr=["t1"], xr=[b_rpend])
                        Wi, rW = "Wi", "t1"
                    yield

                    def halves(o, nm):
                        for hh in range(2):
                            ps_ = slice(hh * 64, (hh + 1) * 64)
                            yield (EXP[gb][o][ps_, :, 0:nch, hh * 64:(hh + 1) * 64],
                                   lambda name: TB[name][ps_, :, :n].rearrange("p u (c t) -> p u c t", t=64))
                    for (oap, iv) in halves(0, None):
                        op("dve", lambda e: e.scalar_tensor_tensor(out=oap, in0=iv("kk"), scalar=-1.0, in1=iv("Wp"), op0=ALU.mult, op1=ALU.mult),
                           R=["kk", "Wp"], xw=[bex])
                    for (oap, iv) in halves(1, None):
                        op("pool", lambda e: e.tensor_tensor(out=oap, in0=iv("rs"), in1=iv(Wi), op=ALU.mult), R=["rs", Wi], xw=[bex])
                    yield
                    for (oap, iv) in halves(2, None):
                        op("dve", lambda e: e.tensor_tensor(out=oap, in0=iv("bb"), in1=iv(rW), op=ALU.mult), R=["bb", rW], xw=[bex])
                    for (oap, iv) in halves(3, None):
                        op("pool", lambda e: e.tensor_tensor(out=oap, in0=iv("kd"), in1=iv(rW), op=ALU.mult), R=["kd", rW], xw=[bex])
                    yield
                    for (oap, iv) in halves(4, None):
                        op("act", lambda e: e.copy(out=oap, in_=iv("vs")), R=["vs"], xw=[bex])
                    op("pool", lambda e: e.tensor_tensor(out=T("t2"), in0=T("rs"), in1=bc(rwp[:, 6, 0:NG]), op=ALU.mult), R=["rs"], Wr=["t2"], xr=[b_rwp])
                    for (oap, iv) in halves(5, None):
                        op("dve", lambda e: e.tensor_tensor(out=oap, in0=iv("t2"), in1=iv("kd"), op=ALU.mult), R=["t2", "kd"], xw=[bex])
                    yield

                def chunk_group(gb, d, half, seg0, n, ch):
                    tok0 = seg0 + ch * 64
                    nt = max(0, min(64, L - tok0))
                    units = range(NG)
                    yi = nxt("yst", 2)
                    for u in units:
                        E = exp_[gb][u]
                        A, Rr, B, K, V, RK = [E[o][:, ch, :] for o in range(NOPS)]
                        be = b_exp[gb][u]
                        (p1, bp1) = bank()
                        S.emit("pe", lambda e: e.matmul(p1[:, 0:128], A, B, start=True, stop=True), reads=[be], writes=[bp1])
                        S.emit("pe", lambda e: e.matmul(p1[:, 128:256], B, A, start=True, stop=True), reads=[be], writes=[bp1])
                        S.emit("dve", lambda e: e.tensor_tensor(out=PP[u][0][:], in0=p1[:, 0:256], in1=m12[d][:], op=ALU.mult),
                               reads=[bp1, b_mm], writes=[b_PP[u][0]])
                        S.emit("dve", lambda e: e.tensor_tensor(out=TT[u][0][:], in0=PP[u][0][:, 128:256], in1=ident[:], op=ALU.add),
                               reads=[b_PP[u][0], b_ident], writes=[b_TT[u][0]])
                        (p2, bp2) = bank()
                        S.emit("pe", lambda e: e.matmul(p2[:, 0:128], K, A, start=True, stop=True), reads=[be], writes=[bp2])
                        S.emit("pe", lambda e: e.matmul(p2[:, 128:256], B, Rr, start=True, stop=True), reads=[be], writes=[bp2])
                        S.emit("pe", lambda e: e.matmul(p2[:, 256:384], K, Rr, start=True, stop=True), reads=[be], writes=[bp2])
                        S.emit("dve", lambda e: e.tensor_tensor(out=L3[u][:], in0=p2[:, 0:384], in1=m345[d][:], op=ALU.mult),
                               reads=[bp2, b_mm], writes=[b_L3[u]])
                        (p3, bp3) = bank()
                        p3b = p3.bitcast(BF16)
                        S.emit("pe", lambda e: e.transpose(p3b[:, 0:128], B, ident[:]), reads=[be, b_ident], writes=[bp3])
                        S.emit("pe", lambda e: e.transpose(p3b[:, 128:256], K, ident[:]), reads=[be, b_ident], writes=[bp3])
                        S.emit("pe", lambda e: e.matmul(p3[:, 128:192], V, istack[:], start=True, stop=True), reads=[be, b_rc], writes=[bp3])
                        S.emit("pe", lambda e: e.matmul(p3[:, 192:193], RK, onesb[:], start=True, stop=True), reads=[be, b_rc], writes=[bp3])
                        S.emit("act", lambda e: e.copy(out=BK_[u][:], in_=p3b[:, 0:256]), reads=[bp3], writes=[b_BK[u]])
                        S.emit("act", lambda e: e.copy(out=Vst[u][:], in_=p3[:, 128:192]), reads=[bp3], writes=[b_V[u]])
                        S.emit("act", lambda e: e.copy(out=coef[u][:], in_=p3[:, 192:193]), reads=[bp3], writes=[b_V[u]])
                        S.emit("act", lambda e: e.activation(out=Bst[yi][:, u, :], in_=Vst[u][:], func=AF.Copy, scale=coef[u][:, 0:1]),
                               reads=[b_V[u]], writes=[b_Bst[yi]])
                    yield
                    for j in range(6):
                        if j > 0:
                            yield
                        cur, new = j % 2, (j + 1) % 2
                        for u in units:
                            (pq, bpq) = bank()
                            Pj = PP[u][cur][:, 0:128]
                            PjT = PP[u][cur][:, 128:256]
                            if j < 5:
                                S.emit("pe", lambda e: e.matmul(pq[:, 0:128], PjT, Pj, start=True, stop=True), reads=[b_PP[u][cur]], writes=[bpq])
                                S.emit("pe", lambda e: e.matmul(pq[:, 128:256], Pj, PjT, start=True, stop=True), reads=[b_PP[u][cur]], writes=[bpq])
                            if j >= 1:
                                tc_, tn_ = (j - 1) % 2, j % 2
                                S.emit("pe", lambda e: e.matmul(pq[:, 256:384], ident[:], TT[u][tc_][:], start=True, stop=False),
                                       reads=[b_ident, b_TT[u][tc_]], writes=[bpq])
                                S.emit("pe", lambda e: e.matmul(pq[:, 256:384], Pj, TT[u][tc_][:], start=False, stop=True),
                                       reads=[b_PP[u][cur], b_TT[u][tc_]], writes=[bpq])
                                if u % 4 != 3:
                                    S.emit("act", lambda e: e.copy(out=TT[u][tn_][:], in_=pq[:, 256:384]), reads=[bpq], writes=[b_TT[u][tn_]])
                                else:
                                    S.emit("dve", lambda e: e.tensor_copy(out=TT[u][tn_][:], in_=pq[:, 256:384]), reads=[bpq], writes=[b_TT[u][tn_]])
                            if j < 5:
                                if u % 4 != 3:
                                    S.emit("act", lambda e: e.copy(out=PP[u][new][:], in_=pq[:, 0:256]), reads=[bpq], writes=[b_PP[u][new]])
                                else:
                                    S.emit("dve", lambda e: e.tensor_copy(out=PP[u][new][:], in_=pq[:, 0:256]), reads=[bpq], writes=[b_PP[u][new]])
                    tfin = 5 % 2
                    yield
                    for u in units:
                        c = half * NG + u
                        E = exp_[gb][u]
                        A = E[0][:, ch, :]
                        (px, bpx) = bank()
                        S.emit("pe", lambda e: e.matmul(px[:, 0:64], A, Sb[d][c][:], start=True, stop=False), reads=[b_exp[gb][u], b_S[d][c]], writes=[bpx])
                        S.emit("pe", lambda e: e.matmul(px[:, 0:64], L3[u][:, 0:128], Vst[u][:], start=False, stop=True), reads=[b_L3[u], b_V[u]], writes=[bpx])
                        S.emit("act", lambda e: e.copy(out=Xb[u][:], in_=px[:, 0:64]), reads=[bpx], writes=[b_X[u]])
                    yield
                    for u in units:
                        (pu, bpu) = bank()
                        S.emit("pe", lambda e: e.matmul(pu[:, 0:64], TT[u][tfin][:], Xb[u][:], start=True, stop=True), reads=[b_TT[u][tfin], b_X[u]], writes=[bpu])
                        S.emit("dve", lambda e: e.tensor_copy(out=Ub[u][:], in_=pu[:, 0:64]), reads=[bpu], writes=[b_U[u]])
                    yield
                    for u in units:
                        c = half * NG + u
                        E = exp_[gb][u]
                        Rr = E[1][:, ch, :]
                        (py, bpy) = bank()
                        S.emit("pe", lambda e: e.matmul(py[:, 0:64], Rr, Sb[d][c][:], start=True, stop=False), reads=[b_exp[gb][u], b_S[d][c]], writes=[bpy])
                        S.emit("pe", lambda e: e.matmul(py[:, 0:64], L3[u][:, 128:256], Ub[u][:], start=False, stop=False), reads=[b_L3[u], b_U[u]], writes=[bpy])
                        S.emit("pe", lambda e: e.matmul(py[:, 0:64], L3[u][:, 256:384], Vst[u][:], start=False, stop=True), reads=[b_L3[u], b_V[u]], writes=[bpy])
                        S.emit("act", lambda e: e.copy(out=Yst[yi][:, u, :], in_=py[:, 0:64]), reads=[bpy], writes=[b_Yst[yi]])
                        (pS, bpS) = bank()
                        S.emit("pe", lambda e: e.matmul(pS[:, 0:64], BK_[u][:, 0:128], Ub[u][:], start=True, stop=False), reads=[b_BK[u], b_U[u]], writes=[bpS])
                        S.emit("pe", lambda e: e.matmul(pS[:, 0:64], BK_[u][:, 128:256], Vst[u][:], start=False, stop=True), reads=[b_BK[u], b_V[u]], writes=[bpS])
                        S.emit("dve", lambda e: e.tensor_tensor(out=Sf[d][c][:], in0=Sf[d][c][:], in1=pS[:, 0:64], op=ALU.add),
                               reads=[bpS], writes=[b_S[d][c]])
                        S.emit("dve", lambda e: e.tensor_scalar(out=Sf[d][c][:], in0=Sf[d][c][:], scalar1=pend[gb][u][:, ch:ch + 1], scalar2=None, op0=ALU.mult),
                               reads=[b_exp[gb][u]], writes=[b_S[d][c]])
                        S.emit("act", lambda e: e.copy(out=Sb[d][c][:], in_=Sf[d][c][:]), reads=[b_S[d][c]], writes=[b_S[d][c]])
                    yield
                    if nt > 0:
                        for (stg, bst, dst) in [(Yst[yi], b_Yst[yi], YSs[d][s]), (Bst[yi], b_Bst[yi], BNs[d][s])]:
                            dv = dst[tok0:tok0 + nt, :].rearrange("t (c h v) -> t c h v", h=2, v=64)
                            for hh in range(2):
                                S.dma("sp", dv[:, half * NG:(half + 1) * NG, hh, :], stg[hh * 64:hh * 64 + nt, :, :], reads=[bst])

                work = []
                for i in range(nseg):
                    for d in range(2):
                        si = i if d == 0 else nseg - 1 - i
                        for half in range(8 // NG):
                            work.append((d, half, segs[si][0], segs[si][1]))
                def run_all(gen):
                    for _ in gen:
                        pass

                def chunks_gen(wi):
                    d, half, seg0, n = work[wi]
                    chs = list(range(n // 64))
                    if d == 1:
                        chs = chs[::-1]
                    for ch in chs:
                        yield from chunk_group(wi % 2, d, half, seg0, n, ch)

                NW = len(work)
                load_group(0, *work[0])
                run_all(prep_group(0, *work[0]))
                if NW > 1:
                    load_group(1, *work[1])
                for wi in range(NW):
                    pg = prep_group((wi + 1) % 2, *work[wi + 1]) if wi + 1 < NW else iter(())
                    nsteps = 14 * (work[wi][3] // 64)
                    psteps = 14 if wi + 1 < NW else 0
                    per = -(-psteps // max(nsteps, 1)) if psteps else 0
                    first = True
                    for _ in chunks_gen(wi):
                        for _k in range(per):
                            next(pg, None)
                    run_all(pg)
                    if wi + 2 < NW:
                        load_group(wi % 2, *work[wi + 2])
                S.barrier()

        for s in range(NS):
            if "C" in phases:
                phase_C(s)

        lnx = sb("lnx", (128, 2, D))
        b_lnx = Buf()
        S.dma("sp", lnx[:, 0, :], W["rw_lnx_w"][0].partition_broadcast(128), writes=[b_lnx])
        S.dma("sp", lnx[:, 1, :], W["rw_lnx_b"][0].partition_broadcast(128), writes=[b_lnx])

        def phase_D(s):
            L = seqL[s]
            TD = 256
            with ExitStack() as pst:
                P = {}
                P["ss"] = sb("d_ss", (128, 4), F32, pst)
                P["b_ss"] = Buf()
                P["junk"] = sb("d_junk", (128, D), BF16, pst)
                P["b_junk"] = Buf()
                P["xn"] = [sb("d_xn", (128, D), BF16, pst) for i in range(2)]
                P["b_xn"] = [Buf() for _ in range(2)]
                P["xT"] = [sb("d_xT", (128, TD), BF16, pst) for i in range(8)]
                P["b_xT"] = [Buf() for _ in range(8)]
                P["wgu"] = [sb("d_wgu", (128, 2, 8, 128), BF16, pst) for i in range(3)]
                P["b_wgu"] = [Buf() for _ in range(3)]
                P["sg"] = [sb("d_sg", (128, TD), F32, pst) for i in range(2)]
                P["b_sg"] = [Buf() for _ in range(2)]
                P["act"] = [sb("d_act", (128, TD), BF16, pst) for i in range(NF)]
                P["b_act"] = [Buf() for _ in range(NF)]
                P["wd"] = sb("d_wd", (128, NF, 512), BF16, pst)
                P["b_wd"] = Buf()
                rr["wgu"] = 0
                rr["sg"] = 0
                hts = [sb("d_h", (128, D), F32, pst) for i in range(2)]
                b_h = [Buf(), Buf()]
                yin = [sb("d_yin", (128, D), F32, pst) for i in range(4)]
                b_yin = Buf()
                ysum = sb("d_ysum", (128, D), F32, pst)
                ytmp = sb("d_ytmp", (128, D), F32, pst)
                b_y = Buf()
                st16 = sb("d_st16", (128, 4, 16), F32, pst)
                b_st = Buf()
                zb = sb("d_zb", (128, D), BF16, pst)
                b_zb = Buf()
                zT = [sb("d_zT", (128, TD), BF16, pst) for i in range(8)]
                b_zT = [Buf() for _ in range(8)]
                oT = [sb("d_oT", (128, TD), BF16, pst) for i in range(8)]
                b_oT = [Buf() for _ in range(8)]
                mT = [sb("d_mT", (128, TD), BF16, pst) for i in range(8)]
                b_mT = [Buf() for _ in range(8)]
                gts = [sb("d_gt", (128, TD), F32, pst) for i in range(4)]
                b_gts = [Buf() for _ in range(4)]
                mtmp = sb("d_mtmp", (128, TD), F32, pst)
                b_mtmp = Buf()
                wbr = [sb("d_wbr", (128, 8, 128), BF16, pst) for i in range(4)]
                b_wbr = [Buf() for _ in range(4)]
                wo = sb("d_wo", (128, 8, 512), BF16, pst)
                b_wo = Buf()
                gdraw = sb("d_gdraw", (128, TD + 2), F32, pst)
                gdsh = sb("d_gdsh", (128, TD), F32, pst)
                sgd = sb("d_sgd", (128, TD), BF16, pst)
                b_gd = Buf()
                ost = sb("d_ost", (128, D), F32, pst)
                b_ost = Buf()
                rr["gts"] = 0
                rr["wbr"] = 0
                for (t0, T) in tiles_of(L, TD):
                    blks = blocks_of(t0, T)
                    lo = max(t0 - 1, 0)
                    hi = min(t0 + T + 1, L)
                    if (t0 - 1 < 0) or (t0 + T + 1 > L):
                        S.emit("pool", lambda e: e.memset(gdraw[:], 0.0), writes=[b_gd])
                    S.dma("sp", gdraw[:, lo - (t0 - 1):hi - (t0 - 1)], RWs[s][26, :, lo:hi], writes=[b_gd])
                    S.emit("pool", lambda e: e.tensor_scalar(out=gdsh[:, :T], in0=gdraw[:, 1:T + 1], scalar1=mus[:, 2, 26:27], scalar2=None, op0=ALU.mult),
                           reads=[b_mus, b_gd], writes=[b_gd])
                    S.emit("dve", lambda e: e.scalar_tensor_tensor(out=gdsh[:, :T], in0=gdraw[:, 0:T], scalar=mus[:, 0, 26:27], in1=gdsh[:, :T],
                                                                    op0=ALU.mult, op1=ALU.add), reads=[b_mus, b_gd], writes=[b_gd])
                    S.emit("dve", lambda e: e.scalar_tensor_tensor(out=gdsh[:, :T], in0=gdraw[:, 2:T + 2], scalar=mus[:, 1, 26:27], in1=gdsh[:, :T],
                                                                    op0=ALU.mult, op1=ALU.add), reads=[b_mus, b_gd], writes=[b_gd])
                    S.emit("act", lambda e: e.activation(out=sgd[:, :T], in_=gdsh[:, :T], func=AF.Sigmoid), reads=[b_gd], writes=[b_gd])
                    for k in range(8):
                        S.dma("sp", oT[k][:, :T], OTs[s][k * 128:(k + 1) * 128, t0:t0 + T], writes=[b_oT[k]])
                    for bi, (tg, o, n) in enumerate(blks):
                        S.dma("sp", hts[bi][:n, :], Hs[s][tg:tg + n, :], writes=[b_h[bi]])
                        for i, src in enumerate([YSs[0][s], YSs[1][s], BNs[0][s], BNs[1][s]]):
                            S.dma("sp", yin[i][:n, :], src[tg:tg + n, :], writes=[b_yin])
                        gbank = [nxt("D", 2), nxt("D", 2)]
                        for hf in range(2):
                            S.emit("pe", lambda e: e.matmul(psD[gbank[hf]][:n, :], sgd[:, o:o + n], g2sb[:, hf * 512:(hf + 1) * 512], start=True, stop=True),
                                   reads=[b_gd, b_w2], writes=[bD[gbank[hf]]])
                        def v3(t):
                            return t[:n, :].rearrange("p (h v) -> p h v", v=64)
                        def bc(col):
                            return st16[:n, col, :].unsqueeze(2).to_broadcast([n, 16, 64])
                        S.emit("dve", lambda e: e.tensor_tensor(out=ysum[:n, :], in0=yin[0][:n, :], in1=yin[1][:n, :], op=ALU.add), reads=[b_yin], writes=[b_y])
                        S.emit("dve", lambda e: e.reduce_sum(out=st16[:n, 0, :], in_=v3(ysum), axis=AX.X), reads=[b_y], writes=[b_st])
                        S.emit("dve", lambda e: e.tensor_scalar(out=st16[:n, 0, :], in0=st16[:n, 0, :], scalar1=1.0 / 64, scalar2=None, op0=ALU.mult),
                               reads=[b_st], writes=[b_st])
                        S.emit("dve", lambda e: e.tensor_tensor(out=v3(ysum), in0=v3(ysum), in1=bc(0), op=ALU.subtract), reads=[b_y, b_st], writes=[b_y])
                        S.emit("pool", lambda e: e.tensor_tensor(out=ytmp[:n, :], in0=ysum[:n, :], in1=ysum[:n, :], op=ALU.mult), reads=[b_y], writes=[b_y])
                        S.emit("dve", lambda e: e.reduce_sum(out=st16[:n, 1, :], in_=v3(ytmp), axis=AX.X), reads=[b_y], writes=[b_st])
                        S.emit("act", lambda e: e.activation(out=st16[:n, 2, :], in_=st16[:n, 1, :], func=AF.Sqrt, bias=epsT[:n, 1:2], scale=1.0 / 64),
                               reads=[b_st, b_eps], writes=[b_st])
                        S.emit("dve", lambda e: e.reciprocal(out=st16[:n, 3, :], in_=st16[:n, 2, :]), reads=[b_st], writes=[b_st])
                        S.emit("dve", lambda e: e.tensor_tensor(out=v3(ysum), in0=v3(ysum), in1=bc(3), op=ALU.mult), reads=[b_y, b_st], writes=[b_y])
                        S.emit("pool", lambda e: e.tensor_tensor(out=ysum[:n, :], in0=ysum[:n, :], in1=lnx[:n, 0, :], op=ALU.mult), reads=[b_y, b_lnx], writes=[b_y])
                        S.emit("pool", lambda e: e.tensor_tensor(out=ysum[:n, :], in0=ysum[:n, :], in1=lnx[:n, 1, :], op=ALU.add), reads=[b_y, b_lnx], writes=[b_y])
                        S.emit("pool", lambda e: e.tensor_tensor(out=ytmp[:n, :], in0=yin[2][:n, :], in1=yin[3][:n, :], op=ALU.add), reads=[b_yin, b_y], writes=[b_y])
                        S.emit("dve", lambda e: e.tensor_tensor(out=ysum[:n, :], in0=ysum[:n, :], in1=ytmp[:n, :], op=ALU.add), reads=[b_y], writes=[b_y])
                        for hf in range(2):
                            S.emit("dve", lambda e: e.tensor_tensor(out=zb[:n, hf * 512:(hf + 1) * 512], in0=ysum[:n, hf * 512:(hf + 1) * 512],
                                                                    in1=psD[gbank[hf]][:n, :], op=ALU.mult),
                                   reads=[b_y, bD[gbank[hf]]], writes=[b_zb])
                        for cp in range(4):
                            tb_ = nxt("T", 2)
                            for cc in range(2):
                                c = cp * 2 + cc
                                S.emit("pe", lambda e: e.transpose(psT[tb_][:, cc * 512:cc * 512 + n], zb[:n, c * 128:(c + 1) * 128], ident[:n, :n]),
                                       reads=[b_zb, b_ident], writes=[bT[tb_]])
                            for cc in range(2):
                                c = cp * 2 + cc
                                S.emit("act", lambda e: e.copy(out=zT[c][:, o:o + n], in_=psT[tb_][:, cc * 512:cc * 512 + n]),
                                       reads=[bT[tb_]], writes=[b_zT[c]])
                    for j in range(8):
                        wa = nxt("wbr", 4)
                        S.dma("sp", wbr[wa][:], WABS[j], writes=[b_wbr[wa]])
                        wr_ = nxt("wbr", 4)
                        S.dma("sp", wbr[wr_][:], WRBS[j], writes=[b_wbr[wr_]])
                        ga = nxt("gts", 4)
                        S.dma("sp", gts[ga][:, :T], GTs[s][j, :, t0:t0 + T], writes=[b_gts[ga]])
                        gb_ = nxt("gts", 4)
                        S.dma("sp", gts[gb_][:, :T], GTs[s][8 + j, :, t0:t0 + T], writes=[b_gts[gb_]])
                        b1, b2 = nxt("G", 4), nxt("G", 4)
                        for k in range(8):
                            S.emit("pe", lambda e: e.matmul(psG[b1][:, :T], wbr[wa][:, k, :], oT[k][:, :T], start=(k == 0), stop=(k == 7)),
                                   reads=[b_wbr[wa], b_oT[k]], writes=[bG[b1]])
                        for k in range(8):
                            S.emit("pe", lambda e: e.matmul(psG[b2][:, :T], wbr[wr_][:, k, :], zT[k][:, :T], start=(k == 0), stop=(k == 7)),
                                   reads=[b_wbr[wr_], b_zT[k]], writes=[bG[b2]])
                        S.emit("dve", lambda e: e.tensor_tensor(out=mtmp[:, :T], in0=gts[ga][:, :T], in1=psG[b1][:, :T], op=ALU.mult),
                               reads=[b_gts[ga], bG[b1]], writes=[b_mtmp])
                        S.emit("dve", lambda e: e.tensor_tensor(out=gts[gb_][:, :T], in0=gts[gb_][:, :T], in1=psG[b2][:, :T], op=ALU.mult),
                               reads=[b_gts[gb_], bG[b2]], writes=[b_gts[gb_]])
                        S.emit("pool", lambda e: e.tensor_tensor(out=mT[j][:, :T], in0=mtmp[:, :T], in1=gts[gb_][:, :T], op=ALU.add),
                               reads=[b_mtmp, b_gts[gb_]], writes=[b_mT[j]])
                    for hf in range(2):
                        S.dma("sp", wo[:], WOS[hf], writes=[b_wo])
                        for bi, (tg, o, n) in enumerate(blks):
                            bank_ = nxt("D", 2)
                            for k in range(8):
                                S.emit("pe", lambda e: e.matmul(psD[bank_][:n, :], mT[k][:, o:o + n], wo[:, k, :], start=(k == 0), stop=(k == 7)),
                                       reads=[b_mT[k], b_wo], writes=[bD[bank_]])
                            S.emit("dve", lambda e: e.tensor_tensor(out=hts[bi][:n, hf * 512:(hf + 1) * 512], in0=hts[bi][:n, hf * 512:(hf + 1) * 512],
                                                                    in1=psD[bank_][:n, :], op=ALU.add), reads=[bD[bank_], b_h[bi]], writes=[b_h[bi]])
                    xts = hts[:len(blks)]
                    bxs = b_h[:len(blks)]
                    ffn(P, 1, blks, xts, bxs, 2)
                    for bi, (tg, o, n) in enumerate(blks):
                        xt, bx = hts[bi], b_h[bi]
                        S.emit("pool", lambda e: e.memset(P["ss"][:, 0:1], 0.0), writes=[P["b_ss"]])
                        S.emit("act", lambda e: e.activation(out=P["junk"][:n, :], in_=xt[:n, :], func=AF.Square, accum_out=P["ss"][:n, 0:1]),
                               reads=[bx], writes=[P["b_ss"], P["b_junk"]])
                        S.emit("act", lambda e: e.activation(out=P["ss"][:n, 1:2], in_=P["ss"][:n, 0:1], func=AF.Sqrt, bias=epsT[:n, 0:1], scale=1.0 / D),
                               reads=[b_eps], writes=[P["b_ss"]])
                        S.emit("dve", lambda e: e.reciprocal(out=P["ss"][:n, 2:3], in_=P["ss"][:n, 1:2]), writes=[P["b_ss"]])
                        S.emit("act", lambda e: e.activation(out=ost[:n, :], in_=xt[:n, :], func=AF.Copy, scale=P["ss"][:n, 2:3]),
                               reads=[bx, P["b_ss"]], writes=[b_ost])
                        S.emit("dve", lambda e: e.tensor_tensor(out=ost[:n, :], in0=ost[:n, :], in1=fin_g[:n, :], op=ALU.mult), reads=[b_ost, b_fin], writes=[b_ost])
                        lo_t = max(tg, NMETA)
                        if tg + n > lo_t:
                            S.dma("pool", ydst(s)[lo_t - NMETA:tg + n - NMETA, :], ost[lo_t - tg:n, :], reads=[b_ost])
                S.barrier()

        for s in range(NS):
            if "D" in phases:
                phase_D(s)
    return nc


_NC_CACHE = {}


def _rel_bucket_np(rel):
    try:
        import jax
        import jax.numpy as jnp
        with jax.default_device(jax.devices("cpu")[0]):
            r = jnp.asarray(rel, dtype=jnp.int32)
            nb = 16
            max_exact = 8
            n = jnp.abs(r)
            nf = jnp.maximum(n, 1).astype(jnp.float32)
            large = max_exact + (jnp.log(nf / max_exact) / math.log(128 / max_exact) * (nb - max_exact)).astype(jnp.int32)
            large = jnp.minimum(large, nb - 1)
            out = (r > 0).astype(jnp.int32) * nb + jnp.where(n < max_exact, n, large)
            return np.asarray(out)
    except Exception:
        rel = np.asarray(rel, dtype=np.int32)
        n = np.abs(rel)
        nf = np.maximum(n, 1).astype(np.float32)
        large = 8 + (np.log(nf / np.float32(8)) / np.float32(math.log(16.0)) * np.float32(8)).astype(np.int32)
        large = np.minimum(large, 15)
        return (rel > 0).astype(np.int32) * 16 + np.where(n < 8, n, large)


def _consts():
    p = np.arange(128)[:, None]
    m = np.arange(896)[None, :]
    bkt = _rel_bucket_np(p - m + 384).astype(np.float32)
    t = np.arange(64)
    lo = (t[None, :] < t[:, None]).astype(np.float32)
    up = lo.T.copy()
    eye = np.eye(64, dtype=np.float32)

    def bd(mm):
        z = np.zeros((128, 128), np.float32)
        z[:64, :64] = mm
        z[64:, 64:] = mm
        return z

    masks = np.stack([bd(lo), bd(up), bd(lo + eye), bd(up + eye)], axis=1)
    istack = np.concatenate([eye, eye], axis=0)
    bones = bd(np.ones((64, 64), np.float32))
    return {"c_ident": np.eye(128, dtype=np.float32), "c_bk": bkt, "c_masks": masks, "c_istack": istack, "c_bones": bones}


def kernel(**inputs):
    xp = np.ascontiguousarray(inputs["x_prompt"], dtype=np.float32)
    xs = np.ascontiguousarray(inputs["x_sample"], dtype=np.float32)
    S0, S1 = xp.shape[1], xs.shape[1]
    ncores = 8
    key = (S0, S1)
    if key not in _NC_CACHE:
        _NC_CACHE[key] = build_nc(S0, S1)
    nc = _NC_CACHE[key]
    shared = {k: np.ascontiguousarray(v, dtype=np.float32) for k, v in inputs.items()
              if k not in ("x_prompt", "x_sample")}
    shared.update(_consts())
    in_maps = []
    for c in range(ncores):
        m = dict(shared)
        m["x_prompt"] = xp[2 * c:2 * c + 2]
        m["x_sample"] = xs[2 * c:2 * c + 2]
        in_maps.append(m)
    res = run_bass_kernel_spmd(nc, in_maps, core_ids=list(range(ncores)))
    yp = np.concatenate([r["y_prompt"] for r in res.results], axis=0)
    ys = np.concatenate([r["y_sample"] for r in res.results], axis=0)
    return (yp.astype(np.float32), ys.astype(np.float32))
```

```python
import math
from contextlib import ExitStack
import numpy as np
import concourse.bass as bass
import concourse.mybir as mybir
from concourse.bass_utils import run_bass_kernel_spmd

F32 = mybir.dt.float32
BF16 = mybir.dt.bfloat16
AF = mybir.ActivationFunctionType
ALU = mybir.AluOpType
AX = mybir.AxisListType

D = 1024
DFF = 2816
NF = DFF // 128
NMETA = 16
EPS = 1e-6
LNX_EPS = 64e-5
NIN = 8576
EP = 30000
KDMA = 8


class Buf:
    __slots__ = ("w", "weng", "r", "excl")

    def __init__(self, excl=False):
        self.w = None
        self.weng = None
        self.r = {}
        self.excl = excl


class Sched:
    def __init__(self, nc, st):
        self.nc = nc
        self.engs = {"pe": nc.tensor, "act": nc.scalar, "dve": nc.vector, "pool": nc.gpsimd, "sp": nc.sync}
        self.cnt = {e: 0 for e in self.engs}
        self.seen = {e: {} for e in self.engs}
        self.sems = {}
        self.st = st
        self.dqi = {"sp": 0, "pool": 0}
        self.last = {}

    def sem(self, key):
        if key not in self.sems:
            name = "s_" + "_".join(str(x) for x in key)
            self.sems[key] = self.st.enter_context(self.nc.semaphore(name))
        return self.sems[key]

    def _deps(self, e, reads, writes):
        deps = {}

        def add(tok):
            if tok is None:
                return
            k, v = tok
            if deps.get(k, 0) < v:
                deps[k] = v

        for b in reads:
            add(b.w)
            if b.excl:
                for src, tok in b.r.items():
                    if src != e:
                        add(tok)
        for b in writes:
            if not (e == "pe" and b.weng == "pe"):
                add(b.w)
            for tok in b.r.values():
                add(tok)
        return deps

    def _wait(self, e, deps):
        seen = self.seen[e]
        eng = self.engs[e]
        for k, v in deps.items():
            if seen.get(k, 0) < v:
                eng.wait_ge(self.sem(k), v)
                seen[k] = v

    def _mark(self, src, tok, e, reads, writes):
        self.last[src] = tok
        for b in reads:
            b.r[src] = tok
        for b in writes:
            b.w = tok
            b.weng = e
            b.r = {}

    def emit(self, e, fn, reads=(), writes=()):
        self._wait(e, self._deps(e, reads, writes))
        ins = fn(self.engs[e])
        c = self.cnt[e]
        self.cnt[e] += 1
        k = (e, c // EP)
        v = c % EP + 1
        ins.then_inc(self.sem(k), 1)
        self._mark(e, (k, v), e, reads, writes)

    def dma(self, q, out, in_, reads=(), writes=(), **kw):
        deps = self._deps("dma", reads, writes)
        i = self.dqi[q]
        self.dqi[q] += 1
        slot = i % KDMA
        k = ("d", q, slot)
        v = 16 * (i // KDMA + 1)
        if i >= KDMA:
            if deps.get(k, 0) < v - 16:
                deps[k] = v - 16
        self._wait(q, deps)
        ins = self.engs[q].dma_start(out=out, in_=in_, **kw)
        ins.then_inc(self.sem(k), 16)
        self._mark(k, (k, v), "dma", reads, writes)

    def barrier(self):
        deps = {}
        for tok in self.last.values():
            k, v = tok
            if deps.get(k, 0) < v:
                deps[k] = v
        for e in self.engs:
            self._wait(e, deps)


def blocks_of(t0, T):
    out = []
    o = 0
    while o < T:
        n = min(128, T - o)
        out.append((t0 + o, o, n))
        o += n
    return out


def tiles_of(L, TT=512):
    out = []
    t = 0
    while t < L:
        T = min(TT, L - t)
        out.append((t, T))
        t += T
    return out


def build_nc(S0, S1, debug=False, phases="ABCD"):
    nc = bass.Bass("TRN2", target_bir_lowering=False)
    seqS = [S0, S0, S1, S1]
    seqL = [s + NMETA for s in seqS]
    NS = len(seqS)

    def din(name, shape):
        return nc.dram_tensor(name, list(shape), F32, kind="ExternalInput").ap()

    def dscr(name, shape, dt=F32):
        kind = "ExternalOutput" if (debug and name.startswith("dbg_")) else "Internal"
        return nc.dram_tensor(name, list(shape), dt, kind=kind).ap()

    x_in = [din("x_prompt", (2, S0, D)), din("x_sample", (2, S1, D))]
    y_out = [nc.dram_tensor("y_prompt", [2, S0, D], F32, kind="ExternalOutput").ap(),
             nc.dram_tensor("y_sample", [2, S1, D], F32, kind="ExternalOutput").ap()]

    def xsrc(s):
        return x_in[s // 2][s % 2]

    def ydst(s):
        return y_out[s // 2][s % 2]

    meta = din("meta_tokens", (NMETA, D))
    rel_bias = din("rel_bias", (32, 8))
    W = {}
    for nm, shp in [("ffn1_norm", (1, D)), ("ffn1_w_gate", (1, D, DFF)), ("ffn1_w_up", (1, D, DFF)),
                    ("ffn1_w_down", (1, DFF, D)), ("mix_norm", (1, D)), ("w_in", (1, D, NIN)),
                    ("attn_lambda_q1", (1, 64)), ("attn_lambda_k1", (1, 64)), ("attn_lambda_q2", (1, 64)),
                    ("attn_lambda_k2", (1, 64)), ("attn_subln", (1, 128)), ("w_attn_branch", (1, D, D)),
                    ("rw_mu_prev", (1, 3456)), ("rw_mu_next", (1, 3456)), ("rw_w0", (1, 2, D)),
                    ("rw_w2", (1, 2, 64, D)), ("rw_a0", (1, 2, D)), ("rw_a2", (1, 2, 64, D)),
                    ("rw_g2", (1, 128, D)), ("rw_k_k", (1, D)), ("rw_k_a", (1, D)), ("rw_r_k", (1, 16, 64)),
                    ("rw_lnx_w", (1, D)), ("rw_lnx_b", (1, D)), ("w_rw_branch", (1, D, D)), ("w_out", (1, D, D)),
                    ("ffn2_norm", (1, D)), ("ffn2_w_gate", (1, D, DFF)), ("ffn2_w_up", (1, D, DFF)),
                    ("ffn2_w_down", (1, DFF, D)), ("final_norm", (D,))]:
        W[nm] = din(nm, shp)
    c_ident = din("c_ident", (128, 128))
    c_bk = din("c_bk", (128, 896))
    c_masks = din("c_masks", (128, 4, 128))
    c_istack = din("c_istack", (128, 64))
    c_bones = din("c_bones", (128, 128))

    WGU = [dscr(f"wgu{i}", (NF, 128, 2, 8, 128), BF16) for i in range(2)]
    WD = [dscr(f"wd{i}", (2, 128, NF, 512), BF16) for i in range(2)]
    WINS = dscr("wins", (67, 128, 8, 128), BF16)
    WVS = dscr("wvs", (2, 128, 8, 512), BF16)
    WABS = dscr("wabs", (8, 128, 8, 128), BF16)
    WRBS = dscr("wrbs", (8, 128, 8, 128), BF16)
    WOS = dscr("wos", (2, 128, 8, 512), BF16)
    Hs = [dscr(f"dbg_h{s}", (seqL[s], D)) for s in range(NS)]
    QTs = [dscr(f"dbg_qt{s}", (8, 128, seqL[s]), BF16) for s in range(NS)]
    KTs = [dscr(f"dbg_kt{s}", (8, 128, seqL[s]), BF16) for s in range(NS)]
    Vs = [dscr(f"dbg_v{s}", (seqL[s], D), BF16) for s in range(NS)]
    RWs = [dscr(f"dbg_rw{s}", (27, 128, seqL[s])) for s in range(NS)]
    GTs = [dscr(f"dbg_gt{s}", (16, 128, seqL[s])) for s in range(NS)]
    OTs = [dscr(f"dbg_ot{s}", (D, seqL[s]), BF16) for s in range(NS)]
    YSs = [[dscr(f"dbg_y{d}{s}", (seqL[s], D)) for s in range(NS)] for d in range(2)]
    BNs = [[dscr(f"dbg_bn{d}{s}", (seqL[s], D)) for s in range(NS)] for d in range(2)]

    with ExitStack() as st:
        S = Sched(nc, st)

        uid = [0]

        def sb(name, shape, dt=F32, stack=st):
            uid[0] += 1
            return stack.enter_context(nc.sbuf_tensor(f"{name}_{uid[0]}", list(shape), dt))

        psT = [st.enter_context(nc.psum_tensor(f"psT{i}", [128, 1024], BF16)) for i in range(2)]
        psG = [st.enter_context(nc.psum_tensor(f"psG{i}", [128, 512], F32)) for i in range(4)]
        psD = [st.enter_context(nc.psum_tensor(f"psD{i}", [128, 512], F32)) for i in range(2)]
        bT = [Buf(True) for _ in range(2)]
        bG = [Buf(True) for _ in range(4)]
        bD = [Buf(True) for _ in range(2)]
        rr = {"T": 0, "G": 0, "D": 0}

        def nxt(kind, n):
            i = rr[kind]
            rr[kind] = (i + 1) % n
            return i

        ident = sb("ident", (128, 128), BF16)
        b_ident = Buf()
        S.dma("pool", ident[:], c_ident[:, :], writes=[b_ident])
        epsT = sb("epsT", (128, 2))
        b_eps = Buf()
        S.emit("pool", lambda e: e.memset(epsT[:, 0:1], EPS), writes=[b_eps])
        S.emit("pool", lambda e: e.memset(epsT[:, 1:2], LNX_EPS), writes=[b_eps])
        gains = sb("gains", (128, 4, 8))
        b_gains = Buf()
        for i, nm in enumerate(["ffn1_norm", "mix_norm", "ffn2_norm"]):
            S.dma("sp", gains[:, i, :], W[nm][0].rearrange("(c p) -> p c", p=128), writes=[b_gains],
                  allow_slow_non_contiguous=True)
        fin_g = sb("fin_g", (128, D))
        b_fin = Buf()
        S.dma("sp", fin_g[:], W["final_norm"].partition_broadcast(128), writes=[b_fin])

        tb = sb("tb", (128, 256))
        b_tb = Buf()
        S.dma("sp", tb[:], rel_bias.rearrange("b h -> (b h)").partition_broadcast(128), writes=[b_tb])
        lamv = sb("lamv", (128, 4, 64))
        b_lamv = Buf()
        for i, nm in enumerate(["attn_lambda_q1", "attn_lambda_k1", "attn_lambda_q2", "attn_lambda_k2"]):
            S.dma("sp", lamv[:, i, :], W[nm][0].partition_broadcast(128), writes=[b_lamv])
        lams = sb("lams", (128, 8))
        b_lams = Buf()
        for i in range(2):
            S.emit("dve", lambda e: e.tensor_tensor(out=lamv[:, 2 * i, :], in0=lamv[:, 2 * i, :], in1=lamv[:, 2 * i + 1, :],
                                                    op=ALU.mult), reads=[b_lamv], writes=[b_lamv])
            S.emit("dve", lambda e: e.reduce_sum(out=lams[:, i:i + 1], in_=lamv[:, 2 * i, :], axis=AX.X),
                   reads=[b_lamv], writes=[b_lams])
        S.emit("act", lambda e: e.activation(out=lams[:, 2:4], in_=lams[:, 0:2], func=AF.Exp), reads=[b_lams], writes=[b_lams])
        S.emit("dve", lambda e: e.tensor_tensor(out=lams[:, 4:5], in0=lams[:, 3:4], in1=lams[:, 2:3], op=ALU.subtract),
               reads=[b_lams], writes=[b_lams])
        LAM_INIT = 0.8 - 0.6 * math.exp(-0.3 * 0)
        S.emit("dve", lambda e: e.tensor_scalar(out=lams[:, 5:6], in0=lams[:, 4:5], scalar1=-LAM_INIT, scalar2=None, op0=ALU.add),
               reads=[b_lams], writes=[b_lams])
        gsub = sb("gsub", (128, 1))
        b_gsub = Buf()
        S.dma("sp", gsub[:], W["attn_subln"][0].rearrange("(p o) -> p o", o=1), writes=[b_gsub])
        S.emit("pool", lambda e: e.tensor_scalar(out=gsub[:], in0=gsub[:], scalar1=1.0 - LAM_INIT, scalar2=None, op0=ALU.mult),
               reads=[b_gsub], writes=[b_gsub])

        masks = sb("masks", (128, 4, 128))
        b_masks = Buf()
        S.dma("sp", masks[:], c_masks[:, :, :], writes=[b_masks])
        m12 = [sb(f"m12_{d}", (128, 256)) for d in range(2)]
        m345 = [sb(f"m345_{d}", (128, 384)) for d in range(2)]
        b_mm = Buf()
        for d in range(2):
            lo, up, loi, upi = (0, 1, 2, 3) if d == 0 else (1, 0, 3, 2)
            for (dst, off, mi) in [(m12[d], 0, lo), (m12[d], 128, up), (m345[d], 0, up), (m345[d], 128, upi), (m345[d], 256, upi)]:
                S.emit("pool", lambda e: e.tensor_copy(out=dst[:, off:off + 128], in_=masks[:, mi, :]), reads=[b_masks], writes=[b_mm])
        istack = sb("istack", (128, 64), BF16)
        bones = sb("bones", (128, 128), BF16)
        onesb = sb("onesb", (128, 1), BF16)
        b_rc = Buf()
        S.dma("pool", istack[:], c_istack[:, :], writes=[b_rc])
        S.dma("pool", bones[:], c_bones[:, :], writes=[b_rc])
        S.emit("pool", lambda e: e.memset(onesb[:], 1.0), writes=[b_rc])
        mus = sb("mus", (128, 3, 27))
        b_mus = Buf()
        for i, nm in enumerate(["rw_mu_prev", "rw_mu_next"]):
            S.dma("sp", mus[:, i, :], W[nm][0].rearrange("(c p) -> p c", p=128), writes=[b_mus], allow_slow_non_contiguous=True)
        S.emit("dve", lambda e: e.tensor_tensor(out=mus[:, 2, :], in0=mus[:, 0, :], in1=mus[:, 1, :], op=ALU.add), reads=[b_mus], writes=[b_mus])
        S.emit("dve", lambda e: e.tensor_scalar(out=mus[:, 2, :], in0=mus[:, 2, :], scalar1=-1.0, scalar2=1.0, op0=ALU.mult, op1=ALU.add),
               reads=[b_mus], writes=[b_mus])
        rwp = sb("rwp", (128, 7, 8))
        b_rwp = Buf()
        for i, src in enumerate([W["rw_w0"][0, 0], W["rw_w0"][0, 1], W["rw_a0"][0, 0], W["rw_a0"][0, 1], W["rw_k_k"][0], W["rw_k_a"][0],
                                 W["rw_r_k"][0].rearrange("h n -> (h n)")]):
            S.dma("sp", rwp[:, i, :], src.rearrange("(c p) -> p c", p=128), writes=[b_rwp], allow_slow_non_contiguous=True)
        w2sb = sb("w2sb", (128, D), BF16)
        a2sb = sb("a2sb", (128, D), BF16)
        g2sb = sb("g2sb", (128, D), BF16)
        b_w2 = Buf()
        S.dma("pool", w2sb[:], W["rw_w2"][0].rearrange("d r c -> (d r) c"), writes=[b_w2])
        S.dma("pool", a2sb[:], W["rw_a2"][0].rearrange("d r c -> (d r) c"), writes=[b_w2])
        S.dma("pool", g2sb[:], W["rw_g2"][0], writes=[b_w2])

        def prep_weights():
            for i, pre in enumerate(["ffn1", "ffn2"]):
                for which, nm in enumerate(["w_gate", "w_up"]):
                    src = W[f"{pre}_{nm}"][0].rearrange("(k p) (f j) -> f p k j", p=128, j=128)
                    for f in range(NF):
                        S.dma("pool", WGU[i][f, :, which, :, :], src[f])
                src = W[f"{pre}_w_down"][0].rearrange("(f p) (h j) -> h p f j", p=128, j=512)
                for h in range(2):
                    for f0 in range(0, NF, 11):
                        S.dma("pool", WD[i][h, :, f0:f0 + 11, :], src[h][:, f0:f0 + 11, :])
            src = W["w_in"][0].rearrange("(k p) (c j) -> c p k j", p=128, j=128)
            for c in range(67):
                if 16 <= c < 24:
                    continue
                S.dma("pool", WINS[c], src[c])
            src = W["w_in"][0][:, 2048:3072].rearrange("(k p) (h j) -> h p k j", p=128, j=512)
            for h in range(2):
                S.dma("pool", WVS[h], src[h])
            for (dst, nm) in [(WABS, "w_attn_branch"), (WRBS, "w_rw_branch")]:
                src = W[nm][0].rearrange("(k p) (c j) -> c p k j", p=128, j=128)
                for c in range(8):
                    S.dma("pool", dst[c], src[c])
            src = W["w_out"][0].rearrange("(k p) (h j) -> h p k j", p=128, j=512)
            for h in range(2):
                S.dma("pool", WOS[h], src[h])

        prep_weights()
        S.barrier()

        def load_x_block(s, t0, n, xt, bx):
            if t0 == 0:
                S.dma("sp", xt[0:NMETA, :], meta[:, :], writes=[bx])
                S.dma("sp", xt[NMETA:n, :], xsrc(s)[0:n - NMETA, :], writes=[bx])
            else:
                S.dma("sp", xt[0:n, :], xsrc(s)[t0 - NMETA:t0 - NMETA + n, :], writes=[bx])

        def rmsnorm_T(P, blks, xts, bxs, gi, outT, boutT):
            for (tg, o, n), xt, bx in zip(blks, xts, bxs):
                S.emit("pool", lambda e: e.memset(P["ss"][:, 0:1], 0.0), writes=[P["b_ss"]])
                S.emit("act", lambda e: e.activation(out=P["junk"][:n, :], in_=xt[:n, :], func=AF.Square,
                                                     accum_out=P["ss"][:n, 0:1]),
                       reads=[bx], writes=[P["b_ss"], P["b_junk"]])
                S.emit("act", lambda e: e.activation(out=P["ss"][:n, 1:2], in_=P["ss"][:n, 0:1], func=AF.Sqrt,
                                                     bias=epsT[:n, 0:1], scale=1.0 / D),
                       reads=[b_eps], writes=[P["b_ss"]])
                S.emit("dve", lambda e: e.reciprocal(out=P["ss"][:n, 2:3], in_=P["ss"][:n, 1:2]), writes=[P["b_ss"]])
                bi = o // 128
                S.emit("act", lambda e: e.activation(out=P["xn"][bi][:n, :], in_=xt[:n, :], func=AF.Copy,
                                                     scale=P["ss"][:n, 2:3]),
                       reads=[bx, P["b_ss"]], writes=[P["b_xn"][bi]])
            T = sum(b[2] for b in blks)
            for cp in range(4):
                bank = nxt("T", 2)
                for cc in range(2):
                    c = cp * 2 + cc
                    for (tg, o, n) in blks:
                        bi = o // 128
                        S.emit("pe", lambda e: e.transpose(psT[bank][:, cc * 512 + o:cc * 512 + o + n],
                                                           P["xn"][bi][:n, c * 128:(c + 1) * 128], ident[:n, :n]),
                               reads=[P["b_xn"][bi], b_ident], writes=[bT[bank]])
                for cc in range(2):
                    c = cp * 2 + cc
                    S.emit("dve", lambda e: e.tensor_scalar(out=outT[c][:, :T], in0=psT[bank][:, cc * 512:cc * 512 + T],
                                                            scalar1=gains[:, gi, c:c + 1], scalar2=None, op0=ALU.mult),
                           reads=[bT[bank], b_gains], writes=[boutT[c]])

        def ffn(P, fi, blks, xts, bxs, gi):
            T = sum(b[2] for b in blks)
            rmsnorm_T(P, blks, xts, bxs, gi, P["xT"], P["b_xT"])
            for f in range(NF):
                ws = nxt("wgu", len(P["wgu"]))
                S.dma("sp", P["wgu"][ws][:], WGU[fi][f], writes=[P["b_wgu"][ws]])
                banks = [nxt("G", 4), nxt("G", 4)]
                for which in range(2):
                    for k in range(8):
                        S.emit("pe", lambda e: e.matmul(psG[banks[which]][:, :T], P["wgu"][ws][:, which, k, :],
                                                        P["xT"][k][:, :T], start=(k == 0), stop=(k == 7)),
                               reads=[P["b_wgu"][ws], P["b_xT"][k]], writes=[bG[banks[which]]])
                sgi = nxt("sg", 2)
                S.emit("act", lambda e: e.activation(out=P["sg"][sgi][:, :T], in_=psG[banks[0]][:, :T], func=AF.Silu),
                       reads=[bG[banks[0]]], writes=[P["b_sg"][sgi]])
                S.emit("dve", lambda e: e.tensor_tensor(out=P["act"][f][:, :T], in0=P["sg"][sgi][:, :T],
                                                        in1=psG[banks[1]][:, :T], op=ALU.mult),
                       reads=[P["b_sg"][sgi], bG[banks[1]]], writes=[P["b_act"][f]])
            for h in range(2):
                S.dma("sp", P["wd"][:], WD[fi][h], writes=[P["b_wd"]])
                for (tg, o, n), xt, bx in zip(blks, xts, bxs):
                    bank = nxt("D", 2)
                    for f in range(NF):
                        S.emit("pe", lambda e: e.matmul(psD[bank][:n, :], P["act"][f][:, o:o + n], P["wd"][:, f, :],
                                                        start=(f == 0), stop=(f == NF - 1)),
                               reads=[P["b_act"][f], P["b_wd"]], writes=[bD[bank]])
                    S.emit("dve", lambda e: e.scalar_tensor_tensor(out=xt[:n, h * 512:(h + 1) * 512], in0=psD[bank][:n, :],
                                                                   scalar=0.5, in1=xt[:n, h * 512:(h + 1) * 512],
                                                                   op0=ALU.mult, op1=ALU.add),
                           reads=[bD[bank], bx], writes=[bx])

        def phase_A(s):
            L = seqL[s]
            with ExitStack() as pst:
                P = {}
                P["ss"] = sb("a_ss", (128, 4), stack=pst)
                P["b_ss"] = Buf()
                P["junk"] = sb("a_junk", (128, D), BF16, stack=pst)
                P["b_junk"] = Buf()
                P["xn"] = [sb(f"a_xn{i}", (128, D), BF16, stack=pst) for i in range(4)]
                P["b_xn"] = [Buf() for _ in range(4)]
                P["xT"] = [sb(f"a_xT{i}", (128, 512), BF16, stack=pst) for i in range(8)]
                P["b_xT"] = [Buf() for _ in range(8)]
                P["uT"] = [sb(f"a_uT{i}", (128, 512), BF16, stack=pst) for i in range(8)]
                P["b_uT"] = [Buf() for _ in range(8)]
                P["wgu"] = [sb(f"a_wgu{i}", (128, 2, 8, 128), BF16, stack=pst) for i in range(4)]
                P["b_wgu"] = [Buf() for _ in range(4)]
                P["sg"] = [sb(f"a_sg{i}", (128, 512), stack=pst) for i in range(2)]
                P["b_sg"] = [Buf() for _ in range(2)]
                P["act"] = [sb(f"a_act{i}", (128, 512), BF16, stack=pst) for i in range(NF)]
                P["b_act"] = [Buf() for _ in range(NF)]
                P["wd"] = sb("a_wd", (128, NF, 512), BF16, stack=pst)
                P["b_wd"] = Buf()
                rr["wgu"] = 0
                rr["sg"] = 0
                xsets = [[sb(f"a_x{j}_{i}", (128, D), stack=pst) for i in range(4)] for j in range(2)]
                bxsets = [[Buf() for _ in range(4)] for j in range(2)]
                win = [sb(f"a_win{i}", (128, 8, 128), BF16, stack=pst) for i in range(5)]
                b_win = [Buf() for _ in range(5)]
                wv = sb("a_wv", (128, 8, 512), BF16, stack=pst)
                b_wv = Buf()
                stf = [sb(f"a_stf{i}", (128, 512), stack=pst) for i in range(3)]
                b_stf = [Buf() for _ in range(3)]
                stb = [sb(f"a_stb{i}", (128, 512), BF16, stack=pst) for i in range(4)]
                b_stb = [Buf() for _ in range(4)]
                rr["win"] = 0
                rr["stf"] = 0
                rr["stb"] = 0
                tlist = tiles_of(L)

                def load_tile(ti):
                    t0_, T_ = tlist[ti]
                    bl = blocks_of(t0_, T_)
                    for (tg, o, n), xt, bx in zip(bl, xsets[ti % 2], bxsets[ti % 2]):
                        load_x_block(s, tg, n, xt, bx)

                load_tile(0)
                for ti, (t0, T) in enumerate(tlist):
                    blks = blocks_of(t0, T)
                    xts = xsets[ti % 2][:len(blks)]
                    bxs = bxsets[ti % 2][:len(blks)]
                    if ti + 1 < len(tlist):
                        load_tile(ti + 1)
                    ffn(P, 0, blks, xts, bxs, 0)
                    for (tg, o, n), xt, bx in zip(blks, xts, bxs):
                        S.dma("pool", Hs[s][tg:tg + n, :], xt[:n, :], reads=[bx])
                    rmsnorm_T(P, blks, xts, bxs, 1, P["uT"], P["b_uT"])
                    for c in list(range(0, 16)) + list(range(24, 67)):
                        ws = nxt("win", 5)
                        S.dma("sp", win[ws][:], WINS[c], writes=[b_win[ws]])
                        bank = nxt("G", 4)
                        for k in range(8):
                            S.emit("pe", lambda e: e.matmul(psG[bank][:, :T], win[ws][:, k, :], P["uT"][k][:, :T],
                                                            start=(k == 0), stop=(k == 7)),
                                   reads=[b_win[ws], P["b_uT"][k]], writes=[bG[bank]])
                        if c < 16:
                            si = nxt("stb", 4)
                            S.emit("act", lambda e: e.copy(out=stb[si][:, :T], in_=psG[bank][:, :T]),
                                   reads=[bG[bank]], writes=[b_stb[si]])
                            dst = (QTs if c < 8 else KTs)[s][c % 8, :, t0:t0 + T]
                            S.dma("pool", dst, stb[si][:, :T], reads=[b_stb[si]])
                        else:
                            si = nxt("stf", 3)
                            if c < 51:
                                S.emit("act", lambda e: e.copy(out=stf[si][:, :T], in_=psG[bank][:, :T]),
                                       reads=[bG[bank]], writes=[b_stf[si]])
                                dst = RWs[s][c - 24, :, t0:t0 + T]
                            else:
                                S.emit("act", lambda e: e.activation(out=stf[si][:, :T], in_=psG[bank][:, :T],
                                                                     func=AF.Sigmoid),
                                       reads=[bG[bank]], writes=[b_stf[si]])
                                dst = GTs[s][c - 51, :, t0:t0 + T]
                            S.dma("pool", dst, stf[si][:, :T], reads=[b_stf[si]])
                    for h in range(2):
                        S.dma("sp", wv[:], WVS[h], writes=[b_wv])
                        for (tg, o, n) in blks:
                            bank = nxt("D", 2)
                            for k in range(8):
                                S.emit("pe", lambda e: e.matmul(psD[bank][:n, :], P["uT"][k][:, o:o + n], wv[:, k, :],
                                                                start=(k == 0), stop=(k == 7)),
                                       reads=[P["b_uT"][k], b_wv], writes=[bD[bank]])
                            si = nxt("stb", 4)
                            S.emit("act", lambda e: e.copy(out=stb[si][:n, :], in_=psD[bank][:n, :]),
                                   reads=[bD[bank]], writes=[b_stb[si]])
                            S.dma("pool", Vs[s][tg:tg + n, h * 512:(h + 1) * 512], stb[si][:n, :], reads=[b_stb[si]])
                S.barrier()


        def phase_B(s):
            L = seqL[s]
            TQ = 384
            qtiles = tiles_of(L, TQ)
            kblocks = tiles_of(L, 128)
            nkb = len(kblocks)
            nfull = L // 128
            ntail = L - nfull * 128
            accb = [psD[0][:], psD[1][:], psT[1][:].bitcast(F32)]
            bacc = [bD[0], bD[1], bT[1]]
            with ExitStack() as pst:
                kt = [sb("b_kt", (128, L), BF16, pst) for _ in range(2)]
                qt = [sb("b_qt", (128, L), BF16, pst) for _ in range(2)]
                va = [sb("b_va", (128, nkb, 129), BF16, pst) for _ in range(2)]
                b_kqv = [Buf(), Buf()]
                PT = [sb("b_PT", (128, TQ), BF16, pst) for _ in range(4)]
                b_PT = [Buf() for _ in range(4)]
                tmpS = [sb("b_tmpS", (128, TQ), F32, pst) for _ in range(2)]
                b_tmpS = [Buf() for _ in range(2)]
                rd = sb("b_rd", (128, 8), F32, pst)
                b_rd = Buf()
                o1 = sb("b_o1", (128, 128), F32, pst)
                oo = sb("b_oo", (128, 128), F32, pst)
                junk = sb("b_junk", (128, 128), F32, pst)
                on = sb("b_on", (128, 128), BF16, pst)
                b_o = Buf()
                ost = [sb("b_ost", (128, TQ), BF16, pst) for _ in range(2)]
                b_ost = [Buf(), Buf()]
                rr["PT"] = 0
                rr["tmpS"] = 0
                rr["ost"] = 0
                for i in range(2):
                    S.emit("pool", lambda e: e.memset(va[i][:, :, 128:129], 1.0), writes=[b_kqv[i]])
                for h in range(8):
                    i = h % 2
                    S.dma("sp", kt[i][:], KTs[s][h], writes=[b_kqv[i]])
                    S.dma("sp", qt[i][:], QTs[s][h], writes=[b_kqv[i]])
                    if nfull:
                        S.dma("sp", va[i][:, 0:nfull, 0:128],
                              Vs[s][0:nfull * 128, h * 128:(h + 1) * 128].rearrange("(kb p) c -> p kb c", p=128),
                              writes=[b_kqv[i]])
                    if ntail:
                        S.dma("sp", va[i][0:ntail, nfull, 0:128], Vs[s][nfull * 128:L, h * 128:(h + 1) * 128],
                              writes=[b_kqv[i]])
                    for (qt0, TQn) in qtiles:
                        subs = blocks_of(qt0, TQn)

                        def qk(kbi):
                            k0, nk = kblocks[kbi]
                            banks = [nxt("G", 4), nxt("G", 4)]
                            for c in range(2):
                                S.emit("pe", lambda e: e.matmul(psG[banks[c]][:nk, :TQn], kt[i][c * 64:(c + 1) * 64, k0:k0 + nk],
                                                                qt[i][c * 64:(c + 1) * 64, qt0:qt0 + TQn], start=True, stop=True),
                                       reads=[b_kqv[i]], writes=[bG[banks[c]]])
                            return banks

                        nb = qk(0)
                        for kbi, (k0, nk) in enumerate(kblocks):
                            banks = nb
                            if kbi + 1 < nkb:
                                nb = qk(kbi + 1)
                            d = k0 - qt0
                            relmin = d - (TQn - 1)
                            relmax = d + nk - 1
                            near = not (relmin >= 91 or relmax <= -91)
                            for c in range(2):
                                pi = nxt("PT", 4)
                                if near:
                                    ti = nxt("tmpS", 2)
                                    m0 = 384 - d
                                    S.emit("dve", lambda e: e.scalar_tensor_tensor(
                                        out=tmpS[ti][:nk, :TQn], in0=psG[banks[c]][:nk, :TQn], scalar=0.125,
                                        in1=Gs[h][:nk, m0:m0 + TQn], op0=ALU.mult, op1=ALU.add),
                                        reads=[bG[banks[c]], b_Gs[h]], writes=[b_tmpS[ti]])
                                    S.emit("act", lambda e: e.activation(out=PT[pi][:nk, :TQn], in_=tmpS[ti][:nk, :TQn], func=AF.Exp),
                                           reads=[b_tmpS[ti]], writes=[b_PT[pi]])
                                else:
                                    col = (31 if relmin >= 91 else 15) * 8 + h
                                    S.emit("act", lambda e: e.activation(out=PT[pi][:nk, :TQn], in_=psG[banks[c]][:nk, :TQn], func=AF.Exp,
                                                                         scale=0.125, bias=tb[:nk, col:col + 1]),
                                           reads=[bG[banks[c]], b_tb], writes=[b_PT[pi]])
                                for si, (qg, o, nq) in enumerate(subs):
                                    first = (kbi == 0 and c == 0)
                                    last = (kbi == nkb - 1 and c == 1)
                                    S.emit("pe", lambda e: e.matmul(accb[si][:nq, c * 129:(c + 1) * 129], PT[pi][:nk, o:o + nq],
                                                                    va[i][:nk, kbi, :], start=first, stop=last),
                                           reads=[b_PT[pi], b_kqv[i]], writes=[bacc[si]])
                        oi = nxt("ost", 2)
                        for si, (qg, o, nq) in enumerate(subs):
                            A = accb[si]
                            S.emit("dve", lambda e: e.reciprocal(out=rd[:nq, 0:1], in_=A[:nq, 128:129]), reads=[bacc[si]], writes=[b_rd])
                            S.emit("dve", lambda e: e.reciprocal(out=rd[:nq, 1:2], in_=A[:nq, 257:258]), reads=[bacc[si]], writes=[b_rd])
                            S.emit("dve", lambda e: e.tensor_scalar(out=rd[:nq, 2:3], in0=rd[:nq, 1:2], scalar1=lams[:nq, 5:6],
                                                                    scalar2=None, op0=ALU.mult), reads=[b_rd, b_lams], writes=[b_rd])
                            S.emit("act", lambda e: e.activation(out=o1[:nq, :], in_=A[:nq, 0:128], func=AF.Copy, scale=rd[:nq, 0:1]),
                                   reads=[bacc[si], b_rd], writes=[b_o])
                            S.emit("dve", lambda e: e.scalar_tensor_tensor(out=oo[:nq, :], in0=A[:nq, 129:257], scalar=rd[:nq, 2:3],
                                                                           in1=o1[:nq, :], op0=ALU.mult, op1=ALU.add),
                                   reads=[bacc[si], b_rd, b_o], writes=[b_o])
                            S.emit("pool", lambda e: e.memset(rd[:, 3:4], 0.0), writes=[b_rd])
                            S.emit("act", lambda e: e.activation(out=junk[:nq, :], in_=oo[:nq, :], func=AF.Square, accum_out=rd[:nq, 3:4]),
                                   reads=[b_o], writes=[b_rd, b_o])
                            S.emit("act", lambda e: e.activation(out=rd[:nq, 4:5], in_=rd[:nq, 3:4], func=AF.Sqrt, bias=epsT[:nq, 0:1],
                                                                 scale=1.0 / 128), reads=[b_rd, b_eps], writes=[b_rd])
                            S.emit("dve", lambda e: e.reciprocal(out=rd[:nq, 5:6], in_=rd[:nq, 4:5]), reads=[b_rd], writes=[b_rd])
                            S.emit("act", lambda e: e.activation(out=on[:nq, :], in_=oo[:nq, :], func=AF.Copy, scale=rd[:nq, 5:6]),
                                   reads=[b_rd, b_o], writes=[b_o])
                            S.emit("pe", lambda e: e.transpose(psT[0][:, o:o + nq], on[:nq, :], ident[:nq, :nq]),
                                   reads=[b_o, b_ident], writes=[bT[0]])
                            S.emit("dve", lambda e: e.tensor_scalar(out=ost[oi][:, o:o + nq], in0=psT[0][:, o:o + nq], scalar1=gsub[:, 0:1],
                                                                    scalar2=None, op0=ALU.mult),
                                   reads=[bT[0], b_gsub], writes=[b_ost[oi]])
                        S.dma("pool", OTs[s][h * 128:(h + 1) * 128, qt0:qt0 + TQn], ost[oi][:, :TQn], reads=[b_ost[oi]])
                S.barrier()

        with ExitStack() as gst:
            bk = sb("bk", (128, 896), F32, gst)
            b_bk = Buf()
            S.dma("sp", bk[:], c_bk[:, :], writes=[b_bk])
            Gs = [sb(f"G{h}", (128, 896), F32, gst) for h in range(8)]
            b_Gs = [Buf() for _ in range(8)]
            gm = [sb("gm", (128, 896), F32, gst) for _ in range(1)]
            b_gm = [Buf()]
            if "A" in phases:
                phase_A(0)
            for h in range(8):
                S.emit("pool", lambda e: e.memset(Gs[h][:], 0.0), writes=[b_Gs[h]])
            for b in range(32):
                mi = 0
                S.emit("pool", lambda e: e.tensor_scalar(out=gm[mi][:], in0=bk[:], scalar1=float(b), scalar2=None, op0=ALU.is_equal),
                       reads=[b_bk], writes=[b_gm[mi]])
                for h in range(8):
                    S.emit("dve", lambda e: e.scalar_tensor_tensor(out=Gs[h][:], in0=gm[mi][:], scalar=tb[:, b * 8 + h:b * 8 + h + 1], in1=Gs[h][:],
                                                                   op0=ALU.mult, op1=ALU.add), reads=[b_gm[mi], b_tb], writes=[b_Gs[h]])

            for s in range(1, NS):
                if "A" in phases:
                    phase_A(s)
            for s in range(NS):
                if "B" in phases:
                    phase_B(s)
            S.barrier()

        def phase_C(s):
            L = seqL[s]
            SEG = 128
            Lp = ((L + 63) // 64) * 64
            segs = tiles_of(Lp, SEG)
            nseg = len(segs)
            allb = [(psG[i][:], bG[i]) for i in range(4)] + [(psD[i][:], bD[i]) for i in range(2)] + [(psT[1][:].bitcast(F32), bT[1])]
            rr["ps"] = 0

            def bank():
                i = nxt("ps", len(allb))
                return allb[i]

            EXC = 2.0 ** 0
            with ExitStack() as pst:
                NG = 8
                def mk(name, shape, dt=F32, n=1):
                    return [sb(name, shape, dt, pst) for _ in range(n)]
                nchm = SEG // 64
                RAW = [sb("c_RAW", (128, NG, SEG + 2), F32, pst) for _ in range(3)]
                b_RAW = [Buf() for _ in range(3)]
                rawd = [mk("c_rawd", (128, SEG + 2), F32, 2) for _ in range(2)]
                b_raw = [Buf(), Buf()]
                NOPS = 6
                EXP = [[sb("c_EXP", (128, NG, nchm, 128), BF16, pst) for o in range(NOPS)] for _ in range(2)]
                PEND = [sb("c_PEND", (128, NG, nchm), F32, pst) for _ in range(2)]
                RPEND = sb("c_RPEND", (128, NG, nchm), F32, pst)
                b_rpend = Buf()
                b_expg = [Buf(), Buf()]
                exp_ = [[[EXP[gb][o][:, u] for o in range(NOPS)] for u in range(NG)] for gb in range(2)]
                b_exp = [[b_expg[gb] for u in range(NG)] for gb in range(2)]
                pend = [[PEND[gb][:, u] for u in range(NG)] for gb in range(2)]
                for gb in range(2):
                    for o in range(NOPS):
                        S.emit("pool", lambda e: e.memset(EXP[gb][o][:], 0.0), writes=[b_expg[gb]])
                TN = ["rs", "ks", "vs", "kk", "t1", "t2", "w", "al", "kd", "bb", "P", "rW", "Wp", "Wi", "rw", "sg"]
                TB = {nm: sb("c_" + nm, (128, NG, SEG), F32, pst) for nm in TN}
                BB = {nm: Buf() for nm in TN}
                SQ = sb("c_SQ", (128, NG, SEG), BF16, pst)
                b_SQ = Buf()
                zer = sb("c_zer", (128, 64), F32, pst)
                b_zer = Buf()
                S.emit("pool", lambda e: e.memset(zer[:], 0.0), writes=[b_zer])
                dsh = [[sb("c_dsh", (128, SEG), F32, pst) for _ in range(2)] for _ in range(2)]
                twd = [sb("c_twd", (128, SEG), BF16, pst) for _ in range(2)]
                tad = [sb("c_tad", (128, SEG), BF16, pst) for _ in range(2)]
                b_d = [Buf(), Buf()]
                Sf = [[sb("c_S", (128, 64), F32, pst) for c in range(8)] for d in range(2)]
                Sb = [[sb("c_Sb", (128, 64), BF16, pst) for c in range(8)] for d in range(2)]
                b_S = [[Buf() for c in range(8)] for d in range(2)]
                for d in range(2):
                    for c in range(8):
                        S.emit("pool", lambda e: e.memset(Sf[d][c][:], 0.0), writes=[b_S[d][c]])
                        S.emit("pool", lambda e: e.memset(Sb[d][c][:], 0.0), writes=[b_S[d][c]])
                PP = [[sb("c_PP", (128, 256), BF16, pst) for _ in range(2)] for u in range(NG)]
                b_PP = [[Buf(), Buf()] for u in range(NG)]
                L3 = [sb("c_L3", (128, 384), BF16, pst) for u in range(NG)]
                b_L3 = [Buf() for u in range(NG)]
                TT = [[sb("c_TT", (128, 128), BF16, pst) for _ in range(2)] for u in range(NG)]
                b_TT = [[Buf(), Buf()] for u in range(NG)]
                BK_ = [sb("c_BK", (128, 256), BF16, pst) for u in range(NG)]
                b_BK = [Buf() for u in range(NG)]
                Vst = [sb("c_Vst", (128, 64), BF16, pst) for u in range(NG)]
                coef = [sb("c_coef", (128, 1), F32, pst) for u in range(NG)]
                b_V = [Buf() for u in range(NG)]
                Xb = [sb("c_Xb", (128, 64), BF16, pst) for u in range(NG)]
                Ub = [sb("c_Ub", (128, 64), BF16, pst) for u in range(NG)]
                b_X = [Buf() for u in range(NG)]
                b_U = [Buf() for u in range(NG)]
                Yst = [sb("c_Yst", (128, NG, 64), F32, pst) for _ in range(2)]
                Bst = [sb("c_Bst", (128, NG, 64), F32, pst) for _ in range(2)]
                b_Yst = [Buf(), Buf()]
                b_Bst = [Buf(), Buf()]
                rr["yst"] = 0

                def shift(dst, src, j, n, rb, wb, eng="dve"):
                    S.emit(eng, lambda e: e.tensor_scalar(out=dst[:, :n], in0=src[:, 1:n + 1], scalar1=mus[:, 2, j:j + 1], scalar2=None, op0=ALU.mult),
                           reads=[b_mus] + rb, writes=wb)
                    S.emit(eng, lambda e: e.scalar_tensor_tensor(out=dst[:, :n], in0=src[:, 0:n], scalar=mus[:, 0, j:j + 1], in1=dst[:, :n],
                                                                 op0=ALU.mult, op1=ALU.add), reads=[b_mus] + rb, writes=wb)
                    S.emit(eng, lambda e: e.scalar_tensor_tensor(out=dst[:, :n], in0=src[:, 2:n + 2], scalar=mus[:, 1, j:j + 1], in1=dst[:, :n],
                                                                 op0=ALU.mult, op1=ALU.add), reads=[b_mus] + rb, writes=wb)

                def load_group(gb, d, half, seg0, n):
                    lo = max(seg0 - 1, 0)
                    hi = min(seg0 + n + 1, L)
                    edge = (seg0 - 1 < 0) or (seg0 + n + 1 > L)
                    for i in range(3):
                        if edge:
                            S.emit("pool", lambda e: e.memset(RAW[i][:], 0.0), writes=[b_RAW[i]])
                        S.dma("sp", RAW[i][:, :, lo - (seg0 - 1):hi - (seg0 - 1)],
                              RWs[s][i * 8:(i + 1) * 8, :, lo:hi].rearrange("u p t -> p u t"), writes=[b_RAW[i]])
                    for (t, j) in [(rawd[gb][0], 24), (rawd[gb][1], 25)]:
                        if edge:
                            S.emit("pool", lambda e: e.memset(t[:], 0.0), writes=[b_raw[gb]])
                        S.dma("sp", t[:, lo - (seg0 - 1):hi - (seg0 - 1)], RWs[s][j, :, lo:hi], writes=[b_raw[gb]])

                def prep_group(gb, d, half, seg0, n):
                    nch = n // 64
                    npad0 = max(0, min(n, L - seg0))
                    bex = b_expg[gb]

                    def op(eng, fn, R=(), Wr=(), xr=(), xw=()):
                        S.emit(eng, fn, reads=[BB[x] for x in R] + list(xr), writes=[BB[x] for x in Wr] + list(xw))

                    def T(nm):
                        return TB[nm][:, :, :n]

                    def bc(ap2):
                        return ap2.unsqueeze(2).to_broadcast([128, NG, n])

                    shift(dsh[gb][0], rawd[gb][0], 24, n, [b_raw[gb]], [b_d[gb]])
                    shift(dsh[gb][1], rawd[gb][1], 25, n, [b_raw[gb]], [b_d[gb]])
                    S.emit("act", lambda e: e.activation(out=twd[gb][:, :n], in_=dsh[gb][0][:, :n], func=AF.Tanh), reads=[b_d[gb]], writes=[b_d[gb]])
                    S.emit("act", lambda e: e.copy(out=tad[gb][:, :n], in_=dsh[gb][1][:, :n]), reads=[b_d[gb]], writes=[b_d[gb]])
                    yield
                    for i, nm in enumerate(["rs", "ks", "vs"]):
                        j0 = i * 8
                        tmpn = "t1" if i % 2 == 0 else "t2"
                        op("dve", lambda e: e.tensor_tensor(out=T(nm), in0=RAW[i][:, :, 1:n + 1], in1=bc(mus[:, 2, j0:j0 + NG]), op=ALU.mult),
                           Wr=[nm], xr=[b_RAW[i], b_mus])
                        op("pool", lambda e: e.tensor_tensor(out=T(tmpn), in0=RAW[i][:, :, 0:n], in1=bc(mus[:, 0, j0:j0 + NG]), op=ALU.mult),
                           Wr=[tmpn], xr=[b_RAW[i], b_mus])
                        op("dve", lambda e: e.tensor_tensor(out=T(nm), in0=T(nm), in1=T(tmpn), op=ALU.add), R=[nm, tmpn], Wr=[nm])
                        op("pool", lambda e: e.tensor_tensor(out=T(tmpn), in0=RAW[i][:, :, 2:n + 2], in1=bc(mus[:, 1, j0:j0 + NG]), op=ALU.mult),
                           R=[nm], Wr=[tmpn], xr=[b_RAW[i], b_mus])
                        op("dve", lambda e: e.tensor_tensor(out=T(nm), in0=T(nm), in1=T(tmpn), op=ALU.add), R=[nm, tmpn], Wr=[nm])
                        if npad0 < n:
                            op("pool", lambda e: e.memset(TB[nm][:, :, npad0:n], 0.0), Wr=[nm])
                        yield
                    op("pool", lambda e: e.tensor_tensor(out=T("kk"), in0=T("ks"), in1=bc(rwp[:, 4, 0:NG]), op=ALU.mult), R=["ks"], Wr=["kk"], xr=[b_rwp])
                    op("dve", lambda e: e.tensor_tensor(out=SQ[:, :, :n], in0=T("kk"), in1=T("kk"), op=ALU.mult), R=["kk"], xw=[b_SQ])
                    for j in range(NG // 4):
                        (pa, bpa) = bank()
                        for uu in range(4):
                            u = j * 4 + uu
                            S.emit("pe", lambda e: e.matmul(pa[:, uu * 128:uu * 128 + n], bones[:], SQ[:, u, :n], start=True, stop=True),
                                   reads=[b_SQ, b_rc], writes=[bpa])
                        op("dve", lambda e: e.tensor_scalar(out=TB["t1"][:, j * 4:(j + 1) * 4, :n], in0=pa[:, :].rearrange("p (u t) -> p u t", t=128)[:, :, :n],
                                                            scalar1=1e-24, scalar2=None, op0=ALU.max), Wr=["t1"], xr=[bpa])
                    op("act", lambda e: e.activation(out=T("t1"), in_=T("t1"), func=AF.Ln), R=["t1"], Wr=["t1"])
                    op("act", lambda e: e.activation(out=T("t1"), in_=T("t1"), func=AF.Exp, scale=-0.5), R=["t1"], Wr=["t1"])
                    op("dve", lambda e: e.tensor_tensor(out=T("kk"), in0=T("kk"), in1=T("t1"), op=ALU.mult), R=["kk", "t1"], Wr=["kk"])
                    yield
                    rows = slice(d * 64, (d + 1) * 64)
                    for (wsb, src, dstn, bi) in [(w2sb, twd, "sg", 0), (a2sb, tad, "al", 2)]:
                        for j in range(NG // 4):
                            (pw, bpw) = bank()
                            for uu in range(4):
                                u = j * 4 + uu
                                S.emit("pe", lambda e: e.matmul(pw[:, uu * 128:uu * 128 + n], wsb[rows, u * 128:(u + 1) * 128], src[gb][rows, :n],
                                                                start=True, stop=True), reads=[b_w2, b_d[gb]], writes=[bpw])
                            for uu in range(4):
                                u = j * 4 + uu
                                op("act", lambda e: e.activation(out=TB[dstn][:, u, :n], in_=pw[:, uu * 128:uu * 128 + n], func=AF.Sigmoid,
                                                                 bias=rwp[:, bi + d, u:u + 1]), Wr=[dstn], xr=[bpw, b_rwp])
                        yield
                    op("act", lambda e: e.activation(out=T("w"), in_=T("sg"), func=AF.Exp, scale=-math.exp(-0.5)), R=["sg"], Wr=["w"])
                    op("act", lambda e: e.activation(out=T("rw"), in_=T("sg"), func=AF.Exp, scale=math.exp(-0.5)), R=["sg"], Wr=["rw"])
                    if npad0 < n:
                        op("pool", lambda e: e.memset(TB["w"][:, :, npad0:n], 1.0), Wr=["w"])
                        op("pool", lambda e: e.memset(TB["rw"][:, :, npad0:n], 1.0), Wr=["rw"])
                    op("dve", lambda e: e.tensor_scalar(out=T("t1"), in0=T("al"), scalar1=-1.0, scalar2=None, op0=ALU.add), R=["al"], Wr=["t1"])
                    op("pool", lambda e: e.tensor_tensor(out=T("t1"), in0=T("t1"), in1=bc(rwp[:, 5, 0:NG]), op=ALU.mult), R=["t1"], Wr=["t1"], xr=[b_rwp])
                    op("dve", lambda e: e.scalar_tensor_tensor(out=T("kd"), in0=T("t1"), scalar=1.0, in1=T("ks"), op0=ALU.add, op1=ALU.mult),
                       R=["t1", "ks"], Wr=["kd"])
                    op("pool", lambda e: e.tensor_tensor(out=T("bb"), in0=T("kk"), in1=T("al"), op=ALU.mult), R=["kk", "al"], Wr=["bb"])
                    yield
                    for u in range(NG):
                        for c_ in range(nch):
                            op("dve", lambda e: e.tensor_tensor_scan(out=TB["P"][:, u, c_ * 64:(c_ + 1) * 64], data0=TB["w"][:, u, c_ * 64:(c_ + 1) * 64],
                                                                     data1=zer[:, :], initial=1.0, op0=ALU.mult, op1=ALU.add),
                               R=["w"], Wr=["P"], xr=[b_zer])
                            op("dve", lambda e: e.tensor_tensor_scan(out=TB["rW"][:, u, c_ * 64:(c_ + 1) * 64], data0=TB["rw"][:, u, c_ * 64:(c_ + 1) * 64],
                                                                     data1=zer[:, :], initial=1.0, op0=ALU.mult, op1=ALU.add),
                               R=["rw"], Wr=["rW"], xr=[b_zer])
                    yield
                    P4 = T("P").rearrange("p u (c t) -> p u c t", t=64)
                    op("dve", lambda e: e.tensor_copy(out=PEND[gb][:, :, :nch], in_=P4[:, :, :, 63]), R=["P"], xw=[bex])
                    pe4 = PEND[gb][:, :, :nch].unsqueeze(3).to_broadcast([128, NG, nch, 64])

                    def v4(nm):
                        return T(nm).rearrange("p u (c t) -> p u c t", t=64)
                    if d == 0:
                        op("pool", lambda e: e.tensor_tensor(out=T("Wp"), in0=T("P"), in1=T("rw"), op=ALU.mult), R=["P", "rw"], Wr=["Wp"])
                        Wi, rW = "P", "rW"
                    else:
                        rP4 = T("rW").rearrange("p u (c t) -> p u c t", t=64)
                        op("dve", lambda e: e.tensor_copy(out=RPEND[:, :, :nch], in_=rP4[:, :, :, 63]), R=["rW"], xw=[b_rpend])
                        rpe4 = RPEND[:, :, :nch].unsqueeze(3).to_broadcast([128, NG, nch, 64])
                        op("pool", lambda e: e.tensor_tensor(out=T("t1"), in0=T("P"), in1=T("rw"), op=ALU.mult), R=["P", "rw"], Wr=["t1"])
                        op("dve", lambda e: e.tensor_tensor(out=v4("Wp"), in0=v4("rW"), in1=pe4, op=ALU.mult), R=["rW"], Wr=["Wp"], xr=[bex])
                        op("pool", lambda e: e.tensor_tensor(out=T("Wi"), in0=T("Wp"), in1=T("w"), op=ALU.mult), R=["Wp", "w"], Wr=["Wi"])
                        op("dve", lambda e: e.tensor_tensor(out=v4("t1"), in0=v4("t1"), in1=rpe4, op=ALU.mult), R=["t1"], Wr=["t1"], xr=[b_rpend])
                        Wi, rW = "Wi", "t1"
                    yield

                    def halves(o, nm):
                        for hh in range(2):
                            ps_ = slice(hh * 64, (hh + 1) * 64)
                            yield (EXP[gb][o][ps_, :, 0:nch, hh * 64:(hh + 1) * 64],
                                   lambda name: TB[name][ps_, :, :n].rearrange("p u (c t) -> p u c t", t=64))
                    for (oap, iv) in halves(0, None):
                        op("dve", lambda e: e.scalar_tensor_tensor(out=oap, in0=iv("kk"), scalar=-1.0, in1=iv("Wp"), op0=ALU.mult, op1=ALU.mult),
                           R=["kk", "Wp"], xw=[bex])
                    for (oap, iv) in halves(1, None):
                        op("pool", lambda e: e.tensor_tensor(out=oap, in0=iv("rs"), in1=iv(Wi), op=ALU.mult), R=["rs", Wi], xw=[bex])
                    yield
                    for (oap, iv) in halves(2, None):
                        op("dve", lambda e: e.tensor_tensor(out=oap, in0=iv("bb"), in1=iv(rW), op=ALU.mult), R=["bb", rW], xw=[bex])
                    for (oap, iv) in halves(3, None):
                        op("pool", lambda e: e.tensor_tensor(out=oap, in0=iv("kd"), in1=iv(rW), op=ALU.mult), R=["kd", rW], xw=[bex])
                    yield
                    for (oap, iv) in halves(4, None):
                        op("act", lambda e: e.copy(out=oap, in_=iv("vs")), R=["vs"], xw=[bex])
                    op("pool", lambda e: e.tensor_tensor(out=T("t2"), in0=T("rs"), in1=bc(rwp[:, 6, 0:NG]), op=ALU.mult), R=["rs"], Wr=["t2"], xr=[b_rwp])
                    for (oap, iv) in halves(5, None):
                        op("dve", lambda e: e.tensor_tensor(out=oap, in0=iv("t2"), in1=iv("kd"), op=ALU.mult), R=["t2", "kd"], xw=[bex])
                    yield

                def chunk_group(gb, d, half, seg0, n, ch):
                    tok0 = seg0 + ch * 64
                    nt = max(0, min(64, L - tok0))
                    units = range(NG)
                    yi = nxt("yst", 2)
                    for u in units:
                        E = exp_[gb][u]
                        A, Rr, B, K, V, RK = [E[o][:, ch, :] for o in range(NOPS)]
                        be = b_exp[gb][u]
                        (p1, bp1) = bank()
                        S.emit("pe", lambda e: e.matmul(p1[:, 0:128], A, B, start=True, stop=True), reads=[be], writes=[bp1])
                        S.emit("pe", lambda e: e.matmul(p1[:, 128:256], B, A, start=True, stop=True), reads=[be], writes=[bp1])
                        S.emit("dve", lambda e: e.tensor_tensor(out=PP[u][0][:], in0=p1[:, 0:256], in1=m12[d][:], op=ALU.mult),
                               reads=[bp1, b_mm], writes=[b_PP[u][0]])
                        S.emit("dve", lambda e: e.tensor_tensor(out=TT[u][0][:], in0=PP[u][0][:, 128:256], in1=ident[:], op=ALU.add),
                               reads=[b_PP[u][0], b_ident], writes=[b_TT[u][0]])
                        (p2, bp2) = bank()
                        S.emit("pe", lambda e: e.matmul(p2[:, 0:128], K, A, start=True, stop=True), reads=[be], writes=[bp2])
                        S.emit("pe", lambda e: e.matmul(p2[:, 128:256], B, Rr, start=True, stop=True), reads=[be], writes=[bp2])
                        S.emit("pe", lambda e: e.matmul(p2[:, 256:384], K, Rr, start=True, stop=True), reads=[be], writes=[bp2])
                        S.emit("dve", lambda e: e.tensor_tensor(out=L3[u][:], in0=p2[:, 0:384], in1=m345[d][:], op=ALU.mult),
                               reads=[bp2, b_mm], writes=[b_L3[u]])
                        (p3, bp3) = bank()
                        p3b = p3.bitcast(BF16)
                        S.emit("pe", lambda e: e.transpose(p3b[:, 0:128], B, ident[:]), reads=[be, b_ident], writes=[bp3])
                        S.emit("pe", lambda e: e.transpose(p3b[:, 128:256], K, ident[:]), reads=[be, b_ident], writes=[bp3])
                        S.emit("pe", lambda e: e.matmul(p3[:, 128:192], V, istack[:], start=True, stop=True), reads=[be, b_rc], writes=[bp3])
                        S.emit("pe", lambda e: e.matmul(p3[:, 192:193], RK, onesb[:], start=True, stop=True), reads=[be, b_rc], writes=[bp3])
                        S.emit("act", lambda e: e.copy(out=BK_[u][:], in_=p3b[:, 0:256]), reads=[bp3], writes=[b_BK[u]])
                        S.emit("act", lambda e: e.copy(out=Vst[u][:], in_=p3[:, 128:192]), reads=[bp3], writes=[b_V[u]])
                        S.emit("act", lambda e: e.copy(out=coef[u][:], in_=p3[:, 192:193]), reads=[bp3], writes=[b_V[u]])
                        S.emit("act", lambda e: e.activation(out=Bst[yi][:, u, :], in_=Vst[u][:], func=AF.Copy, scale=coef[u][:, 0:1]),
                               reads=[b_V[u]], writes=[b_Bst[yi]])
                    yield
                    for j in range(6):
                        if j > 0:
                            yield
                        cur, new = j % 2, (j + 1) % 2
                        for u in units:
                            (pq, bpq) = bank()
                            Pj = PP[u][cur][:, 0:128]
                            PjT = PP[u][cur][:, 128:256]
                            if j < 5:
                                S.emit("pe", lambda e: e.matmul(pq[:, 0:128], PjT, Pj, start=True, stop=True), reads=[b_PP[u][cur]], writes=[bpq])
                                S.emit("pe", lambda e: e.matmul(pq[:, 128:256], Pj, PjT, start=True, stop=True), reads=[b_PP[u][cur]], writes=[bpq])
                            if j >= 1:
                                tc_, tn_ = (j - 1) % 2, j % 2
                                S.emit("pe", lambda e: e.matmul(pq[:, 256:384], ident[:], TT[u][tc_][:], start=True, stop=False),
                                       reads=[b_ident, b_TT[u][tc_]], writes=[bpq])
                                S.emit("pe", lambda e: e.matmul(pq[:, 256:384], Pj, TT[u][tc_][:], start=False, stop=True),
                                       reads=[b_PP[u][cur], b_TT[u][tc_]], writes=[bpq])
                                if u % 4 != 3:
                                    S.emit("act", lambda e: e.copy(out=TT[u][tn_][:], in_=pq[:, 256:384]), reads=[bpq], writes=[b_TT[u][tn_]])
                                else:
                                    S.emit("dve", lambda e: e.tensor_copy(out=TT[u][tn_][:], in_=pq[:, 256:384]), reads=[bpq], writes=[b_TT[u][tn_]])
                            if j < 5:
                                if u % 4 != 3:
                                    S.emit("act", lambda e: e.copy(out=PP[u][new][:], in_=pq[:, 0:256]), reads=[bpq], writes=[b_PP[u][new]])
                                else:
                                    S.emit("dve", lambda e: e.tensor_copy(out=PP[u][new][:], in_=pq[:, 0:256]), reads=[bpq], writes=[b_PP[u][new]])
                    tfin = 5 % 2
                    yield
                    for u in units:
                        c = half * NG + u
                        E = exp_[gb][u]
                        A = E[0][:, ch, :]
                        (px, bpx) = bank()
                        S.emit("pe", lambda e: e.matmul(px[:, 0:64], A, Sb[d][c][:], start=True, stop=False), reads=[b_exp[gb][u], b_S[d][c]], writes=[bpx])
                        S.emit("pe", lambda e: e.matmul(px[:, 0:64], L3[u][:, 0:128], Vst[u][:], start=False, stop=True), reads=[b_L3[u], b_V[u]], writes=[bpx])
                        S.emit("act", lambda e: e.copy(out=Xb[u][:], in_=px[:, 0:64]), reads=[bpx], writes=[b_X[u]])
                    yield
                    for u in units:
                        (pu, bpu) = bank()
                        S.emit("pe", lambda e: e.matmul(pu[:, 0:64], TT[u][tfin][:], Xb[u][:], start=True, stop=True), reads=[b_TT[u][tfin], b_X[u]], writes=[bpu])
                        S.emit("dve", lambda e: e.tensor_copy(out=Ub[u][:], in_=pu[:, 0:64]), reads=[bpu], writes=[b_U[u]])
                    yield
                    for u in units:
                        c = half * NG + u
                        E = exp_[gb][u]
                        Rr = E[1][:, ch, :]
                        (py, bpy) = bank()
                        S.emit("pe", lambda e: e.matmul(py[:, 0:64], Rr, Sb[d][c][:], start=True, stop=False), reads=[b_exp[gb][u], b_S[d][c]], writes=[bpy])
                        S.emit("pe", lambda e: e.matmul(py[:, 0:64], L3[u][:, 128:256], Ub[u][:], start=False, stop=False), reads=[b_L3[u], b_U[u]], writes=[bpy])
                        S.emit("pe", lambda e: e.matmul(py[:, 0:64], L3[u][:, 256:384], Vst[u][:], start=False, stop=True), reads=[b_L3[u], b_V[u]], writes=[bpy])
                        S.emit("act", lambda e: e.copy(out=Yst[yi][:, u, :], in_=py[:, 0:64]), reads=[bpy], writes=[b_Yst[yi]])
                        (pS, bpS) = bank()
                        S.emit("pe", lambda e: e.matmul(pS[:, 0:64], BK_[u][:, 0:128], Ub[u][:], start=True, stop=False), reads=[b_BK[u], b_U[u]], writes=[bpS])
                        S.emit("pe", lambda e: e.matmul(pS[:, 0:64], BK_[u][:, 128:256], Vst[u][:], start=False, stop=True), reads=[b_BK[u], b_V[u]], writes=[bpS])
                        S.emit("dve", lambda e: e.tensor_tensor(out=Sf[d][c][:], in0=Sf[d][c][:], in1=pS[:, 0:64], op=ALU.add),
                               reads=[bpS], writes=[b_S[d][c]])
                        S.emit("dve", lambda e: e.tensor_scalar(out=Sf[d][c][:], in0=Sf[d][c][:], scalar1=pend[gb][u][:, ch:ch + 1], scalar2=None, op0=ALU.mult),
                               reads=[b_exp[gb][u]], writes=[b_S[d][c]])
                        S.emit("act", lambda e: e.copy(out=Sb[d][c][:], in_=Sf[d][c][:]), reads=[b_S[d][c]], writes=[b_S[d][c]])
                    yield
                    if nt > 0:
                        for (stg, bst, dst) in [(Yst[yi], b_Yst[yi], YSs[d][s]), (Bst[yi], b_Bst[yi], BNs[d][s])]:
                            dv = dst[tok0:tok0 + nt, :].rearrange("t (c h v) -> t c h v", h=2, v=64)
                            for hh in range(2):
                                S.dma("sp", dv[:, half * NG:(half + 1) * NG, hh, :], stg[hh * 64:hh * 64 + nt, :, :], reads=[bst])

                work = []
                for i in range(nseg):
                    for d in range(2):
                        si = i if d == 0 else nseg - 1 - i
                        for half in range(8 // NG):
                            work.append((d, half, segs[si][0], segs[si][1]))
                def run_all(gen):
                    for _ in gen:
                        pass

                def chunks_gen(wi):
                    d, half, seg0, n = work[wi]
                    chs = list(range(n // 64))
                    if d == 1:
                        chs = chs[::-1]
                    for ch in chs:
                        yield from chunk_group(wi % 2, d, half, seg0, n, ch)

                NW = len(work)
                load_group(0, *work[0])
                run_all(prep_group(0, *work[0]))
                if NW > 1:
                    load_group(1, *work[1])
                for wi in range(NW):
                    pg = prep_group((wi + 1) % 2, *work[wi + 1]) if wi + 1 < NW else iter(())
                    nsteps = 14 * (work[wi][3] // 64)
                    psteps = 14 if wi + 1 < NW else 0
                    per = -(-psteps // max(nsteps, 1)) if psteps else 0
                    first = True
                    for _ in chunks_gen(wi):
                        for _k in range(per):
                            next(pg, None)
                    run_all(pg)
                    if wi + 2 < NW:
                        load_group(wi % 2, *work[wi + 2])
                S.barrier()

        for s in range(NS):
            if "C" in phases:
                phase_C(s)

        lnx = sb("lnx", (128, 2, D))
        b_lnx = Buf()
        S.dma("sp", lnx[:, 0, :], W["rw_lnx_w"][0].partition_broadcast(128), writes=[b_lnx])
        S.dma("sp", lnx[:, 1, :], W["rw_lnx_b"][0].partition_broadcast(128), writes=[b_lnx])

        def phase_D(s):
            L = seqL[s]
            TD = 512
            with ExitStack() as pst:
                P = {}
                P["ss"] = sb("d_ss", (128, 4), F32, pst)
                P["b_ss"] = Buf()
                P["junk"] = sb("d_junk", (128, D), BF16, pst)
                P["b_junk"] = Buf()
                P["xn"] = [sb("d_xn", (128, D), BF16, pst) for i in range(4)]
                P["b_xn"] = [Buf() for _ in range(4)]
                P["xT"] = [sb("d_xT", (128, TD), BF16, pst) for i in range(8)]
                P["b_xT"] = [Buf() for _ in range(8)]
                P["wgu"] = [sb("d_wgu", (128, 2, 8, 128), BF16, pst) for i in range(3)]
                P["b_wgu"] = [Buf() for _ in range(3)]
                P["sg"] = [sb("d_sg", (128, TD), F32, pst) for i in range(2)]
                P["b_sg"] = [Buf() for _ in range(2)]
                P["act"] = [sb("d_act", (128, TD), BF16, pst) for i in range(NF)]
                P["b_act"] = [Buf() for _ in range(NF)]
                P["wd"] = sb("d_wd", (128, NF, 512), BF16, pst)
                P["b_wd"] = Buf()
                rr["wgu"] = 0
                rr["sg"] = 0
                hts = [sb("d_h", (128, D), F32, pst) for i in range(4)]
                b_h = [Buf() for _ in range(4)]
                yin = [sb("d_yin", (128, D), F32, pst) for i in range(4)]
                b_yin = Buf()
                ysum = sb("d_ysum", (128, D), F32, pst)
                ytmp = sb("d_ytmp", (128, D), F32, pst)
                b_y = Buf()
                st16 = sb("d_st16", (128, 4, 16), F32, pst)
                b_st = Buf()
                zb = sb("d_zb", (128, D), BF16, pst)
                b_zb = Buf()
                zT = [sb("d_zT", (128, TD), BF16, pst) for i in range(8)]
                b_zT = [Buf() for _ in range(8)]
                oT = [sb("d_oT", (128, TD), BF16, pst) for i in range(8)]
                b_oT = [Buf() for _ in range(8)]
                mT = [sb("d_mT", (128, TD), BF16, pst) for i in range(8)]
                b_mT = [Buf() for _ in range(8)]
                gts = [sb("d_gt", (128, TD), F32, pst) for i in range(4)]
                b_gts = [Buf() for _ in range(4)]
                mtmp = sb("d_mtmp", (128, TD), F32, pst)
                b_mtmp = Buf()
                wbr = [sb("d_wbr", (128, 8, 128), BF16, pst) for i in range(4)]
                b_wbr = [Buf() for _ in range(4)]
                wo = sb("d_wo", (128, 8, 512), BF16, pst)
                b_wo = Buf()
                gdraw = sb("d_gdraw", (128, TD + 2), F32, pst)
                gdsh = sb("d_gdsh", (128, TD), F32, pst)
                sgd = sb("d_sgd", (128, TD), BF16, pst)
                b_gd = Buf()
                ost = sb("d_ost", (128, D), F32, pst)
                b_ost = Buf()
                rr["gts"] = 0
                rr["wbr"] = 0
                for (t0, T) in tiles_of(L, TD):
                    blks = blocks_of(t0, T)
                    lo = max(t0 - 1, 0)
                    hi = min(t0 + T + 1, L)
                    if (t0 - 1 < 0) or (t0 + T + 1 > L):
                        S.emit("pool", lambda e: e.memset(gdraw[:], 0.0), writes=[b_gd])
                    S.dma("sp", gdraw[:, lo - (t0 - 1):hi - (t0 - 1)], RWs[s][26, :, lo:hi], writes=[b_gd])
                    S.emit("pool", lambda e: e.tensor_scalar(out=gdsh[:, :T], in0=gdraw[:, 1:T + 1], scalar1=mus[:, 2, 26:27], scalar2=None, op0=ALU.mult),
                           reads=[b_mus, b_gd], writes=[b_gd])
                    S.emit("dve", lambda e: e.scalar_tensor_tensor(out=gdsh[:, :T], in0=gdraw[:, 0:T], scalar=mus[:, 0, 26:27], in1=gdsh[:, :T],
                                                                    op0=ALU.mult, op1=ALU.add), reads=[b_mus, b_gd], writes=[b_gd])
                    S.emit("dve", lambda e: e.scalar_tensor_tensor(out=gdsh[:, :T], in0=gdraw[:, 2:T + 2], scalar=mus[:, 1, 26:27], in1=gdsh[:, :T],
                                                                    op0=ALU.mult, op1=ALU.add), reads=[b_mus, b_gd], writes=[b_gd])
                    S.emit("act", lambda e: e.activation(out=sgd[:, :T], in_=gdsh[:, :T], func=AF.Sigmoid), reads=[b_gd], writes=[b_gd])
                    for k in range(8):
                        S.dma("sp", oT[k][:, :T], OTs[s][k * 128:(k + 1) * 128, t0:t0 + T], writes=[b_oT[k]])
                    for bi, (tg, o, n) in enumerate(blks):
                        S.dma("sp", hts[bi][:n, :], Hs[s][tg:tg + n, :], writes=[b_h[bi]])
                        for i, src in enumerate([YSs[0][s], YSs[1][s], BNs[0][s], BNs[1][s]]):
                            S.dma("sp", yin[i][:n, :], src[tg:tg + n, :], writes=[b_yin])
                        gbank = [nxt("D", 2), nxt("D", 2)]
                        for hf in range(2):
                            S.emit("pe", lambda e: e.matmul(psD[gbank[hf]][:n, :], sgd[:, o:o + n], g2sb[:, hf * 512:(hf + 1) * 512], start=True, stop=True),
                                   reads=[b_gd, b_w2], writes=[bD[gbank[hf]]])
                        def v3(t):
                            return t[:n, :].rearrange("p (h v) -> p h v", v=64)
                        def bc(col):
                            return st16[:n, col, :].unsqueeze(2).to_broadcast([n, 16, 64])
                        S.emit("dve", lambda e: e.tensor_tensor(out=ysum[:n, :], in0=yin[0][:n, :], in1=yin[1][:n, :], op=ALU.add), reads=[b_yin], writes=[b_y])
                        S.emit("dve", lambda e: e.reduce_sum(out=st16[:n, 0, :], in_=v3(ysum), axis=AX.X), reads=[b_y], writes=[b_st])
                        S.emit("dve", lambda e: e.tensor_scalar(out=st16[:n, 0, :], in0=st16[:n, 0, :], scalar1=1.0 / 64, scalar2=None, op0=ALU.mult),
                               reads=[b_st], writes=[b_st])
                        S.emit("dve", lambda e: e.tensor_tensor(out=v3(ysum), in0=v3(ysum), in1=bc(0), op=ALU.subtract), reads=[b_y, b_st], writes=[b_y])
                        S.emit("pool", lambda e: e.tensor_tensor(out=ytmp[:n, :], in0=ysum[:n, :], in1=ysum[:n, :], op=ALU.mult), reads=[b_y], writes=[b_y])
                        S.emit("dve", lambda e: e.reduce_sum(out=st16[:n, 1, :], in_=v3(ytmp), axis=AX.X), reads=[b_y], writes=[b_st])
                        S.emit("act", lambda e: e.activation(out=st16[:n, 2, :], in_=st16[:n, 1, :], func=AF.Sqrt, bias=epsT[:n, 1:2], scale=1.0 / 64),
                               reads=[b_st, b_eps], writes=[b_st])
                        S.emit("dve", lambda e: e.reciprocal(out=st16[:n, 3, :], in_=st16[:n, 2, :]), reads=[b_st], writes=[b_st])
                        S.emit("dve", lambda e: e.tensor_tensor(out=v3(ysum), in0=v3(ysum), in1=bc(3), op=ALU.mult), reads=[b_y, b_st], writes=[b_y])
                        S.emit("pool", lambda e: e.tensor_tensor(out=ysum[:n, :], in0=ysum[:n, :], in1=lnx[:n, 0, :], op=ALU.mult), reads=[b_y, b_lnx], writes=[b_y])
                        S.emit("pool", lambda e: e.tensor_tensor(out=ysum[:n, :], in0=ysum[:n, :], in1=lnx[:n, 1, :], op=ALU.add), reads=[b_y, b_lnx], writes=[b_y])
                        S.emit("pool", lambda e: e.tensor_tensor(out=ytmp[:n, :], in0=yin[2][:n, :], in1=yin[3][:n, :], op=ALU.add), reads=[b_yin, b_y], writes=[b_y])
                        S.emit("dve", lambda e: e.tensor_tensor(out=ysum[:n, :], in0=ysum[:n, :], in1=ytmp[:n, :], op=ALU.add), reads=[b_y], writes=[b_y])
                        for hf in range(2):
                            S.emit("dve", lambda e: e.tensor_tensor(out=zb[:n, hf * 512:(hf + 1) * 512], in0=ysum[:n, hf * 512:(hf + 1) * 512],
                                                                    in1=psD[gbank[hf]][:n, :], op=ALU.mult),
                                   reads=[b_y, bD[gbank[hf]]], writes=[b_zb])
                        for cp in range(4):
                            tb_ = nxt("T", 2)
                            for cc in range(2):
                                c = cp * 2 + cc
                                S.emit("pe", lambda e: e.transpose(psT[tb_][:, cc * 512:cc * 512 + n], zb[:n, c * 128:(c + 1) * 128], ident[:n, :n]),
                                       reads=[b_zb, b_ident], writes=[bT[tb_]])
                            for cc in range(2):
                                c = cp * 2 + cc
                                S.emit("act", lambda e: e.copy(out=zT[c][:, o:o + n], in_=psT[tb_][:, cc * 512:cc * 512 + n]),
                                       reads=[bT[tb_]], writes=[b_zT[c]])
                    for j in range(8):
                        wa = nxt("wbr", 4)
                        S.dma("sp", wbr[wa][:], WABS[j], writes=[b_wbr[wa]])
                        wr_ = nxt("wbr", 4)
                        S.dma("sp", wbr[wr_][:], WRBS[j], writes=[b_wbr[wr_]])
                        ga = nxt("gts", 4)
                        S.dma("sp", gts[ga][:, :T], GTs[s][j, :, t0:t0 + T], writes=[b_gts[ga]])
                        gb_ = nxt("gts", 4)
                        S.dma("sp", gts[gb_][:, :T], GTs[s][8 + j, :, t0:t0 + T], writes=[b_gts[gb_]])
                        b1, b2 = nxt("G", 4), nxt("G", 4)
                        for k in range(8):
                            S.emit("pe", lambda e: e.matmul(psG[b1][:, :T], wbr[wa][:, k, :], oT[k][:, :T], start=(k == 0), stop=(k == 7)),
                                   reads=[b_wbr[wa], b_oT[k]], writes=[bG[b1]])
                        for k in range(8):
                            S.emit("pe", lambda e: e.matmul(psG[b2][:, :T], wbr[wr_][:, k, :], zT[k][:, :T], start=(k == 0), stop=(k == 7)),
                                   reads=[b_wbr[wr_], b_zT[k]], writes=[bG[b2]])
                        S.emit("dve", lambda e: e.tensor_tensor(out=mtmp[:, :T], in0=gts[ga][:, :T], in1=psG[b1][:, :T], op=ALU.mult),
                               reads=[b_gts[ga], bG[b1]], writes=[b_mtmp])
                        S.emit("dve", lambda e: e.tensor_tensor(out=gts[gb_][:, :T], in0=gts[gb_][:, :T], in1=psG[b2][:, :T], op=ALU.mult),
                               reads=[b_gts[gb_], bG[b2]], writes=[b_gts[gb_]])
                        S.emit("pool", lambda e: e.tensor_tensor(out=mT[j][:, :T], in0=mtmp[:, :T], in1=gts[gb_][:, :T], op=ALU.add),
                               reads=[b_mtmp, b_gts[gb_]], writes=[b_mT[j]])
                    for hf in range(2):
                        S.dma("sp", wo[:], WOS[hf], writes=[b_wo])
                        for bi, (tg, o, n) in enumerate(blks):
                            bank_ = nxt("D", 2)
                            for k in range(8):
                                S.emit("pe", lambda e: e.matmul(psD[bank_][:n, :], mT[k][:, o:o + n], wo[:, k, :], start=(k == 0), stop=(k == 7)),
                                       reads=[b_mT[k], b_wo], writes=[bD[bank_]])
                            S.emit("dve", lambda e: e.tensor_tensor(out=hts[bi][:n, hf * 512:(hf + 1) * 512], in0=hts[bi][:n, hf * 512:(hf + 1) * 512],
                                                                    in1=psD[bank_][:n, :], op=ALU.add), reads=[bD[bank_], b_h[bi]], writes=[b_h[bi]])
                    xts = hts[:len(blks)]
                    bxs = b_h[:len(blks)]
                    ffn(P, 1, blks, xts, bxs, 2)
                    for bi, (tg, o, n) in enumerate(blks):
                        xt, bx = hts[bi], b_h[bi]
                        S.emit("pool", lambda e: e.memset(P["ss"][:, 0:1], 0.0), writes=[P["b_ss"]])
                        S.emit("act", lambda e: e.activation(out=P["junk"][:n, :], in_=xt[:n, :], func=AF.Square, accum_out=P["ss"][:n, 0:1]),
                               reads=[bx], writes=[P["b_ss"], P["b_junk"]])
                        S.emit("act", lambda e: e.activation(out=P["ss"][:n, 1:2], in_=P["ss"][:n, 0:1], func=AF.Sqrt, bias=epsT[:n, 0:1], scale=1.0 / D),
                               reads=[b_eps], writes=[P["b_ss"]])
                        S.emit("dve", lambda e: e.reciprocal(out=P["ss"][:n, 2:3], in_=P["ss"][:n, 1:2]), writes=[P["b_ss"]])
                        S.emit("act", lambda e: e.activation(out=ost[:n, :], in_=xt[:n, :], func=AF.Copy, scale=P["ss"][:n, 2:3]),
                               reads=[bx, P["b_ss"]], writes=[b_ost])
                        S.emit("dve", lambda e: e.tensor_tensor(out=ost[:n, :], in0=ost[:n, :], in1=fin_g[:n, :], op=ALU.mult), reads=[b_ost, b_fin], writes=[b_ost])
                        lo_t = max(tg, NMETA)
                        if tg + n > lo_t:
                            S.dma("pool", ydst(s)[lo_t - NMETA:tg + n - NMETA, :], ost[lo_t - tg:n, :], reads=[b_ost])
                S.barrier()

        for s in range(NS):
            if "D" in phases:
                phase_D(s)
    return nc


_NC_CACHE = {}


def _rel_bucket_np(rel):
    try:
        import jax
        import jax.numpy as jnp
        with jax.default_device(jax.devices("cpu")[0]):
            r = jnp.asarray(rel, dtype=jnp.int32)
            nb = 16
            max_exact = 8
            n = jnp.abs(r)
            nf = jnp.maximum(n, 1).astype(jnp.float32)
            large = max_exact + (jnp.log(nf / max_exact) / math.log(128 / max_exact) * (nb - max_exact)).astype(jnp.int32)
            large = jnp.minimum(large, nb - 1)
            out = (r > 0).astype(jnp.int32) * nb + jnp.where(n < max_exact, n, large)
            return np.asarray(out)
    except Exception:
        rel = np.asarray(rel, dtype=np.int32)
        n = np.abs(rel)
        nf = np.maximum(n, 1).astype(np.float32)
        large = 8 + (np.log(nf / np.float32(8)) / np.float32(math.log(16.0)) * np.float32(8)).astype(np.int32)
        large = np.minimum(large, 15)
        return (rel > 0).astype(np.int32) * 16 + np.where(n < 8, n, large)


def _consts():
    p = np.arange(128)[:, None]
    m = np.arange(896)[None, :]
    bkt = _rel_bucket_np(p - m + 384).astype(np.float32)
    t = np.arange(64)
    lo = (t[None, :] < t[:, None]).astype(np.float32)
    up = lo.T.copy()
    eye = np.eye(64, dtype=np.float32)

    def bd(mm):
        z = np.zeros((128, 128), np.float32)
        z[:64, :64] = mm
        z[64:, 64:] = mm
        return z

    masks = np.stack([bd(lo), bd(up), bd(lo + eye), bd(up + eye)], axis=1)
    istack = np.concatenate([eye, eye], axis=0)
    bones = bd(np.ones((64, 64), np.float32))
    return {"c_ident": np.eye(128, dtype=np.float32), "c_bk": bkt, "c_masks": masks, "c_istack": istack, "c_bones": bones}


def kernel(**inputs):
    xp = np.ascontiguousarray(inputs["x_prompt"], dtype=np.float32)
    xs = np.ascontiguousarray(inputs["x_sample"], dtype=np.float32)
    S0, S1 = xp.shape[1], xs.shape[1]
    ncores = 8
    key = (S0, S1)
    if key not in _NC_CACHE:
        _NC_CACHE[key] = build_nc(S0, S1)
    nc = _NC_CACHE[key]
    shared = {k: np.ascontiguousarray(v, dtype=np.float32) for k, v in inputs.items()
              if k not in ("x_prompt", "x_sample")}
    shared.update(_consts())
    in_maps = []
    for c in range(ncores):
        m = dict(shared)
        m["x_prompt"] = xp[2 * c:2 * c + 2]
        m["x_sample"] = xs[2 * c:2 * c + 2]
        in_maps.append(m)
    res = run_bass_kernel_spmd(nc, in_maps, core_ids=list(range(ncores)))
    yp = np.concatenate([r["y_prompt"] for r in res.results], axis=0)
    ys = np.concatenate([r["y_sample"] for r in res.results], axis=0)
    return (yp.astype(np.float32), ys.astype(np.float32))
```

```python
import math
from contextlib import ExitStack
import numpy as np
import concourse.bass as bass
import concourse.mybir as mybir
from concourse.bass_utils import run_bass_kernel_spmd

F32 = mybir.dt.float32
BF16 = mybir.dt.bfloat16
AF = mybir.ActivationFunctionType
ALU = mybir.AluOpType
AX = mybir.AxisListType

D = 1024
DFF = 2816
NF = DFF // 128
NMETA = 16
EPS = 1e-6
LNX_EPS = 64e-5
NIN = 8576
EP = 30000
KDMA = 8


class Buf:
    __slots__ = ("w", "weng", "r", "excl")

    def __init__(self, excl=False):
        self.w = None
        self.weng = None
        self.r = {}
        self.excl = excl


class Sched:
    def __init__(self, nc, st):
        self.nc = nc
        self.engs = {"pe": nc.tensor, "act": nc.scalar, "dve": nc.vector, "pool": nc.gpsimd, "sp": nc.sync}
        self.cnt = {e: 0 for e in self.engs}
        self.seen = {e: {} for e in self.engs}
        self.sems = {}
        self.st = st
        self.dqi = {"sp": 0, "pool": 0}
        self.last = {}

    def sem(self, key):
        if key not in self.sems:
            name = "s_" + "_".join(str(x) for x in key)
            self.sems[key] = self.st.enter_context(self.nc.semaphore(name))
        return self.sems[key]

    def _deps(self, e, reads, writes):
        deps = {}

        def add(tok):
            if tok is None:
                return
            k, v = tok
            if deps.get(k, 0) < v:
                deps[k] = v

        for b in reads:
            add(b.w)
            if b.excl:
                for src, tok in b.r.items():
                    if src != e:
                        add(tok)
        for b in writes:
            if not (e == "pe" and b.weng == "pe"):
                add(b.w)
            for tok in b.r.values():
                add(tok)
        return deps

    def _wait(self, e, deps):
        seen = self.seen[e]
        eng = self.engs[e]
        for k, v in deps.items():
            if seen.get(k, 0) < v:
                eng.wait_ge(self.sem(k), v)
                seen[k] = v

    def _mark(self, src, tok, e, reads, writes):
        self.last[src] = tok
        for b in reads:
            b.r[src] = tok
        for b in writes:
            b.w = tok
            b.weng = e
            b.r = {}

    def emit(self, e, fn, reads=(), writes=()):
        self._wait(e, self._deps(e, reads, writes))
        ins = fn(self.engs[e])
        c = self.cnt[e]
        self.cnt[e] += 1
        k = (e, c // EP)
        v = c % EP + 1
        ins.then_inc(self.sem(k), 1)
        self._mark(e, (k, v), e, reads, writes)

    def dma(self, q, out, in_, reads=(), writes=(), **kw):
        deps = self._deps("dma", reads, writes)
        i = self.dqi[q]
        self.dqi[q] += 1
        slot = i % KDMA
        k = ("d", q, slot)
        v = 16 * (i // KDMA + 1)
        if i >= KDMA:
            if deps.get(k, 0) < v - 16:
                deps[k] = v - 16
        self._wait(q, deps)
        ins = self.engs[q].dma_start(out=out, in_=in_, **kw)
        ins.then_inc(self.sem(k), 16)
        self._mark(k, (k, v), "dma", reads, writes)

    def barrier(self):
        deps = {}
        for tok in self.last.values():
            k, v = tok
            if deps.get(k, 0) < v:
                deps[k] = v
        for e in self.engs:
            self._wait(e, deps)


def blocks_of(t0, T):
    out = []
    o = 0
    while o < T:
        n = min(128, T - o)
        out.append((t0 + o, o, n))
        o += n
    return out


def tiles_of(L, TT=512):
    out = []
    t = 0
    while t < L:
        T = min(TT, L - t)
        out.append((t, T))
        t += T
    return out


def build_nc(S0, S1, debug=False, phases="ABCD"):
    nc = bass.Bass("TRN2", target_bir_lowering=False)
    seqS = [S0, S0, S1, S1]
    seqL = [s + NMETA for s in seqS]
    NS = len(seqS)

    def din(name, shape):
        return nc.dram_tensor(name, list(shape), F32, kind="ExternalInput").ap()

    def dscr(name, shape, dt=F32):
        kind = "ExternalOutput" if (debug and name.startswith("dbg_")) else "Internal"
        return nc.dram_tensor(name, list(shape), dt, kind=kind).ap()

    x_in = [din("x_prompt", (2, S0, D)), din("x_sample", (2, S1, D))]
    y_out = [nc.dram_tensor("y_prompt", [2, S0, D], F32, kind="ExternalOutput").ap(),
             nc.dram_tensor("y_sample", [2, S1, D], F32, kind="ExternalOutput").ap()]

    def xsrc(s):
        return x_in[s // 2][s % 2]

    def ydst(s):
        return y_out[s // 2][s % 2]

    meta = din("meta_tokens", (NMETA, D))
    rel_bias = din("rel_bias", (32, 8))
    W = {}
    for nm, shp in [("ffn1_norm", (1, D)), ("ffn1_w_gate", (1, D, DFF)), ("ffn1_w_up", (1, D, DFF)),
                    ("ffn1_w_down", (1, DFF, D)), ("mix_norm", (1, D)), ("w_in", (1, D, NIN)),
                    ("attn_lambda_q1", (1, 64)), ("attn_lambda_k1", (1, 64)), ("attn_lambda_q2", (1, 64)),
                    ("attn_lambda_k2", (1, 64)), ("attn_subln", (1, 128)), ("w_attn_branch", (1, D, D)),
                    ("rw_mu_prev", (1, 3456)), ("rw_mu_next", (1, 3456)), ("rw_w0", (1, 2, D)),
                    ("rw_w2", (1, 2, 64, D)), ("rw_a0", (1, 2, D)), ("rw_a2", (1, 2, 64, D)),
                    ("rw_g2", (1, 128, D)), ("rw_k_k", (1, D)), ("rw_k_a", (1, D)), ("rw_r_k", (1, 16, 64)),
                    ("rw_lnx_w", (1, D)), ("rw_lnx_b", (1, D)), ("w_rw_branch", (1, D, D)), ("w_out", (1, D, D)),
                    ("ffn2_norm", (1, D)), ("ffn2_w_gate", (1, D, DFF)), ("ffn2_w_up", (1, D, DFF)),
                    ("ffn2_w_down", (1, DFF, D)), ("final_norm", (D,))]:
        W[nm] = din(nm, shp)
    c_ident = din("c_ident", (128, 128))
    c_bk = din("c_bk", (128, 896))
    c_masks = din("c_masks", (128, 4, 128))
    c_istack = din("c_istack", (128, 64))
    c_bones = din("c_bones", (128, 128))

    WGU = [dscr(f"wgu{i}", (NF, 128, 2, 8, 128), BF16) for i in range(2)]
    WD = [dscr(f"wd{i}", (2, 128, NF, 512), BF16) for i in range(2)]
    WINS = dscr("wins", (67, 128, 8, 128), BF16)
    WVS = dscr("wvs", (2, 128, 8, 512), BF16)
    WABS = dscr("wabs", (8, 128, 8, 128), BF16)
    WRBS = dscr("wrbs", (8, 128, 8, 128), BF16)
    WOS = dscr("wos", (2, 128, 8, 512), BF16)
    Hs = [dscr(f"dbg_h{s}", (seqL[s], D)) for s in range(NS)]
    QTs = [dscr(f"dbg_qt{s}", (8, 128, seqL[s]), BF16) for s in range(NS)]
    KTs = [dscr(f"dbg_kt{s}", (8, 128, seqL[s]), BF16) for s in range(NS)]
    Vs = [dscr(f"dbg_v{s}", (seqL[s], D), BF16) for s in range(NS)]
    RWs = [dscr(f"dbg_rw{s}", (27, 128, seqL[s])) for s in range(NS)]
    GTs = [dscr(f"dbg_gt{s}", (16, 128, seqL[s])) for s in range(NS)]
    OTs = [dscr(f"dbg_ot{s}", (D, seqL[s]), BF16) for s in range(NS)]
    YSs = [[dscr(f"dbg_y{d}{s}", (seqL[s], D)) for s in range(NS)] for d in range(2)]
    BNs = [[dscr(f"dbg_bn{d}{s}", (seqL[s], D)) for s in range(NS)] for d in range(2)]

    with ExitStack() as st:
        S = Sched(nc, st)

        uid = [0]

        def sb(name, shape, dt=F32, stack=st):
            uid[0] += 1
            return stack.enter_context(nc.sbuf_tensor(f"{name}_{uid[0]}", list(shape), dt))

        psT = [st.enter_context(nc.psum_tensor(f"psT{i}", [128, 1024], BF16)) for i in range(2)]
        psG = [st.enter_context(nc.psum_tensor(f"psG{i}", [128, 512], F32)) for i in range(4)]
        psD = [st.enter_context(nc.psum_tensor(f"psD{i}", [128, 512], F32)) for i in range(2)]
        bT = [Buf(True) for _ in range(2)]
        bG = [Buf(True) for _ in range(4)]
        bD = [Buf(True) for _ in range(2)]
        rr = {"T": 0, "G": 0, "D": 0}

        def nxt(kind, n):
            i = rr[kind]
            rr[kind] = (i + 1) % n
            return i

        ident = sb("ident", (128, 128), BF16)
        b_ident = Buf()
        S.dma("pool", ident[:], c_ident[:, :], writes=[b_ident])
        epsT = sb("epsT", (128, 2))
        b_eps = Buf()
        S.emit("pool", lambda e: e.memset(epsT[:, 0:1], EPS), writes=[b_eps])
        S.emit("pool", lambda e: e.memset(epsT[:, 1:2], LNX_EPS), writes=[b_eps])
        gains = sb("gains", (128, 4, 8))
        b_gains = Buf()
        for i, nm in enumerate(["ffn1_norm", "mix_norm", "ffn2_norm"]):
            S.dma("sp", gains[:, i, :], W[nm][0].rearrange("(c p) -> p c", p=128), writes=[b_gains],
                  allow_slow_non_contiguous=True)
        fin_g = sb("fin_g", (128, D))
        b_fin = Buf()
        S.dma("sp", fin_g[:], W["final_norm"].partition_broadcast(128), writes=[b_fin])

        tb = sb("tb", (128, 256))
        b_tb = Buf()
        S.dma("sp", tb[:], rel_bias.rearrange("b h -> (b h)").partition_broadcast(128), writes=[b_tb])
        lamv = sb("lamv", (128, 4, 64))
        b_lamv = Buf()
        for i, nm in enumerate(["attn_lambda_q1", "attn_lambda_k1", "attn_lambda_q2", "attn_lambda_k2"]):
            S.dma("sp", lamv[:, i, :], W[nm][0].partition_broadcast(128), writes=[b_lamv])
        lams = sb("lams", (128, 8))
        b_lams = Buf()
        for i in range(2):
            S.emit("dve", lambda e: e.tensor_tensor(out=lamv[:, 2 * i, :], in0=lamv[:, 2 * i, :], in1=lamv[:, 2 * i + 1, :],
                                                    op=ALU.mult), reads=[b_lamv], writes=[b_lamv])
            S.emit("dve", lambda e: e.reduce_sum(out=lams[:, i:i + 1], in_=lamv[:, 2 * i, :], axis=AX.X),
                   reads=[b_lamv], writes=[b_lams])
        S.emit("act", lambda e: e.activation(out=lams[:, 2:4], in_=lams[:, 0:2], func=AF.Exp), reads=[b_lams], writes=[b_lams])
        S.emit("dve", lambda e: e.tensor_tensor(out=lams[:, 4:5], in0=lams[:, 3:4], in1=lams[:, 2:3], op=ALU.subtract),
               reads=[b_lams], writes=[b_lams])
        LAM_INIT = 0.8 - 0.6 * math.exp(-0.3 * 0)
        S.emit("dve", lambda e: e.tensor_scalar(out=lams[:, 5:6], in0=lams[:, 4:5], scalar1=-LAM_INIT, scalar2=None, op0=ALU.add),
               reads=[b_lams], writes=[b_lams])
        gsub = sb("gsub", (128, 1))
        b_gsub = Buf()
        S.dma("sp", gsub[:], W["attn_subln"][0].rearrange("(p o) -> p o", o=1), writes=[b_gsub])
        S.emit("pool", lambda e: e.tensor_scalar(out=gsub[:], in0=gsub[:], scalar1=1.0 - LAM_INIT, scalar2=None, op0=ALU.mult),
               reads=[b_gsub], writes=[b_gsub])

        masks = sb("masks", (128, 4, 128))
        b_masks = Buf()
        S.dma("sp", masks[:], c_masks[:, :, :], writes=[b_masks])
        m12 = [sb(f"m12_{d}", (128, 256)) for d in range(2)]
        m345 = [sb(f"m345_{d}", (128, 384)) for d in range(2)]
        b_mm = Buf()
        for d in range(2):
            lo, up, loi, upi = (0, 1, 2, 3) if d == 0 else (1, 0, 3, 2)
            for (dst, off, mi) in [(m12[d], 0, lo), (m12[d], 128, up), (m345[d], 0, up), (m345[d], 128, upi), (m345[d], 256, upi)]:
                S.emit("pool", lambda e: e.tensor_copy(out=dst[:, off:off + 128], in_=masks[:, mi, :]), reads=[b_masks], writes=[b_mm])
        istack = sb("istack", (128, 64), BF16)
        bones = sb("bones", (128, 128), BF16)
        onesb = sb("onesb", (128, 1), BF16)
        b_rc = Buf()
        S.dma("pool", istack[:], c_istack[:, :], writes=[b_rc])
        S.dma("pool", bones[:], c_bones[:, :], writes=[b_rc])
        S.emit("pool", lambda e: e.memset(onesb[:], 1.0), writes=[b_rc])
        mus = sb("mus", (128, 3, 27))
        b_mus = Buf()
        for i, nm in enumerate(["rw_mu_prev", "rw_mu_next"]):
            S.dma("sp", mus[:, i, :], W[nm][0].rearrange("(c p) -> p c", p=128), writes=[b_mus], allow_slow_non_contiguous=True)
        S.emit("dve", lambda e: e.tensor_tensor(out=mus[:, 2, :], in0=mus[:, 0, :], in1=mus[:, 1, :], op=ALU.add), reads=[b_mus], writes=[b_mus])
        S.emit("dve", lambda e: e.tensor_scalar(out=mus[:, 2, :], in0=mus[:, 2, :], scalar1=-1.0, scalar2=1.0, op0=ALU.mult, op1=ALU.add),
               reads=[b_mus], writes=[b_mus])
        rwp = sb("rwp", (128, 7, 8))
        b_rwp = Buf()
        for i, src in enumerate([W["rw_w0"][0, 0], W["rw_w0"][0, 1], W["rw_a0"][0, 0], W["rw_a0"][0, 1], W["rw_k_k"][0], W["rw_k_a"][0],
                                 W["rw_r_k"][0].rearrange("h n -> (h n)")]):
            S.dma("sp", rwp[:, i, :], src.rearrange("(c p) -> p c", p=128), writes=[b_rwp], allow_slow_non_contiguous=True)
        w2sb = sb("w2sb", (128, D), BF16)
        a2sb = sb("a2sb", (128, D), BF16)
        g2sb = sb("g2sb", (128, D), BF16)
        b_w2 = Buf()
        S.dma("pool", w2sb[:], W["rw_w2"][0].rearrange("d r c -> (d r) c"), writes=[b_w2])
        S.dma("pool", a2sb[:], W["rw_a2"][0].rearrange("d r c -> (d r) c"), writes=[b_w2])
        S.dma("pool", g2sb[:], W["rw_g2"][0], writes=[b_w2])

        def prep_weights():
            for i, pre in enumerate(["ffn1", "ffn2"]):
                for which, nm in enumerate(["w_gate", "w_up"]):
                    src = W[f"{pre}_{nm}"][0].rearrange("(k p) (f j) -> f p k j", p=128, j=128)
                    for f in range(NF):
                        S.dma("pool", WGU[i][f, :, which, :, :], src[f])
                src = W[f"{pre}_w_down"][0].rearrange("(f p) (h j) -> h p f j", p=128, j=512)
                for h in range(2):
                    for f0 in range(0, NF, 11):
                        S.dma("pool", WD[i][h, :, f0:f0 + 11, :], src[h][:, f0:f0 + 11, :])
            src = W["w_in"][0].rearrange("(k p) (c j) -> c p k j", p=128, j=128)
            for c in range(67):
                if 16 <= c < 24:
                    continue
                S.dma("pool", WINS[c], src[c])
            src = W["w_in"][0][:, 2048:3072].rearrange("(k p) (h j) -> h p k j", p=128, j=512)
            for h in range(2):
                S.dma("pool", WVS[h], src[h])
            for (dst, nm) in [(WABS, "w_attn_branch"), (WRBS, "w_rw_branch")]:
                src = W[nm][0].rearrange("(k p) (c j) -> c p k j", p=128, j=128)
                for c in range(8):
                    S.dma("pool", dst[c], src[c])
            src = W["w_out"][0].rearrange("(k p) (h j) -> h p k j", p=128, j=512)
            for h in range(2):
                S.dma("pool", WOS[h], src[h])

        prep_weights()
        S.barrier()

        def load_x_block(s, t0, n, xt, bx):
            if t0 == 0:
                S.dma("sp", xt[0:NMETA, :], meta[:, :], writes=[bx])
                S.dma("sp", xt[NMETA:n, :], xsrc(s)[0:n - NMETA, :], writes=[bx])
            else:
                S.dma("sp", xt[0:n, :], xsrc(s)[t0 - NMETA:t0 - NMETA + n, :], writes=[bx])

        def rmsnorm_T(P, blks, xts, bxs, gi, outT, boutT):
            for (tg, o, n), xt, bx in zip(blks, xts, bxs):
                S.emit("pool", lambda e: e.memset(P["ss"][:, 0:1], 0.0), writes=[P["b_ss"]])
                S.emit("act", lambda e: e.activation(out=P["junk"][:n, :], in_=xt[:n, :], func=AF.Square,
                                                     accum_out=P["ss"][:n, 0:1]),
                       reads=[bx], writes=[P["b_ss"], P["b_junk"]])
                S.emit("act", lambda e: e.activation(out=P["ss"][:n, 1:2], in_=P["ss"][:n, 0:1], func=AF.Sqrt,
                                                     bias=epsT[:n, 0:1], scale=1.0 / D),
                       reads=[b_eps], writes=[P["b_ss"]])
                S.emit("dve", lambda e: e.reciprocal(out=P["ss"][:n, 2:3], in_=P["ss"][:n, 1:2]), writes=[P["b_ss"]])
                bi = o // 128
                S.emit("act", lambda e: e.activation(out=P["xn"][bi][:n, :], in_=xt[:n, :], func=AF.Copy,
                                                     scale=P["ss"][:n, 2:3]),
                       reads=[bx, P["b_ss"]], writes=[P["b_xn"][bi]])
            T = sum(b[2] for b in blks)
            for cp in range(4):
                bank = nxt("T", 2)
                for cc in range(2):
                    c = cp * 2 + cc
                    for (tg, o, n) in blks:
                        bi = o // 128
                        S.emit("pe", lambda e: e.transpose(psT[bank][:, cc * 512 + o:cc * 512 + o + n],
                                                           P["xn"][bi][:n, c * 128:(c + 1) * 128], ident[:n, :n]),
                               reads=[P["b_xn"][bi], b_ident], writes=[bT[bank]])
                for cc in range(2):
                    c = cp * 2 + cc
                    S.emit("dve", lambda e: e.tensor_scalar(out=outT[c][:, :T], in0=psT[bank][:, cc * 512:cc * 512 + T],
                                                            scalar1=gains[:, gi, c:c + 1], scalar2=None, op0=ALU.mult),
                           reads=[bT[bank], b_gains], writes=[boutT[c]])

        def ffn(P, fi, blks, xts, bxs, gi):
            T = sum(b[2] for b in blks)
            rmsnorm_T(P, blks, xts, bxs, gi, P["xT"], P["b_xT"])
            for f in range(NF):
                ws = nxt("wgu", len(P["wgu"]))
                S.dma("sp", P["wgu"][ws][:], WGU[fi][f], writes=[P["b_wgu"][ws]])
                banks = [nxt("G", 4), nxt("G", 4)]
                for which in range(2):
                    for k in range(8):
                        S.emit("pe", lambda e: e.matmul(psG[banks[which]][:, :T], P["wgu"][ws][:, which, k, :],
                                                        P["xT"][k][:, :T], start=(k == 0), stop=(k == 7)),
                               reads=[P["b_wgu"][ws], P["b_xT"][k]], writes=[bG[banks[which]]])
                sgi = nxt("sg", 2)
                S.emit("act", lambda e: e.activation(out=P["sg"][sgi][:, :T], in_=psG[banks[0]][:, :T], func=AF.Silu),
                       reads=[bG[banks[0]]], writes=[P["b_sg"][sgi]])
                S.emit("dve", lambda e: e.tensor_tensor(out=P["act"][f][:, :T], in0=P["sg"][sgi][:, :T],
                                                        in1=psG[banks[1]][:, :T], op=ALU.mult),
                       reads=[P["b_sg"][sgi], bG[banks[1]]], writes=[P["b_act"][f]])
            for h in range(2):
                S.dma("sp", P["wd"][:], WD[fi][h], writes=[P["b_wd"]])
                for (tg, o, n), xt, bx in zip(blks, xts, bxs):
                    bank = nxt("D", 2)
                    for f in range(NF):
                        S.emit("pe", lambda e: e.matmul(psD[bank][:n, :], P["act"][f][:, o:o + n], P["wd"][:, f, :],
                                                        start=(f == 0), stop=(f == NF - 1)),
                               reads=[P["b_act"][f], P["b_wd"]], writes=[bD[bank]])
                    S.emit("dve", lambda e: e.scalar_tensor_tensor(out=xt[:n, h * 512:(h + 1) * 512], in0=psD[bank][:n, :],
                                                                   scalar=0.5, in1=xt[:n, h * 512:(h + 1) * 512],
                                                                   op0=ALU.mult, op1=ALU.add),
                           reads=[bD[bank], bx], writes=[bx])

        def phase_A(s):
            L = seqL[s]
            with ExitStack() as pst:
                P = {}
                P["ss"] = sb("a_ss", (128, 4), stack=pst)
                P["b_ss"] = Buf()
                P["junk"] = sb("a_junk", (128, D), BF16, stack=pst)
                P["b_junk"] = Buf()
                P["xn"] = [sb(f"a_xn{i}", (128, D), BF16, stack=pst) for i in range(4)]
                P["b_xn"] = [Buf() for _ in range(4)]
                P["xT"] = [sb(f"a_xT{i}", (128, 512), BF16, stack=pst) for i in range(8)]
                P["b_xT"] = [Buf() for _ in range(8)]
                P["uT"] = [sb(f"a_uT{i}", (128, 512), BF16, stack=pst) for i in range(8)]
                P["b_uT"] = [Buf() for _ in range(8)]
                P["wgu"] = [sb(f"a_wgu{i}", (128, 2, 8, 128), BF16, stack=pst) for i in range(4)]
                P["b_wgu"] = [Buf() for _ in range(4)]
                P["sg"] = [sb(f"a_sg{i}", (128, 512), stack=pst) for i in range(2)]
                P["b_sg"] = [Buf() for _ in range(2)]
                P["act"] = [sb(f"a_act{i}", (128, 512), BF16, stack=pst) for i in range(NF)]
                P["b_act"] = [Buf() for _ in range(NF)]
                P["wd"] = sb("a_wd", (128, NF, 512), BF16, stack=pst)
                P["b_wd"] = Buf()
                rr["wgu"] = 0
                rr["sg"] = 0
                xsets = [[sb(f"a_x{j}_{i}", (128, D), stack=pst) for i in range(4)] for j in range(2)]
                bxsets = [[Buf() for _ in range(4)] for j in range(2)]
                win = [sb(f"a_win{i}", (128, 8, 128), BF16, stack=pst) for i in range(5)]
                b_win = [Buf() for _ in range(5)]
                wv = sb("a_wv", (128, 8, 512), BF16, stack=pst)
                b_wv = Buf()
                stf = [sb(f"a_stf{i}", (128, 512), stack=pst) for i in range(3)]
                b_stf = [Buf() for _ in range(3)]
                stb = [sb(f"a_stb{i}", (128, 512), BF16, stack=pst) for i in range(4)]
                b_stb = [Buf() for _ in range(4)]
                rr["win"] = 0
                rr["stf"] = 0
                rr["stb"] = 0
                tlist = tiles_of(L)

                def load_tile(ti):
                    t0_, T_ = tlist[ti]
                    bl = blocks_of(t0_, T_)
                    for (tg, o, n), xt, bx in zip(bl, xsets[ti % 2], bxsets[ti % 2]):
                        load_x_block(s, tg, n, xt, bx)

                load_tile(0)
                for ti, (t0, T) in enumerate(tlist):
                    blks = blocks_of(t0, T)
                    xts = xsets[ti % 2][:len(blks)]
                    bxs = bxsets[ti % 2][:len(blks)]
                    if ti + 1 < len(tlist):
                        load_tile(ti + 1)
                    ffn(P, 0, blks, xts, bxs, 0)
                    for (tg, o, n), xt, bx in zip(blks, xts, bxs):
                        S.dma("pool", Hs[s][tg:tg + n, :], xt[:n, :], reads=[bx])
                    rmsnorm_T(P, blks, xts, bxs, 1, P["uT"], P["b_uT"])
                    for c in list(range(0, 16)) + list(range(24, 67)):
                        ws = nxt("win", 5)
                        S.dma("sp", win[ws][:], WINS[c], writes=[b_win[ws]])
                        bank = nxt("G", 4)
                        for k in range(8):
                            S.emit("pe", lambda e: e.matmul(psG[bank][:, :T], win[ws][:, k, :], P["uT"][k][:, :T],
                                                            start=(k == 0), stop=(k == 7)),
                                   reads=[b_win[ws], P["b_uT"][k]], writes=[bG[bank]])
                        if c < 16:
                            si = nxt("stb", 4)
                            S.emit("act", lambda e: e.copy(out=stb[si][:, :T], in_=psG[bank][:, :T]),
                                   reads=[bG[bank]], writes=[b_stb[si]])
                            dst = (QTs if c < 8 else KTs)[s][c % 8, :, t0:t0 + T]
                            S.dma("pool", dst, stb[si][:, :T], reads=[b_stb[si]])
                        else:
                            si = nxt("stf", 3)
                            if c < 51:
                                S.emit("act", lambda e: e.copy(out=stf[si][:, :T], in_=psG[bank][:, :T]),
                                       reads=[bG[bank]], writes=[b_stf[si]])
                                dst = RWs[s][c - 24, :, t0:t0 + T]
                            else:
                                S.emit("act", lambda e: e.activation(out=stf[si][:, :T], in_=psG[bank][:, :T],
                                                                     func=AF.Sigmoid),
                                       reads=[bG[bank]], writes=[b_stf[si]])
                                dst = GTs[s][c - 51, :, t0:t0 + T]
                            S.dma("pool", dst, stf[si][:, :T], reads=[b_stf[si]])
                    for h in range(2):
                        S.dma("sp", wv[:], WVS[h], writes=[b_wv])
                        for (tg, o, n) in blks:
                            bank = nxt("D", 2)
                            for k in range(8):
                                S.emit("pe", lambda e: e.matmul(psD[bank][:n, :], P["uT"][k][:, o:o + n], wv[:, k, :],
                                                                start=(k == 0), stop=(k == 7)),
                                       reads=[P["b_uT"][k], b_wv], writes=[bD[bank]])
                            si = nxt("stb", 4)
                            S.emit("act", lambda e: e.copy(out=stb[si][:n, :], in_=psD[bank][:n, :]),
                                   reads=[bD[bank]], writes=[b_stb[si]])
                            S.dma("pool", Vs[s][tg:tg + n, h * 512:(h + 1) * 512], stb[si][:n, :], reads=[b_stb[si]])
                S.barrier()


        def phase_B(s):
            L = seqL[s]
            TQ = 384
            qtiles = tiles_of(L, TQ)
            kblocks = tiles_of(L, 128)
            nkb = len(kblocks)
            nfull = L // 128
            ntail = L - nfull * 128
            accb = [psD[0][:], psD[1][:], psT[1][:].bitcast(F32)]
            bacc = [bD[0], bD[1], bT[1]]
            with ExitStack() as pst:
                kt = [sb("b_kt", (128, L), BF16, pst) for _ in range(2)]
                qt = [sb("b_qt", (128, L), BF16, pst) for _ in range(2)]
                va = [sb("b_va", (128, nkb, 129), BF16, pst) for _ in range(2)]
                b_kqv = [Buf(), Buf()]
                PT = [sb("b_PT", (128, TQ), BF16, pst) for _ in range(4)]
                b_PT = [Buf() for _ in range(4)]
                tmpS = [sb("b_tmpS", (128, TQ), F32, pst) for _ in range(2)]
                b_tmpS = [Buf() for _ in range(2)]
                rd = sb("b_rd", (128, 8), F32, pst)
                b_rd = Buf()
                o1 = sb("b_o1", (128, 128), F32, pst)
                oo = sb("b_oo", (128, 128), F32, pst)
                junk = sb("b_junk", (128, 128), F32, pst)
                on = sb("b_on", (128, 128), BF16, pst)
                b_o = Buf()
                ost = [sb("b_ost", (128, TQ), BF16, pst) for _ in range(2)]
                b_ost = [Buf(), Buf()]
                rr["PT"] = 0
                rr["tmpS"] = 0
                rr["ost"] = 0
                for i in range(2):
                    S.emit("pool", lambda e: e.memset(va[i][:, :, 128:129], 1.0), writes=[b_kqv[i]])
                for h in range(8):
                    i = h % 2
                    S.dma("sp", kt[i][:], KTs[s][h], writes=[b_kqv[i]])
                    S.dma("sp", qt[i][:], QTs[s][h], writes=[b_kqv[i]])
                    if nfull:
                        S.dma("sp", va[i][:, 0:nfull, 0:128],
                              Vs[s][0:nfull * 128, h * 128:(h + 1) * 128].rearrange("(kb p) c -> p kb c", p=128),
                              writes=[b_kqv[i]])
                    if ntail:
                        S.dma("sp", va[i][0:ntail, nfull, 0:128], Vs[s][nfull * 128:L, h * 128:(h + 1) * 128],
                              writes=[b_kqv[i]])
                    for (qt0, TQn) in qtiles:
                        subs = blocks_of(qt0, TQn)

                        def qk(kbi):
                            k0, nk = kblocks[kbi]
                            banks = [nxt("G", 4), nxt("G", 4)]
                            for c in range(2):
                                S.emit("pe", lambda e: e.matmul(psG[banks[c]][:nk, :TQn], kt[i][c * 64:(c + 1) * 64, k0:k0 + nk],
                                                                qt[i][c * 64:(c + 1) * 64, qt0:qt0 + TQn], start=True, stop=True),
                                       reads=[b_kqv[i]], writes=[bG[banks[c]]])
                            return banks

                        nb = qk(0)
                        for kbi, (k0, nk) in enumerate(kblocks):
                            banks = nb
                            if kbi + 1 < nkb:
                                nb = qk(kbi + 1)
                            d = k0 - qt0
                            relmin = d - (TQn - 1)
                            relmax = d + nk - 1
                            near = not (relmin >= 91 or relmax <= -91)
                            for c in range(2):
                                pi = nxt("PT", 4)
                                if near:
                                    ti = nxt("tmpS", 2)
                                    m0 = 384 - d
                                    S.emit("dve", lambda e: e.scalar_tensor_tensor(
                                        out=tmpS[ti][:nk, :TQn], in0=psG[banks[c]][:nk, :TQn], scalar=0.125,
                                        in1=Gs[h][:nk, m0:m0 + TQn], op0=ALU.mult, op1=ALU.add),
                                        reads=[bG[banks[c]], b_Gs[h]], writes=[b_tmpS[ti]])
                                    S.emit("act", lambda e: e.activation(out=PT[pi][:nk, :TQn], in_=tmpS[ti][:nk, :TQn], func=AF.Exp),
                                           reads=[b_tmpS[ti]], writes=[b_PT[pi]])
                                else:
                                    col = (31 if relmin >= 91 else 15) * 8 + h
                                    S.emit("act", lambda e: e.activation(out=PT[pi][:nk, :TQn], in_=psG[banks[c]][:nk, :TQn], func=AF.Exp,
                                                                         scale=0.125, bias=tb[:nk, col:col + 1]),
                                           reads=[bG[banks[c]], b_tb], writes=[b_PT[pi]])
                                for si, (qg, o, nq) in enumerate(subs):
                                    first = (kbi == 0 and c == 0)
                                    last = (kbi == nkb - 1 and c == 1)
                                    S.emit("pe", lambda e: e.matmul(accb[si][:nq, c * 129:(c + 1) * 129], PT[pi][:nk, o:o + nq],
                                                                    va[i][:nk, kbi, :], start=first, stop=last),
                                           reads=[b_PT[pi], b_kqv[i]], writes=[bacc[si]])
                        oi = nxt("ost", 2)
                        for si, (qg, o, nq) in enumerate(subs):
                            A = accb[si]
                            S.emit("dve", lambda e: e.reciprocal(out=rd[:nq, 0:1], in_=A[:nq, 128:129]), reads=[bacc[si]], writes=[b_rd])
                            S.emit("dve", lambda e: e.reciprocal(out=rd[:nq, 1:2], in_=A[:nq, 257:258]), reads=[bacc[si]], writes=[b_rd])
                            S.emit("dve", lambda e: e.tensor_scalar(out=rd[:nq, 2:3], in0=rd[:nq, 1:2], scalar1=lams[:nq, 5:6],
                                                                    scalar2=None, op0=ALU.mult), reads=[b_rd, b_lams], writes=[b_rd])
                            S.emit("dve", lambda e: e.tensor_scalar(out=o1[:nq, :], in0=A[:nq, 0:128], scalar1=rd[:nq, 0:1], scalar2=None, op0=ALU.mult),
                                   reads=[bacc[si], b_rd], writes=[b_o])
                            S.emit("dve", lambda e: e.scalar_tensor_tensor(out=oo[:nq, :], in0=A[:nq, 129:257], scalar=rd[:nq, 2:3],
                                                                           in1=o1[:nq, :], op0=ALU.mult, op1=ALU.add),
                                   reads=[bacc[si], b_rd, b_o], writes=[b_o])
                            S.emit("pool", lambda e: e.memset(rd[:, 3:4], 0.0), writes=[b_rd])
                            S.emit("act", lambda e: e.activation(out=junk[:nq, :], in_=oo[:nq, :], func=AF.Square, accum_out=rd[:nq, 3:4]),
                                   reads=[b_o], writes=[b_rd, b_o])
                            S.emit("act", lambda e: e.activation(out=rd[:nq, 4:5], in_=rd[:nq, 3:4], func=AF.Sqrt, bias=epsT[:nq, 0:1],
                                                                 scale=1.0 / 128), reads=[b_rd, b_eps], writes=[b_rd])
                            S.emit("dve", lambda e: e.reciprocal(out=rd[:nq, 5:6], in_=rd[:nq, 4:5]), reads=[b_rd], writes=[b_rd])
                            S.emit("dve", lambda e: e.tensor_scalar(out=on[:nq, :], in0=oo[:nq, :], scalar1=rd[:nq, 5:6], scalar2=None, op0=ALU.mult),
                                   reads=[b_rd, b_o], writes=[b_o])
                            S.emit("pe", lambda e: e.transpose(psT[0][:, o:o + nq], on[:nq, :], ident[:nq, :nq]),
                                   reads=[b_o, b_ident], writes=[bT[0]])
                            S.emit("dve", lambda e: e.tensor_scalar(out=ost[oi][:, o:o + nq], in0=psT[0][:, o:o + nq], scalar1=gsub[:, 0:1],
                                                                    scalar2=None, op0=ALU.mult),
                                   reads=[bT[0], b_gsub], writes=[b_ost[oi]])
                        S.dma("pool", OTs[s][h * 128:(h + 1) * 128, qt0:qt0 + TQn], ost[oi][:, :TQn], reads=[b_ost[oi]])
                S.barrier()

        with ExitStack() as gst:
            bk = sb("bk", (128, 896), F32, gst)
            b_bk = Buf()
            S.dma("sp", bk[:], c_bk[:, :], writes=[b_bk])
            Gs = [sb(f"G{h}", (128, 896), F32, gst) for h in range(8)]
            b_Gs = [Buf() for _ in range(8)]
            gm = [sb("gm", (128, 896), F32, gst) for _ in range(1)]
            b_gm = [Buf()]
            if "A" in phases:
                phase_A(0)
            for h in range(8):
                S.emit("pool", lambda e: e.memset(Gs[h][:], 0.0), writes=[b_Gs[h]])
            for b in range(32):
                mi = 0
                S.emit("pool", lambda e: e.tensor_scalar(out=gm[mi][:], in0=bk[:], scalar1=float(b), scalar2=None, op0=ALU.is_equal),
                       reads=[b_bk], writes=[b_gm[mi]])
                for h in range(8):
                    S.emit("dve", lambda e: e.scalar_tensor_tensor(out=Gs[h][:], in0=gm[mi][:], scalar=tb[:, b * 8 + h:b * 8 + h + 1], in1=Gs[h][:],
                                                                   op0=ALU.mult, op1=ALU.add), reads=[b_gm[mi], b_tb], writes=[b_Gs[h]])

            for s in range(1, NS):
                if "A" in phases:
                    phase_A(s)
            for s in range(NS):
                if "B" in phases:
                    phase_B(s)
            S.barrier()

        def phase_C(s):
            L = seqL[s]
            SEG = 128
            Lp = ((L + 63) // 64) * 64
            segs = tiles_of(Lp, SEG)
            nseg = len(segs)
            allb = [(psG[i][:], bG[i]) for i in range(4)] + [(psD[i][:], bD[i]) for i in range(2)] + [(psT[1][:].bitcast(F32), bT[1])]
            rr["ps"] = 0

            def bank():
                i = nxt("ps", len(allb))
                return allb[i]

            EXC = 2.0 ** 0
            with ExitStack() as pst:
                NG = 8
                def mk(name, shape, dt=F32, n=1):
                    return [sb(name, shape, dt, pst) for _ in range(n)]
                nchm = SEG // 64
                RAW = [sb("c_RAW", (128, NG, SEG + 2), F32, pst) for _ in range(3)]
                b_RAW = [Buf() for _ in range(3)]
                rawd = [mk("c_rawd", (128, SEG + 2), F32, 2) for _ in range(2)]
                b_raw = [Buf(), Buf()]
                NOPS = 6
                EXP = [[sb("c_EXP", (128, NG, nchm, 128), BF16, pst) for o in range(NOPS)] for _ in range(2)]
                PEND = [sb("c_PEND", (128, NG, nchm), F32, pst) for _ in range(2)]
                RPEND = sb("c_RPEND", (128, NG, nchm), F32, pst)
                b_rpend = Buf()
                b_expg = [Buf(), Buf()]
                exp_ = [[[EXP[gb][o][:, u] for o in range(NOPS)] for u in range(NG)] for gb in range(2)]
                b_exp = [[b_expg[gb] for u in range(NG)] for gb in range(2)]
                pend = [[PEND[gb][:, u] for u in range(NG)] for gb in range(2)]
                for gb in range(2):
                    for o in range(NOPS):
                        S.emit("pool", lambda e: e.memset(EXP[gb][o][:], 0.0), writes=[b_expg[gb]])
                TN = ["rs", "ks", "vs", "kk", "t1", "t2", "w", "al", "kd", "bb", "P", "rW", "Wp", "Wi", "rw", "sg"]
                TB = {nm: sb("c_" + nm, (128, NG, SEG), F32, pst) for nm in TN}
                BB = {nm: Buf() for nm in TN}
                SQ = sb("c_SQ", (128, NG, SEG), BF16, pst)
                b_SQ = Buf()
                zer = sb("c_zer", (128, 64), F32, pst)
                b_zer = Buf()
                S.emit("pool", lambda e: e.memset(zer[:], 0.0), writes=[b_zer])
                dsh = [[sb("c_dsh", (128, SEG), F32, pst) for _ in range(2)] for _ in range(2)]
                twd = [sb("c_twd", (128, SEG), BF16, pst) for _ in range(2)]
                tad = [sb("c_tad", (128, SEG), BF16, pst) for _ in range(2)]
                b_d = [Buf(), Buf()]
                Sf = [[sb("c_S", (128, 64), F32, pst) for c in range(8)] for d in range(2)]
                Sb = [[sb("c_Sb", (128, 64), BF16, pst) for c in range(8)] for d in range(2)]
                b_S = [[Buf() for c in range(8)] for d in range(2)]
                for d in range(2):
                    for c in range(8):
                        S.emit("pool", lambda e: e.memset(Sf[d][c][:], 0.0), writes=[b_S[d][c]])
                        S.emit("pool", lambda e: e.memset(Sb[d][c][:], 0.0), writes=[b_S[d][c]])
                PP = [[sb("c_PP", (128, 256), BF16, pst) for _ in range(2)] for u in range(NG)]
                b_PP = [[Buf(), Buf()] for u in range(NG)]
                L3 = [sb("c_L3", (128, 384), BF16, pst) for u in range(NG)]
                b_L3 = [Buf() for u in range(NG)]
                TT = [[sb("c_TT", (128, 128), BF16, pst) for _ in range(2)] for u in range(NG)]
                b_TT = [[Buf(), Buf()] for u in range(NG)]
                BK_ = [sb("c_BK", (128, 256), BF16, pst) for u in range(NG)]
                b_BK = [Buf() for u in range(NG)]
                Vst = [sb("c_Vst", (128, 64), BF16, pst) for u in range(NG)]
                coef = [sb("c_coef", (128, 1), F32, pst) for u in range(NG)]
                b_V = [Buf() for u in range(NG)]
                Xb = [sb("c_Xb", (128, 64), BF16, pst) for u in range(NG)]
                Ub = [sb("c_Ub", (128, 64), BF16, pst) for u in range(NG)]
                b_X = [Buf() for u in range(NG)]
                b_U = [Buf() for u in range(NG)]
                Yst = [sb("c_Yst", (128, NG, 64), F32, pst) for _ in range(2)]
                Bst = [sb("c_Bst", (128, NG, 64), F32, pst) for _ in range(2)]
                b_Yst = [Buf(), Buf()]
                b_Bst = [Buf(), Buf()]
                rr["yst"] = 0

                def shift(dst, src, j, n, rb, wb, eng="dve"):
                    S.emit(eng, lambda e: e.tensor_scalar(out=dst[:, :n], in0=src[:, 1:n + 1], scalar1=mus[:, 2, j:j + 1], scalar2=None, op0=ALU.mult),
                           reads=[b_mus] + rb, writes=wb)
                    S.emit(eng, lambda e: e.scalar_tensor_tensor(out=dst[:, :n], in0=src[:, 0:n], scalar=mus[:, 0, j:j + 1], in1=dst[:, :n],
                                                                 op0=ALU.mult, op1=ALU.add), reads=[b_mus] + rb, writes=wb)
                    S.emit(eng, lambda e: e.scalar_tensor_tensor(out=dst[:, :n], in0=src[:, 2:n + 2], scalar=mus[:, 1, j:j + 1], in1=dst[:, :n],
                                                                 op0=ALU.mult, op1=ALU.add), reads=[b_mus] + rb, writes=wb)

                def load_group(gb, d, half, seg0, n):
                    lo = max(seg0 - 1, 0)
                    hi = min(seg0 + n + 1, L)
                    edge = (seg0 - 1 < 0) or (seg0 + n + 1 > L)
                    for i in range(3):
                        if edge:
                            S.emit("pool", lambda e: e.memset(RAW[i][:], 0.0), writes=[b_RAW[i]])
                        S.dma("sp", RAW[i][:, :, lo - (seg0 - 1):hi - (seg0 - 1)],
                              RWs[s][i * 8:(i + 1) * 8, :, lo:hi].rearrange("u p t -> p u t"), writes=[b_RAW[i]])
                    for (t, j) in [(rawd[gb][0], 24), (rawd[gb][1], 25)]:
                        if edge:
                            S.emit("pool", lambda e: e.memset(t[:], 0.0), writes=[b_raw[gb]])
                        S.dma("sp", t[:, lo - (seg0 - 1):hi - (seg0 - 1)], RWs[s][j, :, lo:hi], writes=[b_raw[gb]])

                def prep_group(gb, d, half, seg0, n):
                    nch = n // 64
                    npad0 = max(0, min(n, L - seg0))
                    bex = b_expg[gb]

                    def op(eng, fn, R=(), Wr=(), xr=(), xw=()):
                        S.emit(eng, fn, reads=[BB[x] for x in R] + list(xr), writes=[BB[x] for x in Wr] + list(xw))

                    def T(nm):
                        return TB[nm][:, :, :n]

                    def bc(ap2):
                        return ap2.unsqueeze(2).to_broadcast([128, NG, n])

                    shift(dsh[gb][0], rawd[gb][0], 24, n, [b_raw[gb]], [b_d[gb]])
                    shift(dsh[gb][1], rawd[gb][1], 25, n, [b_raw[gb]], [b_d[gb]])
                    S.emit("act", lambda e: e.activation(out=twd[gb][:, :n], in_=dsh[gb][0][:, :n], func=AF.Tanh), reads=[b_d[gb]], writes=[b_d[gb]])
                    S.emit("act", lambda e: e.copy(out=tad[gb][:, :n], in_=dsh[gb][1][:, :n]), reads=[b_d[gb]], writes=[b_d[gb]])
                    yield
                    for i, nm in enumerate(["rs", "ks", "vs"]):
                        j0 = i * 8
                        tmpn = "t1" if i % 2 == 0 else "t2"
                        op("dve", lambda e: e.tensor_tensor(out=T(nm), in0=RAW[i][:, :, 1:n + 1], in1=bc(mus[:, 2, j0:j0 + NG]), op=ALU.mult),
                           Wr=[nm], xr=[b_RAW[i], b_mus])
                        op("pool", lambda e: e.tensor_tensor(out=T(tmpn), in0=RAW[i][:, :, 0:n], in1=bc(mus[:, 0, j0:j0 + NG]), op=ALU.mult),
                           Wr=[tmpn], xr=[b_RAW[i], b_mus])
                        op("dve", lambda e: e.tensor_tensor(out=T(nm), in0=T(nm), in1=T(tmpn), op=ALU.add), R=[nm, tmpn], Wr=[nm])
                        op("pool", lambda e: e.tensor_tensor(out=T(tmpn), in0=RAW[i][:, :, 2:n + 2], in1=bc(mus[:, 1, j0:j0 + NG]), op=ALU.mult),
                           R=[nm], Wr=[tmpn], xr=[b_RAW[i], b_mus])
                        op("dve", lambda e: e.tensor_tensor(out=T(nm), in0=T(nm), in1=T(tmpn), op=ALU.add), R=[nm, tmpn], Wr=[nm])
                        if npad0 < n:
                            op("pool", lambda e: e.memset(TB[nm][:, :, npad0:n], 0.0), Wr=[nm])
                        yield
                    op("pool", lambda e: e.tensor_tensor(out=T("kk"), in0=T("ks"), in1=bc(rwp[:, 4, 0:NG]), op=ALU.mult), R=["ks"], Wr=["kk"], xr=[b_rwp])
                    op("dve", lambda e: e.tensor_tensor(out=SQ[:, :, :n], in0=T("kk"), in1=T("kk"), op=ALU.mult), R=["kk"], xw=[b_SQ])
                    for j in range(NG // 4):
                        (pa, bpa) = bank()
                        for uu in range(4):
                            u = j * 4 + uu
                            S.emit("pe", lambda e: e.matmul(pa[:, uu * 128:uu * 128 + n], bones[:], SQ[:, u, :n], start=True, stop=True),
                                   reads=[b_SQ, b_rc], writes=[bpa])
                        op("dve", lambda e: e.tensor_scalar(out=TB["t1"][:, j * 4:(j + 1) * 4, :n], in0=pa[:, :].rearrange("p (u t) -> p u t", t=128)[:, :, :n],
                                                            scalar1=1e-24, scalar2=None, op0=ALU.max), Wr=["t1"], xr=[bpa])
                    op("act", lambda e: e.activation(out=T("t1"), in_=T("t1"), func=AF.Ln), R=["t1"], Wr=["t1"])
                    op("act", lambda e: e.activation(out=T("t1"), in_=T("t1"), func=AF.Exp, scale=-0.5), R=["t1"], Wr=["t1"])
                    op("dve", lambda e: e.tensor_tensor(out=T("kk"), in0=T("kk"), in1=T("t1"), op=ALU.mult), R=["kk", "t1"], Wr=["kk"])
                    yield
                    rows = slice(d * 64, (d + 1) * 64)
                    for (wsb, src, dstn, bi) in [(w2sb, twd, "sg", 0), (a2sb, tad, "al", 2)]:
                        for j in range(NG // 4):
                            (pw, bpw) = bank()
                            for uu in range(4):
                                u = j * 4 + uu
                                S.emit("pe", lambda e: e.matmul(pw[:, uu * 128:uu * 128 + n], wsb[rows, u * 128:(u + 1) * 128], src[gb][rows, :n],
                                                                start=True, stop=True), reads=[b_w2, b_d[gb]], writes=[bpw])
                            for uu in range(4):
                                u = j * 4 + uu
                                op("act", lambda e: e.activation(out=TB[dstn][:, u, :n], in_=pw[:, uu * 128:uu * 128 + n], func=AF.Sigmoid,
                                                                 bias=rwp[:, bi + d, u:u + 1]), Wr=[dstn], xr=[bpw, b_rwp])
                        yield
                    op("act", lambda e: e.activation(out=T("w"), in_=T("sg"), func=AF.Exp, scale=-math.exp(-0.5)), R=["sg"], Wr=["w"])
                    op("act", lambda e: e.activation(out=T("rw"), in_=T("sg"), func=AF.Exp, scale=math.exp(-0.5)), R=["sg"], Wr=["rw"])
                    if npad0 < n:
                        op("pool", lambda e: e.memset(TB["w"][:, :, npad0:n], 1.0), Wr=["w"])
                        op("pool", lambda e: e.memset(TB["rw"][:, :, npad0:n], 1.0), Wr=["rw"])
                    op("dve", lambda e: e.tensor_scalar(out=T("t1"), in0=T("al"), scalar1=-1.0, scalar2=None, op0=ALU.add), R=["al"], Wr=["t1"])
                    op("pool", lambda e: e.tensor_tensor(out=T("t1"), in0=T("t1"), in1=bc(rwp[:, 5, 0:NG]), op=ALU.mult), R=["t1"], Wr=["t1"], xr=[b_rwp])
                    op("dve", lambda e: e.scalar_tensor_tensor(out=T("kd"), in0=T("t1"), scalar=1.0, in1=T("ks"), op0=ALU.add, op1=ALU.mult),
                       R=["t1", "ks"], Wr=["kd"])
                    op("pool", lambda e: e.tensor_tensor(out=T("bb"), in0=T("kk"), in1=T("al"), op=ALU.mult), R=["kk", "al"], Wr=["bb"])
                    yield
                    for u in range(NG):
                        for c_ in range(nch):
                            op("dve", lambda e: e.tensor_tensor_scan(out=TB["P"][:, u, c_ * 64:(c_ + 1) * 64], data0=TB["w"][:, u, c_ * 64:(c_ + 1) * 64],
                                                                     data1=zer[:, :], initial=1.0, op0=ALU.mult, op1=ALU.add),
                               R=["w"], Wr=["P"], xr=[b_zer])
                            op("dve", lambda e: e.tensor_tensor_scan(out=TB["rW"][:, u, c_ * 64:(c_ + 1) * 64], data0=TB["rw"][:, u, c_ * 64:(c_ + 1) * 64],
                                                                     data1=zer[:, :], initial=1.0, op0=ALU.mult, op1=ALU.add),
                               R=["rw"], Wr=["rW"], xr=[b_zer])
                    yield
                    P4 = T("P").rearrange("p u (c t) -> p u c t", t=64)
                    op("dve", lambda e: e.tensor_copy(out=PEND[gb][:, :, :nch], in_=P4[:, :, :, 63]), R=["P"], xw=[bex])
                    pe4 = PEND[gb][:, :, :nch].unsqueeze(3).to_broadcast([128, NG, nch, 64])

                    def v4(nm):
                        return T(nm).rearrange("p u (c t) -> p u c t", t=64)
                    if d == 0:
                        op("pool", lambda e: e.tensor_tensor(out=T("Wp"), in0=T("P"), in1=T("rw"), op=ALU.mult), R=["P", "rw"], Wr=["Wp"])
                        Wi, rW = "P", "rW"
                    else:
                        rP4 = T("rW").rearrange("p u (c t) -> p u c t", t=64)
                        op("dve", lambda e: e.tensor_copy(out=RPEND[:, :, :nch], in_=rP4[:, :, :, 63]), R=["rW"], xw=[b_rpend])
                        rpe4 = RPEND[:, :, :nch].unsqueeze(3).to_broadcast([128, NG, nch, 64])
                        op("pool", lambda e: e.tensor_tensor(out=T("t1"), in0=T("P"), in1=T("rw"), op=ALU.mult), R=["P", "rw"], Wr=["t1"])
                        op("dve", lambda e: e.tensor_tensor(out=v4("Wp"), in0=v4("rW"), in1=pe4, op=ALU.mult), R=["rW"], Wr=["Wp"], xr=[bex])
                        op("pool", lambda e: e.tensor_tensor(out=T("Wi"), in0=T("Wp"), in1=T("w"), op=ALU.mult), R=["Wp", "w"], Wr=["Wi"])
                        op("dve", lambda e: e.tensor_tensor(out=v4("t1"), in0=v4("t1"), in1=rpe4, op=ALU.mult), R=["t1"], Wr=["t1"], xr=[b_rpend])
                        Wi, rW = "Wi", "t1"
                    yield

                    def halves(o, nm):
                        for hh in range(2):
                            ps_ = slice(hh * 64, (hh + 1) * 64)
                            yield (EXP[gb][o][ps_, :, 0:nch, hh * 64:(hh + 1) * 64],
                                   lambda name: TB[name][ps_, :, :n].rearrange("p u (c t) -> p u c t", t=64))
                    for (oap, iv) in halves(0, None):
                        op("dve", lambda e: e.scalar_tensor_tensor(out=oap, in0=iv("kk"), scalar=-1.0, in1=iv("Wp"), op0=ALU.mult, op1=ALU.mult),
                           R=["kk", "Wp"], xw=[bex])
                    for (oap, iv) in halves(1, None):
                        op("pool", lambda e: e.tensor_tensor(out=oap, in0=iv("rs"), in1=iv(Wi), op=ALU.mult), R=["rs", Wi], xw=[bex])
                    yield
                    for (oap, iv) in halves(2, None):
                        op("dve", lambda e: e.tensor_tensor(out=oap, in0=iv("bb"), in1=iv(rW), op=ALU.mult), R=["bb", rW], xw=[bex])
                    for (oap, iv) in halves(3, None):
                        op("pool", lambda e: e.tensor_tensor(out=oap, in0=iv("kd"), in1=iv(rW), op=ALU.mult), R=["kd", rW], xw=[bex])
                    yield
                    for (oap, iv) in halves(4, None):
                        op("act", lambda e: e.copy(out=oap, in_=iv("vs")), R=["vs"], xw=[bex])
                    op("pool", lambda e: e.tensor_tensor(out=T("t2"), in0=T("rs"), in1=bc(rwp[:, 6, 0:NG]), op=ALU.mult), R=["rs"], Wr=["t2"], xr=[b_rwp])
                    for (oap, iv) in halves(5, None):
                        op("dve", lambda e: e.tensor_tensor(out=oap, in0=iv("t2"), in1=iv("kd"), op=ALU.mult), R=["t2", "kd"], xw=[bex])
                    yield

                def chunk_group(gb, d, half, seg0, n, ch):
                    tok0 = seg0 + ch * 64
                    nt = max(0, min(64, L - tok0))
                    units = range(NG)
                    yi = nxt("yst", 2)
                    for u in units:
                        E = exp_[gb][u]
                        A, Rr, B, K, V, RK = [E[o][:, ch, :] for o in range(NOPS)]
                        be = b_exp[gb][u]
                        (p1, bp1) = bank()
                        S.emit("pe", lambda e: e.matmul(p1[:, 0:128], A, B, start=True, stop=True), reads=[be], writes=[bp1])
                        S.emit("pe", lambda e: e.matmul(p1[:, 128:256], B, A, start=True, stop=True), reads=[be], writes=[bp1])
                        S.emit("dve", lambda e: e.tensor_tensor(out=PP[u][0][:], in0=p1[:, 0:256], in1=m12[d][:], op=ALU.mult),
                               reads=[bp1, b_mm], writes=[b_PP[u][0]])
                        S.emit("dve", lambda e: e.tensor_tensor(out=TT[u][0][:], in0=PP[u][0][:, 128:256], in1=ident[:], op=ALU.add),
                               reads=[b_PP[u][0], b_ident], writes=[b_TT[u][0]])
                        (p2, bp2) = bank()
                        S.emit("pe", lambda e: e.matmul(p2[:, 0:128], K, A, start=True, stop=True), reads=[be], writes=[bp2])
                        S.emit("pe", lambda e: e.matmul(p2[:, 128:256], B, Rr, start=True, stop=True), reads=[be], writes=[bp2])
                        S.emit("pe", lambda e: e.matmul(p2[:, 256:384], K, Rr, start=True, stop=True), reads=[be], writes=[bp2])
                        S.emit("dve", lambda e: e.tensor_tensor(out=L3[u][:], in0=p2[:, 0:384], in1=m345[d][:], op=ALU.mult),
                               reads=[bp2, b_mm], writes=[b_L3[u]])
                        (p3, bp3) = bank()
                        p3b = p3.bitcast(BF16)
                        S.emit("pe", lambda e: e.transpose(p3b[:, 0:128], B, ident[:]), reads=[be, b_ident], writes=[bp3])
                        S.emit("pe", lambda e: e.transpose(p3b[:, 128:256], K, ident[:]), reads=[be, b_ident], writes=[bp3])
                        S.emit("pe", lambda e: e.matmul(p3[:, 128:192], V, istack[:], start=True, stop=True), reads=[be, b_rc], writes=[bp3])
                        S.emit("pe", lambda e: e.matmul(p3[:, 192:193], RK, onesb[:], start=True, stop=True), reads=[be, b_rc], writes=[bp3])
                        S.emit("act", lambda e: e.copy(out=BK_[u][:], in_=p3b[:, 0:256]), reads=[bp3], writes=[b_BK[u]])
                        S.emit("act", lambda e: e.copy(out=Vst[u][:], in_=p3[:, 128:192]), reads=[bp3], writes=[b_V[u]])
                        S.emit("act", lambda e: e.copy(out=coef[u][:], in_=p3[:, 192:193]), reads=[bp3], writes=[b_V[u]])
                        S.emit("act", lambda e: e.activation(out=Bst[yi][:, u, :], in_=Vst[u][:], func=AF.Copy, scale=coef[u][:, 0:1]),
                               reads=[b_V[u]], writes=[b_Bst[yi]])
                    yield
                    for j in range(6):
                        if j > 0:
                            yield
                        cur, new = j % 2, (j + 1) % 2
                        for u in units:
                            (pq, bpq) = bank()
                            Pj = PP[u][cur][:, 0:128]
                            PjT = PP[u][cur][:, 128:256]
                            if j < 5:
                                S.emit("pe", lambda e: e.matmul(pq[:, 0:128], PjT, Pj, start=True, stop=True), reads=[b_PP[u][cur]], writes=[bpq])
                                S.emit("pe", lambda e: e.matmul(pq[:, 128:256], Pj, PjT, start=True, stop=True), reads=[b_PP[u][cur]], writes=[bpq])
                            if j >= 1:
                                tc_, tn_ = (j - 1) % 2, j % 2
                                S.emit("pe", lambda e: e.matmul(pq[:, 256:384], ident[:], TT[u][tc_][:], start=True, stop=False),
                                       reads=[b_ident, b_TT[u][tc_]], writes=[bpq])
                                S.emit("pe", lambda e: e.matmul(pq[:, 256:384], Pj, TT[u][tc_][:], start=False, stop=True),
                                       reads=[b_PP[u][cur], b_TT[u][tc_]], writes=[bpq])
                                if u % 4 != 3:
                                    S.emit("act", lambda e: e.copy(out=TT[u][tn_][:], in_=pq[:, 256:384]), reads=[bpq], writes=[b_TT[u][tn_]])
                                else:
                                    S.emit("dve", lambda e: e.tensor_copy(out=TT[u][tn_][:], in_=pq[:, 256:384]), reads=[bpq], writes=[b_TT[u][tn_]])
                            if j < 5:
                                if u % 4 != 3:
                                    S.emit("act", lambda e: e.copy(out=PP[u][new][:], in_=pq[:, 0:256]), reads=[bpq], writes=[b_PP[u][new]])
                                else:
                                    S.emit("dve", lambda e: e.tensor_copy(out=PP[u][new][:], in_=pq[:, 0:256]), reads=[bpq], writes=[b_PP[u][new]])
                    tfin = 5 % 2
                    yield
                    for u in units:
                        c = half * NG + u
                        E = exp_[gb][u]
                        A = E[0][:, ch, :]
                        (px, bpx) = bank()
                        S.emit("pe", lambda e: e.matmul(px[:, 0:64], A, Sb[d][c][:], start=True, stop=False), reads=[b_exp[gb][u], b_S[d][c]], writes=[bpx])
                        S.emit("pe", lambda e: e.matmul(px[:, 0:64], L3[u][:, 0:128], Vst[u][:], start=False, stop=True), reads=[b_L3[u], b_V[u]], writes=[bpx])
                        S.emit("act", lambda e: e.copy(out=Xb[u][:], in_=px[:, 0:64]), reads=[bpx], writes=[b_X[u]])
                    yield
                    for u in units:
                        (pu, bpu) = bank()
                        S.emit("pe", lambda e: e.matmul(pu[:, 0:64], TT[u][tfin][:], Xb[u][:], start=True, stop=True), reads=[b_TT[u][tfin], b_X[u]], writes=[bpu])
                        S.emit("dve", lambda e: e.tensor_copy(out=Ub[u][:], in_=pu[:, 0:64]), reads=[bpu], writes=[b_U[u]])
                    yield
                    for u in units:
                        c = half * NG + u
                        E = exp_[gb][u]
                        Rr = E[1][:, ch, :]
                        (py, bpy) = bank()
                        S.emit("pe", lambda e: e.matmul(py[:, 0:64], Rr, Sb[d][c][:], start=True, stop=False), reads=[b_exp[gb][u], b_S[d][c]], writes=[bpy])
                        S.emit("pe", lambda e: e.matmul(py[:, 0:64], L3[u][:, 128:256], Ub[u][:], start=False, stop=False), reads=[b_L3[u], b_U[u]], writes=[bpy])
                        S.emit("pe", lambda e: e.matmul(py[:, 0:64], L3[u][:, 256:384], Vst[u][:], start=False, stop=True), reads=[b_L3[u], b_V[u]], writes=[bpy])
                        S.emit("act", lambda e: e.copy(out=Yst[yi][:, u, :], in_=py[:, 0:64]), reads=[bpy], writes=[b_Yst[yi]])
                        (pS, bpS) = bank()
                        S.emit("pe", lambda e: e.matmul(pS[:, 0:64], BK_[u][:, 0:128], Ub[u][:], start=True, stop=False), reads=[b_BK[u], b_U[u]], writes=[bpS])
                        S.emit("pe", lambda e: e.matmul(pS[:, 0:64], BK_[u][:, 128:256], Vst[u][:], start=False, stop=True), reads=[b_BK[u], b_V[u]], writes=[bpS])
                        S.emit("dve", lambda e: e.tensor_tensor(out=Sf[d][c][:], in0=Sf[d][c][:], in1=pS[:, 0:64], op=ALU.add),
                               reads=[bpS], writes=[b_S[d][c]])
                        S.emit("dve", lambda e: e.tensor_scalar(out=Sf[d][c][:], in0=Sf[d][c][:], scalar1=pend[gb][u][:, ch:ch + 1], scalar2=None, op0=ALU.mult),
                               reads=[b_exp[gb][u]], writes=[b_S[d][c]])
                        S.emit("act", lambda e: e.copy(out=Sb[d][c][:], in_=Sf[d][c][:]), reads=[b_S[d][c]], writes=[b_S[d][c]])
                    yield
                    if nt > 0:
                        for (stg, bst, dst) in [(Yst[yi], b_Yst[yi], YSs[d][s]), (Bst[yi], b_Bst[yi], BNs[d][s])]:
                            dv = dst[tok0:tok0 + nt, :].rearrange("t (c h v) -> t c h v", h=2, v=64)
                            for hh in range(2):
                                S.dma("sp", dv[:, half * NG:(half + 1) * NG, hh, :], stg[hh * 64:hh * 64 + nt, :, :], reads=[bst])

                work = []
                for i in range(nseg):
                    for d in range(2):
                        si = i if d == 0 else nseg - 1 - i
                        for half in range(8 // NG):
                            work.append((d, half, segs[si][0], segs[si][1]))
                def run_all(gen):
                    for _ in gen:
                        pass

                def chunks_gen(wi):
                    d, half, seg0, n = work[wi]
                    chs = list(range(n // 64))
                    if d == 1:
                        chs = chs[::-1]
                    for ch in chs:
                        yield from chunk_group(wi % 2, d, half, seg0, n, ch)

                NW = len(work)
                load_group(0, *work[0])
                run_all(prep_group(0, *work[0]))
                if NW > 1:
                    load_group(1, *work[1])
                for wi in range(NW):
                    pg = prep_group((wi + 1) % 2, *work[wi + 1]) if wi + 1 < NW else iter(())
                    nsteps = 14 * (work[wi][3] // 64)
                    psteps = 14 if wi + 1 < NW else 0
                    per = -(-psteps // max(nsteps, 1)) if psteps else 0
                    first = True
                    for _ in chunks_gen(wi):
                        for _k in range(per):
                            next(pg, None)
                    run_all(pg)
                    if wi + 2 < NW:
                        load_group(wi % 2, *work[wi + 2])
                S.barrier()

        for s in range(NS):
            if "C" in phases:
                phase_C(s)

        lnx = sb("lnx", (128, 2, D))
        b_lnx = Buf()
        S.dma("sp", lnx[:, 0, :], W["rw_lnx_w"][0].partition_broadcast(128), writes=[b_lnx])
        S.dma("sp", lnx[:, 1, :], W["rw_lnx_b"][0].partition_broadcast(128), writes=[b_lnx])

        def phase_D(s):
            L = seqL[s]
            TD = 512
            with ExitStack() as pst:
                P = {}
                P["ss"] = sb("d_ss", (128, 4), F32, pst)
                P["b_ss"] = Buf()
                P["junk"] = sb("d_junk", (128, D), BF16, pst)
                P["b_junk"] = Buf()
                P["xn"] = [sb("d_xn", (128, D), BF16, pst) for i in range(4)]
                P["b_xn"] = [Buf() for _ in range(4)]
                P["xT"] = [sb("d_xT", (128, TD), BF16, pst) for i in range(8)]
                P["b_xT"] = [Buf() for _ in range(8)]
                P["wgu"] = [sb("d_wgu", (128, 2, 8, 128), BF16, pst) for i in range(3)]
                P["b_wgu"] = [Buf() for _ in range(3)]
                P["sg"] = [sb("d_sg", (128, TD), F32, pst) for i in range(2)]
                P["b_sg"] = [Buf() for _ in range(2)]
                P["act"] = [sb("d_act", (128, TD), BF16, pst) for i in range(NF)]
                P["b_act"] = [Buf() for _ in range(NF)]
                P["wd"] = sb("d_wd", (128, NF, 512), BF16, pst)
                P["b_wd"] = Buf()
                rr["wgu"] = 0
                rr["sg"] = 0
                hts = [sb("d_h", (128, D), F32, pst) for i in range(4)]
                b_h = [Buf() for _ in range(4)]
                yin = [sb("d_yin", (128, D), F32, pst) for i in range(4)]
                b_yin = Buf()
                ysum = sb("d_ysum", (128, D), F32, pst)
                ytmp = sb("d_ytmp", (128, D), F32, pst)
                b_y = Buf()
                st16 = sb("d_st16", (128, 4, 16), F32, pst)
                b_st = Buf()
                zb = sb("d_zb", (128, D), BF16, pst)
                b_zb = Buf()
                zT = [sb("d_zT", (128, TD), BF16, pst) for i in range(8)]
                b_zT = [Buf() for _ in range(8)]
                oT = [sb("d_oT", (128, TD), BF16, pst) for i in range(8)]
                b_oT = [Buf() for _ in range(8)]
                mT = [sb("d_mT", (128, TD), BF16, pst) for i in range(8)]
                b_mT = [Buf() for _ in range(8)]
                gts = [sb("d_gt", (128, TD), F32, pst) for i in range(4)]
                b_gts = [Buf() for _ in range(4)]
                mtmp = sb("d_mtmp", (128, TD), F32, pst)
                b_mtmp = Buf()
                wbr = [sb("d_wbr", (128, 8, 128), BF16, pst) for i in range(4)]
                b_wbr = [Buf() for _ in range(4)]
                wo = sb("d_wo", (128, 8, 512), BF16, pst)
                b_wo = Buf()
                gdraw = sb("d_gdraw", (128, TD + 2), F32, pst)
                gdsh = sb("d_gdsh", (128, TD), F32, pst)
                sgd = sb("d_sgd", (128, TD), BF16, pst)
                b_gd = Buf()
                ost = sb("d_ost", (128, D), F32, pst)
                b_ost = Buf()
                rr["gts"] = 0
                rr["wbr"] = 0
                for (t0, T) in tiles_of(L, TD):
                    blks = blocks_of(t0, T)
                    lo = max(t0 - 1, 0)
                    hi = min(t0 + T + 1, L)
                    if (t0 - 1 < 0) or (t0 + T + 1 > L):
                        S.emit("pool", lambda e: e.memset(gdraw[:], 0.0), writes=[b_gd])
                    S.dma("sp", gdraw[:, lo - (t0 - 1):hi - (t0 - 1)], RWs[s][26, :, lo:hi], writes=[b_gd])
                    S.emit("pool", lambda e: e.tensor_scalar(out=gdsh[:, :T], in0=gdraw[:, 1:T + 1], scalar1=mus[:, 2, 26:27], scalar2=None, op0=ALU.mult),
                           reads=[b_mus, b_gd], writes=[b_gd])
                    S.emit("dve", lambda e: e.scalar_tensor_tensor(out=gdsh[:, :T], in0=gdraw[:, 0:T], scalar=mus[:, 0, 26:27], in1=gdsh[:, :T],
                                                                    op0=ALU.mult, op1=ALU.add), reads=[b_mus, b_gd], writes=[b_gd])
                    S.emit("dve", lambda e: e.scalar_tensor_tensor(out=gdsh[:, :T], in0=gdraw[:, 2:T + 2], scalar=mus[:, 1, 26:27], in1=gdsh[:, :T],
                                                                    op0=ALU.mult, op1=ALU.add), reads=[b_mus, b_gd], writes=[b_gd])
                    S.emit("act", lambda e: e.activation(out=sgd[:, :T], in_=gdsh[:, :T], func=AF.Sigmoid), reads=[b_gd], writes=[b_gd])
                    for k in range(8):
                        S.dma("sp", oT[k][:, :T], OTs[s][k * 128:(k + 1) * 128, t0:t0 + T], writes=[b_oT[k]])
                    for bi, (tg, o, n) in enumerate(blks):
                        S.dma("sp", hts[bi][:n, :], Hs[s][tg:tg + n, :], writes=[b_h[bi]])
                        for i, src in enumerate([YSs[0][s], YSs[1][s], BNs[0][s], BNs[1][s]]):
                            S.dma("sp", yin[i][:n, :], src[tg:tg + n, :], writes=[b_yin])
                        gbank = [nxt("D", 2), nxt("D", 2)]
                        for hf in range(2):
                            S.emit("pe", lambda e: e.matmul(psD[gbank[hf]][:n, :], sgd[:, o:o + n], g2sb[:, hf * 512:(hf + 1) * 512], start=True, stop=True),
                                   reads=[b_gd, b_w2], writes=[bD[gbank[hf]]])
                        def v3(t):
                            return t[:n, :].rearrange("p (h v) -> p h v", v=64)
                        def bc(col):
                            return st16[:n, col, :].unsqueeze(2).to_broadcast([n, 16, 64])
                        S.emit("dve", lambda e: e.tensor_tensor(out=ysum[:n, :], in0=yin[0][:n, :], in1=yin[1][:n, :], op=ALU.add), reads=[b_yin], writes=[b_y])
                        S.emit("dve", lambda e: e.reduce_sum(out=st16[:n, 0, :], in_=v3(ysum), axis=AX.X), reads=[b_y], writes=[b_st])
                        S.emit("dve", lambda e: e.tensor_scalar(out=st16[:n, 0, :], in0=st16[:n, 0, :], scalar1=1.0 / 64, scalar2=None, op0=ALU.mult),
                               reads=[b_st], writes=[b_st])
                        S.emit("dve", lambda e: e.tensor_tensor(out=v3(ysum), in0=v3(ysum), in1=bc(0), op=ALU.subtract), reads=[b_y, b_st], writes=[b_y])
                        S.emit("pool", lambda e: e.tensor_tensor(out=ytmp[:n, :], in0=ysum[:n, :], in1=ysum[:n, :], op=ALU.mult), reads=[b_y], writes=[b_y])
                        S.emit("dve", lambda e: e.reduce_sum(out=st16[:n, 1, :], in_=v3(ytmp), axis=AX.X), reads=[b_y], writes=[b_st])
                        S.emit("act", lambda e: e.activation(out=st16[:n, 2, :], in_=st16[:n, 1, :], func=AF.Sqrt, bias=epsT[:n, 1:2], scale=1.0 / 64),
                               reads=[b_st, b_eps], writes=[b_st])
                        S.emit("dve", lambda e: e.reciprocal(out=st16[:n, 3, :], in_=st16[:n, 2, :]), reads=[b_st], writes=[b_st])
                        S.emit("dve", lambda e: e.tensor_tensor(out=v3(ysum), in0=v3(ysum), in1=bc(3), op=ALU.mult), reads=[b_y, b_st], writes=[b_y])
                        S.emit("pool", lambda e: e.tensor_tensor(out=ysum[:n, :], in0=ysum[:n, :], in1=lnx[:n, 0, :], op=ALU.mult), reads=[b_y, b_lnx], writes=[b_y])
                        S.emit("pool", lambda e: e.tensor_tensor(out=ysum[:n, :], in0=ysum[:n, :], in1=lnx[:n, 1, :], op=ALU.add), reads=[b_y, b_lnx], writes=[b_y])
                        S.emit("pool", lambda e: e.tensor_tensor(out=ytmp[:n, :], in0=yin[2][:n, :], in1=yin[3][:n, :], op=ALU.add), reads=[b_yin, b_y], writes=[b_y])
                        S.emit("dve", lambda e: e.tensor_tensor(out=ysum[:n, :], in0=ysum[:n, :], in1=ytmp[:n, :], op=ALU.add), reads=[b_y], writes=[b_y])
                        for hf in range(2):
                            S.emit("dve", lambda e: e.tensor_tensor(out=zb[:n, hf * 512:(hf + 1) * 512], in0=ysum[:n, hf * 512:(hf + 1) * 512],
                                                                    in1=psD[gbank[hf]][:n, :], op=ALU.mult),
                                   reads=[b_y, bD[gbank[hf]]], writes=[b_zb])
                        for cp in range(4):
                            tb_ = nxt("T", 2)
                            for cc in range(2):
                                c = cp * 2 + cc
                                S.emit("pe", lambda e: e.transpose(psT[tb_][:, cc * 512:cc * 512 + n], zb[:n, c * 128:(c + 1) * 128], ident[:n, :n]),
                                       reads=[b_zb, b_ident], writes=[bT[tb_]])
                            for cc in range(2):
                                c = cp * 2 + cc
                                S.emit("act", lambda e: e.copy(out=zT[c][:, o:o + n], in_=psT[tb_][:, cc * 512:cc * 512 + n]),
                                       reads=[bT[tb_]], writes=[b_zT[c]])
                    for j in range(8):
                        wa = nxt("wbr", 4)
                        S.dma("sp", wbr[wa][:], WABS[j], writes=[b_wbr[wa]])
                        wr_ = nxt("wbr", 4)
                        S.dma("sp", wbr[wr_][:], WRBS[j], writes=[b_wbr[wr_]])
                        ga = nxt("gts", 4)
                        S.dma("sp", gts[ga][:, :T], GTs[s][j, :, t0:t0 + T], writes=[b_gts[ga]])
                        gb_ = nxt("gts", 4)
                        S.dma("sp", gts[gb_][:, :T], GTs[s][8 + j, :, t0:t0 + T], writes=[b_gts[gb_]])
                        b1, b2 = nxt("G", 4), nxt("G", 4)
                        for k in range(8):
                            S.emit("pe", lambda e: e.matmul(psG[b1][:, :T], wbr[wa][:, k, :], oT[k][:, :T], start=(k == 0), stop=(k == 7)),
                                   reads=[b_wbr[wa], b_oT[k]], writes=[bG[b1]])
                        for k in range(8):
                            S.emit("pe", lambda e: e.matmul(psG[b2][:, :T], wbr[wr_][:, k, :], zT[k][:, :T], start=(k == 0), stop=(k == 7)),
                                   reads=[b_wbr[wr_], b_zT[k]], writes=[bG[b2]])
                        S.emit("dve", lambda e: e.tensor_tensor(out=mtmp[:, :T], in0=gts[ga][:, :T], in1=psG[b1][:, :T], op=ALU.mult),
                               reads=[b_gts[ga], bG[b1]], writes=[b_mtmp])
                        S.emit("dve", lambda e: e.tensor_tensor(out=gts[gb_][:, :T], in0=gts[gb_][:, :T], in1=psG[b2][:, :T], op=ALU.mult),
                               reads=[b_gts[gb_], bG[b2]], writes=[b_gts[gb_]])
                        S.emit("pool", lambda e: e.tensor_tensor(out=mT[j][:, :T], in0=mtmp[:, :T], in1=gts[gb_][:, :T], op=ALU.add),
                               reads=[b_mtmp, b_gts[gb_]], writes=[b_mT[j]])
                    for hf in range(2):
                        S.dma("sp", wo[:], WOS[hf], writes=[b_wo])
                        for bi, (tg, o, n) in enumerate(blks):
                            bank_ = nxt("D", 2)
                            for k in range(8):
                                S.emit("pe", lambda e: e.matmul(psD[bank_][:n, :], mT[k][:, o:o + n], wo[:, k, :], start=(k == 0), stop=(k == 7)),
                                       reads=[b_mT[k], b_wo], writes=[bD[bank_]])
                            S.emit("dve", lambda e: e.tensor_tensor(out=hts[bi][:n, hf * 512:(hf + 1) * 512], in0=hts[bi][:n, hf * 512:(hf + 1) * 512],
                                                                    in1=psD[bank_][:n, :], op=ALU.add), reads=[bD[bank_], b_h[bi]], writes=[b_h[bi]])
                    xts = hts[:len(blks)]
                    bxs = b_h[:len(blks)]
                    ffn(P, 1, blks, xts, bxs, 2)
                    for bi, (tg, o, n) in enumerate(blks):
                        xt, bx = hts[bi], b_h[bi]
                        S.emit("pool", lambda e: e.memset(P["ss"][:, 0:1], 0.0), writes=[P["b_ss"]])
                        S.emit("act", lambda e: e.activation(out=P["junk"][:n, :], in_=xt[:n, :], func=AF.Square, accum_out=P["ss"][:n, 0:1]),
                               reads=[bx], writes=[P["b_ss"], P["b_junk"]])
                        S.emit("act", lambda e: e.activation(out=P["ss"][:n, 1:2], in_=P["ss"][:n, 0:1], func=AF.Sqrt, bias=epsT[:n, 0:1], scale=1.0 / D),
                               reads=[b_eps], writes=[P["b_ss"]])
                        S.emit("dve", lambda e: e.reciprocal(out=P["ss"][:n, 2:3], in_=P["ss"][:n, 1:2]), writes=[P["b_ss"]])
                        S.emit("act", lambda e: e.activation(out=ost[:n, :], in_=xt[:n, :], func=AF.Copy, scale=P["ss"][:n, 2:3]),
                               reads=[bx, P["b_ss"]], writes=[b_ost])
                        S.emit("dve", lambda e: e.tensor_tensor(out=ost[:n, :], in0=ost[:n, :], in1=fin_g[:n, :], op=ALU.mult), reads=[b_ost, b_fin], writes=[b_ost])
                        lo_t = max(tg, NMETA)
                        if tg + n > lo_t:
                            S.dma("pool", ydst(s)[lo_t - NMETA:tg + n - NMETA, :], ost[lo_t - tg:n, :], reads=[b_ost])
                S.barrier()

        for s in range(NS):
            if "D" in phases:
                phase_D(s)
    return nc


_NC_CACHE = {}


def _rel_bucket_np(rel):
    try:
        import jax
        import jax.numpy as jnp
        with jax.default_device(jax.devices("cpu")[0]):
            r = jnp.asarray(rel, dtype=jnp.int32)
            nb = 16
            max_exact = 8
            n = jnp.abs(r)
            nf = jnp.maximum(n, 1).astype(jnp.float32)
            large = max_exact + (jnp.log(nf / max_exact) / math.log(128 / max_exact) * (nb - max_exact)).astype(jnp.int32)
            large = jnp.minimum(large, nb - 1)
            out = (r > 0).astype(jnp.int32) * nb + jnp.where(n < max_exact, n, large)
            return np.asarray(out)
    except Exception:
        rel = np.asarray(rel, dtype=np.int32)
        n = np.abs(rel)
        nf = np.maximum(n, 1).astype(np.float32)
        large = 8 + (np.log(nf / np.float32(8)) / np.float32(math.log(16.0)) * np.float32(8)).astype(np.int32)
        large = np.minimum(large, 15)
        return (rel > 0).astype(np.int32) * 16 + np.where(n < 8, n, large)


def _consts():
    p = np.arange(128)[:, None]
    m = np.arange(896)[None, :]
    bkt = _rel_bucket_np(p - m + 384).astype(np.float32)
    t = np.arange(64)
    lo = (t[None, :] < t[:, None]).astype(np.float32)
    up = lo.T.copy()
    eye = np.eye(64, dtype=np.float32)

    def bd(mm):
        z = np.zeros((128, 128), np.float32)
        z[:64, :64] = mm
        z[64:, 64:] = mm
        return z

    masks = np.stack([bd(lo), bd(up), bd(lo + eye), bd(up + eye)], axis=1)
    istack = np.concatenate([eye, eye], axis=0)
    bones = bd(np.ones((64, 64), np.float32))
    return {"c_ident": np.eye(128, dtype=np.float32), "c_bk": bkt, "c_masks": masks, "c_istack": istack, "c_bones": bones}


def kernel(**inputs):
    xp = np.ascontiguousarray(inputs["x_prompt"], dtype=np.float32)
    xs = np.ascontiguousarray(inputs["x_sample"], dtype=np.float32)
    S0, S1 = xp.shape[1], xs.shape[1]
    ncores = 8
    key = (S0, S1)
    if key not in _NC_CACHE:
        _NC_CACHE[key] = build_nc(S0, S1)
    nc = _NC_CACHE[key]
    shared = {k: np.ascontiguousarray(v, dtype=np.float32) for k, v in inputs.items()
              if k not in ("x_prompt", "x_sample")}
    shared.update(_consts())
    in_maps = []
    for c in range(ncores):
        m = dict(shared)
        m["x_prompt"] = xp[2 * c:2 * c + 2]
        m["x_sample"] = xs[2 * c:2 * c + 2]
        in_maps.append(m)
    res = run_bass_kernel_spmd(nc, in_maps, core_ids=list(range(ncores)))
    yp = np.concatenate([r["y_prompt"] for r in res.results], axis=0)
    ys = np.concatenate([r["y_sample"] for r in res.results], axis=0)
    return (yp.astype(np.float32), ys.astype(np.float32))
```
